# Optimizing a Trainium2 kernel written in Bass

```python
import math
import jax
import jax.numpy as jnp
from jax import lax
import numpy as np

D_MODEL = 2048
BATCH = 4
SEQ = 8192
DEPTH = 1

D_FF = 5632
EPS = 1e-6
SSM_WIDTH = 1024
SSM_GROUP = 16
SSM_GROUPS = SSM_WIDTH // SSM_GROUP
SSM_STATE = 64
DT_MIN = 1e-3
DT_MAX = 1e-1
N_HEADS = 16
N_KV = 4
HEADS_PER_KV = N_HEADS // N_KV
HEAD_DIM = 64
CMP_LEN = 32
CMP_STRIDE = 16
CMP_HID = 256
SEL_BLOCK = 64
N_SEL = 16
WINDOW = 512
Q_BLOCK = 128
SEL_FORCE = 1e4
NEG_INF = -1e30
N_BUCKETS = 32
MAX_DIST = 1024
Q_COLS = N_HEADS * HEAD_DIM
KV_COLS = N_KV * HEAD_DIM
NSA_GATE_COLS = 3 * N_HEADS
IN_COLS = SSM_WIDTH + Q_COLS + 6 * KV_COLS + NSA_GATE_COLS + 2 * D_MODEL

kernel_name = "hybrid_s5_nsa_macaron_block"


def _rmsnorm(x, g):
    xf = x.astype(jnp.float32)
    ms = jnp.mean(xf * xf, axis=-1, keepdims=True)
    return (xf * lax.rsqrt(ms + EPS)).astype(x.dtype) * g


def _swiglu(x, wg, wu, wd):
    return (jax.nn.silu(x @ wg) * (x @ wu)) @ wd


def _rel_bucket(dist):
    dist = jnp.maximum(dist, 0)
    max_exact = N_BUCKETS // 2
    d_f = jnp.maximum(dist, 1).astype(jnp.float32)
    large = max_exact + (jnp.log(d_f / max_exact) / math.log(MAX_DIST / max_exact)
                         * (N_BUCKETS - max_exact)).astype(jnp.int32)
    large = jnp.minimum(large, N_BUCKETS - 1)
    return jnp.where(dist < max_exact, dist, large)


def _masked_softmax(logits_f32, valid):
    return jax.nn.softmax(jnp.where(valid, logits_f32, NEG_INF), axis=-1)


def _ssm_combine(e1, e2):
    a1r, a1i, b1r, b1i = e1
    a2r, a2i, b2r, b2i = e2
    ar = a1r * a2r - a1i * a2i
    ai = a1r * a2i + a1i * a2r
    br = a2r * b1r - a2i * b1i + b2r
    bi = a2r * b1i + a2i * b1r + b2i
    return (ar, ai, br, bi)


def _s5(u, a_re, a_im, log_dt, b_re, b_im, c_re, c_im, d_skip):
    b_, t_, _ = u.shape
    ug = u.reshape(b_, t_, SSM_GROUPS, SSM_GROUP)
    dt = jnp.exp(log_dt)[:, None]
    lam_re = jnp.minimum(a_re, -1e-4)
    lam_im = a_im
    mag = jnp.exp(lam_re * dt)
    ab_re = mag * jnp.cos(lam_im * dt)
    ab_im = mag * jnp.sin(lam_im * dt)
    den = lam_re * lam_re + lam_im * lam_im
    n_re = ab_re - 1.0
    n_im = ab_im
    co_re = (n_re * lam_re + n_im * lam_im) / den
    co_im = (n_im * lam_re - n_re * lam_im) / den
    bb_re = co_re[..., None] * b_re - co_im[..., None] * b_im
    bb_im = co_re[..., None] * b_im + co_im[..., None] * b_re
    bu_re = jnp.einsum('btgc,gpc->tbgp', ug, bb_re)
    bu_im = jnp.einsum('btgc,gpc->tbgp', ug, bb_im)
    a_shape = (t_, 1, SSM_GROUPS, SSM_STATE)
    a_r = jnp.broadcast_to(ab_re[None, None], a_shape)
    a_i = jnp.broadcast_to(ab_im[None, None], a_shape)
    _, _, x_re, x_im = lax.associative_scan(_ssm_combine, (a_r, a_i, bu_re, bu_im), axis=0)
    y = (jnp.einsum('tbgp,gcp->btgc', x_re, c_re)
         - jnp.einsum('tbgp,gcp->btgc', x_im, c_im)
         + d_skip.reshape(SSM_GROUPS, SSM_GROUP) * ug)
    return y.reshape(b_, t_, SSM_WIDTH)


def _compress(raw, pos, w1, w2):
    b_, t_, _ = raw.shape
    n_sub = CMP_LEN // CMP_STRIDE
    nh = t_ // CMP_STRIDE
    nc = nh - n_sub + 1
    kk = raw.reshape(b_, nh, CMP_STRIDE, N_KV, HEAD_DIM)
    blocks = jnp.concatenate([kk[:, i:i + nc] for i in range(n_sub)], axis=2)
    blocks = blocks + pos[None, None, :, None, :]
    flat = blocks.transpose(0, 1, 3, 2, 4).reshape(b_, nc, N_KV, CMP_LEN * HEAD_DIM)
    return jax.nn.gelu(flat @ w1) @ w2


def _nsa(q, kc_raw, vc_raw, ks_raw, vs_raw, kw_raw, vw_raw, g_logits,
         cmp_pos, ck_w1, ck_w2, cv_w1, cv_w2, rel_bias):
    b_, t_, _ = q.shape
    G, HPG, dh = N_KV, HEADS_PER_KV, HEAD_DIM
    scale = dh ** -0.5
    f32 = jnp.float32
    qh = q.reshape(b_, t_, G, HPG, dh)
    gates = jax.nn.sigmoid(g_logits.reshape(b_, t_, G, HPG, 3))
    k_cmp = _compress(kc_raw, cmp_pos, ck_w1, ck_w2)
    v_cmp = _compress(vc_raw, cmp_pos, cv_w1, cv_w2)
    nc = k_cmp.shape[1]
    ns = t_ // SEL_BLOCK
    k_sel = min(N_SEL, ns)
    kblk = ks_raw.reshape(b_, ns, SEL_BLOCK, G, dh).transpose(0, 3, 1, 2, 4)
    vblk = vs_raw.reshape(b_, ns, SEL_BLOCK, G, dh).transpose(0, 3, 1, 2, 4)
    pad = ((0, 0), (WINDOW, 0), (0, 0), (0, 0))
    kwp = jnp.pad(kw_raw.reshape(b_, t_, G, dh), pad)
    vwp = jnp.pad(vw_raw.reshape(b_, t_, G, dh), pad)
    cmp_start = jnp.arange(nc) * CMP_STRIDE
    cmp_end = cmp_start + CMP_LEN - 1
    blk_ids = jnp.arange(ns)
    sel_start = blk_ids * SEL_BLOCK
    overlap = ((cmp_start[:, None] < sel_start[None, :] + SEL_BLOCK)
               & (cmp_start[:, None] + CMP_LEN > sel_start[None, :])).astype(f32)
    table_g = rel_bias.reshape(N_BUCKETS, G, HPG).transpose(1, 0, 2)
    bi = jnp.arange(b_)[:, None, None, None]
    gi = jnp.arange(G)[None, :, None, None]

    def dense_bias(dist):
        bias = rel_bias[_rel_bucket(dist)].reshape(dist.shape + (G, HPG))
        return bias.transpose(2, 3, 0, 1).astype(f32)

    def block(n):
        q0 = n * Q_BLOCK
        t = q0 + jnp.arange(Q_BLOCK)
        qb = lax.dynamic_slice_in_dim(qh, q0, Q_BLOCK, axis=1)
        gb = lax.dynamic_slice_in_dim(gates, q0, Q_BLOCK, axis=1)
        dist_c = t[:, None] - cmp_end[None, :]
        s_c = jnp.einsum('bqghd,bkgd->bghqk', qb, k_cmp).astype(f32) * scale + dense_bias(dist_c)
        p_c = _masked_softmax(s_c, dist_c >= 0) * (t >= CMP_LEN - 1).astype(f32)[:, None]
        o_c = jnp.einsum('bghqk,bkgd->bqghd', p_c.astype(v_cmp.dtype), v_cmp)
        imp = jnp.einsum('bghqk,kj->bgqj', p_c, overlap)
        valid_b = sel_start[None, :] <= t[:, None]
        cur = (t // SEL_BLOCK)[:, None]
        forced = valid_b & ((blk_ids[None, :] == 0) | (blk_ids[None, :] == cur)
                            | (blk_ids[None, :] == cur - 1))
        score = jnp.where(forced, SEL_FORCE, jnp.where(valid_b, imp, -SEL_FORCE))
        _, idx = lax.top_k(score, k_sel)
        ks_len = k_sel * SEL_BLOCK
        kg = kblk[bi, gi, idx].reshape(b_, G, Q_BLOCK, ks_len, dh)
        vg = vblk[bi, gi, idx].reshape(b_, G, Q_BLOCK, ks_len, dh)
        pos_s = (idx[..., None] * SEL_BLOCK + jnp.arange(SEL_BLOCK)).reshape(b_, G, Q_BLOCK, ks_len)
        dist_s = t[None, None, :, None] - pos_s
        bias_s = jnp.moveaxis(table_g[gi, _rel_bucket(dist_s)], -1, 2).astype(f32)
        s_s = jnp.einsum('bqghd,bgqkd->bghqk', qb, kg).astype(f32) * scale + bias_s
        p_s = _masked_softmax(s_s, (dist_s >= 0)[:, :, None])
        o_s = jnp.einsum('bghqk,bgqkd->bqghd', p_s.astype(vg.dtype), vg)
        kwb = lax.dynamic_slice_in_dim(kwp, q0, Q_BLOCK + WINDOW, axis=1)
        vwb = lax.dynamic_slice_in_dim(vwp, q0, Q_BLOCK + WINDOW, axis=1)
        pos_w = q0 - WINDOW + jnp.arange(Q_BLOCK + WINDOW)
        dist_w = t[:, None] - pos_w[None, :]
        valid_w = (dist_w >= 0) & (dist_w < WINDOW) & (pos_w >= 0)[None, :]
        s_w = jnp.einsum('bqghd,bkgd->bghqk', qb, kwb).astype(f32) * scale + dense_bias(dist_w)
        p_w = _masked_softmax(s_w, valid_w)
        o_w = jnp.einsum('bghqk,bkgd->bqghd', p_w.astype(vwb.dtype), vwb)
        return gb[..., 0:1] * o_c + gb[..., 1:2] * o_s + gb[..., 2:3] * o_w

    outs = lax.map(block, jnp.arange(t_ // Q_BLOCK))
    return jnp.moveaxis(outs, 0, 1).reshape(b_, t_, N_HEADS * dh)


def setup_inputs(seed: int = 0) -> dict:
    key = jax.random.key(seed)
    ks = iter(jax.random.split(key, 48))
    L = DEPTH

    def nrm(shape, scale):
        return jax.random.normal(next(ks), shape, jnp.float32) * scale

    def gain():
        return 1.0 + nrm((L, D_MODEL), 0.05)

    G, P, C = SSM_GROUPS, SSM_STATE, SSM_GROUP
    return {
        "x": nrm((BATCH, SEQ, D_MODEL), 1.0),
        "ffn1_pre_g": gain(),
        "ffn1_w_gate": nrm((L, D_MODEL, D_FF), D_MODEL ** -0.5),
        "ffn1_w_up": nrm((L, D_MODEL, D_FF), D_MODEL ** -0.5),
        "ffn1_w_down": nrm((L, D_FF, D_MODEL), D_FF ** -0.5),
        "ffn1_post_g": gain(),
        "mix_pre_g": gain(),
        "w_in": nrm((L, D_MODEL, IN_COLS), D_MODEL ** -0.5),
        "ssm_a_re": -0.5 + nrm((L, G, P), 0.01),
        "ssm_a_im": math.pi * jnp.arange(P, dtype=jnp.float32)[None, None, :] + nrm((L, G, P), 0.01),
        "ssm_log_dt": jax.random.uniform(next(ks), (L, G), jnp.float32,
                                         minval=math.log(DT_MIN), maxval=math.log(DT_MAX)),
        "ssm_b_re": nrm((L, G, P, C), (2 * C) ** -0.5),
        "ssm_b_im": nrm((L, G, P, C), (2 * C) ** -0.5),
        "ssm_c_re": nrm((L, G, C, P), (2 * P) ** -0.5),
        "ssm_c_im": nrm((L, G, C, P), (2 * P) ** -0.5),
        "ssm_d": nrm((L, SSM_WIDTH), 1.0),
        "ssm_glu_w1": nrm((L, SSM_WIDTH, D_MODEL), SSM_WIDTH ** -0.5),
        "ssm_glu_w2": nrm((L, SSM_WIDTH, D_MODEL), SSM_WIDTH ** -0.5),
        "cmp_pos": nrm((L, CMP_LEN, HEAD_DIM), 0.02),
        "cmp_k_w1": nrm((L, CMP_LEN * HEAD_DIM, CMP_HID), (CMP_LEN * HEAD_DIM) ** -0.5),
        "cmp_k_w2": nrm((L, CMP_HID, HEAD_DIM), CMP_HID ** -0.5),
        "cmp_v_w1": nrm((L, CMP_LEN * HEAD_DIM, CMP_HID), (CMP_LEN * HEAD_DIM) ** -0.5),
        "cmp_v_w2": nrm((L, CMP_HID, HEAD_DIM), CMP_HID ** -0.5),
        "nsa_w_o": nrm((L, N_HEADS * HEAD_DIM, D_MODEL), (N_HEADS * HEAD_DIM) ** -0.5),
        "w_out": nrm((L, D_MODEL, D_MODEL), D_MODEL ** -0.5),
        "mix_post_g": gain(),
        "ffn2_pre_g": gain(),
        "ffn2_w_gate": nrm((L, D_MODEL, D_FF), D_MODEL ** -0.5),
        "ffn2_w_up": nrm((L, D_MODEL, D_FF), D_MODEL ** -0.5),
        "ffn2_w_down": nrm((L, D_FF, D_MODEL), D_FF ** -0.5),
        "ffn2_post_g": gain(),
        "rel_bias": nrm((N_BUCKETS, N_HEADS), 0.5),
    }


def reference(x, ffn1_pre_g, ffn1_w_gate, ffn1_w_up, ffn1_w_down, ffn1_post_g,
              mix_pre_g, w_in, ssm_a_re, ssm_a_im, ssm_log_dt, ssm_b_re, ssm_b_im,
              ssm_c_re, ssm_c_im, ssm_d, ssm_glu_w1, ssm_glu_w2, cmp_pos,
              cmp_k_w1, cmp_k_w2, cmp_v_w1, cmp_v_w2, nsa_w_o, w_out, mix_post_g,
              ffn2_pre_g, ffn2_w_gate, ffn2_w_up, ffn2_w_down, ffn2_post_g, rel_bias):
    widths = [SSM_WIDTH, Q_COLS] + [KV_COLS] * 6 + [NSA_GATE_COLS, D_MODEL, D_MODEL]
    split_at = np.cumsum(widths)[:-1].tolist()
    h = x
    for l in range(DEPTH):
        f1 = _swiglu(_rmsnorm(h, ffn1_pre_g[l]), ffn1_w_gate[l], ffn1_w_up[l], ffn1_w_down[l])
        h = h + 0.5 * _rmsnorm(f1, ffn1_post_g[l])
        u = _rmsnorm(h, mix_pre_g[l])
        proj = u @ w_in[l]
        (u_ssm, q, kc, vc, ksl, vsl, kw, vw, g_nsa, g_a, g_b) = jnp.split(proj, split_at, axis=-1)
        y_ssm = jax.nn.gelu(_s5(u_ssm, ssm_a_re[l], ssm_a_im[l], ssm_log_dt[l], ssm_b_re[l],
                                ssm_b_im[l], ssm_c_re[l], ssm_c_im[l], ssm_d[l]))
        y_a = (y_ssm @ ssm_glu_w1[l]) * jax.nn.sigmoid(y_ssm @ ssm_glu_w2[l])
        o_nsa = _nsa(q, kc, vc, ksl, vsl, kw, vw, g_nsa, cmp_pos[l], cmp_k_w1[l], cmp_k_w2[l],
                     cmp_v_w1[l], cmp_v_w2[l], rel_bias)
        y_b = o_nsa @ nsa_w_o[l]
        mixed = (jax.nn.sigmoid(g_a) * y_a + jax.nn.sigmoid(g_b) * y_b) @ w_out[l]
        h = h + _rmsnorm(mixed, mix_post_g[l])
        f2 = _swiglu(_rmsnorm(h, ffn2_pre_g[l]), ffn2_w_gate[l], ffn2_w_up[l], ffn2_w_down[l])
        h = h + 0.5 * _rmsnorm(f2, ffn2_post_g[l])
    return h
```

```python
import math
from contextlib import ExitStack
import numpy as np
import ml_dtypes
import concourse.bass as bass
import concourse.mybir as mybir
from concourse.bass_utils import run_bass_kernel_spmd

F32 = mybir.dt.float32
BF16 = mybir.dt.bfloat16
ALU = mybir.AluOpType
AF = mybir.ActivationFunctionType
AX = mybir.AxisListType

D = 2048
DFF = 5632
EPS = 1e-6
NB = 4
SEQ = 8192
INC = 7728
BIG = 30000.0


class Ins:
    __slots__ = ("eng", "fn", "deps", "dma", "idx", "sig", "val", "semi", "prewait")

    def __init__(s, eng, fn, dma):
        s.eng = eng; s.fn = fn; s.dma = dma; s.deps = []; s.idx = 0; s.sig = False; s.val = 0
        s.semi = 0; s.prewait = None


class Sch:
    ENGS = ["pe", "dve", "act", "pool", "sp"]
    NS = 10

    def __init__(s, nc, es):
        s.nc = nc
        s.es = es
        s.eobj = {"pe": nc.tensor, "dve": nc.vector, "act": nc.scalar, "pool": nc.gpsimd, "sp": nc.sync}
        s.prog = {e: [] for e in s.ENGS}
        s.csem = {e: es.enter_context(nc.semaphore("cs_" + e)) for e in s.ENGS}
        s.dsem = {e: [es.enter_context(nc.semaphore("ds_%s%d" % (e, i))) for i in range(s.NS)]
                  for e in ("sp", "pool", "act")}
        s.ndma = {e: 0 for e in ("sp", "pool", "act")}
        s.dmas = {e: [] for e in ("sp", "pool", "act")}
        s.lastw = {}
        s.readers = {}
        s.known = {e: {f: -1 for f in s.ENGS} for e in s.ENGS}
        s.seen = {e: set() for e in s.ENGS}
        s.bar_from = {e: 0 for e in ("sp", "pool", "act")}

    def _dep(s, ins, d):
        if d is None or d is ins:
            return
        if d.dma:
            if id(d) in s.seen[ins.eng]:
                return
            s.seen[ins.eng].add(id(d))
            ins.deps.append(d)
        else:
            if d.eng == "pe" and ins.eng == "pe" and not ins.dma:
                return
            if s.known[ins.eng][d.eng] >= d.idx:
                return
            s.known[ins.eng][d.eng] = d.idx
            ins.deps.append(d)

    def _emit(s, eng, fn, r, w, dma):
        ins = Ins(eng, fn, dma)
        ins.idx = len(s.prog[eng])
        for k in list(r) + list(w):
            s._dep(ins, s.lastw.get(k))
        for k in w:
            best = {}
            for d in s.readers.get(k, ()):
                if d.dma:
                    s._dep(ins, d)
                elif d.eng not in best or best[d.eng].idx < d.idx:
                    best[d.eng] = d
            for d in best.values():
                s._dep(ins, d)
        if dma:
            n = s.ndma[eng]; s.ndma[eng] += 1
            ins.semi = n % s.NS; ins.val = (n // s.NS + 1) * 16
            if n >= s.NS:
                ins.prewait = s.dmas[eng][n - s.NS]
            s.dmas[eng].append(ins)
            ins.sig = True
        s.prog[eng].append(ins)
        for k in w:
            s.lastw[k] = ins; s.readers[k] = []
        for k in r:
            s.readers.setdefault(k, []).append(ins)
        return ins

    def I(s, eng, fn, r=(), w=()):
        return s._emit(eng, fn, r, w, False)

    def barrier(s):
        lasts = []
        for e in s.ENGS:
            for ins in reversed(s.prog[e]):
                if not ins.dma:
                    lasts.append(ins); break
        dm = [d for e in s.dmas for d in s.dmas[e][s.bar_from[e]:]]
        for e in s.dmas:
            s.bar_from[e] = len(s.dmas[e])
        for e in s.ENGS:
            ins = Ins(e, lambda eng: eng.nop(), False)
            ins.idx = len(s.prog[e])
            for d in lasts + dm:
                s._dep(ins, d)
            s.prog[e].append(ins)

    def DMA(s, eng, out, in_, r=(), w=(), **kw):
        return s._emit(eng, lambda e: e.dma_start(out=out, in_=in_, **kw), r, w, True)

    def finish(s, final):
        for e in s.ENGS:
            for ins in s.prog[e]:
                for d in ins.deps:
                    d.sig = True
        EPOCH = 30000
        s.csems = {}
        for e in s.ENGS:
            c = 0; ep = 0
            s.csems[e] = [s.csem[e]]
            for ins in s.prog[e]:
                if not ins.dma and ins.sig:
                    if c == EPOCH:
                        c = 0; ep += 1
                        s.csems[e].append(s.es.enter_context(s.nc.semaphore("cs_%s_%d" % (e, ep))))
                    c += 1; ins.val = c; ins.semi = ep
        s.maxval = {e: max([i.val for i in s.prog[e] if not i.dma] + [0]) for e in s.ENGS}

        def semof(d):
            return s.dsem[d.eng][d.semi] if d.dma else s.csems[d.eng][d.semi]

        with s.nc.Block() as block:
            def run(e):
                def body(eng):
                    for ins in s.prog[e]:
                        if ins.prewait is not None:
                            eng.wait_ge(semof(ins.prewait), ins.prewait.val)
                        for d in ins.deps:
                            eng.wait_ge(semof(d), d.val)
                        o = ins.fn(eng)
                        if ins.sig:
                            o.then_inc(semof(ins), 16 if ins.dma else 1)
                    if e == "sp":
                        for d in final:
                            eng.wait_ge(semof(d), d.val)
                return body
            block.tensor(run("pe"))
            block.vector(run("dve"))
            block.scalar(run("act"))
            block.gpsimd(run("pool"))
            block.sync(run("sp"))


def dram_ap(t, off, pat):
    return bass.AP(t.tensor if hasattr(t, "tensor") else t, off, pat)


class K:
    def __init__(k, T, stages=9, debug=False):
        k.T = T
        k.debug = debug
        k.NT = T // 512
        k.OWN = T // 2
        k.stages = stages
        k.nc = nc = bass.Bass("TRN2", target_bir_lowering=False)
        k.es = ExitStack()
        k.s = Sch(nc, k.es)
        k.inp = {}
        k.final = []

    def din(k, name, shape, dt=F32):
        a = k.nc.dram_tensor(name, list(shape), dt, kind="ExternalInput").ap()
        k.inp[name] = a
        return a

    def dscr(k, name, shape, dt):
        return k.nc.dram_tensor(name, list(shape), dt, kind="Internal").ap()

    def sb(k, es, name, shape, dt):
        return es.enter_context(k.nc.sbuf_tensor(name, list(shape), dt))

    def conv(k, name, shape, nsplit=1):
        src = k.din(name, shape)
        dst = k.dscr(name + "_b", shape, BF16)
        rows = shape[0]
        step = rows // nsplit
        for i in range(nsplit):
            k.s.DMA("pool", dst[i * step:(i + 1) * step, :], src[i * step:(i + 1) * step, :], w=[name + "_b"])
        return dst

    def rstd(k, src_ap, junk, ss, rs, key_r, scale_out=1.0):
        s = k.s
        s.I("act", lambda e: e.activation(out=junk[:], in_=src_ap, func=AF.Square, accum_out=ss[:]),
            r=key_r, w=[junk.name, ss.name])
        s.I("act", lambda e: e.activation(out=rs[:], in_=ss[:], func=AF.Sqrt, bias=k.epsb[:],
                                          scale=1.0 / (D * scale_out * scale_out)),
            r=[ss.name], w=[rs.name])
        s.I("dve", lambda e: e.reciprocal(out=rs[:], in_=rs[:]), r=[rs.name], w=[rs.name])

    def ffn(k, tag, src, dst, pre_g, post_g, wg, wu, wd, psb, skey="x", dkey="out", tiles=None, dst_off=0):
        s = k.s; nc = k.nc
        with ExitStack() as es:
            gpre = k.sb(es, tag + "gpre", [128, D], F32)
            gpost = k.sb(es, tag + "gpost", [128, D], F32)
            s.DMA("sp", gpre[:], pre_g.partition_broadcast(128), w=[gpre.name])
            s.DMA("sp", gpost[:], post_g.partition_broadcast(128), w=[gpost.name])
            xT = k.sb(es, tag + "xT", [128, 16, 512], BF16)
            hT = k.sb(es, tag + "hT", [128, 44, 512], BF16)
            wb = [k.sb(es, tag + "wb%d" % i, [128, 16, 512], BF16) for i in range(3)]
            f1 = k.sb(es, tag + "f1", [128, 4, D], F32)
            xs = [k.sb(es, tag + "xs%d" % i, [128, D], F32) for i in range(2)]
            xn = [k.sb(es, tag + "xn%d" % i, [128, D], BF16) for i in range(2)]
            junk = k.sb(es, tag + "junk", [128, D], F32)
            sg = [k.sb(es, tag + "sg%d" % i, [128, 512], F32) for i in range(2)]
            ss = [k.sb(es, tag + "ss%d" % i, [128, 1], F32) for i in range(2)]
            rs = [k.sb(es, tag + "rs%d" % i, [128, 1], F32) for i in range(2)]
            wbi = [0]

            def nextw():
                b = wb[wbi[0] % 3]; wbi[0] += 1
                return b

            xi = 0
            for tt in (tiles if tiles is not None else range(k.NT)):
                t0 = tt * 512
                for sub in range(4):
                    x_ = xs[xi % 2]; xn_ = xn[xi % 2]; ss_ = ss[xi % 2]; rs_ = rs[xi % 2]; xi += 1
                    rows = src[t0 + sub * 128: t0 + sub * 128 + 128, :]
                    s.DMA("sp", x_[:], rows, r=[skey], w=[x_.name])
                    k.rstd(x_[:], junk, ss_, rs_, [x_.name])
                    s.I("dve", lambda e, x_=x_, xn_=xn_, rs_=rs_: e.scalar_tensor_tensor(
                        out=xn_[:], in0=x_[:], scalar=rs_[:], in1=gpre[:], op0=ALU.mult, op1=ALU.mult),
                        r=[x_.name, rs_.name, gpre.name], w=[xn_.name])
                    for half in range(2):
                        pst = psb[6 + half]
                        pv = pst[:].bitcast(BF16)
                        for j in range(8):
                            kk = half * 8 + j
                            s.I("pe", lambda e, pv=pv, xn_=xn_, kk=kk, j=j: e.transpose(
                                out=pv[:, j * 128:(j + 1) * 128], in_=xn_[:, kk * 128:(kk + 1) * 128],
                                identity=k.ident[:]),
                                r=[xn_.name, "ident"], w=[pst.name])
                        eng = "act" if half == 0 else "dve"
                        if eng == "act":
                            s.I("act", lambda e, pv=pv, half=half, sub=sub: e.copy(
                                out=xT[:, half * 8:half * 8 + 8, sub * 128:(sub + 1) * 128],
                                in_=pv.rearrange("p (a b) -> p a b", a=8)),
                                r=[pst.name], w=[xT.name])
                        else:
                            s.I("dve", lambda e, pv=pv, half=half, sub=sub: e.tensor_copy(
                                out=xT[:, half * 8:half * 8 + 8, sub * 128:(sub + 1) * 128],
                                in_=pv.rearrange("p (a b) -> p a b", a=8)),
                                r=[pst.name], w=[xT.name])
                pi = 0
                for cc in range(11):
                    wgc = nextw()
                    s.DMA("sp", wgc[:], wg[:, cc * 512:(cc + 1) * 512].rearrange("(a p) c -> p a c", p=128),
                          r=[tag + "wg"], w=[wgc.name])
                    wuc = nextw()
                    s.DMA("sp", wuc[:], wu[:, cc * 512:(cc + 1) * 512].rearrange("(a p) c -> p a c", p=128),
                          r=[tag + "wu"], w=[wuc.name])
                    for fs in range(4):
                        f = cc * 4 + fs
                        pg = psb[(pi * 2) % 6]; pu = psb[(pi * 2 + 1) % 6]; sg_ = sg[pi % 2]; pi += 1
                        for kk in range(16):
                            s.I("pe", lambda e, pg=pg, wgc=wgc, kk=kk, fs=fs: e.matmul(
                                pg[:], lhsT=wgc[:, kk, fs * 128:(fs + 1) * 128], rhs=xT[:, kk, :],
                                start=(kk == 0), stop=(kk == 15)), r=[wgc.name, xT.name], w=[pg.name])
                        for kk in range(16):
                            s.I("pe", lambda e, pu=pu, wuc=wuc, kk=kk, fs=fs: e.matmul(
                                pu[:], lhsT=wuc[:, kk, fs * 128:(fs + 1) * 128], rhs=xT[:, kk, :],
                                start=(kk == 0), stop=(kk == 15)), r=[wuc.name, xT.name], w=[pu.name])
                        s.I("act", lambda e, pg=pg, sg_=sg_: e.activation(out=sg_[:], in_=pg[:], func=AF.Silu),
                            r=[pg.name], w=[sg_.name])
                        s.I("dve", lambda e, pu=pu, sg_=sg_, f=f: e.tensor_tensor(
                            out=hT[:, f, :], in0=sg_[:], in1=pu[:], op=ALU.mult),
                            r=[pu.name, sg_.name], w=[hT.name])
                for c4 in range(4):
                    base = 0 if c4 % 2 == 0 else 4
                    f0 = 0
                    for nf in (16, 16, 12):
                        wdc = nextw()
                        s.DMA("sp", wdc[:, 0:nf, :],
                              wd[f0 * 128:(f0 + nf) * 128, c4 * 512:(c4 + 1) * 512].rearrange("(a p) c -> p a c", p=128),
                              r=[tag + "wd"], w=[wdc.name])
                        for sub in range(4):
                            po = psb[base + sub]
                            for fi in range(nf):
                                f = f0 + fi
                                s.I("pe", lambda e, po=po, wdc=wdc, fi=fi, f=f, sub=sub: e.matmul(
                                    po[:], lhsT=hT[:, f, sub * 128:(sub + 1) * 128], rhs=wdc[:, fi, :],
                                    start=(f == 0), stop=(f == 43)), r=[wdc.name, hT.name], w=[po.name])
                        f0 += nf
                    for sub in range(4):
                        po = psb[base + sub]
                        if sub % 2 == 0:
                            s.I("act", lambda e, po=po, sub=sub, c4=c4: e.copy(
                                out=f1[:, sub, c4 * 512:(c4 + 1) * 512], in_=po[:]),
                                r=[po.name], w=[f1.name + str(sub)])
                        else:
                            s.I("dve", lambda e, po=po, sub=sub, c4=c4: e.tensor_copy(
                                out=f1[:, sub, c4 * 512:(c4 + 1) * 512], in_=po[:]),
                                r=[po.name], w=[f1.name + str(sub)])
                for sub in range(4):
                    x_ = xs[xi % 2]; ss_ = ss[xi % 2]; rs_ = rs[xi % 2]; xi += 1
                    rows = src[t0 + sub * 128: t0 + sub * 128 + 128, :]
                    s.DMA("sp", x_[:], rows, r=[skey], w=[x_.name])
                    k.rstd(f1[:, sub, :], junk, ss_, rs_, [f1.name + str(sub)], scale_out=0.5)
                    s.I("dve", lambda e, sub=sub, rs_=rs_: e.scalar_tensor_tensor(
                        out=f1[:, sub, :], in0=f1[:, sub, :], scalar=rs_[:], in1=gpost[:],
                        op0=ALU.mult, op1=ALU.mult),
                        r=[f1.name + str(sub), rs_.name, gpost.name], w=[f1.name + str(sub)])
                    s.I("pool", lambda e, sub=sub, x_=x_: e.tensor_tensor(
                        out=x_[:], in0=f1[:, sub, :], in1=x_[:], op=ALU.add),
                        r=[f1.name + str(sub), x_.name], w=[x_.name])
                    d = s.DMA("sp", dst[t0 - dst_off + sub * 128: t0 - dst_off + sub * 128 + 128, :], x_[:], r=[x_.name], w=[dkey])
                    k.final.append(d)
            s.barrier()


    @staticmethod
    def _n(xs):
        return [x if isinstance(x, str) else x.name for x in xs]

    def TT(k, eng, out, in0, in1, op, r, w):
        k.s.I(eng, lambda e: e.tensor_tensor(out=out, in0=in0, in1=in1, op=op), k._n(r), k._n(w))

    def TS(k, eng, out, in0, s1, s2, op0, op1, r, w):
        if op1 is None:
            k.s.I(eng, lambda e: e.tensor_scalar(out=out, in0=in0, scalar1=s1, scalar2=None, op0=op0), k._n(r), k._n(w))
        else:
            k.s.I(eng, lambda e: e.tensor_scalar(out=out, in0=in0, scalar1=s1, scalar2=s2, op0=op0, op1=op1), k._n(r), k._n(w))

    def STT(k, out, in0, scalar, in1, op0, op1, r, w):
        k.s.I("dve", lambda e: e.scalar_tensor_tensor(out=out, in0=in0, scalar=scalar, in1=in1, op0=op0, op1=op1),
              k._n(r), k._n(w))

    def ACT(k, out, in_, func, r, w, **kw):
        k.s.I("act", lambda e: e.activation(out=out, in_=in_, func=func, **kw), k._n(r), k._n(w))

    def MM(k, out, lhsT, rhs, start, stop, r, w):
        k.s.I("pe", lambda e: e.matmul(out, lhsT=lhsT, rhs=rhs, start=start, stop=stop), k._n(r), k._n(w))

    def TR(k, out, in_, r, w):
        k.s.I("pe", lambda e: e.transpose(out=out, in_=in_, identity=k.ident[:]), k._n(r) + ["ident"], k._n(w))

    def CP(k, eng, out, in_, r, w):
        if eng == "act":
            k.s.I("act", lambda e: e.copy(out=out, in_=in_), k._n(r), k._n(w))
        else:
            k.s.I(eng, lambda e: e.tensor_copy(out=out, in_=in_), k._n(r), k._n(w))

    def LD(k, out, in_, r, w, eng="sp", **kw):
        return k.s.DMA(eng, out, in_, k._n(r), k._n(w), **kw)

    def gelu_tanh(k, y_, a_, b_, out_ap, wname):
        k.ACT(a_[:], y_[:], AF.Square, [y_], [a_])
        k.TS("dve", a_[:], a_[:], 0.044715, 1.0, ALU.mult, ALU.add, [a_], [a_])
        k.TT("dve", b_[:], a_[:], y_[:], ALU.mult, [a_, y_], [b_])
        k.ACT(b_[:], b_[:], AF.Sigmoid, [b_], [b_], scale=1.5957691216057308)
        k.TT("dve", out_ap, b_[:], y_[:], ALU.mult, [b_, y_], [wname])

    def prenormT(k, src, skey, t0, gpre, xT, bufs, psb):
        s = k.s
        xs, xn, junk, ss, rs, ctr = bufs
        for sub in range(4):
            i = ctr[0] % 2; ctr[0] += 1
            x_ = xs[i]; xn_ = xn[i]; ss_ = ss[i]; rs_ = rs[i]
            k.LD(x_[:], src[t0 + sub * 128: t0 + sub * 128 + 128, :], [skey], [x_])
            k.rstd(x_[:], junk, ss_, rs_, [x_.name])
            k.STT(xn_[:], x_[:], rs_[:], gpre[:], ALU.mult, ALU.mult, [x_, rs_, gpre], [xn_])
            for half in range(2):
                pst = psb[6 + half]
                pv = pst[:].bitcast(BF16)
                for j in range(8):
                    kk = half * 8 + j
                    k.TR(pv[:, j * 128:(j + 1) * 128], xn_[:, kk * 128:(kk + 1) * 128], [xn_], [pst])
                k.CP("act" if half == 0 else "dve", xT[:, half * 8:half * 8 + 8, sub * 128:(sub + 1) * 128],
                     pv.rearrange("p (a b) -> p a b", a=8), [pst], [xT])

    def down_post(k, tag, hT, KT, wd, wkey, src, skey, dst, dkey, gpost, scale_out, t0, f1, wb, wbi, bufs, psb):
        s = k.s
        xs, xn, junk, ss, rs, ctr = bufs
        groups = []
        f0 = 0
        while f0 < KT:
            nf = min(16, KT - f0); groups.append((f0, nf)); f0 += nf
        for c4 in range(4):
            base = 0 if c4 % 2 == 0 else 4
            for (f0, nf) in groups:
                wdc = wb[wbi[0] % 3]; wbi[0] += 1
                k.LD(wdc[:, 0:nf, :], wd[f0 * 128:(f0 + nf) * 128, c4 * 512:(c4 + 1) * 512].rearrange("(a p) c -> p a c", p=128),
                     [wkey], [wdc])
                for sub in range(4):
                    po = psb[base + sub]
                    for fi in range(nf):
                        f = f0 + fi
                        k.MM(po[:], hT[:, f, sub * 128:(sub + 1) * 128], wdc[:, fi, :], f == 0, f == KT - 1, [wdc, hT], [po])
            for sub in range(4):
                po = psb[base + sub]
                k.CP("act" if sub % 2 == 0 else "dve", f1[:, sub, c4 * 512:(c4 + 1) * 512], po[:], [po], [f1.name + str(sub)])
        for sub in range(4):
            i = ctr[0] % 2; ctr[0] += 1
            x_ = xs[i]; ss_ = ss[i]; rs_ = rs[i]
            k.LD(x_[:], src[t0 + sub * 128: t0 + sub * 128 + 128, :], [skey], [x_])
            k.rstd(f1[:, sub, :], junk, ss_, rs_, [f1.name + str(sub)], scale_out=scale_out)
            k.STT(f1[:, sub, :], f1[:, sub, :], rs_[:], gpost[:], ALU.mult, ALU.mult,
                  [f1.name + str(sub), rs_, gpost], [f1.name + str(sub)])
            k.TT("pool", x_[:], f1[:, sub, :], x_[:], ALU.add, [f1.name + str(sub), x_], [x_])
            d = k.LD(dst[t0 + sub * 128: t0 + sub * 128 + 128, :], x_[:], [x_], [dkey])
            k.final.append(d)

    def normbufs(k, es, tag):
        xs = [k.sb(es, tag + "xs%d" % i, [128, D], F32) for i in range(2)]
        xn = [k.sb(es, tag + "xn%d" % i, [128, D], BF16) for i in range(2)]
        junk = k.sb(es, tag + "junk", [128, D], F32)
        ss = [k.sb(es, tag + "ss%d" % i, [128, 1], F32) for i in range(2)]
        rs = [k.sb(es, tag + "rs%d" % i, [128, 1], F32) for i in range(2)]
        return (xs, xn, junk, ss, rs, [0])

    def proj(k, h1, gmix, win, sc, psb):
        s = k.s; T = k.T
        with ExitStack() as es:
            gpre = k.sb(es, "pj_g", [128, D], F32)
            k.LD(gpre[:], gmix.partition_broadcast(128), [], [gpre])
            uT = k.sb(es, "pj_uT", [128, 16, 512], BF16)
            wb = [k.sb(es, "pj_wb%d" % i, [128, 16, 512], BF16) for i in range(3)]
            bufs = k.normbufs(es, "pj_")
            s32 = [k.sb(es, "pj_s32%d" % i, [128, 512], F32) for i in range(2)]
            s16 = [k.sb(es, "pj_s16%d" % i, [128, 512], BF16) for i in range(4)]
            sg = [k.sb(es, "pj_sg%d" % i, [128, 48], F32) for i in range(2)]
            cnt = {"w": 0, "p": 0, "a": 0, "b": 0, "g": 0, "e": 0}

            def nps():
                p = psb[cnt["p"] % 6]; cnt["p"] += 1
                return p

            chunks = [(c * 512, 512) for c in range(7)] + [(3584, 48)] + [(3632 + 512 * i, 512) for i in range(8)]
            for tt in range(k.NT):
                t0 = tt * 512
                k.prenormT(h1, "h1", t0, gpre, uT, bufs, psb)
                for ci, (c0, wd_) in enumerate(chunks):
                    if t0 < k.OWN and ci not in (0, 1, 4, 5, 6):
                        continue
                    wc = wb[cnt["w"] % 3]; cnt["w"] += 1
                    k.LD(wc[:, :, 0:wd_], win[:, c0:c0 + wd_].rearrange("(a p) c -> p a c", p=128), ["win"], [wc])

                    def fm(off, M, kind, dst):
                        p = nps()
                        for kk in range(16):
                            k.MM(p[0:M, :], wc[:, kk, off:off + M], uT[:, kk, :], kk == 0, kk == 15, [wc, uT], [p])
                        if kind == "us":
                            st = s32[cnt["a"] % 2]; cnt["a"] += 1
                            cnt["e"] += 1
                            k.CP("act" if cnt["e"] % 2 else "dve", st[0:M, :], p[0:M, :], [p], [st])
                        else:
                            st = s16[cnt["b"] % 4]; cnt["b"] += 1
                            if kind == "q":
                                k.s.I("act", lambda e, st=st, p=p, M=M: e.mul(out=st[0:M, :], in_=p[0:M, :], mul=0.125), [p.name], [st.name])
                            elif kind == "sig":
                                k.ACT(st[0:M, :], p[0:M, :], AF.Sigmoid, [p], [st])
                            else:
                                k.CP("dve", st[0:M, :], p[0:M, :], [p], [st])
                        k.LD(dst, st[0:M, :], [st], ["pjout"])

                    def tm(off, N, kind, dst_t):
                        for sub in range(4):
                            p = nps()
                            for kk in range(16):
                                k.MM(p[:, 0:N], uT[:, kk, sub * 128:(sub + 1) * 128], wc[:, kk, off:off + N], kk == 0, kk == 15, [wc, uT], [p])
                            if kind == "sig":
                                st = sg[cnt["g"] % 2]; cnt["g"] += 1
                                k.ACT(st[:, 0:N], p[:, 0:N], AF.Sigmoid, [p], [st])
                            else:
                                st = s16[cnt["b"] % 4]; cnt["b"] += 1
                                k.CP("dve", st[:, 0:N], p[:, 0:N], [p], [st])
                            k.LD(dst_t[t0 + sub * 128: t0 + sub * 128 + 128, :], st[:, 0:N], [st], ["pjout"])

                    ts_ = slice(t0, t0 + 512)
                    if ci in (0, 1):
                        for j in range(4):
                            fm(j * 128, 128, "us", sc["usT"][(ci * 4 + j) * 128:(ci * 4 + j + 1) * 128, ts_])
                    elif ci in (2, 3):
                        for j in range(8):
                            fm(j * 64, 64, "q", sc["qT"][(ci - 2) * 8 + j, :, ts_])
                    elif ci == 4:
                        for g in range(4):
                            fm(g * 64, 64, "k", sc["kcT"][g, :, ts_])
                        for g in range(4):
                            fm(256 + g * 64, 64, "k", sc["vcT"][g, :, ts_])
                    elif ci == 5:
                        for g in range(4):
                            fm(g * 64, 64, "k", sc["ksT"][g, :, ts_])
                        tm(256, 256, "k", sc["Vs"])
                    elif ci == 6:
                        for g in range(4):
                            fm(g * 64, 64, "k", sc["kwT"][g, :, ts_])
                        tm(256, 256, "k", sc["Vw"])
                    elif ci == 7:
                        tm(0, 48, "sig", sc["gates"])
                    elif ci < 12:
                        for j in range(4):
                            fm(j * 128, 128, "sig", sc["gaT"][((ci - 8) * 4 + j) * 128:((ci - 8) * 4 + j + 1) * 128, ts_])
                    else:
                        for j in range(4):
                            fm(j * 128, 128, "sig", sc["gbT"][((ci - 12) * 4 + j) * 128:((ci - 12) * 4 + j + 1) * 128, ts_])
            s.barrier()


    def compress(k, es_out, sc, w, psb):
        s = k.s; T = k.T
        NC = T // 16 - 1
        NIT = (NC + 127) // 128
        KcT = k.sb(es_out, "KcT", [64, 4, 512], BF16)
        Vc = k.sb(es_out, "Vc", [128, 4, 4, 64], BF16)
        with ExitStack() as es:
            raw = k.sb(es, "cp_raw", [64, T], BF16)
            w1 = k.sb(es, "cp_w1", [64, 32, 256], BF16)
            w2 = k.sb(es, "cp_w2", [128, 2, 64], BF16)
            posf = k.sb(es, "cp_posf", [64, 32], F32)
            posb = k.sb(es, "cp_posb", [64, 32], BF16)
            bias = k.sb(es, "cp_bias", [128, 2], F32)
            y_ = k.sb(es, "cp_y", [128, 512], F32); a_ = k.sb(es, "cp_a", [128, 512], F32); b_ = k.sb(es, "cp_b", [128, 512], F32)
            hid = k.sb(es, "cp_hid", [128, 2, 512], BF16)
            k.LD(posf[:], k.din("cmp_posT", [64, 32]), [], [posf])
            k.CP("dve", posb[:], posf[:], [posf], [posb])
            for kind in range(2):
                rawsc = sc["kcT"] if kind == 0 else sc["vcT"]
                w1d, w2d = (w["ck1"], w["ck2"]) if kind == 0 else (w["cv1"], w["cv2"])
                k.LD(w1[:], w1d.rearrange("(l d) c -> d l c", d=64), ["cw"], [w1])
                k.LD(w2[:], w2d.rearrange("(a p) d -> p a d", p=128), ["cw"], [w2])
                for ht in range(2):
                    p = psb[ht]
                    for l in range(32):
                        k.MM(p[:, 0:1], w1[:, l, ht * 128:(ht + 1) * 128], posb[:, l:l + 1], l == 0, l == 31, [w1, posb], [p])
                    k.CP("dve", bias[:, ht:ht + 1], p[:, 0:1], [p], [bias])
                for g in range(4):
                    k.LD(raw[:], rawsc[g, :, :], ["pjout"], [raw])
                    for ht in range(2):
                        p = psb[2 + ht]
                        for l in range(32):
                            k.MM(p[:, 0:NC], w1[:, l, ht * 128:(ht + 1) * 128], raw[:, l: l + 16 * (NC - 1) + 1: 16],
                                 l == 0, l == 31, [w1, raw], [p])
                        k.TS("dve", y_[:, 0:NC], p[:, 0:NC], bias[:, ht:ht + 1], None, ALU.add, None, [p, bias], [y_])
                        k.gelu_tanh(y_, a_, b_, hid[:, ht, :], hid.name)
                    if kind == 0:
                        p = psb[4]
                        for ht in range(2):
                            k.MM(p[0:64, 0:NC], w2[:, ht, :], hid[:, ht, 0:NC], ht == 0, ht == 1, [w2, hid], [p])
                        k.CP("act", KcT[:, g, 0:NC], p[0:64, 0:NC], [p], [KcT])
                    else:
                        for it in range(NIT):
                            rows = min(128, NC - it * 128)
                            p = psb[4 + it % 2]
                            for ht in range(2):
                                k.MM(p[0:rows, 0:64], hid[:, ht, it * 128: it * 128 + rows], w2[:, ht, :], ht == 0, ht == 1, [w2, hid], [p])
                            k.CP("act", Vc[0:rows, it, g, :], p[0:rows, 0:64], [p], [Vc])
            s.barrier()
        return KcT, Vc

    def bias_tables(k, psb):
        s = k.s
        WS, WW = 1152, 768
        SKS = k.dscr("SKS", [16, 128 * (WS + 1)], F32)
        SKW = k.dscr("SKW", [16, 128 * (WW + 1)], F32)
        rb = k.din("rel_bias", [32, 16])
        with ExitStack() as es:
            ohs = k.sb(es, "bt_ohs", [33, WS], F32); ohw = k.sb(es, "bt_ohw", [33, WW], F32)
            k.LD(ohs[:], k.din("c_ohs", [33, WS]), [], [ohs]); k.LD(ohw[:], k.din("c_ohw", [33, WW]), [], [ohw])
            rbw = k.sb(es, "bt_rbw", [33, 16], F32); rbs = k.sb(es, "bt_rbs", [33, 16], F32); r31 = k.sb(es, "bt_r31", [33, 16], F32)
            s.I("dve", lambda e: e.memset(rbw[:], 1.0), w=[rbw.name])
            s.I("dve", lambda e: e.memset(rbs[:], 1.0), w=[rbs.name])
            s.I("dve", lambda e: e.memset(r31[:], 0.0), w=[r31.name])
            k.LD(rbw[0:32, :], rb, [rbw], [rbw])
            k.LD(r31[0:32, :], rb[31, :].partition_broadcast(32), [r31], [r31])
            k.TT("dve", rbs[0:32, :], rbw[0:32, :], r31[0:32, :], ALU.subtract, [rbw, r31], [rbs])
            ones = k.sb(es, "bt_ones", [33, 128], F32)
            s.I("dve", lambda e: e.memset(ones[:], 1.0), w=[ones.name])
            lh = [k.sb(es, "bt_lh%d" % i, [33, 128], F32) for i in range(2)]
            rep = [k.sb(es, "bt_rep%d" % i, [128, WS], F32) for i in range(2)]
            n = 0
            for (rbt, oh, W, SK) in ((rbs, ohs, WS, SKS), (rbw, ohw, WW, SKW)):
                for h in range(16):
                    l_ = lh[n % 2]; r_ = rep[n % 2]; n += 1
                    k.TS("dve", l_[:], ones[:], rbt[:, h:h + 1], None, ALU.mult, None, [ones, rbt], [l_])
                    for c0 in range(0, W, 512):
                        wd_ = min(512, W - c0)
                        p = psb[(c0 // 512) % 4 + 4 * (n % 2)]
                        k.MM(p[:, 0:wd_], l_[:], oh[:, c0:c0 + wd_], True, True, [l_, oh], [p])
                        k.CP("act" if (c0 // 512) % 2 else "dve", r_[:, c0:c0 + wd_], p[:, 0:wd_], [p], [r_])
                    dst = bass.AP(SK.tensor, h * 128 * (W + 1), [[W + 1, 128], [1, W]])
                    k.LD(dst, r_[:, 0:W], [r_], ["SK"])
            s.barrier()
        return SKS, SKW, WS, WW

    def nsa(k, sc, KcT, Vc, SKS, SKW, WS, WW, psb):
        s = k.s; T = k.T
        NC = T // 16 - 1
        NQ = T // 128
        NKT = T // 128
        LAGB, LAGC, NBUF = 2, 4, 6
        with ExitStack() as es:
            KsT = k.sb(es, "ns_KsT", [64, T], BF16); KwT = k.sb(es, "ns_KwT", [64, T], BF16)
            Vs = k.sb(es, "ns_Vs", [128, NKT, 64], BF16); Vw = k.sb(es, "ns_Vw", [128, NKT, 64], BF16)
            TS_ = k.sb(es, "ns_TS", [128, 4, 1024], F32); TW_ = k.sb(es, "ns_TW", [128, 4, 640], F32)
            TC_ = k.sb(es, "ns_TC", [128, 4, 58], F32)
            b31 = k.sb(es, "ns_b31", [128, 16], F32)
            k.LD(b31[:], k.inp["rel_bias"][31, :].partition_broadcast(128), [], [b31])
            vmrel = k.sb(es, "ns_vm", [128, 256], F32); acrel = k.sb(es, "ns_ac", [128, 256], F32)
            k.LD(vmrel[:], k.din("c_vmrel", [128, 256]), [], [vmrel]); k.LD(acrel[:], k.din("c_acrel", [128, 256]), [], [acrel])
            ccm = k.sb(es, "ns_ccm", [128, 128], F32); cca = k.sb(es, "ns_cca", [128, 128], F32)
            cmpm = k.sb(es, "ns_cmpm", [128, 512], F32); wmk = k.sb(es, "ns_wmk", [128, 1], F32)
            k.LD(ccm[:], k.din("pc_cm", [128, 128]), [], [ccm]); k.LD(cca[:], k.din("pc_ca", [128, 128]), [], [cca])
            k.LD(cmpm[:], k.din("pc_cmpmask", [128, 512]), [], [cmpm]); k.LD(wmk[:], k.din("pc_wmask", [128, 1]), [], [wmk])
            qT4 = [k.sb(es, "ns_q%d" % i, [64, 4, 128], BF16) for i in range(2)]
            gt = [k.sb(es, "ns_gt%d" % i, [128, 48], F32) for i in range(2)]
            pcn = k.sb(es, "ns_pcn", [128, 4, 512], F32)
            s.I("pool", lambda e: e.memset(pcn[:], 0.0), w=[pcn.name])
            tmp = [k.sb(es, "ns_tmp%d" % i, [128, 512], F32) for i in range(NBUF)]
            P_ = [k.sb(es, "ns_P%d" % i, [128, 512], BF16) for i in range(NBUF)]
            pT = [k.sb(es, "ns_pT%d" % i, [128, 4, 128], BF16) for i in range(NBUF)]
            ps4 = k.sb(es, "ns_ps4", [128, 512], F32)
            imp = k.sb(es, "ns_imp", [128, 128], F32); sc1 = k.sb(es, "ns_sc1", [128, 128], F32); sc2 = k.sb(es, "ns_sc2", [128, 128], F32)
            m8a = k.sb(es, "ns_m8a", [128, 8], F32); m8b = k.sb(es, "ns_m8b", [128, 8], F32)
            mk = [k.sb(es, "ns_mk%d" % i, [128, 128], F32) for i in range(2)]
            rsc = [k.sb(es, "ns_rsc%d" % i, [128, 4], F32) for i in range(2)]
            rss = [k.sb(es, "ns_rss%d" % i, [128, 4, 32], F32) for i in range(2)]
            rsw = [k.sb(es, "ns_rsw%d" % i, [128, 4, 2], F32) for i in range(2)]
            rt = [k.sb(es, "ns_rt%d" % i, [128, 4, 4], F32) for i in range(2)]
            oacc = k.sb(es, "ns_oacc", [128, 256], F32)
            ob = [k.sb(es, "ns_ob%d" % i, [128, 256], BF16) for i in range(2)]
            cn = {"o": 0}

            def bc64(ap2):
                return bass.AP(ap2.tensor, ap2.offset, [list(ap2.ap[0]), list(ap2.ap[1]), [0, 64]])

            for g in range(4):
                k.LD(KsT[:], sc["ksT"][g, :, :], ["pjout"], [KsT]); k.LD(KwT[:], sc["kwT"][g, :, :], ["pjout"], [KwT])
                k.LD(Vs[:], sc["Vs"][:, g * 64:(g + 1) * 64].rearrange("(n p) d -> p n d", p=128), ["pjout"], [Vs])
                k.LD(Vw[:], sc["Vw"][:, g * 64:(g + 1) * 64].rearrange("(n p) d -> p n d", p=128), ["pjout"], [Vw])
                for hl in range(4):
                    h = 4 * g + hl
                    k.LD(TS_[:, hl, :], bass.AP(SKS.tensor, h * 128 * (WS + 1) + 127, [[WS, 128], [1, 1024]]), ["SK"], [TS_])
                    k.LD(TW_[:, hl, :], bass.AP(SKW.tensor, h * 128 * (WW + 1) + 127, [[WW, 128], [1, 640]]), ["SK"], [TW_])
                    k.LD(TC_[:, hl, :], bass.AP(SKS.tensor, h * 128 * (WS + 1) + 238, [[WS, 128], [16, 58]]), ["SK"], [TC_],
                         allow_slow_non_contiguous=True)
                items = []
                for n in range(NQ // 2, NQ):
                    q0 = 128 * n; par = n % 2
                    ncv = min(NC, 8 * n + 7)
                    blk = []
                    for hl in range(4):
                        blk.append(dict(kind="c", n=n, hl=hl, wdt=ncv))
                    w0 = max(0, q0 - 512)
                    pieces = []
                    a_ = w0
                    while a_ < q0 + 128:
                        b_ = min(a_ + 512, q0 + 128)
                        if a_ < k.OWN < b_:
                            b_ = k.OWN
                        pieces.append((a_, b_ - a_)); a_ = b_
                    assert len(pieces) <= 2
                    for hl in range(4):
                        for pi_, (s0, wdt) in enumerate(pieces):
                            blk.append(dict(kind="w", n=n, hl=hl, s0=s0, wdt=wdt, pi=pi_, first=pi_ == 0, last=pi_ == len(pieces) - 1,
                                            npc=len(pieces)))
                    nch = n // 4 + 1
                    for hl in range(4):
                        for c in range(nch):
                            s0 = 512 * c
                            blk.append(dict(kind="s", n=n, hl=hl, s0=s0, wdt=min(512, q0 + 128 - s0), c=c, first=c == 0, last=c == nch - 1, nch=nch))
                    blk[0]["load"] = True
                    blk[3]["topk"] = True
                    blk[-1]["combine"] = True
                    blk[-1]["npc_w"] = len(pieces)
                    items += blk
                for i, it in enumerate(items):
                    it["i"] = i

                def stageA(it):
                    n = it["n"]; hl = it["hl"]; h = 4 * g + hl; q0 = 128 * n; par = n % 2; i = it["i"]
                    q_ = qT4[par]; g_ = gt[par]
                    if it.get("load"):
                        k.LD(q_[:], sc["qT"][4 * g:4 * g + 4, :, q0:q0 + 128].rearrange("h d t -> d h t"), ["pjout"], [q_])
                        k.LD(g_[:], sc["gates"][q0:q0 + 128, :], ["pjout"], [g_])
                    p = psb[i % 2]; t_ = tmp[i % NBUF]; Pt = P_[i % NBUF]
                    wdt = it["wdt"]
                    if it["kind"] == "c":
                        ncv = wdt
                        i_lo = max(0, 8 * n - 51); c_lo = i_lo - 8 * n + 51
                        k.MM(p[:, 0:ncv], q_[:, hl, :], KcT[:, g, 0:ncv], True, True, [q_, KcT], [p])
                        k.TT("dve", t_[:, 0:ncv], p[:, 0:ncv], cmpm[:, 0:ncv], ALU.add, [p, cmpm], [t_])
                        k.TT("pool", t_[:, i_lo:ncv], t_[:, i_lo:ncv], TC_[:, hl, c_lo:c_lo + ncv - i_lo], ALU.add, [t_, TC_], [t_])
                        rk = rt[par].name + str(hl)
                        k.ACT(t_[:, 0:ncv], t_[:, 0:ncv], AF.Exp, [t_, b31], [t_, rsc[par].name + str(hl)], bias=b31[:, h:h + 1],
                              accum_out=rsc[par][:, hl:hl + 1])
                        k.TS("dve", rt[par][:, hl, 0:1], rsc[par][:, hl:hl + 1], 1e-30, None, ALU.max, None, [rsc[par].name + str(hl)], [rk])
                        s.I("dve", lambda e, hl=hl, par=par: e.reciprocal(out=rt[par][:, hl, 0:1], in_=rt[par][:, hl, 0:1]), [rk], [rk])
                        k.TS("dve", pcn[:, hl, 0:ncv], t_[:, 0:ncv], rt[par][:, hl, 0:1], None, ALU.mult, None, [t_, rk], [pcn])
                        k.CP("pool", Pt[:, 0:ncv], pcn[:, hl, 0:ncv], [pcn], [Pt])
                    elif it["kind"] == "w":
                        s0 = it["s0"]
                        k.MM(p[:, 0:wdt], q_[:, hl, :], KwT[:, s0:s0 + wdt], True, True, [q_, KwT], [p])
                        v_lo = s0 - q0 + 512
                        if s0 < k.OWN:
                            k.STT(t_[:, 0:wdt], p[:, 0:wdt], wmk[:, 0:1], TW_[:, hl, v_lo:v_lo + wdt], ALU.add, ALU.add, [p, wmk, TW_], [t_])
                        else:
                            k.TT("dve", t_[:, 0:wdt], p[:, 0:wdt], TW_[:, hl, v_lo:v_lo + wdt], ALU.add, [p, TW_], [t_])
                        k.ACT(Pt[:, 0:wdt], t_[:, 0:wdt], AF.Exp, [t_], [Pt, rsw[par].name + str(hl)], accum_out=rsw[par][:, hl, it["pi"]:it["pi"] + 1])
                    else:
                        s0 = it["s0"]; c = it["c"]
                        k.MM(p[:, 0:wdt], q_[:, hl, :], KsT[:, s0:s0 + wdt], True, True, [q_, KsT], [p])
                        nb = wdt // 64
                        k.TT("dve", t_[:, 0:wdt].rearrange("p (a b) -> p a b", b=64), p[:, 0:wdt].rearrange("p (a b) -> p a b", b=64),
                             bc64(mk[par][:, 8 * c: 8 * c + nb]), ALU.add, [p, mk[par]], [t_])
                        s_lo = max(s0, q0 - 896)
                        if s_lo < s0 + wdt:
                            v_lo = s_lo - q0 + 896
                            ln = s0 + wdt - s_lo
                            k.TT("pool", t_[:, s_lo - s0: wdt], t_[:, s_lo - s0: wdt], TS_[:, hl, v_lo:v_lo + ln], ALU.add, [t_, TS_], [t_])
                        k.ACT(Pt[:, 0:wdt], t_[:, 0:wdt], AF.Exp, [t_, b31], [Pt, rss[par].name + str(hl)], bias=b31[:, h:h + 1],
                              accum_out=rss[par][:, hl, c:c + 1])
                    if it.get("topk"):
                        k.TT("dve", ps4[:], pcn[:, 0, :], pcn[:, 1, :], ALU.add, [pcn], [ps4])
                        k.TT("dve", ps4[:], ps4[:], pcn[:, 2, :], ALU.add, [pcn, ps4], [ps4])
                        k.TT("dve", ps4[:], ps4[:], pcn[:, 3, :], ALU.add, [pcn, ps4], [ps4])
                        s.I("dve", lambda e: e.tensor_reduce(out=imp[:], in_=ps4[:].rearrange("p (j m) -> p j m", m=4), axis=AX.X, op=ALU.add),
                            [ps4.name], [imp.name])
                        k.TT("dve", imp[:, 1:128], imp[:, 1:128], ps4[:, 3:508:4], ALU.add, [imp, ps4], [imp])
                        so = 127 - 2 * n
                        k.TT("dve", sc1[:], imp[:], vmrel[:, so:so + 128], ALU.mult, [imp, vmrel], [sc1])
                        k.TT("dve", sc1[:], sc1[:], acrel[:, so:so + 128], ALU.add, [sc1, acrel], [sc1])
                        k.TT("dve", sc1[:], sc1[:], ccm[:], ALU.mult, [sc1, ccm], [sc1])
                        k.TT("dve", sc1[:], sc1[:], cca[:], ALU.add, [sc1, cca], [sc1])
                        s.I("dve", lambda e: e.max(out=m8a[:], in_=sc1[:]), [sc1.name], [m8a.name])
                        s.I("dve", lambda e: e.match_replace(out=sc2[:], in_to_replace=m8a[:], in_values=sc1[:], imm_value=-3e4),
                            [sc1.name, m8a.name], [sc2.name])
                        s.I("dve", lambda e: e.max(out=m8b[:], in_=sc2[:]), [sc2.name], [m8b.name])
                        k.TS("dve", mk[par][:], sc1[:], m8b[:, 7:8], BIG, ALU.is_ge, ALU.mult, [sc1, m8b], [mk[par]])
                        k.TS("dve", mk[par][:], mk[par][:], -BIG, None, ALU.add, None, [mk[par]], [mk[par]])

                def stageB(it):
                    i = it["i"]; Pt = P_[i % NBUF]; ptile = pT[i % NBUF]; wdt = it["wdt"]
                    pst = psb[2 + i % 2]; pv = pst[:].bitcast(BF16)
                    nk = (wdt + 127) // 128
                    for kt in range(nk):
                        rows = min(128, wdt - kt * 128)
                        k.TR(pv[0:rows, kt * 128:(kt + 1) * 128], Pt[:, kt * 128: kt * 128 + rows], [Pt], [pst])
                    cn["o"] += 1
                    eng = "act" if cn["o"] % 2 else "dve"
                    if wdt % 128 == 0:
                        k.CP(eng, ptile[:, 0:nk, :], pv[:, 0:nk * 128].rearrange("p (a b) -> p a b", b=128), [pst], [ptile])
                    else:
                        for kt in range(nk):
                            rows = min(128, wdt - kt * 128)
                            k.CP(eng, ptile[0:rows, kt, :], pv[0:rows, kt * 128:(kt + 1) * 128], [pst], [ptile])

                def stageC(it):
                    n = it["n"]; hl = it["hl"]; par = n % 2; i = it["i"]; ptile = pT[i % NBUF]; wdt = it["wdt"]
                    nk = (wdt + 127) // 128
                    pcs = psb[4 + par]
                    if it["kind"] == "c":
                        po = pcs[:, hl * 64:(hl + 1) * 64]; pkey = "po_cs%d" % par
                        for kt in range(nk):
                            rows = min(128, wdt - kt * 128)
                            k.MM(po, ptile[0:rows, kt, :], Vc[0:rows, kt, g, :], kt == 0, kt == nk - 1, [ptile, Vc], [pkey])
                    elif it["kind"] == "w":
                        po = psb[6][:, par * 256 + hl * 64: par * 256 + (hl + 1) * 64]; pkey = "po_w%d" % par
                        kt0 = it["s0"] // 128
                        for kt in range(nk):
                            k.MM(po, ptile[:, kt, :], Vw[:, kt0 + kt, :], it["first"] and kt == 0, it["last"] and kt == nk - 1, [ptile, Vw], [pkey])
                    else:
                        po = pcs[:, 256 + hl * 64: 256 + (hl + 1) * 64]; pkey = "po_cs%d" % par
                        kt0 = it["s0"] // 128
                        for kt in range(nk):
                            k.MM(po, ptile[:, kt, :], Vs[:, kt0 + kt, :], it["first"] and kt == 0, it["last"] and kt == nk - 1, [ptile, Vs], [pkey])
                    if it.get("combine"):
                        q0 = 128 * n; g_ = gt[par]
                        o_ = ob[par]
                        for h2 in range(4):
                            col = g * 12 + h2 * 3
                            rk = rt[par].name + str(h2)
                            nch = n // 4 + 1
                            npc = it["npc_w"]
                            s.I("dve", lambda e, h2=h2, nch=nch, par=par: e.tensor_reduce(out=rt[par][:, h2, 1:2], in_=rss[par][:, h2, 0:nch], axis=AX.X, op=ALU.add),
                                [rss[par].name + str(h2)], [rk])
                            s.I("dve", lambda e, h2=h2, npc=npc, par=par: e.tensor_reduce(out=rt[par][:, h2, 2:3], in_=rsw[par][:, h2, 0:npc], axis=AX.X, op=ALU.add),
                                [rsw[par].name + str(h2)], [rk])
                            s.I("dve", lambda e, h2=h2, par=par: e.reciprocal(out=rt[par][:, h2, 1:3], in_=rt[par][:, h2, 1:3]), [rk], [rk])
                            k.TT("dve", rt[par][:, h2, 1:3], rt[par][:, h2, 1:3], g_[:, col + 1:col + 3], ALU.mult, [rk, g_], [rk])
                            hs = slice(h2 * 64, (h2 + 1) * 64)
                            k.TS("dve", oacc[:, hs], pcs[:, h2 * 64:(h2 + 1) * 64], g_[:, col:col + 1], None, ALU.mult, None, ["po_cs%d" % par, g_], [oacc])
                            k.STT(oacc[:, hs], pcs[:, 256 + h2 * 64: 256 + (h2 + 1) * 64], rt[par][:, h2, 1:2], oacc[:, hs], ALU.mult, ALU.add,
                                  ["po_cs%d" % par, rk, oacc], [oacc])
                            k.STT(o_[:, hs], psb[6][:, par * 256 + h2 * 64: par * 256 + (h2 + 1) * 64], rt[par][:, h2, 2:3], oacc[:, hs], ALU.mult, ALU.add,
                                  ["po_w%d" % par, rk, oacc], [o_])
                        k.LD(sc["onsa"][q0:q0 + 128, g * 256:(g + 1) * 256], o_[:], [o_], ["onsa"])

                N = len(items)
                for t in range(N + LAGC):
                    if t < N:
                        stageA(items[t])
                    if 0 <= t - LAGB < N:
                        stageB(items[t - LAGB])
                    if 0 <= t - LAGC < N:
                        stageC(items[t - LAGC])
            s.barrier()

    def merge(k, sc, w, h1, h2, gpost_ap, psb):
        s = k.s; T = k.T
        with ExitStack() as es:
            gpost = k.sb(es, "mg_g", [128, D], F32)
            k.LD(gpost[:], gpost_ap.partition_broadcast(128), [], [gpost])
            yT = k.sb(es, "mg_yT", [128, 8, 512], BF16)
            oT = k.sb(es, "mg_oT", [128, 8, 512], BF16)
            zT = k.sb(es, "mg_zT", [128, 16, 512], BF16)
            ot = [k.sb(es, "mg_ot%d" % i, [128, 1024], BF16) for i in range(2)]
            wb = [k.sb(es, "mg_wb%d" % i, [128, 16, 512], BF16) for i in range(3)]
            wbi = [0]
            f1 = k.sb(es, "mg_f1", [128, 4, D], F32)
            bufs = k.normbufs(es, "mg_")
            ga = [k.sb(es, "mg_ga%d" % i, [128, 512], BF16) for i in range(2)]
            gb = [k.sb(es, "mg_gb%d" % i, [128, 512], BF16) for i in range(2)]
            t1 = [k.sb(es, "mg_t1%d" % i, [128, 512], F32) for i in range(2)]
            t2 = [k.sb(es, "mg_t2%d" % i, [128, 512], F32) for i in range(2)]
            ci = 0
            for tt in range(k.NT // 2, k.NT):
                t0 = tt * 512
                k.LD(yT[:], sc["ysT"][:, t0:t0 + 512].rearrange("(a p) t -> p a t", p=128), ["ysT"], [yT])
                for sub in range(4):
                    o_ = ot[sub % 2]
                    k.LD(o_[:], sc["onsa"][t0 + sub * 128:t0 + sub * 128 + 128, :], ["onsa"], [o_])
                    pst = psb[6 + sub % 2]; pv = pst[:].bitcast(BF16)
                    for j in range(8):
                        k.TR(pv[:, j * 128:(j + 1) * 128], o_[:, j * 128:(j + 1) * 128], [o_], [pst])
                    k.CP("act" if sub % 2 else "dve", oT[:, :, sub * 128:(sub + 1) * 128], pv.rearrange("p (a b) -> p a b", a=8), [pst], [oT])
                for c4 in range(4):
                    wl = []
                    for nm_ in ("glu1", "glu2", "wo"):
                        wc = wb[wbi[0] % 3]; wbi[0] += 1
                        k.LD(wc[:, 0:8, :], w[nm_][:, c4 * 512:(c4 + 1) * 512].rearrange("(a p) c -> p a c", p=128), ["mw"], [wc])
                        wl.append(wc)
                    for j in range(4):
                        ct = c4 * 4 + j
                        pa, pb, pc_ = psb[(ci * 3) % 6], psb[(ci * 3 + 1) % 6], psb[(ci * 3 + 2) % 6]
                        ga_, gb_, t1_, t2_ = ga[ci % 2], gb[ci % 2], t1[ci % 2], t2[ci % 2]; ci += 1
                        k.LD(ga_[:], sc["gaT"][ct * 128:(ct + 1) * 128, t0:t0 + 512], ["pjout"], [ga_])
                        k.LD(gb_[:], sc["gbT"][ct * 128:(ct + 1) * 128, t0:t0 + 512], ["pjout"], [gb_])
                        for (pp, wc, src_) in ((pa, wl[0], yT), (pb, wl[1], yT), (pc_, wl[2], oT)):
                            for kk in range(8):
                                k.MM(pp[:], wc[:, kk, j * 128:(j + 1) * 128], src_[:, kk, :], kk == 0, kk == 7, [wc, src_], [pp])
                        k.ACT(t1_[:], pb[:], AF.Sigmoid, [pb], [t1_])
                        k.TT("dve", t1_[:], pa[:], t1_[:], ALU.mult, [pa, t1_], [t1_])
                        k.TT("pool", t1_[:], t1_[:], ga_[:], ALU.mult, [t1_, ga_], [t1_])
                        k.TT("dve", t2_[:], pc_[:], gb_[:], ALU.mult, [pc_, gb_], [t2_])
                        k.TT("pool", zT[:, ct, :], t1_[:], t2_[:], ALU.add, [t1_, t2_], [zT])
                k.down_post("mg", zT, 16, w["wout"], "mw", h1, "h1", h2, "h2", gpost, 1.0, t0, f1, wb, wbi, bufs, psb)
            s.barrier()

    def sincos(k, es, tag, ang, shape, want_cos):
        s = k.s
        I32 = mybir.dt.int32
        t = k.sb(es, tag + "t", shape, F32); ti = k.sb(es, tag + "ti", shape, I32)
        r = k.sb(es, tag + "r", shape, F32); m = k.sb(es, tag + "m", shape, F32)
        o = k.sb(es, tag + "o", shape, F32)
        a2 = ang
        if want_cos:
            a2 = k.sb(es, tag + "a2", shape, F32)
            s.I("dve", lambda e: e.tensor_scalar(out=a2[:], in0=ang[:], scalar1=math.pi / 2, scalar2=None, op0=ALU.add),
                r=[ang.name], w=[a2.name])
        s.I("dve", lambda e: e.tensor_scalar(out=t[:], in0=a2[:], scalar1=1.0 / (2 * math.pi), scalar2=None, op0=ALU.mult),
            r=[a2.name], w=[t.name])
        s.I("dve", lambda e: e.tensor_copy(out=ti[:], in_=t[:]), r=[t.name], w=[ti.name])
        s.I("dve", lambda e: e.tensor_copy(out=t[:], in_=ti[:]), r=[ti.name], w=[t.name])
        s.I("dve", lambda e: e.scalar_tensor_tensor(out=r[:], in0=t[:], scalar=-2 * math.pi, in1=a2[:],
                                                    op0=ALU.mult, op1=ALU.add), r=[t.name, a2.name], w=[r.name])
        for (thr, op, fix) in ((math.pi, ALU.is_gt, -2 * math.pi), (-math.pi, ALU.is_lt, 2 * math.pi)):
            s.I("dve", lambda e, thr=thr, op=op, fix=fix: e.tensor_scalar(
                out=m[:], in0=r[:], scalar1=thr, scalar2=fix, op0=op, op1=ALU.mult), r=[r.name], w=[m.name])
            s.I("dve", lambda e: e.tensor_tensor(out=r[:], in0=r[:], in1=m[:], op=ALU.add),
                r=[r.name, m.name], w=[r.name])
        s.I("dve", lambda e: e.tensor_scalar(out=r[:], in0=r[:], scalar1=-math.pi, scalar2=math.pi,
                                             op0=ALU.max, op1=ALU.min), r=[r.name], w=[r.name])
        s.I("act", lambda e: e.activation(out=o[:], in_=r[:], func=AF.Sin), r=[r.name], w=[o.name])
        return o

    def disc(k, es, tag, are, aim, ldt, shape):
        s = k.s
        dt = k.sb(es, tag + "dt", shape, F32); lam = k.sb(es, tag + "lam", shape, F32)
        mag = k.sb(es, tag + "mag", shape, F32); ang = k.sb(es, tag + "ang", shape, F32)
        abr = k.sb(es, tag + "abr", shape, F32); abi = k.sb(es, tag + "abi", shape, F32)
        s.I("act", lambda e: e.activation(out=dt[:], in_=ldt[:], func=AF.Exp), r=[ldt.name], w=[dt.name])
        s.I("dve", lambda e: e.tensor_scalar(out=lam[:], in0=are[:], scalar1=-1e-4, scalar2=None, op0=ALU.min),
            r=[are.name], w=[lam.name])
        s.I("dve", lambda e: e.tensor_tensor(out=mag[:], in0=lam[:], in1=dt[:], op=ALU.mult),
            r=[lam.name, dt.name], w=[mag.name])
        s.I("act", lambda e: e.activation(out=mag[:], in_=mag[:], func=AF.Exp), r=[mag.name], w=[mag.name])
        s.I("dve", lambda e: e.tensor_tensor(out=ang[:], in0=aim[:], in1=dt[:], op=ALU.mult),
            r=[aim.name, dt.name], w=[ang.name])
        sn = k.sincos(es, tag + "s", ang, shape, False)
        cs = k.sincos(es, tag + "c", ang, shape, True)
        s.I("dve", lambda e: e.tensor_tensor(out=abr[:], in0=mag[:], in1=cs[:], op=ALU.mult),
            r=[mag.name, cs.name], w=[abr.name])
        s.I("dve", lambda e: e.tensor_tensor(out=abi[:], in0=mag[:], in1=sn[:], op=ALU.mult),
            r=[mag.name, sn.name], w=[abi.name])
        return abr, abi, lam

    def s5(k, usT, ysT, psb):
        s = k.s; T = k.T
        LM = int(round(math.log2(T)))
        NLV = LM
        tt = lambda e, **kw: e.tensor_tensor(**kw)
        with ExitStack() as es:
            a2r = k.sb(es, "a2r", [128, 64], F32); a2i = k.sb(es, "a2i", [128, 64], F32); l2 = k.sb(es, "l2", [128, 64], F32)
            for t_, n_ in ((a2r, "ssm_aT_re2"), (a2i, "ssm_aT_im2"), (l2, "ssm_ldt2")):
                s.DMA("sp", t_[:], k.din(n_, [128, 64]), w=[t_.name])
            sgn = k.sb(es, "sgn", [128, 1], F32); s.DMA("sp", sgn[:], k.din("c_sgn", [128, 1]), w=[sgn.name])
            mk8 = k.sb(es, "mk8", [128, 8], F32); s.DMA("sp", mk8[:], k.din("c_mask8", [128, 8]), w=[mk8.name])
            Jm = k.sb(es, "Jm", [128, 128], F32); s.DMA("sp", Jm[:], k.din("c_J", [128, 128]), w=[Jm.name])
            Id = k.sb(es, "Idf", [128, 128], F32); s.DMA("sp", Id[:], k.din("c_I", [128, 128]), w=[Id.name])
            dsk = k.sb(es, "dsk", [128, 8], F32); s.DMA("sp", dsk[:], k.din("ssm_dT", [128, 8]), w=[dsk.name])
            PR = k.sb(es, "PR", [128, NLV, 64], F32); PI = k.sb(es, "PI", [128, NLV, 64], F32)
            with ExitStack() as e2:
                abr, abi, _ = k.disc(e2, "d1", a2r, a2i, l2, [128, 64])
                s.I("dve", lambda e, abr=abr: e.tensor_copy(out=PR[:, 0, :], in_=abr[:]), r=[abr.name], w=[PR.name])
                s.I("dve", lambda e, abi=abi: e.tensor_copy(out=PI[:, 0, :], in_=abi[:]), r=[abi.name], w=[PI.name])
                t1 = k.sb(e2, "pw1", [128, 64], F32); t2 = k.sb(e2, "pw2", [128, 64], F32)
                for i in range(NLV - 1):
                    s.I("dve", lambda e, i=i: tt(e, out=t1[:], in0=PR[:, i, :], in1=PR[:, i, :], op=ALU.mult), r=[PR.name], w=[t1.name])
                    s.I("dve", lambda e, i=i: tt(e, out=t2[:], in0=PI[:, i, :], in1=PI[:, i, :], op=ALU.mult), r=[PI.name], w=[t2.name])
                    s.I("dve", lambda e, i=i: tt(e, out=PR[:, i + 1, :], in0=t1[:], in1=t2[:], op=ALU.subtract), r=[t1.name, t2.name], w=[PR.name])
                    s.I("dve", lambda e, i=i: tt(e, out=t1[:], in0=PR[:, i, :], in1=PI[:, i, :], op=ALU.mult), r=[PR.name, PI.name], w=[t1.name])
                    s.I("dve", lambda e, i=i: e.tensor_scalar(out=PI[:, i + 1, :], in0=t1[:], scalar1=2.0, scalar2=None, op0=ALU.mult), r=[t1.name], w=[PI.name])
                s.I("dve", lambda e: e.tensor_scalar(out=PI[:], in0=PI[:], scalar1=sgn[:], scalar2=None, op0=ALU.mult), r=[PI.name, sgn.name], w=[PI.name])
                s.barrier()
            Bb = k.sb(es, "Bb", [128, 8, 128], F32)
            with ExitStack() as e2:
                sh = [128, 8 * 64]
                ar = k.sb(e2, "b_ar", sh, F32); ai = k.sb(e2, "b_ai", sh, F32); ld = k.sb(e2, "b_ld", sh, F32)
                br = k.sb(e2, "b_br", sh, F32); bi = k.sb(e2, "b_bi", sh, F32)
                for t_, n_ in ((ar, "ssm_a_re_b"), (ai, "ssm_a_im_b"), (ld, "ssm_ldt_b"), (br, "ssm_b_re_b"), (bi, "ssm_b_im_b")):
                    s.DMA("sp", t_[:], k.din(n_, sh), w=[t_.name])
                abr, abi, lam = k.disc(e2, "d2", ar, ai, ld, sh)
                den = k.sb(e2, "den", sh, F32); u1 = k.sb(e2, "u1", sh, F32); u2 = k.sb(e2, "u2", sh, F32)
                cor = k.sb(e2, "cor", sh, F32); coi = k.sb(e2, "coi", sh, F32)
                D_ = lambda fn, r, w: s.I("dve", fn, r=[x.name for x in r], w=[x.name for x in w])
                D_(lambda e: tt(e, out=den[:], in0=lam[:], in1=lam[:], op=ALU.mult), [lam], [den])
                D_(lambda e: tt(e, out=u1[:], in0=ai[:], in1=ai[:], op=ALU.mult), [ai], [u1])
                D_(lambda e: tt(e, out=den[:], in0=den[:], in1=u1[:], op=ALU.add), [den, u1], [den])
                D_(lambda e: e.reciprocal(out=den[:], in_=den[:]), [den], [den])
                D_(lambda e: e.tensor_scalar(out=abr[:], in0=abr[:], scalar1=-1.0, scalar2=None, op0=ALU.add), [abr], [abr])
                D_(lambda e: tt(e, out=u1[:], in0=abr[:], in1=lam[:], op=ALU.mult), [abr, lam], [u1])
                D_(lambda e: tt(e, out=u2[:], in0=abi[:], in1=ai[:], op=ALU.mult), [abi, ai], [u2])
                D_(lambda e: tt(e, out=u1[:], in0=u1[:], in1=u2[:], op=ALU.add), [u1, u2], [u1])
                D_(lambda e: tt(e, out=cor[:], in0=u1[:], in1=den[:], op=ALU.mult), [u1, den], [cor])
                D_(lambda e: tt(e, out=u1[:], in0=abi[:], in1=lam[:], op=ALU.mult), [abi, lam], [u1])
                D_(lambda e: tt(e, out=u2[:], in0=abr[:], in1=ai[:], op=ALU.mult), [abr, ai], [u2])
                D_(lambda e: tt(e, out=u1[:], in0=u1[:], in1=u2[:], op=ALU.subtract), [u1, u2], [u1])
                D_(lambda e: tt(e, out=coi[:], in0=u1[:], in1=den[:], op=ALU.mult), [u1, den], [coi])
                v3 = lambda t_: t_[:].rearrange("p (q x) -> p q x", q=8)
                D_(lambda e: tt(e, out=u1[:], in0=cor[:], in1=br[:], op=ALU.mult), [cor, br], [u1])
                D_(lambda e: tt(e, out=u2[:], in0=coi[:], in1=bi[:], op=ALU.mult), [coi, bi], [u2])
                D_(lambda e: tt(e, out=Bb[:, :, 0:64], in0=v3(u1), in1=v3(u2), op=ALU.subtract), [u1, u2], [Bb])
                D_(lambda e: tt(e, out=u1[:], in0=cor[:], in1=bi[:], op=ALU.mult), [cor, bi], [u1])
                D_(lambda e: tt(e, out=u2[:], in0=coi[:], in1=br[:], op=ALU.mult), [coi, br], [u2])
                D_(lambda e: tt(e, out=Bb[:, :, 64:128], in0=v3(u1), in1=v3(u2), op=ALU.add), [u1, u2], [Bb])
                s.barrier()
            Ct = k.sb(es, "Ct", [128, 64, 16], F32)
            s.DMA("sp", Ct[:], k.din("ssm_cT2", [128, 64, 16]), w=[Ct.name])
            s.I("dve", lambda e: e.tensor_scalar(out=Ct[:], in0=Ct[:], scalar1=sgn[:], scalar2=None, op0=ALU.mult),
                r=[Ct.name, sgn.name], w=[Ct.name])
            us = k.sb(es, "us32", [128, T], F32)
            yacc = k.sb(es, "yacc", [128, T], F32)
            Aa = [k.sb(es, "Ast%d" % i, [128, T], F32) for i in range(2)]
            Mt = [k.sb(es, "Mt%d" % i, [128, NLV, 128], F32) for i in range(2)]
            BmP = k.sb(es, "BmP", [128, 8, 128], F32)
            CmP = k.sb(es, "CmP", [128, 8, 128], F32)
            gl = [k.sb(es, "gl%d" % i, [128, 512], F32) for i in range(3)]
            yb = [k.sb(es, "yb%d" % i, [128, 512], BF16) for i in range(2)]
            pc = [0]

            def nps():
                p = psb[pc[0] % 8]; pc[0] += 1
                return p
            ev = [0]

            def evac_copy(dst, src, rk, wk):
                ev[0] += 1
                if ev[0] % 2:
                    s.I("act", lambda e: e.copy(out=dst, in_=src), r=rk, w=wk)
                else:
                    s.I("dve", lambda e: e.tensor_copy(out=dst, in_=src), r=rk, w=wk)

            for q in range(8):
                s.DMA("pool", us[:], usT[q * 128:(q + 1) * 128, :], r=["usT"], w=[us.name])
                for j in range(8):
                    s.I("dve", lambda e, j=j, q=q: e.tensor_scalar(out=BmP[:, j, :], in0=Bb[:, q, :], scalar1=mk8[:, j:j + 1],
                                                              scalar2=None, op0=ALU.mult), r=[Bb.name, mk8.name], w=[BmP.name])
                s.I("pool", lambda e: e.memset(CmP[:], 0.0), w=[CmP.name])
                for j in range(8):
                    s.I("pool", lambda e, j=j, q=q: e.tensor_copy(out=CmP[:, j, j * 16:(j + 1) * 16], in_=Ct[:, q * 8 + j, :]),
                        r=[Ct.name], w=[CmP.name])
                for j in range(8):
                    g = q * 8 + j
                    A = Aa[j % 2]; M = Mt[j % 2]
                    for i in range(NLV):
                        s.I("pool", lambda e, i=i, g=g, M=M: e.tensor_scalar(out=M[:, i, :], in0=Id[:], scalar1=PR[:, i, g:g + 1],
                                                                     scalar2=None, op0=ALU.mult), r=[Id.name, PR.name], w=[M.name])
                        s.I("dve", lambda e, i=i, g=g, M=M: e.scalar_tensor_tensor(out=M[:, i, :], in0=Jm[:], scalar=PI[:, i, g:g + 1],
                                                                           in1=M[:, i, :], op0=ALU.mult, op1=ALU.add),
                            r=[Jm.name, PI.name, M.name], w=[M.name])
                    for c0 in range(0, T, 512):
                        p = nps()
                        s.I("pe", lambda e, p=p, j=j, c0=c0: e.matmul(p[:], lhsT=BmP[:, j, :], rhs=us[:, c0:c0 + 512], start=True, stop=True),
                            r=[BmP.name, us.name], w=[p.name])
                        evac_copy(A[:, c0:c0 + 512], p[:], [p.name], [A.name])
                    for l in range(1, LM + 1):
                        st = 1 << l; h = st >> 1; n = T // st
                        for c0 in range(0, n, 512):
                            m_ = min(512, n - c0)
                            src = A[:, h - 1 + c0 * st: h - 1 + (c0 + m_ - 1) * st + 1: st]
                            dst = A[:, st - 1 + c0 * st: st - 1 + (c0 + m_ - 1) * st + 1: st]
                            p = nps()
                            s.I("pe", lambda e, p=p, M=M, l=l, src=src, m_=m_: e.matmul(p[:, 0:m_], lhsT=M[:, l - 1, :], rhs=src, start=True, stop=True),
                                r=[M.name, A.name], w=[p.name])
                            s.I("dve", lambda e, p=p, dst=dst, m_=m_: tt(e, out=dst, in0=p[:, 0:m_], in1=dst, op=ALU.add),
                                r=[p.name, A.name], w=[A.name])
                    for l in range(LM - 1, 0, -1):
                        st = 1 << l; h = st >> 1; n = T // st
                        lo_i = max(0, n // 2 - 1)
                        for c0 in range(lo_i, n - 1, 512):
                            m_ = min(512, n - 1 - c0)
                            src = A[:, st - 1 + c0 * st: st - 1 + (c0 + m_ - 1) * st + 1: st]
                            dst = A[:, st + h - 1 + c0 * st: st + h - 1 + (c0 + m_ - 1) * st + 1: st]
                            p = nps()
                            s.I("pe", lambda e, p=p, M=M, l=l, src=src, m_=m_: e.matmul(p[:, 0:m_], lhsT=M[:, l - 1, :], rhs=src, start=True, stop=True),
                                r=[M.name, A.name], w=[p.name])
                            s.I("dve", lambda e, p=p, dst=dst, m_=m_: tt(e, out=dst, in0=p[:, 0:m_], in1=dst, op=ALU.add),
                                r=[p.name, A.name], w=[A.name])
                    for c0 in range(k.OWN, T, 512):
                        p = nps()
                        s.I("pe", lambda e, p=p, j=j, c0=c0, A=A: e.matmul(p[:], lhsT=CmP[:, j, :], rhs=A[:, c0:c0 + 512], start=True, stop=True),
                            r=[CmP.name, A.name], w=[p.name])
                        if j == 0:
                            evac_copy(yacc[:, c0:c0 + 512], p[:], [p.name], [yacc.name])
                        else:
                            s.I("dve", lambda e, p=p, c0=c0: tt(e, out=yacc[:, c0:c0 + 512], in0=p[:], in1=yacc[:, c0:c0 + 512], op=ALU.add),
                                r=[p.name, yacc.name], w=[yacc.name])
                for ci, c0 in enumerate(range(k.OWN, T, 512)):
                    y_ = gl[0]; a_ = gl[1]; b_ = gl[2]; o_ = yb[ci % 2]
                    s.I("dve", lambda e, c0=c0, q=q: e.scalar_tensor_tensor(out=y_[:], in0=us[:, c0:c0 + 512], scalar=dsk[:, q:q + 1],
                                                                        in1=yacc[:, c0:c0 + 512], op0=ALU.mult, op1=ALU.add),
                        r=[us.name, dsk.name, yacc.name], w=[y_.name])
                    s.I("act", lambda e: e.activation(out=a_[:], in_=y_[:], func=AF.Square), r=[y_.name], w=[a_.name])
                    s.I("dve", lambda e: e.tensor_scalar(out=a_[:], in0=a_[:], scalar1=0.044715, scalar2=1.0, op0=ALU.mult, op1=ALU.add),
                        r=[a_.name], w=[a_.name])
                    s.I("dve", lambda e: tt(e, out=b_[:], in0=a_[:], in1=y_[:], op=ALU.mult), r=[a_.name, y_.name], w=[b_.name])
                    s.I("act", lambda e: e.activation(out=b_[:], in_=b_[:], func=AF.Sigmoid, scale=1.5957691216057308), r=[b_.name], w=[b_.name])
                    s.I("dve", lambda e, o_=o_: tt(e, out=o_[:], in0=b_[:], in1=y_[:], op=ALU.mult), r=[b_.name, y_.name], w=[o_.name])
                    d = s.DMA("sp", ysT[q * 128:(q + 1) * 128, c0:c0 + 512], o_[:], r=[o_.name], w=["ysT"])
                    k.final.append(d)
            s.barrier()

    def build(k):
        nc = k.nc; s = k.s; es = k.es; T = k.T
        x = k.din("x", [T, D])
        out = k.nc.dram_tensor("out", [T if k.stages < 3 else T // 2, D], F32, kind="ExternalOutput").ap()
        g = {n: k.din(n, [1, D]) for n in ("ffn1_pre_g", "ffn1_post_g", "mix_pre_g", "mix_post_g",
                                           "ffn2_pre_g", "ffn2_post_g")}
        w1g = k.conv("ffn1_w_gate", [D, DFF], 4)
        w1u = k.conv("ffn1_w_up", [D, DFF], 4)
        w1d = k.conv("ffn1_w_down", [DFF, D], 4)
        psb = [es.enter_context(nc.psum_tensor("psb%d" % i, [128, 512], F32)) for i in range(8)]
        k.epsb = k.sb(es, "epsb", [128, 1], F32)
        s.I("dve", lambda e: e.memset(k.epsb[:], EPS), w=[k.epsb.name])
        identf = k.sb(es, "identf", [128, 128], F32)
        k.ident = k.sb(es, "ident", [128, 128], BF16)
        s.I("pool", lambda e: e.memset(identf[:], 1.0), w=[identf.name])
        s.I("pool", lambda e: e.affine_select(out=identf[:], in_=identf[:], pattern=[[-1, 128]],
                                              compare_op=ALU.is_equal, fill=0.0, base=0, channel_multiplier=1),
            r=[identf.name], w=[identf.name])
        s.I("dve", lambda e: e.tensor_copy(out=k.ident[:], in_=identf[:]), r=[identf.name], w=["ident"])
        if k.stages == 1:
            k.ffn("f1", x, out, g["ffn1_pre_g"][0, :], g["ffn1_post_g"][0, :], w1g, w1u, w1d, psb)
        if k.stages == 2:
            usT = k.din("usT", [1024, T])
            ysT = k.nc.dram_tensor("ysT", [1024, T], BF16, kind="ExternalOutput").ap()
            k.s5(usT, ysT, psb)
        if k.stages >= 3:
            w = {}
            win = k.conv("w_in", [D, INC], 4)
            w["glu1"] = k.conv("ssm_glu_w1", [1024, D], 2); w["glu2"] = k.conv("ssm_glu_w2", [1024, D], 2)
            w["wo"] = k.conv("nsa_w_o", [1024, D], 2); w["wout"] = k.conv("w_out", [D, D], 2)
            w["ck1"] = k.conv("cmp_k_w1", [2048, 256]); w["ck2"] = k.conv("cmp_k_w2", [256, 64])
            w["cv1"] = k.conv("cmp_v_w1", [2048, 256]); w["cv2"] = k.conv("cmp_v_w2", [256, 64])
            w2g = k.conv("ffn2_w_gate", [D, DFF], 4); w2u = k.conv("ffn2_w_up", [D, DFF], 4); w2d = k.conv("ffn2_w_down", [DFF, D], 4)
            sc = {}
            for nm_, shp, dt_ in (("h1", [T, D], F32), ("h2", [T, D], F32), ("usT", [1024, T], F32), ("ysT", [1024, T], BF16),
                                  ("qT", [16, 64, T], BF16), ("kcT", [4, 64, T], BF16), ("vcT", [4, 64, T], BF16),
                                  ("ksT", [4, 64, T], BF16), ("kwT", [4, 64, T], BF16), ("Vs", [T, 256], BF16), ("Vw", [T, 256], BF16),
                                  ("gates", [T, 48], F32), ("gaT", [D, T], BF16), ("gbT", [D, T], BF16), ("onsa", [T, 1024], BF16)):
                if k.debug:
                    sc[nm_] = k.nc.dram_tensor("dbg_" + nm_, list(shp), dt_, kind="ExternalOutput").ap()
                else:
                    sc[nm_] = k.dscr("sc_" + nm_, shp, dt_)
            s.barrier()
            k.ffn("f1", x, sc["h1"], g["ffn1_pre_g"][0, :], g["ffn1_post_g"][0, :], w1g, w1u, w1d, psb, "x", "h1")
            k.proj(sc["h1"], g["mix_pre_g"][0, :], win, sc, psb)
            k.s5(sc["usT"], sc["ysT"], psb)
            with ExitStack() as em:
                KcT, Vc = k.compress(em, sc, w, psb)
                SKS, SKW, WS, WW = k.bias_tables(psb)
                k.nsa(sc, KcT, Vc, SKS, SKW, WS, WW, psb)
            k.merge(sc, w, sc["h1"], sc["h2"], g["mix_post_g"][0, :], psb)
            k.ffn("f2", sc["h2"], out, g["ffn2_pre_g"][0, :], g["ffn2_post_g"][0, :], w2g, w2u, w2d, psb, "h2", "out",
                  tiles=range(k.NT // 2, k.NT), dst_off=k.OWN)
        s.finish(k.final)
        es.close()
        return nc


def ssm_layouts(a_re, a_im, log_dt, b_re, b_im, c_re, c_im, d):
    a_re = np.asarray(a_re, np.float32); a_im = np.asarray(a_im, np.float32); log_dt = np.asarray(log_dt, np.float32)
    b_re = np.asarray(b_re, np.float32); b_im = np.asarray(b_im, np.float32)
    c_re = np.asarray(c_re, np.float32); c_im = np.asarray(c_im, np.float32); d = np.asarray(d, np.float32)
    m = {}
    m["ssm_aT_re2"] = np.concatenate([a_re.T, a_re.T], 0)
    m["ssm_aT_im2"] = np.concatenate([a_im.T, a_im.T], 0)
    m["ssm_ldt2"] = np.broadcast_to(log_dt[None, :], (128, 64))
    def lay_a(a):
        t = a.reshape(8, 8, 64)
        t = np.transpose(t, (1, 0, 2))
        return np.broadcast_to(t[:, None], (8, 16, 8, 64)).reshape(128, 512)
    m["ssm_a_re_b"] = lay_a(a_re); m["ssm_a_im_b"] = lay_a(a_im)
    m["ssm_ldt_b"] = lay_a(np.broadcast_to(log_dt[:, None], (64, 64)))
    def lay_b(b):
        t = b.reshape(8, 8, 64, 16)
        t = np.transpose(t, (1, 3, 0, 2))
        return t.reshape(128, 512)
    m["ssm_b_re_b"] = lay_b(b_re); m["ssm_b_im_b"] = lay_b(b_im)
    m["ssm_cT2"] = np.concatenate([np.transpose(c_re, (2, 0, 1)), np.transpose(c_im, (2, 0, 1))], 0)
    m["ssm_dT"] = d.reshape(8, 128).T
    m["c_sgn"] = np.concatenate([np.ones((64, 1)), -np.ones((64, 1))], 0)
    m["c_mask8"] = (np.arange(128)[:, None] // 16 == np.arange(8)[None, :]).astype(np.float32)
    m["c_I"] = np.eye(128)
    m["c_J"] = np.roll(np.eye(128), 64, axis=1)
    return {k_: np.ascontiguousarray(v, dtype=np.float32) for k_, v in m.items()}


def _bucket(d):
    d = np.maximum(d, 0)
    d_f = np.maximum(d, 1).astype(np.float32)
    large = 16 + (np.log(d_f / np.float32(16)) / np.float32(math.log(1024 / 16)) * np.float32(16)).astype(np.int32)
    large = np.minimum(large, 31)
    return np.where(d < 16, d, large)


def nsa_consts():
    m = {}
    WS, WW = 1152, 768
    x = np.arange(WS); d = 1023 - x
    oh = np.zeros((33, WS), np.float32)
    bk = _bucket(d)
    for b in range(32):
        oh[b] = ((d >= 0) & (bk == b))
    oh[32] = np.where(d < 0, -BIG, 0.0)
    m["c_ohs"] = oh
    x = np.arange(WW); d = 639 - x
    oh = np.zeros((33, WW), np.float32)
    bk = _bucket(d)
    ok = (d >= 0) & (d < 512)
    for b in range(32):
        oh[b] = (ok & (bk == b))
    oh[32] = np.where(ok, 0.0, -BIG)
    m["c_ohw"] = oh
    vm = np.zeros((128, 256), np.float32); ac = np.zeros((128, 256), np.float32)
    c = np.arange(256)
    for qi in range(128):
        hi = qi >= 64
        forced = (c == 127) | ((c == 128) if hi else (c == 126))
        invalid = (c > 128) | ((c == 128) & (not hi))
        vm[qi] = (~forced & ~invalid)
        ac[qi] = np.where(forced, 1e4, np.where(invalid, -1e4, 0.0))
    m["c_vmrel"] = vm; m["c_acrel"] = ac
    return m


def percore_consts(T, half):
    m = {}
    nbp = T // 128
    cm = np.ones((128, 128), np.float32); ca = np.zeros((128, 128), np.float32)
    if half == 0:
        cm[:, 0:nbp] = 0.0; ca[:, 0:nbp] = -2e4
        cm[:, nbp] = 0.0; ca[:, nbp] = 1e4
    else:
        cm[:, 0] = 0.0; ca[:, 0] = 1e4
    m["pc_cm"] = cm; m["pc_ca"] = ca
    cmpm = np.zeros((128, 512), np.float32)
    if half == 0:
        cmpm[:, 0:T // 32] = -BIG
    m["pc_cmpmask"] = cmpm
    m["pc_wmask"] = np.full((128, 1), -BIG if half == 0 else 0.0, np.float32)
    return m


def host_inputs(inputs, names):
    m = {}
    sq = lambda n: np.asarray(inputs[n], np.float32)[0]
    lay = ssm_layouts(sq("ssm_a_re"), sq("ssm_a_im"), sq("ssm_log_dt"), sq("ssm_b_re"), sq("ssm_b_im"),
                      sq("ssm_c_re"), sq("ssm_c_im"), sq("ssm_d"))
    lay.update(nsa_consts())
    lay["cmp_posT"] = np.ascontiguousarray(sq("cmp_pos").T)
    for n in names:
        if n == "x":
            continue
        if n in lay:
            m[n] = np.ascontiguousarray(lay[n], dtype=np.float32)
        elif n == "rel_bias":
            m[n] = np.ascontiguousarray(np.asarray(inputs[n], np.float32))
        elif n.endswith("_g"):
            m[n] = np.ascontiguousarray(np.asarray(inputs[n], np.float32).reshape(1, D))
        else:
            m[n] = np.ascontiguousarray(sq(n))
    return m

_CACHE = {}


def _get(T, stages, debug=False):
    key = (T, stages, debug)
    if key not in _CACHE:
        kb = K(T, stages, debug)
        kb.build()
        _CACHE[key] = kb
    return _CACHE[key]


def run(inputs, T, stages, ncores, debug=False):
    kb = _get(T, stages, debug)
    xs = np.asarray(inputs["x"], np.float32)
    in_maps = []
    if stages >= 3:
        shared = host_inputs(inputs, [n for n in kb.inp if n != "x" and not n.startswith("pc_")])
        pcs = [percore_consts(T, 0), percore_consts(T, 1)]
        for c in range(ncores):
            b, half = c // 2, c % 2
            m = dict(shared)
            m.update(pcs[half])
            xb = xs[b % xs.shape[0]].reshape(T, D)
            if half == 0:
                xl = np.concatenate([np.zeros((T // 2, D), np.float32), xb[:T // 2]], 0)
            else:
                xl = xb
            m["x"] = np.ascontiguousarray(xl)
            in_maps.append(m)
    else:
        for c in range(ncores):
            m = {}
            for name in kb.inp:
                if name == "x":
                    continue
                a = np.asarray(inputs[name], np.float32)
                m[name] = np.ascontiguousarray(a.reshape(kb.inp[name].shape))
            m["x"] = np.ascontiguousarray(xs[c % xs.shape[0]].reshape(T, D))
            in_maps.append(m)
    res = run_bass_kernel_spmd(kb.nc, in_maps, core_ids=list(range(ncores)))
    if debug:
        return res.results
    return [r["out"] for r in res.results]


def kernel(**inputs):
    outs = run(inputs, SEQ, 3, 8)
    full = np.empty((NB, SEQ, D), np.float32)
    for c in range(8):
        b, half = c // 2, c % 2
        full[b, half * (SEQ // 2):(half + 1) * (SEQ // 2)] = outs[c]
    return full
```

```python
import math
from contextlib import ExitStack
import numpy as np
import ml_dtypes
import concourse.bass as bass
import concourse.mybir as mybir
from concourse.bass_utils import run_bass_kernel_spmd

F32 = mybir.dt.float32
BF16 = mybir.dt.bfloat16
ALU = mybir.AluOpType
AF = mybir.ActivationFunctionType
AX = mybir.AxisListType

D = 2048
DFF = 5632
EPS = 1e-6
NB = 4
SEQ = 8192
INC = 7728
BIG = 30000.0


class Ins:
    __slots__ = ("eng", "fn", "deps", "dma", "idx", "sig", "val", "semi", "prewait")

    def __init__(s, eng, fn, dma):
        s.eng = eng; s.fn = fn; s.dma = dma; s.deps = []; s.idx = 0; s.sig = False; s.val = 0
        s.semi = 0; s.prewait = None


class Sch:
    ENGS = ["pe", "dve", "act", "pool", "sp"]
    NS = 10

    def __init__(s, nc, es):
        s.nc = nc
        s.es = es
        s.eobj = {"pe": nc.tensor, "dve": nc.vector, "act": nc.scalar, "pool": nc.gpsimd, "sp": nc.sync}
        s.prog = {e: [] for e in s.ENGS}
        s.csem = {e: es.enter_context(nc.semaphore("cs_" + e)) for e in s.ENGS}
        s.dsem = {e: [es.enter_context(nc.semaphore("ds_%s%d" % (e, i))) for i in range(s.NS)]
                  for e in ("sp", "pool", "act")}
        s.ndma = {e: 0 for e in ("sp", "pool", "act")}
        s.dmas = {e: [] for e in ("sp", "pool", "act")}
        s.lastw = {}
        s.readers = {}
        s.known = {e: {f: -1 for f in s.ENGS} for e in s.ENGS}
        s.seen = {e: set() for e in s.ENGS}
        s.bar_from = {e: 0 for e in ("sp", "pool", "act")}

    def _dep(s, ins, d):
        if d is None or d is ins:
            return
        if d.dma:
            if id(d) in s.seen[ins.eng]:
                return
            s.seen[ins.eng].add(id(d))
            ins.deps.append(d)
        else:
            if d.eng == "pe" and ins.eng == "pe" and not ins.dma:
                return
            if s.known[ins.eng][d.eng] >= d.idx:
                return
            s.known[ins.eng][d.eng] = d.idx
            ins.deps.append(d)

    def _emit(s, eng, fn, r, w, dma):
        ins = Ins(eng, fn, dma)
        ins.idx = len(s.prog[eng])
        for k in list(r) + list(w):
            s._dep(ins, s.lastw.get(k))
        for k in w:
            best = {}
            for d in s.readers.get(k, ()):
                if d.dma:
                    s._dep(ins, d)
                elif d.eng not in best or best[d.eng].idx < d.idx:
                    best[d.eng] = d
            for d in best.values():
                s._dep(ins, d)
        if dma:
            n = s.ndma[eng]; s.ndma[eng] += 1
            ins.semi = n % s.NS; ins.val = (n // s.NS + 1) * 16
            if n >= s.NS:
                ins.prewait = s.dmas[eng][n - s.NS]
            s.dmas[eng].append(ins)
            ins.sig = True
        s.prog[eng].append(ins)
        for k in w:
            s.lastw[k] = ins; s.readers[k] = []
        for k in r:
            s.readers.setdefault(k, []).append(ins)
        return ins

    def I(s, eng, fn, r=(), w=()):
        return s._emit(eng, fn, r, w, False)

    def barrier(s):
        lasts = []
        for e in s.ENGS:
            for ins in reversed(s.prog[e]):
                if not ins.dma:
                    lasts.append(ins); break
        dm = [d for e in s.dmas for d in s.dmas[e][s.bar_from[e]:]]
        for e in s.dmas:
            s.bar_from[e] = len(s.dmas[e])
        for e in s.ENGS:
            ins = Ins(e, lambda eng: eng.nop(), False)
            ins.idx = len(s.prog[e])
            for d in lasts + dm:
                s._dep(ins, d)
            s.prog[e].append(ins)

    def DMA(s, eng, out, in_, r=(), w=(), **kw):
        return s._emit(eng, lambda e: e.dma_start(out=out, in_=in_, **kw), r, w, True)

    def finish(s, final):
        for e in s.ENGS:
            for ins in s.prog[e]:
                for d in ins.deps:
                    d.sig = True
        EPOCH = 30000
        s.csems = {}
        for e in s.ENGS:
            c = 0; ep = 0
            s.csems[e] = [s.csem[e]]
            for ins in s.prog[e]:
                if not ins.dma and ins.sig:
                    if c == EPOCH:
                        c = 0; ep += 1
                        s.csems[e].append(s.es.enter_context(s.nc.semaphore("cs_%s_%d" % (e, ep))))
                    c += 1; ins.val = c; ins.semi = ep
        s.maxval = {e: max([i.val for i in s.prog[e] if not i.dma] + [0]) for e in s.ENGS}

        def semof(d):
            return s.dsem[d.eng][d.semi] if d.dma else s.csems[d.eng][d.semi]

        with s.nc.Block() as block:
            def run(e):
                def body(eng):
                    for ins in s.prog[e]:
                        if ins.prewait is not None:
                            eng.wait_ge(semof(ins.prewait), ins.prewait.val)
                        for d in ins.deps:
                            eng.wait_ge(semof(d), d.val)
                        o = ins.fn(eng)
                        if ins.sig:
                            o.then_inc(semof(ins), 16 if ins.dma else 1)
                    if e == "sp":
                        for d in final:
                            eng.wait_ge(semof(d), d.val)
                return body
            block.tensor(run("pe"))
            block.vector(run("dve"))
            block.scalar(run("act"))
            block.gpsimd(run("pool"))
            block.sync(run("sp"))


def dram_ap(t, off, pat):
    return bass.AP(t.tensor if hasattr(t, "tensor") else t, off, pat)


class K:
    def __init__(k, T, stages=9, debug=False):
        k.T = T
        k.debug = debug
        k.NT = T // 512
        k.OWN = T // 2
        k.stages = stages
        k.nc = nc = bass.Bass("TRN2", target_bir_lowering=False)
        k.es = ExitStack()
        k.s = Sch(nc, k.es)
        k.inp = {}
        k.final = []

    def din(k, name, shape, dt=F32):
        a = k.nc.dram_tensor(name, list(shape), dt, kind="ExternalInput").ap()
        k.inp[name] = a
        return a

    def dscr(k, name, shape, dt):
        return k.nc.dram_tensor(name, list(shape), dt, kind="Internal").ap()

    def sb(k, es, name, shape, dt):
        return es.enter_context(k.nc.sbuf_tensor(name, list(shape), dt))

    def conv(k, name, shape, nsplit=1):
        src = k.din(name, shape)
        dst = k.dscr(name + "_b", shape, BF16)
        rows = shape[0]
        step = rows // nsplit
        for i in range(nsplit):
            k.s.DMA("pool", dst[i * step:(i + 1) * step, :], src[i * step:(i + 1) * step, :], w=[name + "_b"])
        return dst

    def rstd(k, src_ap, junk, ss, rs, key_r, scale_out=1.0):
        s = k.s
        s.I("act", lambda e: e.activation(out=junk[:], in_=src_ap, func=AF.Square, accum_out=ss[:]),
            r=key_r, w=[junk.name, ss.name])
        s.I("act", lambda e: e.activation(out=rs[:], in_=ss[:], func=AF.Sqrt, bias=k.epsb[:],
                                          scale=1.0 / (D * scale_out * scale_out)),
            r=[ss.name], w=[rs.name])
        s.I("dve", lambda e: e.reciprocal(out=rs[:], in_=rs[:]), r=[rs.name], w=[rs.name])

    def ffn(k, tag, src, dst, pre_g, post_g, wg, wu, wd, psb, skey="x", dkey="out", tiles=None, dst_off=0, wkeys=None):
        s = k.s; nc = k.nc
        with ExitStack() as es:
            gpre = k.sb(es, tag + "gpre", [128, D], F32)
            gpost = k.sb(es, tag + "gpost", [128, D], F32)
            s.DMA("sp", gpre[:], pre_g.partition_broadcast(128), w=[gpre.name])
            s.DMA("sp", gpost[:], post_g.partition_broadcast(128), w=[gpost.name])
            kg_, ku_, kd_ = wkeys if wkeys is not None else (tag + "wg", tag + "wu", tag + "wd")
            xT = k.sb(es, tag + "xT", [128, 16, 512], BF16)
            hT = k.sb(es, tag + "hT", [128, 44, 512], BF16)
            wb = [k.sb(es, tag + "wb%d" % i, [128, 16, 512], BF16) for i in range(3)]
            f1 = k.sb(es, tag + "f1", [128, 4, D], F32)
            xs = [k.sb(es, tag + "xs%d" % i, [128, D], F32) for i in range(2)]
            xn = [k.sb(es, tag + "xn%d" % i, [128, D], BF16) for i in range(2)]
            junk = k.sb(es, tag + "junk", [128, D], F32)
            sg = [k.sb(es, tag + "sg%d" % i, [128, 512], F32) for i in range(2)]
            ss = [k.sb(es, tag + "ss%d" % i, [128, 1], F32) for i in range(2)]
            rs = [k.sb(es, tag + "rs%d" % i, [128, 1], F32) for i in range(2)]
            wbi = [0]

            def nextw():
                b = wb[wbi[0] % 3]; wbi[0] += 1
                return b

            xi = 0
            for tt in (tiles if tiles is not None else range(k.NT)):
                t0 = tt * 512
                for sub in range(4):
                    x_ = xs[xi % 2]; xn_ = xn[xi % 2]; ss_ = ss[xi % 2]; rs_ = rs[xi % 2]; xi += 1
                    rows = src[t0 + sub * 128: t0 + sub * 128 + 128, :]
                    s.DMA("sp", x_[:], rows, r=[skey], w=[x_.name])
                    k.rstd(x_[:], junk, ss_, rs_, [x_.name])
                    s.I("dve", lambda e, x_=x_, xn_=xn_, rs_=rs_: e.scalar_tensor_tensor(
                        out=xn_[:], in0=x_[:], scalar=rs_[:], in1=gpre[:], op0=ALU.mult, op1=ALU.mult),
                        r=[x_.name, rs_.name, gpre.name], w=[xn_.name])
                    for half in range(2):
                        pst = psb[6 + half]
                        pv = pst[:].bitcast(BF16)
                        for j in range(8):
                            kk = half * 8 + j
                            s.I("pe", lambda e, pv=pv, xn_=xn_, kk=kk, j=j: e.transpose(
                                out=pv[:, j * 128:(j + 1) * 128], in_=xn_[:, kk * 128:(kk + 1) * 128],
                                identity=k.ident[:]),
                                r=[xn_.name, "ident"], w=[pst.name])
                        eng = "act" if half == 0 else "dve"
                        if eng == "act":
                            s.I("act", lambda e, pv=pv, half=half, sub=sub: e.copy(
                                out=xT[:, half * 8:half * 8 + 8, sub * 128:(sub + 1) * 128],
                                in_=pv.rearrange("p (a b) -> p a b", a=8)),
                                r=[pst.name], w=[xT.name])
                        else:
                            s.I("dve", lambda e, pv=pv, half=half, sub=sub: e.tensor_copy(
                                out=xT[:, half * 8:half * 8 + 8, sub * 128:(sub + 1) * 128],
                                in_=pv.rearrange("p (a b) -> p a b", a=8)),
                                r=[pst.name], w=[xT.name])
                pi = 0
                for cc in range(11):
                    wgc = nextw()
                    s.DMA("sp", wgc[:], wg[:, cc * 512:(cc + 1) * 512].rearrange("(a p) c -> p a c", p=128),
                          r=[kg_], w=[wgc.name])
                    wuc = nextw()
                    s.DMA("sp", wuc[:], wu[:, cc * 512:(cc + 1) * 512].rearrange("(a p) c -> p a c", p=128),
                          r=[ku_], w=[wuc.name])
                    for fs in range(4):
                        f = cc * 4 + fs
                        pg = psb[(pi * 2) % 6]; pu = psb[(pi * 2 + 1) % 6]; sg_ = sg[pi % 2]; pi += 1
                        for kk in range(16):
                            s.I("pe", lambda e, pg=pg, wgc=wgc, kk=kk, fs=fs: e.matmul(
                                pg[:], lhsT=wgc[:, kk, fs * 128:(fs + 1) * 128], rhs=xT[:, kk, :],
                                start=(kk == 0), stop=(kk == 15)), r=[wgc.name, xT.name], w=[pg.name])
                        for kk in range(16):
                            s.I("pe", lambda e, pu=pu, wuc=wuc, kk=kk, fs=fs: e.matmul(
                                pu[:], lhsT=wuc[:, kk, fs * 128:(fs + 1) * 128], rhs=xT[:, kk, :],
                                start=(kk == 0), stop=(kk == 15)), r=[wuc.name, xT.name], w=[pu.name])
                        s.I("act", lambda e, pg=pg, sg_=sg_: e.activation(out=sg_[:], in_=pg[:], func=AF.Silu),
                            r=[pg.name], w=[sg_.name])
                        s.I("dve", lambda e, pu=pu, sg_=sg_, f=f: e.tensor_tensor(
                            out=hT[:, f, :], in0=sg_[:], in1=pu[:], op=ALU.mult),
                            r=[pu.name, sg_.name], w=[hT.name])
                for c4 in range(4):
                    base = 0 if c4 % 2 == 0 else 4
                    f0 = 0
                    for nf in (16, 16, 12):
                        wdc = nextw()
                        s.DMA("sp", wdc[:, 0:nf, :],
                              wd[f0 * 128:(f0 + nf) * 128, c4 * 512:(c4 + 1) * 512].rearrange("(a p) c -> p a c", p=128),
                              r=[kd_], w=[wdc.name])
                        for sub in range(4):
                            po = psb[base + sub]
                            for fi in range(nf):
                                f = f0 + fi
                                s.I("pe", lambda e, po=po, wdc=wdc, fi=fi, f=f, sub=sub: e.matmul(
                                    po[:], lhsT=hT[:, f, sub * 128:(sub + 1) * 128], rhs=wdc[:, fi, :],
                                    start=(f == 0), stop=(f == 43)), r=[wdc.name, hT.name], w=[po.name])
                        f0 += nf
                    for sub in range(4):
                        po = psb[base + sub]
                        if sub % 2 == 0:
                            s.I("act", lambda e, po=po, sub=sub, c4=c4: e.copy(
                                out=f1[:, sub, c4 * 512:(c4 + 1) * 512], in_=po[:]),
                                r=[po.name], w=[f1.name + str(sub)])
                        else:
                            s.I("dve", lambda e, po=po, sub=sub, c4=c4: e.tensor_copy(
                                out=f1[:, sub, c4 * 512:(c4 + 1) * 512], in_=po[:]),
                                r=[po.name], w=[f1.name + str(sub)])
                for sub in range(4):
                    x_ = xs[xi % 2]; ss_ = ss[xi % 2]; rs_ = rs[xi % 2]; xi += 1
                    rows = src[t0 + sub * 128: t0 + sub * 128 + 128, :]
                    s.DMA("sp", x_[:], rows, r=[skey], w=[x_.name])
                    k.rstd(f1[:, sub, :], junk, ss_, rs_, [f1.name + str(sub)], scale_out=0.5)
                    s.I("dve", lambda e, sub=sub, rs_=rs_: e.scalar_tensor_tensor(
                        out=f1[:, sub, :], in0=f1[:, sub, :], scalar=rs_[:], in1=gpost[:],
                        op0=ALU.mult, op1=ALU.mult),
                        r=[f1.name + str(sub), rs_.name, gpost.name], w=[f1.name + str(sub)])
                    s.I("pool", lambda e, sub=sub, x_=x_: e.tensor_tensor(
                        out=x_[:], in0=f1[:, sub, :], in1=x_[:], op=ALU.add),
                        r=[f1.name + str(sub), x_.name], w=[x_.name])
                    d = s.DMA("sp", dst[t0 - dst_off + sub * 128: t0 - dst_off + sub * 128 + 128, :], x_[:], r=[x_.name], w=[dkey])
                    k.final.append(d)
            s.barrier()


    @staticmethod
    def _n(xs):
        return [x if isinstance(x, str) else x.name for x in xs]

    def TT(k, eng, out, in0, in1, op, r, w):
        k.s.I(eng, lambda e: e.tensor_tensor(out=out, in0=in0, in1=in1, op=op), k._n(r), k._n(w))

    def TS(k, eng, out, in0, s1, s2, op0, op1, r, w):
        if op1 is None:
            k.s.I(eng, lambda e: e.tensor_scalar(out=out, in0=in0, scalar1=s1, scalar2=None, op0=op0), k._n(r), k._n(w))
        else:
            k.s.I(eng, lambda e: e.tensor_scalar(out=out, in0=in0, scalar1=s1, scalar2=s2, op0=op0, op1=op1), k._n(r), k._n(w))

    def STT(k, out, in0, scalar, in1, op0, op1, r, w):
        k.s.I("dve", lambda e: e.scalar_tensor_tensor(out=out, in0=in0, scalar=scalar, in1=in1, op0=op0, op1=op1),
              k._n(r), k._n(w))

    def ACT(k, out, in_, func, r, w, **kw):
        k.s.I("act", lambda e: e.activation(out=out, in_=in_, func=func, **kw), k._n(r), k._n(w))

    def MM(k, out, lhsT, rhs, start, stop, r, w):
        k.s.I("pe", lambda e: e.matmul(out, lhsT=lhsT, rhs=rhs, start=start, stop=stop), k._n(r), k._n(w))

    def TR(k, out, in_, r, w):
        k.s.I("pe", lambda e: e.transpose(out=out, in_=in_, identity=k.ident[:]), k._n(r) + ["ident"], k._n(w))

    def CP(k, eng, out, in_, r, w):
        if eng == "act":
            k.s.I("act", lambda e: e.copy(out=out, in_=in_), k._n(r), k._n(w))
        else:
            k.s.I(eng, lambda e: e.tensor_copy(out=out, in_=in_), k._n(r), k._n(w))

    def LD(k, out, in_, r, w, eng="sp", **kw):
        return k.s.DMA(eng, out, in_, k._n(r), k._n(w), **kw)

    def gelu_tanh(k, y_, a_, b_, out_ap, wname):
        k.ACT(a_[:], y_[:], AF.Square, [y_], [a_])
        k.TS("dve", a_[:], a_[:], 0.044715, 1.0, ALU.mult, ALU.add, [a_], [a_])
        k.TT("dve", b_[:], a_[:], y_[:], ALU.mult, [a_, y_], [b_])
        k.ACT(b_[:], b_[:], AF.Sigmoid, [b_], [b_], scale=1.5957691216057308)
        k.TT("dve", out_ap, b_[:], y_[:], ALU.mult, [b_, y_], [wname])

    def prenormT(k, src, skey, t0, gpre, xT, bufs, psb):
        s = k.s
        xs, xn, junk, ss, rs, ctr = bufs
        for sub in range(4):
            i = ctr[0] % 2; ctr[0] += 1
            x_ = xs[i]; xn_ = xn[i]; ss_ = ss[i]; rs_ = rs[i]
            k.LD(x_[:], src[t0 + sub * 128: t0 + sub * 128 + 128, :], [skey], [x_])
            k.rstd(x_[:], junk, ss_, rs_, [x_.name])
            k.STT(xn_[:], x_[:], rs_[:], gpre[:], ALU.mult, ALU.mult, [x_, rs_, gpre], [xn_])
            for half in range(2):
                pst = psb[6 + half]
                pv = pst[:].bitcast(BF16)
                for j in range(8):
                    kk = half * 8 + j
                    k.TR(pv[:, j * 128:(j + 1) * 128], xn_[:, kk * 128:(kk + 1) * 128], [xn_], [pst])
                k.CP("act" if half == 0 else "dve", xT[:, half * 8:half * 8 + 8, sub * 128:(sub + 1) * 128],
                     pv.rearrange("p (a b) -> p a b", a=8), [pst], [xT])

    def down_post(k, tag, hT, KT, wd, wkey, src, skey, dst, dkey, gpost, scale_out, t0, f1, wb, wbi, bufs, psb):
        s = k.s
        xs, xn, junk, ss, rs, ctr = bufs
        groups = []
        f0 = 0
        while f0 < KT:
            nf = min(16, KT - f0); groups.append((f0, nf)); f0 += nf
        for c4 in range(4):
            base = 0 if c4 % 2 == 0 else 4
            for (f0, nf) in groups:
                wdc = wb[wbi[0] % 3]; wbi[0] += 1
                k.LD(wdc[:, 0:nf, :], wd[f0 * 128:(f0 + nf) * 128, c4 * 512:(c4 + 1) * 512].rearrange("(a p) c -> p a c", p=128),
                     [wkey], [wdc])
                for sub in range(4):
                    po = psb[base + sub]
                    for fi in range(nf):
                        f = f0 + fi
                        k.MM(po[:], hT[:, f, sub * 128:(sub + 1) * 128], wdc[:, fi, :], f == 0, f == KT - 1, [wdc, hT], [po])
            for sub in range(4):
                po = psb[base + sub]
                k.CP("act" if sub % 2 == 0 else "dve", f1[:, sub, c4 * 512:(c4 + 1) * 512], po[:], [po], [f1.name + str(sub)])
        for sub in range(4):
            i = ctr[0] % 2; ctr[0] += 1
            x_ = xs[i]; ss_ = ss[i]; rs_ = rs[i]
            k.LD(x_[:], src[t0 + sub * 128: t0 + sub * 128 + 128, :], [skey], [x_])
            k.rstd(f1[:, sub, :], junk, ss_, rs_, [f1.name + str(sub)], scale_out=scale_out)
            k.STT(f1[:, sub, :], f1[:, sub, :], rs_[:], gpost[:], ALU.mult, ALU.mult,
                  [f1.name + str(sub), rs_, gpost], [f1.name + str(sub)])
            k.TT("pool", x_[:], f1[:, sub, :], x_[:], ALU.add, [f1.name + str(sub), x_], [x_])
            d = k.LD(dst[t0 + sub * 128: t0 + sub * 128 + 128, :], x_[:], [x_], [dkey])
            k.final.append(d)

    def normbufs(k, es, tag):
        xs = [k.sb(es, tag + "xs%d" % i, [128, D], F32) for i in range(2)]
        xn = [k.sb(es, tag + "xn%d" % i, [128, D], BF16) for i in range(2)]
        junk = k.sb(es, tag + "junk", [128, D], F32)
        ss = [k.sb(es, tag + "ss%d" % i, [128, 1], F32) for i in range(2)]
        rs = [k.sb(es, tag + "rs%d" % i, [128, 1], F32) for i in range(2)]
        return (xs, xn, junk, ss, rs, [0])

    def proj(k, h1, gmix, win, sc, psb):
        s = k.s; T = k.T
        with ExitStack() as es:
            gpre = k.sb(es, "pj_g", [128, D], F32)
            k.LD(gpre[:], gmix.partition_broadcast(128), [], [gpre])
            uT = k.sb(es, "pj_uT", [128, 16, 512], BF16)
            wb = [k.sb(es, "pj_wb%d" % i, [128, 16, 512], BF16) for i in range(3)]
            bufs = k.normbufs(es, "pj_")
            s32 = [k.sb(es, "pj_s32%d" % i, [128, 512], F32) for i in range(2)]
            s16 = [k.sb(es, "pj_s16%d" % i, [128, 512], BF16) for i in range(4)]
            sg = [k.sb(es, "pj_sg%d" % i, [128, 48], F32) for i in range(2)]
            cnt = {"w": 0, "p": 0, "a": 0, "b": 0, "g": 0, "e": 0}

            def nps():
                p = psb[cnt["p"] % 6]; cnt["p"] += 1
                return p

            chunks = [(c * 512, 512) for c in range(7)] + [(3584, 48)] + [(3632 + 512 * i, 512) for i in range(8)]
            for tt in range(k.NT):
                t0 = tt * 512
                k.prenormT(h1, "h1", t0, gpre, uT, bufs, psb)
                for ci, (c0, wd_) in enumerate(chunks):
                    if t0 < k.OWN and ci not in (0, 1, 4, 5, 6):
                        continue
                    wc = wb[cnt["w"] % 3]; cnt["w"] += 1
                    k.LD(wc[:, :, 0:wd_], win[:, c0:c0 + wd_].rearrange("(a p) c -> p a c", p=128), ["win"], [wc])

                    def fm(off, M, kind, dst):
                        p = nps()
                        for kk in range(16):
                            k.MM(p[0:M, :], wc[:, kk, off:off + M], uT[:, kk, :], kk == 0, kk == 15, [wc, uT], [p])
                        if kind == "us":
                            st = s32[cnt["a"] % 2]; cnt["a"] += 1
                            cnt["e"] += 1
                            k.CP("act" if cnt["e"] % 2 else "dve", st[0:M, :], p[0:M, :], [p], [st])
                        else:
                            st = s16[cnt["b"] % 4]; cnt["b"] += 1
                            if kind == "q":
                                k.s.I("act", lambda e, st=st, p=p, M=M: e.mul(out=st[0:M, :], in_=p[0:M, :], mul=0.125), [p.name], [st.name])
                            elif kind == "sig":
                                k.ACT(st[0:M, :], p[0:M, :], AF.Sigmoid, [p], [st])
                            else:
                                k.CP("dve", st[0:M, :], p[0:M, :], [p], [st])
                        k.LD(dst, st[0:M, :], [st], ["pjout"])

                    def tm(off, N, kind, dst_t):
                        for sub in range(4):
                            p = nps()
                            for kk in range(16):
                                k.MM(p[:, 0:N], uT[:, kk, sub * 128:(sub + 1) * 128], wc[:, kk, off:off + N], kk == 0, kk == 15, [wc, uT], [p])
                            if kind == "sig":
                                st = sg[cnt["g"] % 2]; cnt["g"] += 1
                                k.ACT(st[:, 0:N], p[:, 0:N], AF.Sigmoid, [p], [st])
                            else:
                                st = s16[cnt["b"] % 4]; cnt["b"] += 1
                                k.CP("dve", st[:, 0:N], p[:, 0:N], [p], [st])
                            k.LD(dst_t[t0 + sub * 128: t0 + sub * 128 + 128, :], st[:, 0:N], [st], ["pjout"])

                    ts_ = slice(t0, t0 + 512)
                    if ci in (0, 1):
                        for j in range(4):
                            fm(j * 128, 128, "us", sc["usT"][(ci * 4 + j) * 128:(ci * 4 + j + 1) * 128, ts_])
                    elif ci in (2, 3):
                        for j in range(4):
                            h0 = (ci - 2) * 8 + 2 * j
                            fm(j * 128, 128, "q", sc["qT"][h0:h0 + 2, :, ts_].rearrange("a d t -> (a d) t"))
                    elif ci == 4:
                        for g2 in range(2):
                            fm(g2 * 128, 128, "k", sc["kcT"][2 * g2:2 * g2 + 2, :, ts_].rearrange("a d t -> (a d) t"))
                        for g2 in range(2):
                            fm(256 + g2 * 128, 128, "k", sc["vcT"][2 * g2:2 * g2 + 2, :, ts_].rearrange("a d t -> (a d) t"))
                    elif ci == 5:
                        for g2 in range(2):
                            fm(g2 * 128, 128, "k", sc["ksT"][2 * g2:2 * g2 + 2, :, ts_].rearrange("a d t -> (a d) t"))
                        tm(256, 256, "k", sc["Vs"])
                    elif ci == 6:
                        for g2 in range(2):
                            fm(g2 * 128, 128, "k", sc["kwT"][2 * g2:2 * g2 + 2, :, ts_].rearrange("a d t -> (a d) t"))
                        tm(256, 256, "k", sc["Vw"])
                    elif ci == 7:
                        tm(0, 48, "sig", sc["gates"])
                    elif ci < 12:
                        for j in range(4):
                            fm(j * 128, 128, "sig", sc["gaT"][((ci - 8) * 4 + j) * 128:((ci - 8) * 4 + j + 1) * 128, ts_])
                    else:
                        for j in range(4):
                            fm(j * 128, 128, "sig", sc["gbT"][((ci - 12) * 4 + j) * 128:((ci - 12) * 4 + j + 1) * 128, ts_])
            s.barrier()


    def compress(k, es_out, sc, w, psb):
        s = k.s; T = k.T
        NC = T // 16 - 1
        NIT = (NC + 127) // 128
        KcT = k.sb(es_out, "KcT", [64, 4, 512], BF16)
        Vc = k.sb(es_out, "Vc", [128, 4, 4, 64], BF16)
        with ExitStack() as es:
            raw = k.sb(es, "cp_raw", [64, T], BF16)
            w1 = k.sb(es, "cp_w1", [64, 32, 256], BF16)
            w2 = k.sb(es, "cp_w2", [128, 2, 64], BF16)
            posf = k.sb(es, "cp_posf", [64, 32], F32)
            posb = k.sb(es, "cp_posb", [64, 32], BF16)
            bias = k.sb(es, "cp_bias", [128, 2], F32)
            y_ = k.sb(es, "cp_y", [128, 512], F32); a_ = k.sb(es, "cp_a", [128, 512], F32); b_ = k.sb(es, "cp_b", [128, 512], F32)
            hid = k.sb(es, "cp_hid", [128, 2, 512], BF16)
            k.LD(posf[:], k.din("cmp_posT", [64, 32]), [], [posf])
            k.CP("dve", posb[:], posf[:], [posf], [posb])
            for kind in range(2):
                rawsc = sc["kcT"] if kind == 0 else sc["vcT"]
                w1d, w2d = (w["ck1"], w["ck2"]) if kind == 0 else (w["cv1"], w["cv2"])
                k.LD(w1[:], w1d.rearrange("(l d) c -> d l c", d=64), ["cw"], [w1])
                k.LD(w2[:], w2d.rearrange("(a p) d -> p a d", p=128), ["cw"], [w2])
                for ht in range(2):
                    p = psb[ht]
                    for l in range(32):
                        k.MM(p[:, 0:1], w1[:, l, ht * 128:(ht + 1) * 128], posb[:, l:l + 1], l == 0, l == 31, [w1, posb], [p])
                    k.CP("dve", bias[:, ht:ht + 1], p[:, 0:1], [p], [bias])
                for g in range(4):
                    k.LD(raw[:], rawsc[g, :, :], ["pjout"], [raw])
                    for ht in range(2):
                        p = psb[2 + ht]
                        for l in range(32):
                            k.MM(p[:, 0:NC], w1[:, l, ht * 128:(ht + 1) * 128], raw[:, l: l + 16 * (NC - 1) + 1: 16],
                                 l == 0, l == 31, [w1, raw], [p])
                        k.TS("dve", y_[:, 0:NC], p[:, 0:NC], bias[:, ht:ht + 1], None, ALU.add, None, [p, bias], [y_])
                        k.gelu_tanh(y_, a_, b_, hid[:, ht, :], hid.name)
                    if kind == 0:
                        p = psb[4]
                        for ht in range(2):
                            k.MM(p[0:64, 0:NC], w2[:, ht, :], hid[:, ht, 0:NC], ht == 0, ht == 1, [w2, hid], [p])
                        k.CP("act", KcT[:, g, 0:NC], p[0:64, 0:NC], [p], [KcT])
                    else:
                        for it in range(NIT):
                            rows = min(128, NC - it * 128)
                            p = psb[4 + it % 2]
                            for ht in range(2):
                                k.MM(p[0:rows, 0:64], hid[:, ht, it * 128: it * 128 + rows], w2[:, ht, :], ht == 0, ht == 1, [w2, hid], [p])
                            k.CP("act", Vc[0:rows, it, g, :], p[0:rows, 0:64], [p], [Vc])
            s.barrier()
        return KcT, Vc

    def bias_tables(k, psb):
        s = k.s
        WS, WW = 1152, 768
        SKS = k.dscr("SKS", [16, 128 * (WS + 1)], F32)
        SKW = k.dscr("SKW", [16, 128 * (WW + 1)], F32)
        rb = k.din("rel_bias", [32, 16])
        with ExitStack() as es:
            ohs = k.sb(es, "bt_ohs", [33, WS], F32); ohw = k.sb(es, "bt_ohw", [33, WW], F32)
            k.LD(ohs[:], k.din("c_ohs", [33, WS]), [], [ohs]); k.LD(ohw[:], k.din("c_ohw", [33, WW]), [], [ohw])
            rbw = k.sb(es, "bt_rbw", [33, 16], F32); rbs = k.sb(es, "bt_rbs", [33, 16], F32); r31 = k.sb(es, "bt_r31", [33, 16], F32)
            s.I("dve", lambda e: e.memset(rbw[:], 1.0), w=[rbw.name])
            s.I("dve", lambda e: e.memset(rbs[:], 1.0), w=[rbs.name])
            s.I("dve", lambda e: e.memset(r31[:], 0.0), w=[r31.name])
            k.LD(rbw[0:32, :], rb, [rbw], [rbw])
            k.LD(r31[0:32, :], rb[31, :].partition_broadcast(32), [r31], [r31])
            k.TT("dve", rbs[0:32, :], rbw[0:32, :], r31[0:32, :], ALU.subtract, [rbw, r31], [rbs])
            ones = k.sb(es, "bt_ones", [33, 128], F32)
            s.I("dve", lambda e: e.memset(ones[:], 1.0), w=[ones.name])
            lh = [k.sb(es, "bt_lh%d" % i, [33, 128], F32) for i in range(2)]
            rep = [k.sb(es, "bt_rep%d" % i, [128, WS], F32) for i in range(2)]
            n = 0
            for (rbt, oh, W, SK) in ((rbs, ohs, WS, SKS), (rbw, ohw, WW, SKW)):
                for h in range(16):
                    l_ = lh[n % 2]; r_ = rep[n % 2]; n += 1
                    k.TS("dve", l_[:], ones[:], rbt[:, h:h + 1], None, ALU.mult, None, [ones, rbt], [l_])
                    for c0 in range(0, W, 512):
                        wd_ = min(512, W - c0)
                        p = psb[(c0 // 512) % 4 + 4 * (n % 2)]
                        k.MM(p[:, 0:wd_], l_[:], oh[:, c0:c0 + wd_], True, True, [l_, oh], [p])
                        k.CP("act" if (c0 // 512) % 2 else "dve", r_[:, c0:c0 + wd_], p[:, 0:wd_], [p], [r_])
                    dst = bass.AP(SK.tensor, h * 128 * (W + 1), [[W + 1, 128], [1, W]])
                    k.LD(dst, r_[:, 0:W], [r_], ["SK"])
            s.barrier()
        return SKS, SKW, WS, WW

    def nsa(k, sc, KcT, Vc, SKS, SKW, WS, WW, psb):
        s = k.s; T = k.T
        NC = T // 16 - 1
        NQ = T // 128
        NKT = T // 128
        LAGB, LAGC, NBUF = 3, 6, 9
        SB_ = (0, 1, 7)
        with ExitStack() as es:
            KsT = k.sb(es, "ns_KsT", [64, T], BF16); KwT = k.sb(es, "ns_KwT", [64, T], BF16)
            Vs = k.sb(es, "ns_Vs", [128, NKT, 64], BF16); Vw = k.sb(es, "ns_Vw", [128, NKT, 64], BF16)
            TS_ = k.sb(es, "ns_TS", [128, 4, 1024], F32); TW_ = k.sb(es, "ns_TW", [128, 4, 640], F32)
            TC_ = k.sb(es, "ns_TC", [128, 4, 58], F32)
            b31 = k.sb(es, "ns_b31", [128, 16], F32)
            k.LD(b31[:], k.inp["rel_bias"][31, :].partition_broadcast(128), [], [b31])
            vmrel = k.sb(es, "ns_vm", [128, 256], F32); acrel = k.sb(es, "ns_ac", [128, 256], F32)
            k.LD(vmrel[:], k.din("c_vmrel", [128, 256]), [], [vmrel]); k.LD(acrel[:], k.din("c_acrel", [128, 256]), [], [acrel])
            ccm = k.sb(es, "ns_ccm", [128, 128], F32); cca = k.sb(es, "ns_cca", [128, 128], F32)
            cmpm = k.sb(es, "ns_cmpm", [128, 512], F32); wmk = k.sb(es, "ns_wmk", [128, 1], F32)
            k.LD(ccm[:], k.din("pc_cm", [128, 128]), [], [ccm]); k.LD(cca[:], k.din("pc_ca", [128, 128]), [], [cca])
            k.LD(cmpm[:], k.din("pc_cmpmask", [128, 512]), [], [cmpm]); k.LD(wmk[:], k.din("pc_wmask", [128, 1]), [], [wmk])
            qT4 = [k.sb(es, "ns_q%d" % i, [64, 4, 128], BF16) for i in range(2)]
            gt = [k.sb(es, "ns_gt%d" % i, [128, 48], F32) for i in range(2)]
            pcn = k.sb(es, "ns_pcn", [128, 4, 512], F32)
            s.I("pool", lambda e: e.memset(pcn[:], 0.0), w=[pcn.name])
            tmp = [k.sb(es, "ns_tmp%d" % i, [128, 512], F32) for i in range(NBUF)]
            P_ = [k.sb(es, "ns_P%d" % i, [128, 512], BF16) for i in range(NBUF)]
            pT = [k.sb(es, "ns_pT%d" % i, [128, 4, 128], BF16) for i in range(NBUF)]
            ps4 = k.sb(es, "ns_ps4", [128, 512], F32)
            imp = k.sb(es, "ns_imp", [128, 128], F32); sc1 = k.sb(es, "ns_sc1", [128, 128], F32); sc2 = k.sb(es, "ns_sc2", [128, 128], F32)
            m8a = k.sb(es, "ns_m8a", [128, 8], F32); m8b = k.sb(es, "ns_m8b", [128, 8], F32)
            mk = [k.sb(es, "ns_mk%d" % i, [128, 128], F32) for i in range(2)]
            rsc = [k.sb(es, "ns_rsc%d" % i, [128, 4], F32) for i in range(2)]
            rss = [k.sb(es, "ns_rss%d" % i, [128, 4, 32], F32) for i in range(2)]
            rsw = [k.sb(es, "ns_rsw%d" % i, [128, 4, 2], F32) for i in range(2)]
            rt = [k.sb(es, "ns_rt%d" % i, [128, 4, 4], F32) for i in range(2)]
            oacc = k.sb(es, "ns_oacc", [128, 256], F32)
            ob = [k.sb(es, "ns_ob%d" % i, [128, 256], BF16) for i in range(2)]
            cn = {"o": 0}

            def bc64(ap2):
                return bass.AP(ap2.tensor, ap2.offset, [list(ap2.ap[0]), list(ap2.ap[1]), [0, 64]])

            for g in range(4):
                k.LD(KsT[:], sc["ksT"][g, :, :], ["pjout"], [KsT]); k.LD(KwT[:], sc["kwT"][g, :, :], ["pjout"], [KwT])
                k.LD(Vs[:], sc["Vs"][:, g * 64:(g + 1) * 64].rearrange("(n p) d -> p n d", p=128), ["pjout"], [Vs])
                k.LD(Vw[:], sc["Vw"][:, g * 64:(g + 1) * 64].rearrange("(n p) d -> p n d", p=128), ["pjout"], [Vw])
                for hl in range(4):
                    h = 4 * g + hl
                    k.LD(TS_[:, hl, :], bass.AP(SKS.tensor, h * 128 * (WS + 1) + 127, [[WS, 128], [1, 1024]]), ["SK"], [TS_])
                    k.LD(TW_[:, hl, :], bass.AP(SKW.tensor, h * 128 * (WW + 1) + 127, [[WW, 128], [1, 640]]), ["SK"], [TW_])
                    k.LD(TC_[:, hl, :], bass.AP(SKS.tensor, h * 128 * (WS + 1) + 238, [[WS, 128], [16, 58]]), ["SK"], [TC_],
                         allow_slow_non_contiguous=True)
                items = []
                for n in range(NQ // 2, NQ):
                    q0 = 128 * n; par = n % 2
                    ncv = min(NC, 8 * n + 7)
                    blk = []
                    for hl in range(4):
                        blk.append(dict(kind="c", n=n, hl=hl, wdt=ncv))
                    w0 = max(0, q0 - 512)
                    pieces = []
                    a_ = w0
                    while a_ < q0 + 128:
                        b_ = min(a_ + 512, q0 + 128)
                        if a_ < k.OWN < b_:
                            b_ = k.OWN
                        pieces.append((a_, b_ - a_)); a_ = b_
                    assert len(pieces) <= 2
                    for hl in range(4):
                        for pi_, (s0, wdt) in enumerate(pieces):
                            blk.append(dict(kind="w", n=n, hl=hl, s0=s0, wdt=wdt, pi=pi_, first=pi_ == 0, last=pi_ == len(pieces) - 1,
                                            npc=len(pieces)))
                    nch = n // 4 + 1
                    for hl in range(4):
                        for c in range(nch):
                            s0 = 512 * c
                            blk.append(dict(kind="s", n=n, hl=hl, s0=s0, wdt=min(512, q0 + 128 - s0), c=c, first=c == 0, last=c == nch - 1, nch=nch))
                    blk[0]["load"] = True
                    blk[3]["topk"] = True
                    blk[-1]["combine"] = True
                    blk[-1]["npc_w"] = len(pieces)
                    items += blk
                for i, it in enumerate(items):
                    it["i"] = i

                def stageA(it):
                    n = it["n"]; hl = it["hl"]; h = 4 * g + hl; q0 = 128 * n; par = n % 2; i = it["i"]
                    q_ = qT4[par]; g_ = gt[par]
                    if it.get("load"):
                        k.LD(q_[:], sc["qT"][4 * g:4 * g + 4, :, q0:q0 + 128].rearrange("h d t -> d h t"), ["pjout"], [q_])
                        k.LD(g_[:], sc["gates"][q0:q0 + 128, :], ["pjout"], [g_])
                    p = psb[SB_[i % 3]]; t_ = tmp[i % NBUF]; Pt = P_[i % NBUF]
                    wdt = it["wdt"]
                    if it["kind"] == "c":
                        ncv = wdt
                        i_lo = max(0, 8 * n - 51); c_lo = i_lo - 8 * n + 51
                        k.MM(p[:, 0:ncv], q_[:, hl, :], KcT[:, g, 0:ncv], True, True, [q_, KcT], [p])
                        k.TT("dve", t_[:, 0:ncv], p[:, 0:ncv], cmpm[:, 0:ncv], ALU.add, [p, cmpm], [t_])
                        k.TT("pool", t_[:, i_lo:ncv], t_[:, i_lo:ncv], TC_[:, hl, c_lo:c_lo + ncv - i_lo], ALU.add, [t_, TC_], [t_])
                        rk = rt[par].name + str(hl)
                        k.ACT(t_[:, 0:ncv], t_[:, 0:ncv], AF.Exp, [t_, b31], [t_, rsc[par].name + str(hl)], bias=b31[:, h:h + 1],
                              accum_out=rsc[par][:, hl:hl + 1])
                        k.TS("dve", rt[par][:, hl, 0:1], rsc[par][:, hl:hl + 1], 1e-30, None, ALU.max, None, [rsc[par].name + str(hl)], [rk])
                        s.I("dve", lambda e, hl=hl, par=par: e.reciprocal(out=rt[par][:, hl, 0:1], in_=rt[par][:, hl, 0:1]), [rk], [rk])
                        k.TS("dve", pcn[:, hl, 0:ncv], t_[:, 0:ncv], rt[par][:, hl, 0:1], None, ALU.mult, None, [t_, rk], [pcn])
                        k.CP("pool", Pt[:, 0:ncv], pcn[:, hl, 0:ncv], [pcn], [Pt])
                    elif it["kind"] == "w":
                        s0 = it["s0"]
                        k.MM(p[:, 0:wdt], q_[:, hl, :], KwT[:, s0:s0 + wdt], True, True, [q_, KwT], [p])
                        v_lo = s0 - q0 + 512
                        if s0 < k.OWN:
                            k.STT(t_[:, 0:wdt], p[:, 0:wdt], wmk[:, 0:1], TW_[:, hl, v_lo:v_lo + wdt], ALU.add, ALU.add, [p, wmk, TW_], [t_])
                        else:
                            k.TT("dve", t_[:, 0:wdt], p[:, 0:wdt], TW_[:, hl, v_lo:v_lo + wdt], ALU.add, [p, TW_], [t_])
                        k.ACT(Pt[:, 0:wdt], t_[:, 0:wdt], AF.Exp, [t_], [Pt, rsw[par].name + str(hl)], accum_out=rsw[par][:, hl, it["pi"]:it["pi"] + 1])
                    else:
                        s0 = it["s0"]; c = it["c"]
                        k.MM(p[:, 0:wdt], q_[:, hl, :], KsT[:, s0:s0 + wdt], True, True, [q_, KsT], [p])
                        nb = wdt // 64
                        k.TT("dve", t_[:, 0:wdt].rearrange("p (a b) -> p a b", b=64), p[:, 0:wdt].rearrange("p (a b) -> p a b", b=64),
                             bc64(mk[par][:, 8 * c: 8 * c + nb]), ALU.add, [p, mk[par]], [t_])
                        s_lo = max(s0, q0 - 896)
                        if s_lo < s0 + wdt:
                            v_lo = s_lo - q0 + 896
                            ln = s0 + wdt - s_lo
                            k.TT("pool", t_[:, s_lo - s0: wdt], t_[:, s_lo - s0: wdt], TS_[:, hl, v_lo:v_lo + ln], ALU.add, [t_, TS_], [t_])
                        k.ACT(Pt[:, 0:wdt], t_[:, 0:wdt], AF.Exp, [t_, b31], [Pt, rss[par].name + str(hl)], bias=b31[:, h:h + 1],
                              accum_out=rss[par][:, hl, c:c + 1])
                    if it.get("topk"):
                        k.TT("dve", ps4[:], pcn[:, 0, :], pcn[:, 1, :], ALU.add, [pcn], [ps4])
                        k.TT("dve", ps4[:], ps4[:], pcn[:, 2, :], ALU.add, [pcn, ps4], [ps4])
                        k.TT("dve", ps4[:], ps4[:], pcn[:, 3, :], ALU.add, [pcn, ps4], [ps4])
                        s.I("dve", lambda e: e.tensor_reduce(out=imp[:], in_=ps4[:].rearrange("p (j m) -> p j m", m=4), axis=AX.X, op=ALU.add),
                            [ps4.name], [imp.name])
                        k.TT("dve", imp[:, 1:128], imp[:, 1:128], ps4[:, 3:508:4], ALU.add, [imp, ps4], [imp])
                        so = 127 - 2 * n
                        k.TT("dve", sc1[:], imp[:], vmrel[:, so:so + 128], ALU.mult, [imp, vmrel], [sc1])
                        k.TT("dve", sc1[:], sc1[:], acrel[:, so:so + 128], ALU.add, [sc1, acrel], [sc1])
                        k.TT("dve", sc1[:], sc1[:], ccm[:], ALU.mult, [sc1, ccm], [sc1])
                        k.TT("dve", sc1[:], sc1[:], cca[:], ALU.add, [sc1, cca], [sc1])
                        s.I("dve", lambda e: e.max(out=m8a[:], in_=sc1[:]), [sc1.name], [m8a.name])
                        s.I("dve", lambda e: e.match_replace(out=sc2[:], in_to_replace=m8a[:], in_values=sc1[:], imm_value=-3e4),
                            [sc1.name, m8a.name], [sc2.name])
                        s.I("dve", lambda e: e.max(out=m8b[:], in_=sc2[:]), [sc2.name], [m8b.name])
                        k.TS("dve", mk[par][:], sc1[:], m8b[:, 7:8], BIG, ALU.is_ge, ALU.mult, [sc1, m8b], [mk[par]])
                        k.TS("dve", mk[par][:], mk[par][:], -BIG, None, ALU.add, None, [mk[par]], [mk[par]])

                def stageB(it):
                    i = it["i"]; Pt = P_[i % NBUF]; ptile = pT[i % NBUF]; wdt = it["wdt"]
                    pst = psb[2 + i % 2]; pv = pst[:].bitcast(BF16)
                    nk = (wdt + 127) // 128
                    for kt in range(nk):
                        rows = min(128, wdt - kt * 128)
                        k.TR(pv[0:rows, kt * 128:(kt + 1) * 128], Pt[:, kt * 128: kt * 128 + rows], [Pt], [pst])
                    cn["o"] += 1
                    eng = "act" if cn["o"] % 2 else "dve"
                    if wdt % 128 == 0:
                        k.CP(eng, ptile[:, 0:nk, :], pv[:, 0:nk * 128].rearrange("p (a b) -> p a b", b=128), [pst], [ptile])
                    else:
                        for kt in range(nk):
                            rows = min(128, wdt - kt * 128)
                            k.CP(eng, ptile[0:rows, kt, :], pv[0:rows, kt * 128:(kt + 1) * 128], [pst], [ptile])

                def stageC(it):
                    n = it["n"]; hl = it["hl"]; par = n % 2; i = it["i"]; ptile = pT[i % NBUF]; wdt = it["wdt"]
                    nk = (wdt + 127) // 128
                    pcs = psb[4 + par]
                    if it["kind"] == "c":
                        po = pcs[:, hl * 64:(hl + 1) * 64]; pkey = "po_cs%d" % par
                        for kt in range(nk):
                            rows = min(128, wdt - kt * 128)
                            k.MM(po, ptile[0:rows, kt, :], Vc[0:rows, kt, g, :], kt == 0, kt == nk - 1, [ptile, Vc], [pkey])
                    elif it["kind"] == "w":
                        po = psb[6][:, par * 256 + hl * 64: par * 256 + (hl + 1) * 64]; pkey = "po_w%d" % par
                        kt0 = it["s0"] // 128
                        for kt in range(nk):
                            k.MM(po, ptile[:, kt, :], Vw[:, kt0 + kt, :], it["first"] and kt == 0, it["last"] and kt == nk - 1, [ptile, Vw], [pkey])
                    else:
                        po = pcs[:, 256 + hl * 64: 256 + (hl + 1) * 64]; pkey = "po_cs%d" % par
                        kt0 = it["s0"] // 128
                        for kt in range(nk):
                            k.MM(po, ptile[:, kt, :], Vs[:, kt0 + kt, :], it["first"] and kt == 0, it["last"] and kt == nk - 1, [ptile, Vs], [pkey])
                    if it.get("combine"):
                        q0 = 128 * n; g_ = gt[par]
                        o_ = ob[par]
                        for h2 in range(4):
                            col = g * 12 + h2 * 3
                            rk = rt[par].name + str(h2)
                            nch = n // 4 + 1
                            npc = it["npc_w"]
                            s.I("dve", lambda e, h2=h2, nch=nch, par=par: e.tensor_reduce(out=rt[par][:, h2, 1:2], in_=rss[par][:, h2, 0:nch], axis=AX.X, op=ALU.add),
                                [rss[par].name + str(h2)], [rk])
                            s.I("dve", lambda e, h2=h2, npc=npc, par=par: e.tensor_reduce(out=rt[par][:, h2, 2:3], in_=rsw[par][:, h2, 0:npc], axis=AX.X, op=ALU.add),
                                [rsw[par].name + str(h2)], [rk])
                            s.I("dve", lambda e, h2=h2, par=par: e.reciprocal(out=rt[par][:, h2, 1:3], in_=rt[par][:, h2, 1:3]), [rk], [rk])
                            k.TT("dve", rt[par][:, h2, 1:3], rt[par][:, h2, 1:3], g_[:, col + 1:col + 3], ALU.mult, [rk, g_], [rk])
                            hs = slice(h2 * 64, (h2 + 1) * 64)
                            k.TS("dve", oacc[:, hs], pcs[:, h2 * 64:(h2 + 1) * 64], g_[:, col:col + 1], None, ALU.mult, None, ["po_cs%d" % par, g_], [oacc])
                            k.STT(oacc[:, hs], pcs[:, 256 + h2 * 64: 256 + (h2 + 1) * 64], rt[par][:, h2, 1:2], oacc[:, hs], ALU.mult, ALU.add,
                                  ["po_cs%d" % par, rk, oacc], [oacc])
                            k.STT(o_[:, hs], psb[6][:, par * 256 + h2 * 64: par * 256 + (h2 + 1) * 64], rt[par][:, h2, 2:3], oacc[:, hs], ALU.mult, ALU.add,
                                  ["po_w%d" % par, rk, oacc], [o_])
                        k.LD(sc["onsa"][q0:q0 + 128, g * 256:(g + 1) * 256], o_[:], [o_], ["onsa"])

                N = len(items)
                for t in range(N + LAGC):
                    if t < N:
                        stageA(items[t])
                    if 0 <= t - LAGB < N:
                        stageB(items[t - LAGB])
                    if 0 <= t - LAGC < N:
                        stageC(items[t - LAGC])
            s.barrier()

    def merge(k, sc, w, h1, h2, gpost_ap, psb):
        s = k.s; T = k.T
        with ExitStack() as es:
            gpost = k.sb(es, "mg_g", [128, D], F32)
            k.LD(gpost[:], gpost_ap.partition_broadcast(128), [], [gpost])
            yT = k.sb(es, "mg_yT", [128, 8, 512], BF16)
            oT = k.sb(es, "mg_oT", [128, 8, 512], BF16)
            zT = k.sb(es, "mg_zT", [128, 16, 512], BF16)
            ot = [k.sb(es, "mg_ot%d" % i, [128, 1024], BF16) for i in range(2)]
            wb = [k.sb(es, "mg_wb%d" % i, [128, 16, 512], BF16) for i in range(3)]
            wbi = [0]
            f1 = k.sb(es, "mg_f1", [128, 4, D], F32)
            bufs = k.normbufs(es, "mg_")
            ga = [k.sb(es, "mg_ga%d" % i, [128, 512], BF16) for i in range(2)]
            gb = [k.sb(es, "mg_gb%d" % i, [128, 512], BF16) for i in range(2)]
            t1 = [k.sb(es, "mg_t1%d" % i, [128, 512], F32) for i in range(2)]
            t2 = [k.sb(es, "mg_t2%d" % i, [128, 512], F32) for i in range(2)]
            ci = 0
            for tt in range(k.NT // 2, k.NT):
                t0 = tt * 512
                k.LD(yT[:], sc["ysT"][:, t0:t0 + 512].rearrange("(a p) t -> p a t", p=128), ["ysT"], [yT])
                for sub in range(4):
                    o_ = ot[sub % 2]
                    k.LD(o_[:], sc["onsa"][t0 + sub * 128:t0 + sub * 128 + 128, :], ["onsa"], [o_])
                    pst = psb[6 + sub % 2]; pv = pst[:].bitcast(BF16)
                    for j in range(8):
                        k.TR(pv[:, j * 128:(j + 1) * 128], o_[:, j * 128:(j + 1) * 128], [o_], [pst])
                    k.CP("act" if sub % 2 else "dve", oT[:, :, sub * 128:(sub + 1) * 128], pv.rearrange("p (a b) -> p a b", a=8), [pst], [oT])
                for c4 in range(4):
                    wl = []
                    for nm_ in ("glu1", "glu2", "wo"):
                        wc = wb[wbi[0] % 3]; wbi[0] += 1
                        k.LD(wc[:, 0:8, :], w[nm_][:, c4 * 512:(c4 + 1) * 512].rearrange("(a p) c -> p a c", p=128), ["mw"], [wc])
                        wl.append(wc)
                    for j in range(4):
                        ct = c4 * 4 + j
                        pa, pb, pc_ = psb[(ci * 3) % 6], psb[(ci * 3 + 1) % 6], psb[(ci * 3 + 2) % 6]
                        ga_, gb_, t1_, t2_ = ga[ci % 2], gb[ci % 2], t1[ci % 2], t2[ci % 2]; ci += 1
                        k.LD(ga_[:], sc["gaT"][ct * 128:(ct + 1) * 128, t0:t0 + 512], ["pjout"], [ga_])
                        k.LD(gb_[:], sc["gbT"][ct * 128:(ct + 1) * 128, t0:t0 + 512], ["pjout"], [gb_])
                        for (pp, wc, src_) in ((pa, wl[0], yT), (pb, wl[1], yT), (pc_, wl[2], oT)):
                            for kk in range(8):
                                k.MM(pp[:], wc[:, kk, j * 128:(j + 1) * 128], src_[:, kk, :], kk == 0, kk == 7, [wc, src_], [pp])
                        k.ACT(t1_[:], pb[:], AF.Sigmoid, [pb], [t1_])
                        k.TT("dve", t1_[:], pa[:], t1_[:], ALU.mult, [pa, t1_], [t1_])
                        k.TT("pool", t1_[:], t1_[:], ga_[:], ALU.mult, [t1_, ga_], [t1_])
                        k.TT("dve", t2_[:], pc_[:], gb_[:], ALU.mult, [pc_, gb_], [t2_])
                        k.TT("pool", zT[:, ct, :], t1_[:], t2_[:], ALU.add, [t1_, t2_], [zT])
                k.down_post("mg", zT, 16, w["wout"], "mw", h1, "h1", h2, "h2", gpost, 1.0, t0, f1, wb, wbi, bufs, psb)
            s.barrier()

    def sincos(k, es, tag, ang, shape, want_cos):
        s = k.s
        I32 = mybir.dt.int32
        t = k.sb(es, tag + "t", shape, F32); ti = k.sb(es, tag + "ti", shape, I32)
        r = k.sb(es, tag + "r", shape, F32); m = k.sb(es, tag + "m", shape, F32)
        o = k.sb(es, tag + "o", shape, F32)
        a2 = ang
        if want_cos:
            a2 = k.sb(es, tag + "a2", shape, F32)
            s.I("dve", lambda e: e.tensor_scalar(out=a2[:], in0=ang[:], scalar1=math.pi / 2, scalar2=None, op0=ALU.add),
                r=[ang.name], w=[a2.name])
        s.I("dve", lambda e: e.tensor_scalar(out=t[:], in0=a2[:], scalar1=1.0 / (2 * math.pi), scalar2=None, op0=ALU.mult),
            r=[a2.name], w=[t.name])
        s.I("dve", lambda e: e.tensor_copy(out=ti[:], in_=t[:]), r=[t.name], w=[ti.name])
        s.I("dve", lambda e: e.tensor_copy(out=t[:], in_=ti[:]), r=[ti.name], w=[t.name])
        s.I("dve", lambda e: e.scalar_tensor_tensor(out=r[:], in0=t[:], scalar=-2 * math.pi, in1=a2[:],
                                                    op0=ALU.mult, op1=ALU.add), r=[t.name, a2.name], w=[r.name])
        for (thr, op, fix) in ((math.pi, ALU.is_gt, -2 * math.pi), (-math.pi, ALU.is_lt, 2 * math.pi)):
            s.I("dve", lambda e, thr=thr, op=op, fix=fix: e.tensor_scalar(
                out=m[:], in0=r[:], scalar1=thr, scalar2=fix, op0=op, op1=ALU.mult), r=[r.name], w=[m.name])
            s.I("dve", lambda e: e.tensor_tensor(out=r[:], in0=r[:], in1=m[:], op=ALU.add),
                r=[r.name, m.name], w=[r.name])
        s.I("dve", lambda e: e.tensor_scalar(out=r[:], in0=r[:], scalar1=-math.pi, scalar2=math.pi,
                                             op0=ALU.max, op1=ALU.min), r=[r.name], w=[r.name])
        s.I("act", lambda e: e.activation(out=o[:], in_=r[:], func=AF.Sin), r=[r.name], w=[o.name])
        return o

    def disc(k, es, tag, are, aim, ldt, shape):
        s = k.s
        dt = k.sb(es, tag + "dt", shape, F32); lam = k.sb(es, tag + "lam", shape, F32)
        mag = k.sb(es, tag + "mag", shape, F32); ang = k.sb(es, tag + "ang", shape, F32)
        abr = k.sb(es, tag + "abr", shape, F32); abi = k.sb(es, tag + "abi", shape, F32)
        s.I("act", lambda e: e.activation(out=dt[:], in_=ldt[:], func=AF.Exp), r=[ldt.name], w=[dt.name])
        s.I("dve", lambda e: e.tensor_scalar(out=lam[:], in0=are[:], scalar1=-1e-4, scalar2=None, op0=ALU.min),
            r=[are.name], w=[lam.name])
        s.I("dve", lambda e: e.tensor_tensor(out=mag[:], in0=lam[:], in1=dt[:], op=ALU.mult),
            r=[lam.name, dt.name], w=[mag.name])
        s.I("act", lambda e: e.activation(out=mag[:], in_=mag[:], func=AF.Exp), r=[mag.name], w=[mag.name])
        s.I("dve", lambda e: e.tensor_tensor(out=ang[:], in0=aim[:], in1=dt[:], op=ALU.mult),
            r=[aim.name, dt.name], w=[ang.name])
        sn = k.sincos(es, tag + "s", ang, shape, False)
        cs = k.sincos(es, tag + "c", ang, shape, True)
        s.I("dve", lambda e: e.tensor_tensor(out=abr[:], in0=mag[:], in1=cs[:], op=ALU.mult),
            r=[mag.name, cs.name], w=[abr.name])
        s.I("dve", lambda e: e.tensor_tensor(out=abi[:], in0=mag[:], in1=sn[:], op=ALU.mult),
            r=[mag.name, sn.name], w=[abi.name])
        return abr, abi, lam

    def s5(k, usT, ysT, psb):
        s = k.s; T = k.T
        LM = int(round(math.log2(T)))
        NLV = LM
        tt = lambda e, **kw: e.tensor_tensor(**kw)
        with ExitStack() as es:
            a2r = k.sb(es, "a2r", [128, 64], F32); a2i = k.sb(es, "a2i", [128, 64], F32); l2 = k.sb(es, "l2", [128, 64], F32)
            for t_, n_ in ((a2r, "ssm_aT_re2"), (a2i, "ssm_aT_im2"), (l2, "ssm_ldt2")):
                s.DMA("sp", t_[:], k.din(n_, [128, 64]), w=[t_.name])
            sgn = k.sb(es, "sgn", [128, 1], F32); s.DMA("sp", sgn[:], k.din("c_sgn", [128, 1]), w=[sgn.name])
            mk8 = k.sb(es, "mk8", [128, 8], F32); s.DMA("sp", mk8[:], k.din("c_mask8", [128, 8]), w=[mk8.name])
            Jm = k.sb(es, "Jm", [128, 128], F32); s.DMA("sp", Jm[:], k.din("c_J", [128, 128]), w=[Jm.name])
            Id = k.sb(es, "Idf", [128, 128], F32); s.DMA("sp", Id[:], k.din("c_I", [128, 128]), w=[Id.name])
            dsk = k.sb(es, "dsk", [128, 8], F32); s.DMA("sp", dsk[:], k.din("ssm_dT", [128, 8]), w=[dsk.name])
            PR = k.sb(es, "PR", [128, NLV, 64], F32); PI = k.sb(es, "PI", [128, NLV, 64], F32)
            with ExitStack() as e2:
                abr, abi, _ = k.disc(e2, "d1", a2r, a2i, l2, [128, 64])
                s.I("dve", lambda e, abr=abr: e.tensor_copy(out=PR[:, 0, :], in_=abr[:]), r=[abr.name], w=[PR.name])
                s.I("dve", lambda e, abi=abi: e.tensor_copy(out=PI[:, 0, :], in_=abi[:]), r=[abi.name], w=[PI.name])
                t1 = k.sb(e2, "pw1", [128, 64], F32); t2 = k.sb(e2, "pw2", [128, 64], F32)
                for i in range(NLV - 1):
                    s.I("dve", lambda e, i=i: tt(e, out=t1[:], in0=PR[:, i, :], in1=PR[:, i, :], op=ALU.mult), r=[PR.name], w=[t1.name])
                    s.I("dve", lambda e, i=i: tt(e, out=t2[:], in0=PI[:, i, :], in1=PI[:, i, :], op=ALU.mult), r=[PI.name], w=[t2.name])
                    s.I("dve", lambda e, i=i: tt(e, out=PR[:, i + 1, :], in0=t1[:], in1=t2[:], op=ALU.subtract), r=[t1.name, t2.name], w=[PR.name])
                    s.I("dve", lambda e, i=i: tt(e, out=t1[:], in0=PR[:, i, :], in1=PI[:, i, :], op=ALU.mult), r=[PR.name, PI.name], w=[t1.name])
                    s.I("dve", lambda e, i=i: e.tensor_scalar(out=PI[:, i + 1, :], in0=t1[:], scalar1=2.0, scalar2=None, op0=ALU.mult), r=[t1.name], w=[PI.name])
                s.I("dve", lambda e: e.tensor_scalar(out=PI[:], in0=PI[:], scalar1=sgn[:], scalar2=None, op0=ALU.mult), r=[PI.name, sgn.name], w=[PI.name])
                s.barrier()
            Bb = k.sb(es, "Bb", [128, 8, 128], F32)
            with ExitStack() as e2:
                sh = [128, 8 * 64]
                ar = k.sb(e2, "b_ar", sh, F32); ai = k.sb(e2, "b_ai", sh, F32); ld = k.sb(e2, "b_ld", sh, F32)
                br = k.sb(e2, "b_br", sh, F32); bi = k.sb(e2, "b_bi", sh, F32)
                for t_, n_ in ((ar, "ssm_a_re_b"), (ai, "ssm_a_im_b"), (ld, "ssm_ldt_b"), (br, "ssm_b_re_b"), (bi, "ssm_b_im_b")):
                    s.DMA("sp", t_[:], k.din(n_, sh), w=[t_.name])
                abr, abi, lam = k.disc(e2, "d2", ar, ai, ld, sh)
                den = k.sb(e2, "den", sh, F32); u1 = k.sb(e2, "u1", sh, F32); u2 = k.sb(e2, "u2", sh, F32)
                cor = k.sb(e2, "cor", sh, F32); coi = k.sb(e2, "coi", sh, F32)
                D_ = lambda fn, r, w: s.I("dve", fn, r=[x.name for x in r], w=[x.name for x in w])
                D_(lambda e: tt(e, out=den[:], in0=lam[:], in1=lam[:], op=ALU.mult), [lam], [den])
                D_(lambda e: tt(e, out=u1[:], in0=ai[:], in1=ai[:], op=ALU.mult), [ai], [u1])
                D_(lambda e: tt(e, out=den[:], in0=den[:], in1=u1[:], op=ALU.add), [den, u1], [den])
                D_(lambda e: e.reciprocal(out=den[:], in_=den[:]), [den], [den])
                D_(lambda e: e.tensor_scalar(out=abr[:], in0=abr[:], scalar1=-1.0, scalar2=None, op0=ALU.add), [abr], [abr])
                D_(lambda e: tt(e, out=u1[:], in0=abr[:], in1=lam[:], op=ALU.mult), [abr, lam], [u1])
                D_(lambda e: tt(e, out=u2[:], in0=abi[:], in1=ai[:], op=ALU.mult), [abi, ai], [u2])
                D_(lambda e: tt(e, out=u1[:], in0=u1[:], in1=u2[:], op=ALU.add), [u1, u2], [u1])
                D_(lambda e: tt(e, out=cor[:], in0=u1[:], in1=den[:], op=ALU.mult), [u1, den], [cor])
                D_(lambda e: tt(e, out=u1[:], in0=abi[:], in1=lam[:], op=ALU.mult), [abi, lam], [u1])
                D_(lambda e: tt(e, out=u2[:], in0=abr[:], in1=ai[:], op=ALU.mult), [abr, ai], [u2])
                D_(lambda e: tt(e, out=u1[:], in0=u1[:], in1=u2[:], op=ALU.subtract), [u1, u2], [u1])
                D_(lambda e: tt(e, out=coi[:], in0=u1[:], in1=den[:], op=ALU.mult), [u1, den], [coi])
                v3 = lambda t_: t_[:].rearrange("p (q x) -> p q x", q=8)
                D_(lambda e: tt(e, out=u1[:], in0=cor[:], in1=br[:], op=ALU.mult), [cor, br], [u1])
                D_(lambda e: tt(e, out=u2[:], in0=coi[:], in1=bi[:], op=ALU.mult), [coi, bi], [u2])
                D_(lambda e: tt(e, out=Bb[:, :, 0:64], in0=v3(u1), in1=v3(u2), op=ALU.subtract), [u1, u2], [Bb])
                D_(lambda e: tt(e, out=u1[:], in0=cor[:], in1=bi[:], op=ALU.mult), [cor, bi], [u1])
                D_(lambda e: tt(e, out=u2[:], in0=coi[:], in1=br[:], op=ALU.mult), [coi, br], [u2])
                D_(lambda e: tt(e, out=Bb[:, :, 64:128], in0=v3(u1), in1=v3(u2), op=ALU.add), [u1, u2], [Bb])
                s.barrier()
            Ct = k.sb(es, "Ct", [128, 64, 16], F32)
            s.DMA("sp", Ct[:], k.din("ssm_cT2", [128, 64, 16]), w=[Ct.name])
            s.I("dve", lambda e: e.tensor_scalar(out=Ct[:], in0=Ct[:], scalar1=sgn[:], scalar2=None, op0=ALU.mult),
                r=[Ct.name, sgn.name], w=[Ct.name])
            NG = 3
            OW = k.OWN
            us = k.sb(es, "us32", [128, T], F32)
            yacc = k.sb(es, "yacc", [128, T - OW], F32)
            Aa = [k.sb(es, "Ast%d" % i, [128, T], F32) for i in range(NG)]
            Mt = [k.sb(es, "Mt%d" % i, [128, NLV, 128], F32) for i in range(NG)]
            BmP = k.sb(es, "BmP", [128, 8, 128], F32)
            CmP = k.sb(es, "CmP", [128, 8, 128], F32)
            gl = [k.sb(es, "gl%d" % i, [128, 512], F32) for i in range(3)]
            yb = [k.sb(es, "yb%d" % i, [128, 512], BF16) for i in range(2)]
            pc = [0]

            def nps():
                p = psb[pc[0] % 8]; pc[0] += 1
                return p
            ev = [0]

            def evac_copy(dst, src, rk, wk):
                ev[0] += 1
                if ev[0] % 2:
                    s.I("act", lambda e: e.copy(out=dst, in_=src), r=rk, w=wk)
                else:
                    s.I("dve", lambda e: e.tensor_copy(out=dst, in_=src), r=rk, w=wk)

            def group_steps(q, j, slot):
                g = q * 8 + j
                A = Aa[slot]; M = Mt[slot]
                for i in range(NLV):
                    s.I("pool", lambda e, i=i: e.tensor_scalar(out=M[:, i, :], in0=Id[:], scalar1=PR[:, i, g:g + 1],
                                                               scalar2=None, op0=ALU.mult), r=[Id.name, PR.name], w=[M.name])
                    s.I("dve", lambda e, i=i: e.scalar_tensor_tensor(out=M[:, i, :], in0=Jm[:], scalar=PI[:, i, g:g + 1],
                                                                     in1=M[:, i, :], op0=ALU.mult, op1=ALU.add),
                        r=[Jm.name, PI.name, M.name], w=[M.name])
                yield
                for c0 in range(0, T, 512):
                    p = nps()
                    k.MM(p[:], BmP[:, j, :], us[:, c0:c0 + 512], True, True, [BmP, us], [p])
                    evac_copy(A[:, c0:c0 + 512], p[:], [p.name], [A.name])
                    yield
                for l in range(1, LM + 1):
                    st = 1 << l; h = st >> 1; n = T // st
                    for c0 in range(0, n, 512):
                        m_ = min(512, n - c0)
                        src = A[:, h - 1 + c0 * st: h - 1 + (c0 + m_ - 1) * st + 1: st]
                        dst = A[:, st - 1 + c0 * st: st - 1 + (c0 + m_ - 1) * st + 1: st]
                        p = nps()
                        k.MM(p[:, 0:m_], M[:, l - 1, :], src, True, True, [M, A], [p])
                        k.TT("dve", dst, p[:, 0:m_], dst, ALU.add, [p, A], [A])
                        yield
                for l in range(LM - 1, 0, -1):
                    st = 1 << l; h = st >> 1; n = T // st
                    lo_i = max(0, n // 2 - 1)
                    for c0 in range(lo_i, n - 1, 512):
                        m_ = min(512, n - 1 - c0)
                        src = A[:, st - 1 + c0 * st: st - 1 + (c0 + m_ - 1) * st + 1: st]
                        dst = A[:, st + h - 1 + c0 * st: st + h - 1 + (c0 + m_ - 1) * st + 1: st]
                        p = nps()
                        k.MM(p[:, 0:m_], M[:, l - 1, :], src, True, True, [M, A], [p])
                        k.TT("dve", dst, p[:, 0:m_], dst, ALU.add, [p, A], [A])
                        yield
                for c0 in range(OW, T, 512):
                    p = nps()
                    k.MM(p[:], CmP[:, j, :], A[:, c0:c0 + 512], True, True, [CmP, A], [p])
                    if j == 0:
                        evac_copy(yacc[:, c0 - OW:c0 - OW + 512], p[:], [p.name], [yacc.name])
                    else:
                        k.TT("dve", yacc[:, c0 - OW:c0 - OW + 512], p[:], yacc[:, c0 - OW:c0 - OW + 512], ALU.add, [p, yacc], [yacc])
                    yield

            for q in range(8):
                s.DMA("pool", us[:], usT[q * 128:(q + 1) * 128, :], r=["usT"], w=[us.name])
                for j in range(8):
                    s.I("dve", lambda e, j=j, q=q: e.tensor_scalar(out=BmP[:, j, :], in0=Bb[:, q, :], scalar1=mk8[:, j:j + 1],
                                                              scalar2=None, op0=ALU.mult), r=[Bb.name, mk8.name], w=[BmP.name])
                s.I("pool", lambda e: e.memset(CmP[:], 0.0), w=[CmP.name])
                for j in range(8):
                    s.I("pool", lambda e, j=j, q=q: e.tensor_copy(out=CmP[:, j, j * 16:(j + 1) * 16], in_=Ct[:, q * 8 + j, :]),
                        r=[Ct.name], w=[CmP.name])
                j0 = 0
                while j0 < 8:
                    js = list(range(j0, min(8, j0 + NG)))
                    gens = [group_steps(q, j, si) for si, j in enumerate(js)]
                    alive = True
                    while alive:
                        alive = False
                        for gen in gens:
                            try:
                                next(gen); alive = True
                            except StopIteration:
                                pass
                    j0 += len(js)
                for ci, c0 in enumerate(range(k.OWN, T, 512)):
                    y_ = gl[0]; a_ = gl[1]; b_ = gl[2]; o_ = yb[ci % 2]
                    s.I("dve", lambda e, c0=c0, q=q: e.scalar_tensor_tensor(out=y_[:], in0=us[:, c0:c0 + 512], scalar=dsk[:, q:q + 1],
                                                                        in1=yacc[:, c0 - k.OWN:c0 - k.OWN + 512], op0=ALU.mult, op1=ALU.add),
                        r=[us.name, dsk.name, yacc.name], w=[y_.name])
                    s.I("act", lambda e: e.activation(out=a_[:], in_=y_[:], func=AF.Square), r=[y_.name], w=[a_.name])
                    s.I("dve", lambda e: e.tensor_scalar(out=a_[:], in0=a_[:], scalar1=0.044715, scalar2=1.0, op0=ALU.mult, op1=ALU.add),
                        r=[a_.name], w=[a_.name])
                    s.I("dve", lambda e: tt(e, out=b_[:], in0=a_[:], in1=y_[:], op=ALU.mult), r=[a_.name, y_.name], w=[b_.name])
                    s.I("act", lambda e: e.activation(out=b_[:], in_=b_[:], func=AF.Sigmoid, scale=1.5957691216057308), r=[b_.name], w=[b_.name])
                    s.I("dve", lambda e, o_=o_: tt(e, out=o_[:], in0=b_[:], in1=y_[:], op=ALU.mult), r=[b_.name, y_.name], w=[o_.name])
                    d = s.DMA("sp", ysT[q * 128:(q + 1) * 128, c0:c0 + 512], o_[:], r=[o_.name], w=["ysT"])
                    k.final.append(d)
            s.barrier()

    def build(k):
        nc = k.nc; s = k.s; es = k.es; T = k.T
        x = k.din("x", [T, D])
        out = k.nc.dram_tensor("out", [T if k.stages < 3 else T // 2, D], F32, kind="ExternalOutput").ap()
        g = {n: k.din(n, [1, D]) for n in ("ffn1_pre_g", "ffn1_post_g", "mix_pre_g", "mix_post_g",
                                           "ffn2_pre_g", "ffn2_post_g")}
        w1g = k.conv("ffn1_w_gate", [D, DFF], 4)
        w1u = k.conv("ffn1_w_up", [D, DFF], 4)
        w1d = k.conv("ffn1_w_down", [DFF, D], 4)
        psb = [es.enter_context(nc.psum_tensor("psb%d" % i, [128, 512], F32)) for i in range(8)]
        k.epsb = k.sb(es, "epsb", [128, 1], F32)
        s.I("dve", lambda e: e.memset(k.epsb[:], EPS), w=[k.epsb.name])
        identf = k.sb(es, "identf", [128, 128], F32)
        k.ident = k.sb(es, "ident", [128, 128], BF16)
        s.I("pool", lambda e: e.memset(identf[:], 1.0), w=[identf.name])
        s.I("pool", lambda e: e.affine_select(out=identf[:], in_=identf[:], pattern=[[-1, 128]],
                                              compare_op=ALU.is_equal, fill=0.0, base=0, channel_multiplier=1),
            r=[identf.name], w=[identf.name])
        s.I("dve", lambda e: e.tensor_copy(out=k.ident[:], in_=identf[:]), r=[identf.name], w=["ident"])
        if k.stages == 1:
            k.ffn("f1", x, out, g["ffn1_pre_g"][0, :], g["ffn1_post_g"][0, :], w1g, w1u, w1d, psb)
        if k.stages == 2:
            usT = k.din("usT", [1024, T])
            ysT = k.nc.dram_tensor("ysT", [1024, T], BF16, kind="ExternalOutput").ap()
            k.s5(usT, ysT, psb)
        if k.stages >= 3:
            w = {}
            win = k.conv("w_in", [D, INC], 4)
            w["glu1"] = k.conv("ssm_glu_w1", [1024, D], 2); w["glu2"] = k.conv("ssm_glu_w2", [1024, D], 2)
            w["wo"] = k.conv("nsa_w_o", [1024, D], 2); w["wout"] = k.conv("w_out", [D, D], 2)
            w["ck1"] = k.conv("cmp_k_w1", [2048, 256]); w["ck2"] = k.conv("cmp_k_w2", [256, 64])
            w["cv1"] = k.conv("cmp_v_w1", [2048, 256]); w["cv2"] = k.conv("cmp_v_w2", [256, 64])
            w2g = k.conv("ffn2_w_gate", [D, DFF], 4); w2u = k.conv("ffn2_w_up", [D, DFF], 4); w2d = k.conv("ffn2_w_down", [DFF, D], 4)
            sc = {}
            for nm_, shp, dt_ in (("h1", [T, D], F32), ("h2", [T, D], F32), ("usT", [1024, T], F32), ("ysT", [1024, T], BF16),
                                  ("qT", [16, 64, T], BF16), ("kcT", [4, 64, T], BF16), ("vcT", [4, 64, T], BF16),
                                  ("ksT", [4, 64, T], BF16), ("kwT", [4, 64, T], BF16), ("Vs", [T, 256], BF16), ("Vw", [T, 256], BF16),
                                  ("gates", [T, 48], F32), ("gaT", [D, T], BF16), ("gbT", [D, T], BF16), ("onsa", [T, 1024], BF16)):
                if k.debug:
                    sc[nm_] = k.nc.dram_tensor("dbg_" + nm_, list(shp), dt_, kind="ExternalOutput").ap()
                else:
                    sc[nm_] = k.dscr("sc_" + nm_, shp, dt_)
            k.ffn("f1", x, sc["h1"], g["ffn1_pre_g"][0, :], g["ffn1_post_g"][0, :], w1g, w1u, w1d, psb, "x", "h1",
                  wkeys=("ffn1_w_gate_b", "ffn1_w_up_b", "ffn1_w_down_b"))
            k.proj(sc["h1"], g["mix_pre_g"][0, :], win, sc, psb)
            k.s5(sc["usT"], sc["ysT"], psb)
            with ExitStack() as em:
                KcT, Vc = k.compress(em, sc, w, psb)
                SKS, SKW, WS, WW = k.bias_tables(psb)
                k.nsa(sc, KcT, Vc, SKS, SKW, WS, WW, psb)
            k.merge(sc, w, sc["h1"], sc["h2"], g["mix_post_g"][0, :], psb)
            k.ffn("f2", sc["h2"], out, g["ffn2_pre_g"][0, :], g["ffn2_post_g"][0, :], w2g, w2u, w2d, psb, "h2", "out",
                  tiles=range(k.NT // 2, k.NT), dst_off=k.OWN)
        s.finish(k.final)
        es.close()
        return nc


def ssm_layouts(a_re, a_im, log_dt, b_re, b_im, c_re, c_im, d):
    a_re = np.asarray(a_re, np.float32); a_im = np.asarray(a_im, np.float32); log_dt = np.asarray(log_dt, np.float32)
    b_re = np.asarray(b_re, np.float32); b_im = np.asarray(b_im, np.float32)
    c_re = np.asarray(c_re, np.float32); c_im = np.asarray(c_im, np.float32); d = np.asarray(d, np.float32)
    m = {}
    m["ssm_aT_re2"] = np.concatenate([a_re.T, a_re.T], 0)
    m["ssm_aT_im2"] = np.concatenate([a_im.T, a_im.T], 0)
    m["ssm_ldt2"] = np.broadcast_to(log_dt[None, :], (128, 64))
    def lay_a(a):
        t = a.reshape(8, 8, 64)
        t = np.transpose(t, (1, 0, 2))
        return np.broadcast_to(t[:, None], (8, 16, 8, 64)).reshape(128, 512)
    m["ssm_a_re_b"] = lay_a(a_re); m["ssm_a_im_b"] = lay_a(a_im)
    m["ssm_ldt_b"] = lay_a(np.broadcast_to(log_dt[:, None], (64, 64)))
    def lay_b(b):
        t = b.reshape(8, 8, 64, 16)
        t = np.transpose(t, (1, 3, 0, 2))
        return t.reshape(128, 512)
    m["ssm_b_re_b"] = lay_b(b_re); m["ssm_b_im_b"] = lay_b(b_im)
    m["ssm_cT2"] = np.concatenate([np.transpose(c_re, (2, 0, 1)), np.transpose(c_im, (2, 0, 1))], 0)
    m["ssm_dT"] = d.reshape(8, 128).T
    m["c_sgn"] = np.concatenate([np.ones((64, 1)), -np.ones((64, 1))], 0)
    m["c_mask8"] = (np.arange(128)[:, None] // 16 == np.arange(8)[None, :]).astype(np.float32)
    m["c_I"] = np.eye(128)
    m["c_J"] = np.roll(np.eye(128), 64, axis=1)
    return {k_: np.ascontiguousarray(v, dtype=np.float32) for k_, v in m.items()}


def _bucket(d):
    d = np.maximum(d, 0)
    d_f = np.maximum(d, 1).astype(np.float32)
    large = 16 + (np.log(d_f / np.float32(16)) / np.float32(math.log(1024 / 16)) * np.float32(16)).astype(np.int32)
    large = np.minimum(large, 31)
    return np.where(d < 16, d, large)


def nsa_consts():
    m = {}
    WS, WW = 1152, 768
    x = np.arange(WS); d = 1023 - x
    oh = np.zeros((33, WS), np.float32)
    bk = _bucket(d)
    for b in range(32):
        oh[b] = ((d >= 0) & (bk == b))
    oh[32] = np.where(d < 0, -BIG, 0.0)
    m["c_ohs"] = oh
    x = np.arange(WW); d = 639 - x
    oh = np.zeros((33, WW), np.float32)
    bk = _bucket(d)
    ok = (d >= 0) & (d < 512)
    for b in range(32):
        oh[b] = (ok & (bk == b))
    oh[32] = np.where(ok, 0.0, -BIG)
    m["c_ohw"] = oh
    vm = np.zeros((128, 256), np.float32); ac = np.zeros((128, 256), np.float32)
    c = np.arange(256)
    for qi in range(128):
        hi = qi >= 64
        forced = (c == 127) | ((c == 128) if hi else (c == 126))
        invalid = (c > 128) | ((c == 128) & (not hi))
        vm[qi] = (~forced & ~invalid)
        ac[qi] = np.where(forced, 1e4, np.where(invalid, -1e4, 0.0))
    m["c_vmrel"] = vm; m["c_acrel"] = ac
    return m


def percore_consts(T, half):
    m = {}
    nbp = T // 128
    cm = np.ones((128, 128), np.float32); ca = np.zeros((128, 128), np.float32)
    if half == 0:
        cm[:, 0:nbp] = 0.0; ca[:, 0:nbp] = -2e4
        cm[:, nbp] = 0.0; ca[:, nbp] = 1e4
    else:
        cm[:, 0] = 0.0; ca[:, 0] = 1e4
    m["pc_cm"] = cm; m["pc_ca"] = ca
    cmpm = np.zeros((128, 512), np.float32)
    if half == 0:
        cmpm[:, 0:T // 32] = -BIG
    m["pc_cmpmask"] = cmpm
    m["pc_wmask"] = np.full((128, 1), -BIG if half == 0 else 0.0, np.float32)
    return m


def host_inputs(inputs, names):
    m = {}
    sq = lambda n: np.asarray(inputs[n], np.float32)[0]
    lay = ssm_layouts(sq("ssm_a_re"), sq("ssm_a_im"), sq("ssm_log_dt"), sq("ssm_b_re"), sq("ssm_b_im"),
                      sq("ssm_c_re"), sq("ssm_c_im"), sq("ssm_d"))
    lay.update(nsa_consts())
    lay["cmp_posT"] = np.ascontiguousarray(sq("cmp_pos").T)
    for n in names:
        if n == "x":
            continue
        if n in lay:
            m[n] = np.ascontiguousarray(lay[n], dtype=np.float32)
        elif n == "rel_bias":
            m[n] = np.ascontiguousarray(np.asarray(inputs[n], np.float32))
        elif n.endswith("_g"):
            m[n] = np.ascontiguousarray(np.asarray(inputs[n], np.float32).reshape(1, D))
        else:
            m[n] = np.ascontiguousarray(sq(n))
    return m

_CACHE = {}


def _get(T, stages, debug=False):
    key = (T, stages, debug)
    if key not in _CACHE:
        kb = K(T, stages, debug)
        kb.build()
        _CACHE[key] = kb
    return _CACHE[key]


def run(inputs, T, stages, ncores, debug=False):
    kb = _get(T, stages, debug)
    xs = np.asarray(inputs["x"], np.float32)
    in_maps = []
    if stages >= 3:
        shared = host_inputs(inputs, [n for n in kb.inp if n != "x" and not n.startswith("pc_")])
        pcs = [percore_consts(T, 0), percore_consts(T, 1)]
        for c in range(ncores):
            b, half = c // 2, c % 2
            m = dict(shared)
            m.update(pcs[half])
            xb = xs[b % xs.shape[0]].reshape(T, D)
            if half == 0:
                xl = np.concatenate([np.zeros((T // 2, D), np.float32), xb[:T // 2]], 0)
            else:
                xl = xb
            m["x"] = np.ascontiguousarray(xl)
            in_maps.append(m)
    else:
        for c in range(ncores):
            m = {}
            for name in kb.inp:
                if name == "x":
                    continue
                a = np.asarray(inputs[name], np.float32)
                m[name] = np.ascontiguousarray(a.reshape(kb.inp[name].shape))
            m["x"] = np.ascontiguousarray(xs[c % xs.shape[0]].reshape(T, D))
            in_maps.append(m)
    res = run_bass_kernel_spmd(kb.nc, in_maps, core_ids=list(range(ncores)))
    if debug:
        return res.results
    return [r["out"] for r in res.results]


def kernel(**inputs):
    outs = run(inputs, SEQ, 3, 8)
    full = np.empty((NB, SEQ, D), np.float32)
    for c in range(8):
        b, half = c // 2, c % 2
        full[b, half * (SEQ // 2):(half + 1) * (SEQ // 2)] = outs[c]
    return full
```

```python
import math
from contextlib import ExitStack
import numpy as np
import ml_dtypes
import concourse.bass as bass
import concourse.mybir as mybir
from concourse.bass_utils import run_bass_kernel_spmd

F32 = mybir.dt.float32
BF16 = mybir.dt.bfloat16
ALU = mybir.AluOpType
AF = mybir.ActivationFunctionType
AX = mybir.AxisListType

D = 2048
DFF = 5632
EPS = 1e-6
NB = 4
SEQ = 8192
INC = 7728
BIG = 30000.0


class Ins:
    __slots__ = ("eng", "fn", "deps", "dma", "idx", "sig", "val", "semi", "prewait")

    def __init__(s, eng, fn, dma):
        s.eng = eng; s.fn = fn; s.dma = dma; s.deps = []; s.idx = 0; s.sig = False; s.val = 0
        s.semi = 0; s.prewait = None


class Sch:
    ENGS = ["pe", "dve", "act", "pool", "sp"]
    NS = 10

    def __init__(s, nc, es):
        s.nc = nc
        s.es = es
        s.eobj = {"pe": nc.tensor, "dve": nc.vector, "act": nc.scalar, "pool": nc.gpsimd, "sp": nc.sync}
        s.prog = {e: [] for e in s.ENGS}
        s.csem = {e: es.enter_context(nc.semaphore("cs_" + e)) for e in s.ENGS}
        s.dsem = {e: [es.enter_context(nc.semaphore("ds_%s%d" % (e, i))) for i in range(s.NS)]
                  for e in ("sp", "pool", "act")}
        s.ndma = {e: 0 for e in ("sp", "pool", "act")}
        s.dmas = {e: [] for e in ("sp", "pool", "act")}
        s.lastw = {}
        s.readers = {}
        s.known = {e: {f: -1 for f in s.ENGS} for e in s.ENGS}
        s.seen = {e: set() for e in s.ENGS}
        s.bar_from = {e: 0 for e in ("sp", "pool", "act")}

    def _dep(s, ins, d):
        if d is None or d is ins:
            return
        if d.dma:
            if id(d) in s.seen[ins.eng]:
                return
            s.seen[ins.eng].add(id(d))
            ins.deps.append(d)
        else:
            if d.eng == "pe" and ins.eng == "pe" and not ins.dma:
                return
            if s.known[ins.eng][d.eng] >= d.idx:
                return
            s.known[ins.eng][d.eng] = d.idx
            ins.deps.append(d)

    def _emit(s, eng, fn, r, w, dma):
        ins = Ins(eng, fn, dma)
        ins.idx = len(s.prog[eng])
        for k in list(r) + list(w):
            s._dep(ins, s.lastw.get(k))
        for k in w:
            best = {}
            for d in s.readers.get(k, ()):
                if d.dma:
                    s._dep(ins, d)
                elif d.eng not in best or best[d.eng].idx < d.idx:
                    best[d.eng] = d
            for d in best.values():
                s._dep(ins, d)
        if dma:
            n = s.ndma[eng]; s.ndma[eng] += 1
            ins.semi = n % s.NS; ins.val = (n // s.NS + 1) * 16
            if n >= s.NS:
                ins.prewait = s.dmas[eng][n - s.NS]
            s.dmas[eng].append(ins)
            ins.sig = True
        s.prog[eng].append(ins)
        for k in w:
            s.lastw[k] = ins; s.readers[k] = []
        for k in r:
            s.readers.setdefault(k, []).append(ins)
        return ins

    def I(s, eng, fn, r=(), w=()):
        return s._emit(eng, fn, r, w, False)

    def barrier(s):
        lasts = []
        for e in s.ENGS:
            for ins in reversed(s.prog[e]):
                if not ins.dma:
                    lasts.append(ins); break
        dm = [d for e in s.dmas for d in s.dmas[e][s.bar_from[e]:]]
        for e in s.dmas:
            s.bar_from[e] = len(s.dmas[e])
        for e in s.ENGS:
            ins = Ins(e, lambda eng: eng.nop(), False)
            ins.idx = len(s.prog[e])
            for d in lasts + dm:
                s._dep(ins, d)
            s.prog[e].append(ins)

    def DMA(s, eng, out, in_, r=(), w=(), **kw):
        return s._emit(eng, lambda e: e.dma_start(out=out, in_=in_, **kw), r, w, True)

    def finish(s, final):
        for e in s.ENGS:
            for ins in s.prog[e]:
                for d in ins.deps:
                    d.sig = True
        EPOCH = 30000
        s.csems = {}
        for e in s.ENGS:
            c = 0; ep = 0
            s.csems[e] = [s.csem[e]]
            for ins in s.prog[e]:
                if not ins.dma and ins.sig:
                    if c == EPOCH:
                        c = 0; ep += 1
                        s.csems[e].append(s.es.enter_context(s.nc.semaphore("cs_%s_%d" % (e, ep))))
                    c += 1; ins.val = c; ins.semi = ep
        s.maxval = {e: max([i.val for i in s.prog[e] if not i.dma] + [0]) for e in s.ENGS}

        def semof(d):
            return s.dsem[d.eng][d.semi] if d.dma else s.csems[d.eng][d.semi]

        with s.nc.Block() as block:
            def run(e):
                def body(eng):
                    for ins in s.prog[e]:
                        if ins.prewait is not None:
                            eng.wait_ge(semof(ins.prewait), ins.prewait.val)
                        for d in ins.deps:
                            eng.wait_ge(semof(d), d.val)
                        o = ins.fn(eng)
                        if ins.sig:
                            o.then_inc(semof(ins), 16 if ins.dma else 1)
                    if e == "sp":
                        for d in final:
                            eng.wait_ge(semof(d), d.val)
                return body
            block.tensor(run("pe"))
            block.vector(run("dve"))
            block.scalar(run("act"))
            block.gpsimd(run("pool"))
            block.sync(run("sp"))


def dram_ap(t, off, pat):
    return bass.AP(t.tensor if hasattr(t, "tensor") else t, off, pat)


class K:
    def __init__(k, T, stages=9, debug=False):
        k.T = T
        k.debug = debug
        k.NT = T // 512
        k.OWN = T // 2
        k.stages = stages
        k.nc = nc = bass.Bass("TRN2", target_bir_lowering=False)
        k.es = ExitStack()
        k.s = Sch(nc, k.es)
        k.inp = {}
        k.final = []

    def din(k, name, shape, dt=F32):
        a = k.nc.dram_tensor(name, list(shape), dt, kind="ExternalInput").ap()
        k.inp[name] = a
        return a

    def dscr(k, name, shape, dt):
        return k.nc.dram_tensor(name, list(shape), dt, kind="Internal").ap()

    def sb(k, es, name, shape, dt):
        return es.enter_context(k.nc.sbuf_tensor(name, list(shape), dt))

    def conv(k, name, shape, nsplit=1):
        src = k.din(name, shape)
        dst = k.dscr(name + "_b", shape, BF16)
        rows = shape[0]
        step = rows // nsplit
        for i in range(nsplit):
            k.s.DMA("pool", dst[i * step:(i + 1) * step, :], src[i * step:(i + 1) * step, :], w=[name + "_b"])
        return dst

    def rstd(k, src_ap, junk, ss, rs, key_r, scale_out=1.0):
        s = k.s
        s.I("act", lambda e: e.activation(out=junk[:], in_=src_ap, func=AF.Square, accum_out=ss[:]),
            r=key_r, w=[junk.name, ss.name])
        s.I("act", lambda e: e.activation(out=rs[:], in_=ss[:], func=AF.Sqrt, bias=k.epsb[:],
                                          scale=1.0 / (D * scale_out * scale_out)),
            r=[ss.name], w=[rs.name])
        s.I("dve", lambda e: e.reciprocal(out=rs[:], in_=rs[:]), r=[rs.name], w=[rs.name])

    def ffn(k, tag, src, dst, pre_g, post_g, wg, wu, wd, psb, skey="x", dkey="out", tiles=None, dst_off=0, wkeys=None):
        s = k.s; nc = k.nc
        with ExitStack() as es:
            gpre = k.sb(es, tag + "gpre", [128, D], F32)
            gpost = k.sb(es, tag + "gpost", [128, D], F32)
            s.DMA("sp", gpre[:], pre_g.partition_broadcast(128), w=[gpre.name])
            s.DMA("sp", gpost[:], post_g.partition_broadcast(128), w=[gpost.name])
            kg_, ku_, kd_ = wkeys if wkeys is not None else (tag + "wg", tag + "wu", tag + "wd")
            xT = k.sb(es, tag + "xT", [128, 16, 512], BF16)
            hT = k.sb(es, tag + "hT", [128, 44, 512], BF16)
            wb = [k.sb(es, tag + "wb%d" % i, [128, 16, 256], BF16) for i in range(6)]
            f1 = k.sb(es, tag + "f1", [128, 4, D], F32)
            xs = [k.sb(es, tag + "xs%d" % i, [128, D], F32) for i in range(2)]
            xn = [k.sb(es, tag + "xn%d" % i, [128, D], BF16) for i in range(2)]
            junk = k.sb(es, tag + "junk", [128, D], F32)
            sg = [k.sb(es, tag + "sg%d" % i, [128, 512], F32) for i in range(2)]
            ss = [k.sb(es, tag + "ss%d" % i, [128, 1], F32) for i in range(2)]
            rs = [k.sb(es, tag + "rs%d" % i, [128, 1], F32) for i in range(2)]
            wbi = [0]

            def nextw():
                b = wb[wbi[0] % 6]; wbi[0] += 1
                return b

            xi = 0
            for tt in (tiles if tiles is not None else range(k.NT)):
                t0 = tt * 512
                for sub in range(4):
                    x_ = xs[xi % 2]; xn_ = xn[xi % 2]; ss_ = ss[xi % 2]; rs_ = rs[xi % 2]; xi += 1
                    rows = src[t0 + sub * 128: t0 + sub * 128 + 128, :]
                    s.DMA("sp", x_[:], rows, r=[skey], w=[x_.name])
                    k.rstd(x_[:], junk, ss_, rs_, [x_.name])
                    s.I("dve", lambda e, x_=x_, xn_=xn_, rs_=rs_: e.scalar_tensor_tensor(
                        out=xn_[:], in0=x_[:], scalar=rs_[:], in1=gpre[:], op0=ALU.mult, op1=ALU.mult),
                        r=[x_.name, rs_.name, gpre.name], w=[xn_.name])
                    for half in range(2):
                        pst = psb[6 + half]
                        pv = pst[:].bitcast(BF16)
                        for j in range(8):
                            kk = half * 8 + j
                            s.I("pe", lambda e, pv=pv, xn_=xn_, kk=kk, j=j: e.transpose(
                                out=pv[:, j * 128:(j + 1) * 128], in_=xn_[:, kk * 128:(kk + 1) * 128],
                                identity=k.ident[:]),
                                r=[xn_.name, "ident"], w=[pst.name])
                        eng = "act" if half == 0 else "dve"
                        if eng == "act":
                            s.I("act", lambda e, pv=pv, half=half, sub=sub: e.copy(
                                out=xT[:, half * 8:half * 8 + 8, sub * 128:(sub + 1) * 128],
                                in_=pv.rearrange("p (a b) -> p a b", a=8)),
                                r=[pst.name], w=[xT.name])
                        else:
                            s.I("dve", lambda e, pv=pv, half=half, sub=sub: e.tensor_copy(
                                out=xT[:, half * 8:half * 8 + 8, sub * 128:(sub + 1) * 128],
                                in_=pv.rearrange("p (a b) -> p a b", a=8)),
                                r=[pst.name], w=[xT.name])
                pi = 0
                for cc in range(22):
                    wgc = nextw()
                    s.DMA("sp", wgc[:], wg[:, cc * 256:(cc + 1) * 256].rearrange("(a p) c -> p a c", p=128),
                          r=[kg_], w=[wgc.name])
                    wuc = nextw()
                    s.DMA("sp", wuc[:], wu[:, cc * 256:(cc + 1) * 256].rearrange("(a p) c -> p a c", p=128),
                          r=[ku_], w=[wuc.name])
                    for fs in range(2):
                        f = cc * 2 + fs
                        pg = psb[(pi * 2) % 6]; pu = psb[(pi * 2 + 1) % 6]; sg_ = sg[pi % 2]; pi += 1
                        for kk in range(16):
                            s.I("pe", lambda e, pg=pg, wgc=wgc, kk=kk, fs=fs: e.matmul(
                                pg[:], lhsT=wgc[:, kk, fs * 128:(fs + 1) * 128], rhs=xT[:, kk, :],
                                start=(kk == 0), stop=(kk == 15)), r=[wgc.name, xT.name], w=[pg.name])
                        for kk in range(16):
                            s.I("pe", lambda e, pu=pu, wuc=wuc, kk=kk, fs=fs: e.matmul(
                                pu[:], lhsT=wuc[:, kk, fs * 128:(fs + 1) * 128], rhs=xT[:, kk, :],
                                start=(kk == 0), stop=(kk == 15)), r=[wuc.name, xT.name], w=[pu.name])
                        s.I("act", lambda e, pg=pg, sg_=sg_: e.activation(out=sg_[:], in_=pg[:], func=AF.Silu),
                            r=[pg.name], w=[sg_.name])
                        s.I("dve", lambda e, pu=pu, sg_=sg_, f=f: e.tensor_tensor(
                            out=hT[:, f, :], in0=sg_[:], in1=pu[:], op=ALU.mult),
                            r=[pu.name, sg_.name], w=[hT.name])
                for c4 in range(4):
                    base = 0 if c4 % 2 == 0 else 4
                    f0 = 0
                    for nf in (8, 8, 8, 8, 8, 4):
                        wdt_ = nextw()
                        wdc = wdt_[:].rearrange("p a c -> p (a c)").rearrange("p (a c) -> p a c", c=512)
                        s.DMA("sp", wdc[:, 0:nf, :],
                              wd[f0 * 128:(f0 + nf) * 128, c4 * 512:(c4 + 1) * 512].rearrange("(a p) c -> p a c", p=128),
                              r=[kd_], w=[wdt_.name])
                        for sub in range(4):
                            po = psb[base + sub]
                            for fi in range(nf):
                                f = f0 + fi
                                s.I("pe", lambda e, po=po, wdc=wdc, fi=fi, f=f, sub=sub: e.matmul(
                                    po[:], lhsT=hT[:, f, sub * 128:(sub + 1) * 128], rhs=wdc[:, fi, :],
                                    start=(f == 0), stop=(f == 43)), r=[wdt_.name, hT.name], w=[po.name])
                        f0 += nf
                    for sub in range(4):
                        po = psb[base + sub]
                        if sub % 2 == 0:
                            s.I("act", lambda e, po=po, sub=sub, c4=c4: e.copy(
                                out=f1[:, sub, c4 * 512:(c4 + 1) * 512], in_=po[:]),
                                r=[po.name], w=[f1.name + str(sub)])
                        else:
                            s.I("dve", lambda e, po=po, sub=sub, c4=c4: e.tensor_copy(
                                out=f1[:, sub, c4 * 512:(c4 + 1) * 512], in_=po[:]),
                                r=[po.name], w=[f1.name + str(sub)])
                for sub in range(4):
                    x_ = xs[xi % 2]; ss_ = ss[xi % 2]; rs_ = rs[xi % 2]; xi += 1
                    rows = src[t0 + sub * 128: t0 + sub * 128 + 128, :]
                    s.DMA("sp", x_[:], rows, r=[skey], w=[x_.name])
                    k.rstd(f1[:, sub, :], junk, ss_, rs_, [f1.name + str(sub)], scale_out=0.5)
                    s.I("dve", lambda e, sub=sub, rs_=rs_: e.scalar_tensor_tensor(
                        out=f1[:, sub, :], in0=f1[:, sub, :], scalar=rs_[:], in1=gpost[:],
                        op0=ALU.mult, op1=ALU.mult),
                        r=[f1.name + str(sub), rs_.name, gpost.name], w=[f1.name + str(sub)])
                    s.I("pool", lambda e, sub=sub, x_=x_: e.tensor_tensor(
                        out=x_[:], in0=f1[:, sub, :], in1=x_[:], op=ALU.add),
                        r=[f1.name + str(sub), x_.name], w=[x_.name])
                    d = s.DMA("sp", dst[t0 - dst_off + sub * 128: t0 - dst_off + sub * 128 + 128, :], x_[:], r=[x_.name], w=[dkey])
                    k.final.append(d)
            s.barrier()


    @staticmethod
    def _n(xs):
        return [x if isinstance(x, str) else x.name for x in xs]

    def TT(k, eng, out, in0, in1, op, r, w):
        k.s.I(eng, lambda e: e.tensor_tensor(out=out, in0=in0, in1=in1, op=op), k._n(r), k._n(w))

    def TS(k, eng, out, in0, s1, s2, op0, op1, r, w):
        if op1 is None:
            k.s.I(eng, lambda e: e.tensor_scalar(out=out, in0=in0, scalar1=s1, scalar2=None, op0=op0), k._n(r), k._n(w))
        else:
            k.s.I(eng, lambda e: e.tensor_scalar(out=out, in0=in0, scalar1=s1, scalar2=s2, op0=op0, op1=op1), k._n(r), k._n(w))

    def STT(k, out, in0, scalar, in1, op0, op1, r, w):
        k.s.I("dve", lambda e: e.scalar_tensor_tensor(out=out, in0=in0, scalar=scalar, in1=in1, op0=op0, op1=op1),
              k._n(r), k._n(w))

    def ACT(k, out, in_, func, r, w, **kw):
        k.s.I("act", lambda e: e.activation(out=out, in_=in_, func=func, **kw), k._n(r), k._n(w))

    def MM(k, out, lhsT, rhs, start, stop, r, w):
        k.s.I("pe", lambda e: e.matmul(out, lhsT=lhsT, rhs=rhs, start=start, stop=stop), k._n(r), k._n(w))

    def TR(k, out, in_, r, w):
        k.s.I("pe", lambda e: e.transpose(out=out, in_=in_, identity=k.ident[:]), k._n(r) + ["ident"], k._n(w))

    def CP(k, eng, out, in_, r, w):
        if eng == "act":
            k.s.I("act", lambda e: e.copy(out=out, in_=in_), k._n(r), k._n(w))
        else:
            k.s.I(eng, lambda e: e.tensor_copy(out=out, in_=in_), k._n(r), k._n(w))

    def LD(k, out, in_, r, w, eng="sp", **kw):
        return k.s.DMA(eng, out, in_, k._n(r), k._n(w), **kw)

    def gelu_tanh(k, y_, a_, b_, out_ap, wname):
        k.ACT(a_[:], y_[:], AF.Square, [y_], [a_])
        k.TS("dve", a_[:], a_[:], 0.044715, 1.0, ALU.mult, ALU.add, [a_], [a_])
        k.TT("dve", b_[:], a_[:], y_[:], ALU.mult, [a_, y_], [b_])
        k.ACT(b_[:], b_[:], AF.Sigmoid, [b_], [b_], scale=1.5957691216057308)
        k.TT("dve", out_ap, b_[:], y_[:], ALU.mult, [b_, y_], [wname])

    def prenormT(k, src, skey, t0, gpre, xT, bufs, psb):
        s = k.s
        xs, xn, junk, ss, rs, ctr = bufs
        for sub in range(4):
            i = ctr[0] % 2; ctr[0] += 1
            x_ = xs[i]; xn_ = xn[i]; ss_ = ss[i]; rs_ = rs[i]
            k.LD(x_[:], src[t0 + sub * 128: t0 + sub * 128 + 128, :], [skey], [x_])
            k.rstd(x_[:], junk, ss_, rs_, [x_.name])
            k.STT(xn_[:], x_[:], rs_[:], gpre[:], ALU.mult, ALU.mult, [x_, rs_, gpre], [xn_])
            for half in range(2):
                pst = psb[6 + half]
                pv = pst[:].bitcast(BF16)
                for j in range(8):
                    kk = half * 8 + j
                    k.TR(pv[:, j * 128:(j + 1) * 128], xn_[:, kk * 128:(kk + 1) * 128], [xn_], [pst])
                k.CP("act" if half == 0 else "dve", xT[:, half * 8:half * 8 + 8, sub * 128:(sub + 1) * 128],
                     pv.rearrange("p (a b) -> p a b", a=8), [pst], [xT])

    def down_post(k, tag, hT, KT, wd, wkey, src, skey, dst, dkey, gpost, scale_out, t0, f1, wb, wbi, bufs, psb):
        s = k.s
        xs, xn, junk, ss, rs, ctr = bufs
        groups = []
        f0 = 0
        while f0 < KT:
            nf = min(16, KT - f0); groups.append((f0, nf)); f0 += nf
        for c4 in range(4):
            base = 0 if c4 % 2 == 0 else 4
            for (f0, nf) in groups:
                wdc = wb[wbi[0] % 3]; wbi[0] += 1
                k.LD(wdc[:, 0:nf, :], wd[f0 * 128:(f0 + nf) * 128, c4 * 512:(c4 + 1) * 512].rearrange("(a p) c -> p a c", p=128),
                     [wkey], [wdc])
                for sub in range(4):
                    po = psb[base + sub]
                    for fi in range(nf):
                        f = f0 + fi
                        k.MM(po[:], hT[:, f, sub * 128:(sub + 1) * 128], wdc[:, fi, :], f == 0, f == KT - 1, [wdc, hT], [po])
            for sub in range(4):
                po = psb[base + sub]
                k.CP("act" if sub % 2 == 0 else "dve", f1[:, sub, c4 * 512:(c4 + 1) * 512], po[:], [po], [f1.name + str(sub)])
        for sub in range(4):
            i = ctr[0] % 2; ctr[0] += 1
            x_ = xs[i]; ss_ = ss[i]; rs_ = rs[i]
            k.LD(x_[:], src[t0 + sub * 128: t0 + sub * 128 + 128, :], [skey], [x_])
            k.rstd(f1[:, sub, :], junk, ss_, rs_, [f1.name + str(sub)], scale_out=scale_out)
            k.STT(f1[:, sub, :], f1[:, sub, :], rs_[:], gpost[:], ALU.mult, ALU.mult,
                  [f1.name + str(sub), rs_, gpost], [f1.name + str(sub)])
            k.TT("pool", x_[:], f1[:, sub, :], x_[:], ALU.add, [f1.name + str(sub), x_], [x_])
            d = k.LD(dst[t0 + sub * 128: t0 + sub * 128 + 128, :], x_[:], [x_], [dkey])
            k.final.append(d)

    def normbufs(k, es, tag):
        xs = [k.sb(es, tag + "xs%d" % i, [128, D], F32) for i in range(2)]
        xn = [k.sb(es, tag + "xn%d" % i, [128, D], BF16) for i in range(2)]
        junk = k.sb(es, tag + "junk", [128, D], F32)
        ss = [k.sb(es, tag + "ss%d" % i, [128, 1], F32) for i in range(2)]
        rs = [k.sb(es, tag + "rs%d" % i, [128, 1], F32) for i in range(2)]
        return (xs, xn, junk, ss, rs, [0])

    def proj(k, h1, gmix, win, sc, psb):
        s = k.s; T = k.T
        with ExitStack() as es:
            gpre = k.sb(es, "pj_g", [128, D], F32)
            k.LD(gpre[:], gmix.partition_broadcast(128), [], [gpre])
            uT = k.sb(es, "pj_uT", [128, 16, 512], BF16)
            wb = [k.sb(es, "pj_wb%d" % i, [128, 16, 512], BF16) for i in range(3)]
            bufs = k.normbufs(es, "pj_")
            s32 = [k.sb(es, "pj_s32%d" % i, [128, 512], F32) for i in range(2)]
            s16 = [k.sb(es, "pj_s16%d" % i, [128, 512], BF16) for i in range(4)]
            sg = [k.sb(es, "pj_sg%d" % i, [128, 48], F32) for i in range(2)]
            cnt = {"w": 0, "p": 0, "a": 0, "b": 0, "g": 0, "e": 0}

            def nps():
                p = psb[cnt["p"] % 6]; cnt["p"] += 1
                return p

            chunks = [(c * 512, 512) for c in range(7)] + [(3584, 48)] + [(3632 + 512 * i, 512) for i in range(8)]
            for tt in range(k.NT):
                t0 = tt * 512
                k.prenormT(h1, "h1", t0, gpre, uT, bufs, psb)
                for ci, (c0, wd_) in enumerate(chunks):
                    if t0 < k.OWN and ci not in (0, 1, 4, 5, 6):
                        continue
                    wc = wb[cnt["w"] % 3]; cnt["w"] += 1
                    k.LD(wc[:, :, 0:wd_], win[:, c0:c0 + wd_].rearrange("(a p) c -> p a c", p=128), ["win"], [wc])

                    def fm(off, M, kind, dst):
                        p = nps()
                        for kk in range(16):
                            k.MM(p[0:M, :], wc[:, kk, off:off + M], uT[:, kk, :], kk == 0, kk == 15, [wc, uT], [p])
                        if kind == "us":
                            st = s32[cnt["a"] % 2]; cnt["a"] += 1
                            cnt["e"] += 1
                            k.CP("act" if cnt["e"] % 2 else "dve", st[0:M, :], p[0:M, :], [p], [st])
                        else:
                            st = s16[cnt["b"] % 4]; cnt["b"] += 1
                            if kind == "q":
                                k.s.I("act", lambda e, st=st, p=p, M=M: e.mul(out=st[0:M, :], in_=p[0:M, :], mul=0.125), [p.name], [st.name])
                            elif kind == "sig":
                                k.ACT(st[0:M, :], p[0:M, :], AF.Sigmoid, [p], [st])
                            else:
                                k.CP("dve", st[0:M, :], p[0:M, :], [p], [st])
                        k.LD(dst, st[0:M, :], [st], ["pjout"])

                    def tm(off, N, kind, dst_t):
                        for sub in range(4):
                            p = nps()
                            for kk in range(16):
                                k.MM(p[:, 0:N], uT[:, kk, sub * 128:(sub + 1) * 128], wc[:, kk, off:off + N], kk == 0, kk == 15, [wc, uT], [p])
                            if kind == "sig":
                                st = sg[cnt["g"] % 2]; cnt["g"] += 1
                                k.ACT(st[:, 0:N], p[:, 0:N], AF.Sigmoid, [p], [st])
                            else:
                                st = s16[cnt["b"] % 4]; cnt["b"] += 1
                                k.CP("dve", st[:, 0:N], p[:, 0:N], [p], [st])
                            k.LD(dst_t[t0 + sub * 128: t0 + sub * 128 + 128, :], st[:, 0:N], [st], ["pjout"])

                    ts_ = slice(t0, t0 + 512)
                    if ci in (0, 1):
                        for j in range(4):
                            fm(j * 128, 128, "us", sc["usT"][(ci * 4 + j) * 128:(ci * 4 + j + 1) * 128, ts_])
                    elif ci in (2, 3):
                        for j in range(4):
                            h0 = (ci - 2) * 8 + 2 * j
                            fm(j * 128, 128, "q", sc["qT"][h0:h0 + 2, :, ts_].rearrange("a d t -> (a d) t"))
                    elif ci == 4:
                        for g2 in range(2):
                            fm(g2 * 128, 128, "k", sc["kcT"][2 * g2:2 * g2 + 2, :, ts_].rearrange("a d t -> (a d) t"))
                        for g2 in range(2):
                            fm(256 + g2 * 128, 128, "k", sc["vcT"][2 * g2:2 * g2 + 2, :, ts_].rearrange("a d t -> (a d) t"))
                    elif ci == 5:
                        for g2 in range(2):
                            fm(g2 * 128, 128, "k", sc["ksT"][2 * g2:2 * g2 + 2, :, ts_].rearrange("a d t -> (a d) t"))
                        tm(256, 256, "k", sc["Vs"])
                    elif ci == 6:
                        for g2 in range(2):
                            fm(g2 * 128, 128, "k", sc["kwT"][2 * g2:2 * g2 + 2, :, ts_].rearrange("a d t -> (a d) t"))
                        tm(256, 256, "k", sc["Vw"])
                    elif ci == 7:
                        tm(0, 48, "sig", sc["gates"])
                    elif ci < 12:
                        for j in range(4):
                            fm(j * 128, 128, "sig", sc["gaT"][((ci - 8) * 4 + j) * 128:((ci - 8) * 4 + j + 1) * 128, ts_])
                    else:
                        for j in range(4):
                            fm(j * 128, 128, "sig", sc["gbT"][((ci - 12) * 4 + j) * 128:((ci - 12) * 4 + j + 1) * 128, ts_])
            s.barrier()


    def compress(k, es_out, sc, w, psb):
        s = k.s; T = k.T
        NC = T // 16 - 1
        NIT = (NC + 127) // 128
        KcT = k.sb(es_out, "KcT", [128, 4, 512], BF16)
        Vc = k.sb(es_out, "Vc", [128, 4, 4, 64], BF16)
        s.I("pool", lambda e: e.memset(KcT[:], 0.0), w=[KcT.name])
        with ExitStack() as es:
            raw = k.sb(es, "cp_raw", [64, T], BF16)
            w1 = k.sb(es, "cp_w1", [64, 32, 256], BF16)
            w2 = k.sb(es, "cp_w2", [128, 2, 64], BF16)
            posf = k.sb(es, "cp_posf", [64, 32], F32)
            posb = k.sb(es, "cp_posb", [64, 32], BF16)
            bias = k.sb(es, "cp_bias", [128, 2], F32)
            y_ = k.sb(es, "cp_y", [128, 512], F32); a_ = k.sb(es, "cp_a", [128, 512], F32); b_ = k.sb(es, "cp_b", [128, 512], F32)
            hid = k.sb(es, "cp_hid", [128, 2, 512], BF16)
            k.LD(posf[:], k.din("cmp_posT", [64, 32]), [], [posf])
            k.CP("dve", posb[:], posf[:], [posf], [posb])
            for kind in range(2):
                rawsc = sc["kcT"] if kind == 0 else sc["vcT"]
                w1d, w2d = (w["ck1"], w["ck2"]) if kind == 0 else (w["cv1"], w["cv2"])
                k.LD(w1[:], w1d.rearrange("(l d) c -> d l c", d=64), ["cw"], [w1])
                k.LD(w2[:], w2d.rearrange("(a p) d -> p a d", p=128), ["cw"], [w2])
                for ht in range(2):
                    p = psb[ht]
                    for l in range(32):
                        k.MM(p[:, 0:1], w1[:, l, ht * 128:(ht + 1) * 128], posb[:, l:l + 1], l == 0, l == 31, [w1, posb], [p])
                    k.CP("dve", bias[:, ht:ht + 1], p[:, 0:1], [p], [bias])
                for g in range(4):
                    k.LD(raw[:], rawsc[g, :, :], ["pjout"], [raw])
                    for ht in range(2):
                        p = psb[2 + ht]
                        for l in range(32):
                            k.MM(p[:, 0:NC], w1[:, l, ht * 128:(ht + 1) * 128], raw[:, l: l + 16 * (NC - 1) + 1: 16],
                                 l == 0, l == 31, [w1, raw], [p])
                        k.TS("dve", y_[:, 0:NC], p[:, 0:NC], bias[:, ht:ht + 1], None, ALU.add, None, [p, bias], [y_])
                        k.gelu_tanh(y_, a_, b_, hid[:, ht, :], hid.name)
                    if kind == 0:
                        p = psb[4]
                        for ht in range(2):
                            k.MM(p[0:64, 0:NC], w2[:, ht, :], hid[:, ht, 0:NC], ht == 0, ht == 1, [w2, hid], [p])
                        k.CP("act", KcT[0:64, g, 0:NC], p[0:64, 0:NC], [p], [KcT])
                    else:
                        for it in range(NIT):
                            rows = min(128, NC - it * 128)
                            p = psb[4 + it % 2]
                            for ht in range(2):
                                k.MM(p[0:rows, 0:64], hid[:, ht, it * 128: it * 128 + rows], w2[:, ht, :], ht == 0, ht == 1, [w2, hid], [p])
                            k.CP("act", Vc[0:rows, it, g, :], p[0:rows, 0:64], [p], [Vc])
            s.barrier()
        return KcT, Vc

    def bias_tables(k, psb):
        s = k.s
        WS, WW = 1152, 768
        SKS = k.dscr("SKS", [16, 128 * (WS + 1)], F32)
        SKW = k.dscr("SKW", [16, 128 * (WW + 1)], F32)
        rb = k.din("rel_bias", [32, 16])
        with ExitStack() as es:
            ohs = k.sb(es, "bt_ohs", [33, WS], F32); ohw = k.sb(es, "bt_ohw", [33, WW], F32)
            k.LD(ohs[:], k.din("c_ohs", [33, WS]), [], [ohs]); k.LD(ohw[:], k.din("c_ohw", [33, WW]), [], [ohw])
            rbw = k.sb(es, "bt_rbw", [33, 16], F32); rbs = k.sb(es, "bt_rbs", [33, 16], F32); r31 = k.sb(es, "bt_r31", [33, 16], F32)
            s.I("dve", lambda e: e.memset(rbw[:], 1.0), w=[rbw.name])
            s.I("dve", lambda e: e.memset(rbs[:], 1.0), w=[rbs.name])
            s.I("dve", lambda e: e.memset(r31[:], 0.0), w=[r31.name])
            k.LD(rbw[0:32, :], rb, [rbw], [rbw])
            k.LD(r31[0:32, :], rb[31, :].partition_broadcast(32), [r31], [r31])
            k.TT("dve", rbs[0:32, :], rbw[0:32, :], r31[0:32, :], ALU.subtract, [rbw, r31], [rbs])
            ones = k.sb(es, "bt_ones", [33, 128], F32)
            s.I("dve", lambda e: e.memset(ones[:], 1.0), w=[ones.name])
            lh = [k.sb(es, "bt_lh%d" % i, [33, 128], F32) for i in range(2)]
            rep = [k.sb(es, "bt_rep%d" % i, [128, WS], F32) for i in range(2)]
            n = 0
            for (rbt, oh, W, SK) in ((rbs, ohs, WS, SKS), (rbw, ohw, WW, SKW)):
                for h in range(16):
                    l_ = lh[n % 2]; r_ = rep[n % 2]; n += 1
                    k.TS("dve", l_[:], ones[:], rbt[:, h:h + 1], None, ALU.mult, None, [ones, rbt], [l_])
                    for c0 in range(0, W, 512):
                        wd_ = min(512, W - c0)
                        p = psb[(c0 // 512) % 4 + 4 * (n % 2)]
                        k.MM(p[:, 0:wd_], l_[:], oh[:, c0:c0 + wd_], True, True, [l_, oh], [p])
                        k.CP("act" if (c0 // 512) % 2 else "dve", r_[:, c0:c0 + wd_], p[:, 0:wd_], [p], [r_])
                    dst = bass.AP(SK.tensor, h * 128 * (W + 1), [[W + 1, 128], [1, W]])
                    k.LD(dst, r_[:, 0:W], [r_], ["SK"])
            s.barrier()
        return SKS, SKW, WS, WW

    def nsa(k, sc, KcT, Vc, SKS, SKW, WS, WW, psb):
        s = k.s; T = k.T
        NC = T // 16 - 1
        NQ = T // 128
        NKT = T // 128
        LAGB, LAGC, NBUF = 3, 6, 9
        SB_ = (0, 1, 7)
        with ExitStack() as es:
            KsT = k.sb(es, "ns_KsT", [128, T], BF16); KwT = k.sb(es, "ns_KwT", [128, T], BF16)
            s.I("pool", lambda e: e.memset(KsT[64:128, :], 0.0), w=[KsT.name])
            s.I("pool", lambda e: e.memset(KwT[64:128, :], 0.0), w=[KwT.name])
            Vs = k.sb(es, "ns_Vs", [128, NKT, 64], BF16); Vw = k.sb(es, "ns_Vw", [128, NKT, 64], BF16)
            TS_ = k.sb(es, "ns_TS", [128, 4, 1024], F32); TW_ = k.sb(es, "ns_TW", [128, 4, 640], F32)
            TC_ = k.sb(es, "ns_TC", [128, 4, 58], F32)
            b31 = k.sb(es, "ns_b31", [128, 16], F32)
            k.LD(b31[:], k.inp["rel_bias"][31, :].partition_broadcast(128), [], [b31])
            vmrel = k.sb(es, "ns_vm", [128, 256], F32); acrel = k.sb(es, "ns_ac", [128, 256], F32)
            k.LD(vmrel[:], k.din("c_vmrel", [128, 256]), [], [vmrel]); k.LD(acrel[:], k.din("c_acrel", [128, 256]), [], [acrel])
            ccm = k.sb(es, "ns_ccm", [128, 128], F32); cca = k.sb(es, "ns_cca", [128, 128], F32)
            cmpm = k.sb(es, "ns_cmpm", [128, 512], F32); wmk = k.sb(es, "ns_wmk", [128, 1], F32)
            k.LD(ccm[:], k.din("pc_cm", [128, 128]), [], [ccm]); k.LD(cca[:], k.din("pc_ca", [128, 128]), [], [cca])
            k.LD(cmpm[:], k.din("pc_cmpmask", [128, 512]), [], [cmpm]); k.LD(wmk[:], k.din("pc_wmask", [128, 1]), [], [wmk])
            qT4 = [k.sb(es, "ns_q%d" % i, [128, 4, 128], BF16) for i in range(2)]
            for q__ in qT4:
                s.I("pool", lambda e, q__=q__: e.memset(q__[:], 0.0), w=[q__.name])
            gt = [k.sb(es, "ns_gt%d" % i, [128, 48], F32) for i in range(2)]
            pcn = k.sb(es, "ns_pcn", [128, 4, 512], F32)
            s.I("pool", lambda e: e.memset(pcn[:], 0.0), w=[pcn.name])
            tmp = [k.sb(es, "ns_tmp%d" % i, [128, 512], F32) for i in range(NBUF)]
            P_ = [k.sb(es, "ns_P%d" % i, [128, 512], BF16) for i in range(NBUF)]
            pT = [k.sb(es, "ns_pT%d" % i, [128, 4, 128], BF16) for i in range(NBUF)]
            ps4 = k.sb(es, "ns_ps4", [128, 512], F32)
            imp = k.sb(es, "ns_imp", [128, 128], F32); sc1 = k.sb(es, "ns_sc1", [128, 128], F32); sc2 = k.sb(es, "ns_sc2", [128, 128], F32)
            m8a = k.sb(es, "ns_m8a", [128, 8], F32); m8b = k.sb(es, "ns_m8b", [128, 8], F32)
            mk = [k.sb(es, "ns_mk%d" % i, [128, 128], F32) for i in range(2)]
            rsc = [k.sb(es, "ns_rsc%d" % i, [128, 4], F32) for i in range(2)]
            rss = [k.sb(es, "ns_rss%d" % i, [128, 4, 32], F32) for i in range(2)]
            rsw = [k.sb(es, "ns_rsw%d" % i, [128, 4, 2], F32) for i in range(2)]
            rt = [k.sb(es, "ns_rt%d" % i, [128, 4, 4], F32) for i in range(2)]
            oacc = k.sb(es, "ns_oacc", [128, 256], F32)
            ob = [k.sb(es, "ns_ob%d" % i, [128, 256], BF16) for i in range(2)]
            cn = {"o": 0}

            def bc64(ap2):
                return bass.AP(ap2.tensor, ap2.offset, [list(ap2.ap[0]), list(ap2.ap[1]), [0, 64]])

            for g in range(4):
                k.LD(KsT[0:64, :], sc["ksT"][g, :, :], ["pjout"], [KsT]); k.LD(KwT[0:64, :], sc["kwT"][g, :, :], ["pjout"], [KwT])
                k.LD(Vs[:], sc["Vs"][:, g * 64:(g + 1) * 64].rearrange("(n p) d -> p n d", p=128), ["pjout"], [Vs])
                k.LD(Vw[:], sc["Vw"][:, g * 64:(g + 1) * 64].rearrange("(n p) d -> p n d", p=128), ["pjout"], [Vw])
                for hl in range(4):
                    h = 4 * g + hl
                    k.LD(TS_[:, hl, :], bass.AP(SKS.tensor, h * 128 * (WS + 1) + 127, [[WS, 128], [1, 1024]]), ["SK"], [TS_])
                    k.LD(TW_[:, hl, :], bass.AP(SKW.tensor, h * 128 * (WW + 1) + 127, [[WW, 128], [1, 640]]), ["SK"], [TW_])
                    k.LD(TC_[:, hl, :], bass.AP(SKS.tensor, h * 128 * (WS + 1) + 238, [[WS, 128], [16, 58]]), ["SK"], [TC_],
                         allow_slow_non_contiguous=True)
                items = []
                for n in range(NQ // 2, NQ):
                    q0 = 128 * n; par = n % 2
                    ncv = min(NC, 8 * n + 7)
                    blk = []
                    for hl in range(4):
                        blk.append(dict(kind="c", n=n, hl=hl, wdt=ncv))
                    w0 = max(0, q0 - 512)
                    pieces = []
                    a_ = w0
                    while a_ < q0 + 128:
                        b_ = min(a_ + 512, q0 + 128)
                        if a_ < k.OWN < b_:
                            b_ = k.OWN
                        pieces.append((a_, b_ - a_)); a_ = b_
                    assert len(pieces) <= 2
                    for hl in range(4):
                        for pi_, (s0, wdt) in enumerate(pieces):
                            blk.append(dict(kind="w", n=n, hl=hl, s0=s0, wdt=wdt, pi=pi_, first=pi_ == 0, last=pi_ == len(pieces) - 1,
                                            npc=len(pieces)))
                    nch = n // 4 + 1
                    for hl in range(4):
                        for c in range(nch):
                            s0 = 512 * c
                            blk.append(dict(kind="s", n=n, hl=hl, s0=s0, wdt=min(512, q0 + 128 - s0), c=c, first=c == 0, last=c == nch - 1, nch=nch))
                    blk[0]["load"] = True
                    blk[3]["topk"] = True
                    blk[-1]["combine"] = True
                    blk[-1]["npc_w"] = len(pieces)
                    items += blk
                for i, it in enumerate(items):
                    it["i"] = i

                def stageA(it):
                    n = it["n"]; hl = it["hl"]; h = 4 * g + hl; q0 = 128 * n; par = n % 2; i = it["i"]
                    q_ = qT4[par]; g_ = gt[par]
                    if it.get("load"):
                        k.LD(q_[0:64, :, :], sc["qT"][4 * g:4 * g + 4, :, q0:q0 + 128].rearrange("h d t -> d h t"), ["pjout"], [q_])
                        k.LD(g_[:], sc["gates"][q0:q0 + 128, :], ["pjout"], [g_])
                    p = psb[SB_[i % 3]]; t_ = tmp[i % NBUF]; Pt = P_[i % NBUF]
                    wdt = it["wdt"]
                    if it["kind"] == "c":
                        ncv = wdt
                        i_lo = max(0, 8 * n - 51); c_lo = i_lo - 8 * n + 51
                        k.MM(p[:, 0:ncv], q_[:, hl, :], KcT[:, g, 0:ncv], True, True, [q_, KcT], [p])
                        k.TT("dve", t_[:, 0:ncv], p[:, 0:ncv], cmpm[:, 0:ncv], ALU.add, [p, cmpm], [t_])
                        k.TT("pool", t_[:, i_lo:ncv], t_[:, i_lo:ncv], TC_[:, hl, c_lo:c_lo + ncv - i_lo], ALU.add, [t_, TC_], [t_])
                        rk = rt[par].name + str(hl)
                        k.ACT(t_[:, 0:ncv], t_[:, 0:ncv], AF.Exp, [t_, b31], [t_, rsc[par].name + str(hl)], bias=b31[:, h:h + 1],
                              accum_out=rsc[par][:, hl:hl + 1])
                        k.TS("dve", rt[par][:, hl, 0:1], rsc[par][:, hl:hl + 1], 1e-30, None, ALU.max, None, [rsc[par].name + str(hl)], [rk])
                        s.I("dve", lambda e, hl=hl, par=par: e.reciprocal(out=rt[par][:, hl, 0:1], in_=rt[par][:, hl, 0:1]), [rk], [rk])
                        k.TS("dve", pcn[:, hl, 0:ncv], t_[:, 0:ncv], rt[par][:, hl, 0:1], None, ALU.mult, None, [t_, rk], [pcn])
                        k.CP("pool", Pt[:, 0:ncv], pcn[:, hl, 0:ncv], [pcn], [Pt])
                    elif it["kind"] == "w":
                        s0 = it["s0"]
                        k.MM(p[:, 0:wdt], q_[:, hl, :], KwT[:, s0:s0 + wdt], True, True, [q_, KwT], [p])
                        v_lo = s0 - q0 + 512
                        if s0 < k.OWN:
                            k.STT(t_[:, 0:wdt], p[:, 0:wdt], wmk[:, 0:1], TW_[:, hl, v_lo:v_lo + wdt], ALU.add, ALU.add, [p, wmk, TW_], [t_])
                        else:
                            k.TT("dve", t_[:, 0:wdt], p[:, 0:wdt], TW_[:, hl, v_lo:v_lo + wdt], ALU.add, [p, TW_], [t_])
                        k.ACT(Pt[:, 0:wdt], t_[:, 0:wdt], AF.Exp, [t_], [Pt, rsw[par].name + str(hl)], accum_out=rsw[par][:, hl, it["pi"]:it["pi"] + 1])
                    else:
                        s0 = it["s0"]; c = it["c"]
                        k.MM(p[:, 0:wdt], q_[:, hl, :], KsT[:, s0:s0 + wdt], True, True, [q_, KsT], [p])
                        nb = wdt // 64
                        k.TT("dve", t_[:, 0:wdt].rearrange("p (a b) -> p a b", b=64), p[:, 0:wdt].rearrange("p (a b) -> p a b", b=64),
                             bc64(mk[par][:, 8 * c: 8 * c + nb]), ALU.add, [p, mk[par]], [t_])
                        s_lo = max(s0, q0 - 896)
                        if s_lo < s0 + wdt:
                            v_lo = s_lo - q0 + 896
                            ln = s0 + wdt - s_lo
                            k.TT("pool", t_[:, s_lo - s0: wdt], t_[:, s_lo - s0: wdt], TS_[:, hl, v_lo:v_lo + ln], ALU.add, [t_, TS_], [t_])
                        k.ACT(Pt[:, 0:wdt], t_[:, 0:wdt], AF.Exp, [t_, b31], [Pt, rss[par].name + str(hl)], bias=b31[:, h:h + 1],
                              accum_out=rss[par][:, hl, c:c + 1])
                    if it.get("topk"):
                        k.TT("dve", ps4[:], pcn[:, 0, :], pcn[:, 1, :], ALU.add, [pcn], [ps4])
                        k.TT("dve", ps4[:], ps4[:], pcn[:, 2, :], ALU.add, [pcn, ps4], [ps4])
                        k.TT("dve", ps4[:], ps4[:], pcn[:, 3, :], ALU.add, [pcn, ps4], [ps4])
                        s.I("dve", lambda e: e.tensor_reduce(out=imp[:], in_=ps4[:].rearrange("p (j m) -> p j m", m=4), axis=AX.X, op=ALU.add),
                            [ps4.name], [imp.name])
                        k.TT("dve", imp[:, 1:128], imp[:, 1:128], ps4[:, 3:508:4], ALU.add, [imp, ps4], [imp])
                        so = 127 - 2 * n
                        k.TT("dve", sc1[:], imp[:], vmrel[:, so:so + 128], ALU.mult, [imp, vmrel], [sc1])
                        k.TT("dve", sc1[:], sc1[:], acrel[:, so:so + 128], ALU.add, [sc1, acrel], [sc1])
                        k.TT("dve", sc1[:], sc1[:], ccm[:], ALU.mult, [sc1, ccm], [sc1])
                        k.TT("dve", sc1[:], sc1[:], cca[:], ALU.add, [sc1, cca], [sc1])
                        s.I("dve", lambda e: e.max(out=m8a[:], in_=sc1[:]), [sc1.name], [m8a.name])
                        s.I("dve", lambda e: e.match_replace(out=sc2[:], in_to_replace=m8a[:], in_values=sc1[:], imm_value=-3e4),
                            [sc1.name, m8a.name], [sc2.name])
                        s.I("dve", lambda e: e.max(out=m8b[:], in_=sc2[:]), [sc2.name], [m8b.name])
                        k.TS("dve", mk[par][:], sc1[:], m8b[:, 7:8], BIG, ALU.is_ge, ALU.mult, [sc1, m8b], [mk[par]])
                        k.TS("dve", mk[par][:], mk[par][:], -BIG, None, ALU.add, None, [mk[par]], [mk[par]])

                def stageB(it):
                    i = it["i"]; Pt = P_[i % NBUF]; ptile = pT[i % NBUF]; wdt = it["wdt"]
                    pst = psb[2 + i % 2]; pv = pst[:].bitcast(BF16)
                    nk = (wdt + 127) // 128
                    for kt in range(nk):
                        rows = min(128, wdt - kt * 128)
                        k.TR(pv[0:rows, kt * 128:(kt + 1) * 128], Pt[:, kt * 128: kt * 128 + rows], [Pt], [pst])
                    cn["o"] += 1
                    eng = "act" if cn["o"] % 2 else "dve"
                    if wdt % 128 == 0:
                        k.CP(eng, ptile[:, 0:nk, :], pv[:, 0:nk * 128].rearrange("p (a b) -> p a b", b=128), [pst], [ptile])
                    else:
                        for kt in range(nk):
                            rows = min(128, wdt - kt * 128)
                            k.CP(eng, ptile[0:rows, kt, :], pv[0:rows, kt * 128:(kt + 1) * 128], [pst], [ptile])

                def stageC(it):
                    n = it["n"]; hl = it["hl"]; par = n % 2; i = it["i"]; ptile = pT[i % NBUF]; wdt = it["wdt"]
                    nk = (wdt + 127) // 128
                    pcs = psb[4 + par]
                    if it["kind"] == "c":
                        po = pcs[:, hl * 64:(hl + 1) * 64]; pkey = "po_cs%d" % par
                        for kt in range(nk):
                            rows = min(128, wdt - kt * 128)
                            k.MM(po, ptile[0:rows, kt, :], Vc[0:rows, kt, g, :], kt == 0, kt == nk - 1, [ptile, Vc], [pkey])
                    elif it["kind"] == "w":
                        po = psb[6][:, par * 256 + hl * 64: par * 256 + (hl + 1) * 64]; pkey = "po_w%d" % par
                        kt0 = it["s0"] // 128
                        for kt in range(nk):
                            k.MM(po, ptile[:, kt, :], Vw[:, kt0 + kt, :], it["first"] and kt == 0, it["last"] and kt == nk - 1, [ptile, Vw], [pkey])
                    else:
                        po = pcs[:, 256 + hl * 64: 256 + (hl + 1) * 64]; pkey = "po_cs%d" % par
                        kt0 = it["s0"] // 128
                        for kt in range(nk):
                            k.MM(po, ptile[:, kt, :], Vs[:, kt0 + kt, :], it["first"] and kt == 0, it["last"] and kt == nk - 1, [ptile, Vs], [pkey])
                    if it.get("combine"):
                        q0 = 128 * n; g_ = gt[par]
                        o_ = ob[par]
                        for h2 in range(4):
                            col = g * 12 + h2 * 3
                            rk = rt[par].name + str(h2)
                            nch = n // 4 + 1
                            npc = it["npc_w"]
                            s.I("dve", lambda e, h2=h2, nch=nch, par=par: e.tensor_reduce(out=rt[par][:, h2, 1:2], in_=rss[par][:, h2, 0:nch], axis=AX.X, op=ALU.add),
                                [rss[par].name + str(h2)], [rk])
                            s.I("dve", lambda e, h2=h2, npc=npc, par=par: e.tensor_reduce(out=rt[par][:, h2, 2:3], in_=rsw[par][:, h2, 0:npc], axis=AX.X, op=ALU.add),
                                [rsw[par].name + str(h2)], [rk])
                            s.I("dve", lambda e, h2=h2, par=par: e.reciprocal(out=rt[par][:, h2, 1:3], in_=rt[par][:, h2, 1:3]), [rk], [rk])
                            k.TT("dve", rt[par][:, h2, 1:3], rt[par][:, h2, 1:3], g_[:, col + 1:col + 3], ALU.mult, [rk, g_], [rk])
                            hs = slice(h2 * 64, (h2 + 1) * 64)
                            k.TS("dve", oacc[:, hs], pcs[:, h2 * 64:(h2 + 1) * 64], g_[:, col:col + 1], None, ALU.mult, None, ["po_cs%d" % par, g_], [oacc])
                            k.STT(oacc[:, hs], pcs[:, 256 + h2 * 64: 256 + (h2 + 1) * 64], rt[par][:, h2, 1:2], oacc[:, hs], ALU.mult, ALU.add,
                                  ["po_cs%d" % par, rk, oacc], [oacc])
                            k.STT(o_[:, hs], psb[6][:, par * 256 + h2 * 64: par * 256 + (h2 + 1) * 64], rt[par][:, h2, 2:3], oacc[:, hs], ALU.mult, ALU.add,
                                  ["po_w%d" % par, rk, oacc], [o_])
                        k.LD(sc["onsa"][q0:q0 + 128, g * 256:(g + 1) * 256], o_[:], [o_], ["onsa"])

                N = len(items)
                for t in range(N + LAGC):
                    if t < N:
                        stageA(items[t])
                    if 0 <= t - LAGB < N:
                        stageB(items[t - LAGB])
                    if 0 <= t - LAGC < N:
                        stageC(items[t - LAGC])
            s.barrier()

    def merge(k, sc, w, h1, h2, gpost_ap, psb):
        s = k.s; T = k.T
        with ExitStack() as es:
            gpost = k.sb(es, "mg_g", [128, D], F32)
            k.LD(gpost[:], gpost_ap.partition_broadcast(128), [], [gpost])
            yT = k.sb(es, "mg_yT", [128, 8, 512], BF16)
            oT = k.sb(es, "mg_oT", [128, 8, 512], BF16)
            zT = k.sb(es, "mg_zT", [128, 16, 512], BF16)
            ot = [k.sb(es, "mg_ot%d" % i, [128, 1024], BF16) for i in range(2)]
            wb = [k.sb(es, "mg_wb%d" % i, [128, 16, 512], BF16) for i in range(3)]
            wbi = [0]
            f1 = k.sb(es, "mg_f1", [128, 4, D], F32)
            bufs = k.normbufs(es, "mg_")
            ga = [k.sb(es, "mg_ga%d" % i, [128, 512], BF16) for i in range(2)]
            gb = [k.sb(es, "mg_gb%d" % i, [128, 512], BF16) for i in range(2)]
            t1 = [k.sb(es, "mg_t1%d" % i, [128, 512], F32) for i in range(2)]
            t2 = [k.sb(es, "mg_t2%d" % i, [128, 512], F32) for i in range(2)]
            ci = 0
            for tt in range(k.NT // 2, k.NT):
                t0 = tt * 512
                k.LD(yT[:], sc["ysT"][:, t0:t0 + 512].rearrange("(a p) t -> p a t", p=128), ["ysT"], [yT])
                for sub in range(4):
                    o_ = ot[sub % 2]
                    k.LD(o_[:], sc["onsa"][t0 + sub * 128:t0 + sub * 128 + 128, :], ["onsa"], [o_])
                    pst = psb[6 + sub % 2]; pv = pst[:].bitcast(BF16)
                    for j in range(8):
                        k.TR(pv[:, j * 128:(j + 1) * 128], o_[:, j * 128:(j + 1) * 128], [o_], [pst])
                    k.CP("act" if sub % 2 else "dve", oT[:, :, sub * 128:(sub + 1) * 128], pv.rearrange("p (a b) -> p a b", a=8), [pst], [oT])
                for c4 in range(4):
                    wl = []
                    for nm_ in ("glu1", "glu2", "wo"):
                        wc = wb[wbi[0] % 3]; wbi[0] += 1
                        k.LD(wc[:, 0:8, :], w[nm_][:, c4 * 512:(c4 + 1) * 512].rearrange("(a p) c -> p a c", p=128), ["mw"], [wc])
                        wl.append(wc)
                    for j in range(4):
                        ct = c4 * 4 + j
                        pa, pb, pc_ = psb[(ci * 3) % 6], psb[(ci * 3 + 1) % 6], psb[(ci * 3 + 2) % 6]
                        ga_, gb_, t1_, t2_ = ga[ci % 2], gb[ci % 2], t1[ci % 2], t2[ci % 2]; ci += 1
                        k.LD(ga_[:], sc["gaT"][ct * 128:(ct + 1) * 128, t0:t0 + 512], ["pjout"], [ga_])
                        k.LD(gb_[:], sc["gbT"][ct * 128:(ct + 1) * 128, t0:t0 + 512], ["pjout"], [gb_])
                        for (pp, wc, src_) in ((pa, wl[0], yT), (pb, wl[1], yT), (pc_, wl[2], oT)):
                            for kk in range(8):
                                k.MM(pp[:], wc[:, kk, j * 128:(j + 1) * 128], src_[:, kk, :], kk == 0, kk == 7, [wc, src_], [pp])
                        k.ACT(t1_[:], pb[:], AF.Sigmoid, [pb], [t1_])
                        k.TT("dve", t1_[:], pa[:], t1_[:], ALU.mult, [pa, t1_], [t1_])
                        k.TT("pool", t1_[:], t1_[:], ga_[:], ALU.mult, [t1_, ga_], [t1_])
                        k.TT("dve", t2_[:], pc_[:], gb_[:], ALU.mult, [pc_, gb_], [t2_])
                        k.TT("pool", zT[:, ct, :], t1_[:], t2_[:], ALU.add, [t1_, t2_], [zT])
                k.down_post("mg", zT, 16, w["wout"], "mw", h1, "h1", h2, "h2", gpost, 1.0, t0, f1, wb, wbi, bufs, psb)
            s.barrier()

    def sincos(k, es, tag, ang, shape, want_cos):
        s = k.s
        I32 = mybir.dt.int32
        t = k.sb(es, tag + "t", shape, F32); ti = k.sb(es, tag + "ti", shape, I32)
        r = k.sb(es, tag + "r", shape, F32); m = k.sb(es, tag + "m", shape, F32)
        o = k.sb(es, tag + "o", shape, F32)
        a2 = ang
        if want_cos:
            a2 = k.sb(es, tag + "a2", shape, F32)
            s.I("dve", lambda e: e.tensor_scalar(out=a2[:], in0=ang[:], scalar1=math.pi / 2, scalar2=None, op0=ALU.add),
                r=[ang.name], w=[a2.name])
        s.I("dve", lambda e: e.tensor_scalar(out=t[:], in0=a2[:], scalar1=1.0 / (2 * math.pi), scalar2=None, op0=ALU.mult),
            r=[a2.name], w=[t.name])
        s.I("dve", lambda e: e.tensor_copy(out=ti[:], in_=t[:]), r=[t.name], w=[ti.name])
        s.I("dve", lambda e: e.tensor_copy(out=t[:], in_=ti[:]), r=[ti.name], w=[t.name])
        s.I("dve", lambda e: e.scalar_tensor_tensor(out=r[:], in0=t[:], scalar=-2 * math.pi, in1=a2[:],
                                                    op0=ALU.mult, op1=ALU.add), r=[t.name, a2.name], w=[r.name])
        for (thr, op, fix) in ((math.pi, ALU.is_gt, -2 * math.pi), (-math.pi, ALU.is_lt, 2 * math.pi)):
            s.I("dve", lambda e, thr=thr, op=op, fix=fix: e.tensor_scalar(
                out=m[:], in0=r[:], scalar1=thr, scalar2=fix, op0=op, op1=ALU.mult), r=[r.name], w=[m.name])
            s.I("dve", lambda e: e.tensor_tensor(out=r[:], in0=r[:], in1=m[:], op=ALU.add),
                r=[r.name, m.name], w=[r.name])
        s.I("dve", lambda e: e.tensor_scalar(out=r[:], in0=r[:], scalar1=-math.pi, scalar2=math.pi,
                                             op0=ALU.max, op1=ALU.min), r=[r.name], w=[r.name])
        s.I("act", lambda e: e.activation(out=o[:], in_=r[:], func=AF.Sin), r=[r.name], w=[o.name])
        return o

    def disc(k, es, tag, are, aim, ldt, shape):
        s = k.s
        dt = k.sb(es, tag + "dt", shape, F32); lam = k.sb(es, tag + "lam", shape, F32)
        mag = k.sb(es, tag + "mag", shape, F32); ang = k.sb(es, tag + "ang", shape, F32)
        abr = k.sb(es, tag + "abr", shape, F32); abi = k.sb(es, tag + "abi", shape, F32)
        s.I("act", lambda e: e.activation(out=dt[:], in_=ldt[:], func=AF.Exp), r=[ldt.name], w=[dt.name])
        s.I("dve", lambda e: e.tensor_scalar(out=lam[:], in0=are[:], scalar1=-1e-4, scalar2=None, op0=ALU.min),
            r=[are.name], w=[lam.name])
        s.I("dve", lambda e: e.tensor_tensor(out=mag[:], in0=lam[:], in1=dt[:], op=ALU.mult),
            r=[lam.name, dt.name], w=[mag.name])
        s.I("act", lambda e: e.activation(out=mag[:], in_=mag[:], func=AF.Exp), r=[mag.name], w=[mag.name])
        s.I("dve", lambda e: e.tensor_tensor(out=ang[:], in0=aim[:], in1=dt[:], op=ALU.mult),
            r=[aim.name, dt.name], w=[ang.name])
        sn = k.sincos(es, tag + "s", ang, shape, False)
        cs = k.sincos(es, tag + "c", ang, shape, True)
        s.I("dve", lambda e: e.tensor_tensor(out=abr[:], in0=mag[:], in1=cs[:], op=ALU.mult),
            r=[mag.name, cs.name], w=[abr.name])
        s.I("dve", lambda e: e.tensor_tensor(out=abi[:], in0=mag[:], in1=sn[:], op=ALU.mult),
            r=[mag.name, sn.name], w=[abi.name])
        return abr, abi, lam

    def s5(k, usT, ysT, psb):
        s = k.s; T = k.T
        LM = int(round(math.log2(T)))
        NLV = LM
        tt = lambda e, **kw: e.tensor_tensor(**kw)
        with ExitStack() as es:
            a2r = k.sb(es, "a2r", [128, 64], F32); a2i = k.sb(es, "a2i", [128, 64], F32); l2 = k.sb(es, "l2", [128, 64], F32)
            for t_, n_ in ((a2r, "ssm_aT_re2"), (a2i, "ssm_aT_im2"), (l2, "ssm_ldt2")):
                s.DMA("sp", t_[:], k.din(n_, [128, 64]), w=[t_.name])
            sgn = k.sb(es, "sgn", [128, 1], F32); s.DMA("sp", sgn[:], k.din("c_sgn", [128, 1]), w=[sgn.name])
            mk8 = k.sb(es, "mk8", [128, 8], F32); s.DMA("sp", mk8[:], k.din("c_mask8", [128, 8]), w=[mk8.name])
            Jm = k.sb(es, "Jm", [128, 128], F32); s.DMA("sp", Jm[:], k.din("c_J", [128, 128]), w=[Jm.name])
            Id = k.sb(es, "Idf", [128, 128], F32); s.DMA("sp", Id[:], k.din("c_I", [128, 128]), w=[Id.name])
            dsk = k.sb(es, "dsk", [128, 8], F32); s.DMA("sp", dsk[:], k.din("ssm_dT", [128, 8]), w=[dsk.name])
            PR = k.sb(es, "PR", [128, NLV, 64], F32); PI = k.sb(es, "PI", [128, NLV, 64], F32)
            with ExitStack() as e2:
                abr, abi, _ = k.disc(e2, "d1", a2r, a2i, l2, [128, 64])
                s.I("dve", lambda e, abr=abr: e.tensor_copy(out=PR[:, 0, :], in_=abr[:]), r=[abr.name], w=[PR.name])
                s.I("dve", lambda e, abi=abi: e.tensor_copy(out=PI[:, 0, :], in_=abi[:]), r=[abi.name], w=[PI.name])
                t1 = k.sb(e2, "pw1", [128, 64], F32); t2 = k.sb(e2, "pw2", [128, 64], F32)
                for i in range(NLV - 1):
                    s.I("dve", lambda e, i=i: tt(e, out=t1[:], in0=PR[:, i, :], in1=PR[:, i, :], op=ALU.mult), r=[PR.name], w=[t1.name])
                    s.I("dve", lambda e, i=i: tt(e, out=t2[:], in0=PI[:, i, :], in1=PI[:, i, :], op=ALU.mult), r=[PI.name], w=[t2.name])
                    s.I("dve", lambda e, i=i: tt(e, out=PR[:, i + 1, :], in0=t1[:], in1=t2[:], op=ALU.subtract), r=[t1.name, t2.name], w=[PR.name])
                    s.I("dve", lambda e, i=i: tt(e, out=t1[:], in0=PR[:, i, :], in1=PI[:, i, :], op=ALU.mult), r=[PR.name, PI.name], w=[t1.name])
                    s.I("dve", lambda e, i=i: e.tensor_scalar(out=PI[:, i + 1, :], in0=t1[:], scalar1=2.0, scalar2=None, op0=ALU.mult), r=[t1.name], w=[PI.name])
                s.I("dve", lambda e: e.tensor_scalar(out=PI[:], in0=PI[:], scalar1=sgn[:], scalar2=None, op0=ALU.mult), r=[PI.name, sgn.name], w=[PI.name])
                s.barrier()
            Bb = k.sb(es, "Bb", [128, 8, 128], F32)
            with ExitStack() as e2:
                sh = [128, 8 * 64]
                ar = k.sb(e2, "b_ar", sh, F32); ai = k.sb(e2, "b_ai", sh, F32); ld = k.sb(e2, "b_ld", sh, F32)
                br = k.sb(e2, "b_br", sh, F32); bi = k.sb(e2, "b_bi", sh, F32)
                for t_, n_ in ((ar, "ssm_a_re_b"), (ai, "ssm_a_im_b"), (ld, "ssm_ldt_b"), (br, "ssm_b_re_b"), (bi, "ssm_b_im_b")):
                    s.DMA("sp", t_[:], k.din(n_, sh), w=[t_.name])
                abr, abi, lam = k.disc(e2, "d2", ar, ai, ld, sh)
                den = k.sb(e2, "den", sh, F32); u1 = k.sb(e2, "u1", sh, F32); u2 = k.sb(e2, "u2", sh, F32)
                cor = k.sb(e2, "cor", sh, F32); coi = k.sb(e2, "coi", sh, F32)
                D_ = lambda fn, r, w: s.I("dve", fn, r=[x.name for x in r], w=[x.name for x in w])
                D_(lambda e: tt(e, out=den[:], in0=lam[:], in1=lam[:], op=ALU.mult), [lam], [den])
                D_(lambda e: tt(e, out=u1[:], in0=ai[:], in1=ai[:], op=ALU.mult), [ai], [u1])
                D_(lambda e: tt(e, out=den[:], in0=den[:], in1=u1[:], op=ALU.add), [den, u1], [den])
                D_(lambda e: e.reciprocal(out=den[:], in_=den[:]), [den], [den])
                D_(lambda e: e.tensor_scalar(out=abr[:], in0=abr[:], scalar1=-1.0, scalar2=None, op0=ALU.add), [abr], [abr])
                D_(lambda e: tt(e, out=u1[:], in0=abr[:], in1=lam[:], op=ALU.mult), [abr, lam], [u1])
                D_(lambda e: tt(e, out=u2[:], in0=abi[:], in1=ai[:], op=ALU.mult), [abi, ai], [u2])
                D_(lambda e: tt(e, out=u1[:], in0=u1[:], in1=u2[:], op=ALU.add), [u1, u2], [u1])
                D_(lambda e: tt(e, out=cor[:], in0=u1[:], in1=den[:], op=ALU.mult), [u1, den], [cor])
                D_(lambda e: tt(e, out=u1[:], in0=abi[:], in1=lam[:], op=ALU.mult), [abi, lam], [u1])
                D_(lambda e: tt(e, out=u2[:], in0=abr[:], in1=ai[:], op=ALU.mult), [abr, ai], [u2])
                D_(lambda e: tt(e, out=u1[:], in0=u1[:], in1=u2[:], op=ALU.subtract), [u1, u2], [u1])
                D_(lambda e: tt(e, out=coi[:], in0=u1[:], in1=den[:], op=ALU.mult), [u1, den], [coi])
                v3 = lambda t_: t_[:].rearrange("p (q x) -> p q x", q=8)
                D_(lambda e: tt(e, out=u1[:], in0=cor[:], in1=br[:], op=ALU.mult), [cor, br], [u1])
                D_(lambda e: tt(e, out=u2[:], in0=coi[:], in1=bi[:], op=ALU.mult), [coi, bi], [u2])
                D_(lambda e: tt(e, out=Bb[:, :, 0:64], in0=v3(u1), in1=v3(u2), op=ALU.subtract), [u1, u2], [Bb])
                D_(lambda e: tt(e, out=u1[:], in0=cor[:], in1=bi[:], op=ALU.mult), [cor, bi], [u1])
                D_(lambda e: tt(e, out=u2[:], in0=coi[:], in1=br[:], op=ALU.mult), [coi, br], [u2])
                D_(lambda e: tt(e, out=Bb[:, :, 64:128], in0=v3(u1), in1=v3(u2), op=ALU.add), [u1, u2], [Bb])
                s.barrier()
            Ct = k.sb(es, "Ct", [128, 64, 16], F32)
            s.DMA("sp", Ct[:], k.din("ssm_cT2", [128, 64, 16]), w=[Ct.name])
            s.I("dve", lambda e: e.tensor_scalar(out=Ct[:], in0=Ct[:], scalar1=sgn[:], scalar2=None, op0=ALU.mult),
                r=[Ct.name, sgn.name], w=[Ct.name])
            NG = 3
            OW = k.OWN
            us = k.sb(es, "us32", [128, T], F32)
            yacc = k.sb(es, "yacc", [128, T - OW], F32)
            Aa = [k.sb(es, "Ast%d" % i, [128, T], F32) for i in range(NG)]
            Mt = [k.sb(es, "Mt%d" % i, [128, NLV, 128], F32) for i in range(NG)]
            BmP = k.sb(es, "BmP", [128, 8, 128], F32)
            CmP = k.sb(es, "CmP", [128, 8, 128], F32)
            gl = [k.sb(es, "gl%d" % i, [128, 512], F32) for i in range(3)]
            yb = [k.sb(es, "yb%d" % i, [128, 512], BF16) for i in range(2)]
            pc = [0]

            def nps():
                p = psb[pc[0] % 8]; pc[0] += 1
                return p
            ev = [0]

            def evac_copy(dst, src, rk, wk):
                ev[0] += 1
                if ev[0] % 2:
                    s.I("act", lambda e: e.copy(out=dst, in_=src), r=rk, w=wk)
                else:
                    s.I("dve", lambda e: e.tensor_copy(out=dst, in_=src), r=rk, w=wk)

            def group_steps(q, j, slot):
                g = q * 8 + j
                A = Aa[slot]; M = Mt[slot]
                for i in range(NLV):
                    s.I("pool", lambda e, i=i: e.tensor_scalar(out=M[:, i, :], in0=Id[:], scalar1=PR[:, i, g:g + 1],
                                                               scalar2=None, op0=ALU.mult), r=[Id.name, PR.name], w=[M.name])
                    s.I("dve", lambda e, i=i: e.scalar_tensor_tensor(out=M[:, i, :], in0=Jm[:], scalar=PI[:, i, g:g + 1],
                                                                     in1=M[:, i, :], op0=ALU.mult, op1=ALU.add),
                        r=[Jm.name, PI.name, M.name], w=[M.name])
                yield
                for c0 in range(0, T, 512):
                    p = nps()
                    k.MM(p[:], BmP[:, j, :], us[:, c0:c0 + 512], True, True, [BmP, us], [p])
                    evac_copy(A[:, c0:c0 + 512], p[:], [p.name], [A.name])
                    yield
                for l in range(1, LM + 1):
                    st = 1 << l; h = st >> 1; n = T // st
                    for c0 in range(0, n, 512):
                        m_ = min(512, n - c0)
                        src = A[:, h - 1 + c0 * st: h - 1 + (c0 + m_ - 1) * st + 1: st]
                        dst = A[:, st - 1 + c0 * st: st - 1 + (c0 + m_ - 1) * st + 1: st]
                        p = nps()
                        k.MM(p[:, 0:m_], M[:, l - 1, :], src, True, True, [M, A], [p])
                        k.TT("dve", dst, p[:, 0:m_], dst, ALU.add, [p, A], [A])
                        yield
                for l in range(LM - 1, 0, -1):
                    st = 1 << l; h = st >> 1; n = T // st
                    lo_i = max(0, n // 2 - 1)
                    for c0 in range(lo_i, n - 1, 512):
                        m_ = min(512, n - 1 - c0)
                        src = A[:, st - 1 + c0 * st: st - 1 + (c0 + m_ - 1) * st + 1: st]
                        dst = A[:, st + h - 1 + c0 * st: st + h - 1 + (c0 + m_ - 1) * st + 1: st]
                        p = nps()
                        k.MM(p[:, 0:m_], M[:, l - 1, :], src, True, True, [M, A], [p])
                        k.TT("dve", dst, p[:, 0:m_], dst, ALU.add, [p, A], [A])
                        yield
                for c0 in range(OW, T, 512):
                    p = nps()
                    k.MM(p[:], CmP[:, j, :], A[:, c0:c0 + 512], True, True, [CmP, A], [p])
                    if j == 0:
                        evac_copy(yacc[:, c0 - OW:c0 - OW + 512], p[:], [p.name], [yacc.name])
                    else:
                        k.TT("dve", yacc[:, c0 - OW:c0 - OW + 512], p[:], yacc[:, c0 - OW:c0 - OW + 512], ALU.add, [p, yacc], [yacc])
                    yield

            for q in range(8):
                s.DMA("pool", us[:], usT[q * 128:(q + 1) * 128, :], r=["usT"], w=[us.name])
                for j in range(8):
                    s.I("dve", lambda e, j=j, q=q: e.tensor_scalar(out=BmP[:, j, :], in0=Bb[:, q, :], scalar1=mk8[:, j:j + 1],
                                                              scalar2=None, op0=ALU.mult), r=[Bb.name, mk8.name], w=[BmP.name])
                s.I("pool", lambda e: e.memset(CmP[:], 0.0), w=[CmP.name])
                for j in range(8):
                    s.I("pool", lambda e, j=j, q=q: e.tensor_copy(out=CmP[:, j, j * 16:(j + 1) * 16], in_=Ct[:, q * 8 + j, :]),
                        r=[Ct.name], w=[CmP.name])
                j0 = 0
                while j0 < 8:
                    js = list(range(j0, min(8, j0 + NG)))
                    gens = [group_steps(q, j, si) for si, j in enumerate(js)]
                    alive = True
                    while alive:
                        alive = False
                        for gen in gens:
                            try:
                                next(gen); alive = True
                            except StopIteration:
                                pass
                    j0 += len(js)
                for ci, c0 in enumerate(range(k.OWN, T, 512)):
                    y_ = gl[0]; a_ = gl[1]; b_ = gl[2]; o_ = yb[ci % 2]
                    s.I("dve", lambda e, c0=c0, q=q: e.scalar_tensor_tensor(out=y_[:], in0=us[:, c0:c0 + 512], scalar=dsk[:, q:q + 1],
                                                                        in1=yacc[:, c0 - k.OWN:c0 - k.OWN + 512], op0=ALU.mult, op1=ALU.add),
                        r=[us.name, dsk.name, yacc.name], w=[y_.name])
                    s.I("act", lambda e: e.activation(out=a_[:], in_=y_[:], func=AF.Square), r=[y_.name], w=[a_.name])
                    s.I("dve", lambda e: e.tensor_scalar(out=a_[:], in0=a_[:], scalar1=0.044715, scalar2=1.0, op0=ALU.mult, op1=ALU.add),
                        r=[a_.name], w=[a_.name])
                    s.I("dve", lambda e: tt(e, out=b_[:], in0=a_[:], in1=y_[:], op=ALU.mult), r=[a_.name, y_.name], w=[b_.name])
                    s.I("act", lambda e: e.activation(out=b_[:], in_=b_[:], func=AF.Sigmoid, scale=1.5957691216057308), r=[b_.name], w=[b_.name])
                    s.I("dve", lambda e, o_=o_: tt(e, out=o_[:], in0=b_[:], in1=y_[:], op=ALU.mult), r=[b_.name, y_.name], w=[o_.name])
                    d = s.DMA("sp", ysT[q * 128:(q + 1) * 128, c0:c0 + 512], o_[:], r=[o_.name], w=["ysT"])
                    k.final.append(d)
            s.barrier()

    def build(k):
        nc = k.nc; s = k.s; es = k.es; T = k.T
        x = k.din("x", [T, D])
        out = k.nc.dram_tensor("out", [T if k.stages < 3 else T // 2, D], F32, kind="ExternalOutput").ap()
        g = {n: k.din(n, [1, D]) for n in ("ffn1_pre_g", "ffn1_post_g", "mix_pre_g", "mix_post_g",
                                           "ffn2_pre_g", "ffn2_post_g")}
        w1g = k.conv("ffn1_w_gate", [D, DFF], 4)
        w1u = k.conv("ffn1_w_up", [D, DFF], 4)
        w1d = k.conv("ffn1_w_down", [DFF, D], 4)
        psb = [es.enter_context(nc.psum_tensor("psb%d" % i, [128, 512], F32)) for i in range(8)]
        k.epsb = k.sb(es, "epsb", [128, 1], F32)
        s.I("dve", lambda e: e.memset(k.epsb[:], EPS), w=[k.epsb.name])
        identf = k.sb(es, "identf", [128, 128], F32)
        k.ident = k.sb(es, "ident", [128, 128], BF16)
        s.I("pool", lambda e: e.memset(identf[:], 1.0), w=[identf.name])
        s.I("pool", lambda e: e.affine_select(out=identf[:], in_=identf[:], pattern=[[-1, 128]],
                                              compare_op=ALU.is_equal, fill=0.0, base=0, channel_multiplier=1),
            r=[identf.name], w=[identf.name])
        s.I("dve", lambda e: e.tensor_copy(out=k.ident[:], in_=identf[:]), r=[identf.name], w=["ident"])
        if k.stages == 1:
            k.ffn("f1", x, out, g["ffn1_pre_g"][0, :], g["ffn1_post_g"][0, :], w1g, w1u, w1d, psb)
        if k.stages == 2:
            usT = k.din("usT", [1024, T])
            ysT = k.nc.dram_tensor("ysT", [1024, T], BF16, kind="ExternalOutput").ap()
            k.s5(usT, ysT, psb)
        if k.stages >= 3:
            w = {}
            win = k.conv("w_in", [D, INC], 4)
            w["glu1"] = k.conv("ssm_glu_w1", [1024, D], 2); w["glu2"] = k.conv("ssm_glu_w2", [1024, D], 2)
            w["wo"] = k.conv("nsa_w_o", [1024, D], 2); w["wout"] = k.conv("w_out", [D, D], 2)
            w["ck1"] = k.conv("cmp_k_w1", [2048, 256]); w["ck2"] = k.conv("cmp_k_w2", [256, 64])
            w["cv1"] = k.conv("cmp_v_w1", [2048, 256]); w["cv2"] = k.conv("cmp_v_w2", [256, 64])
            w2g = k.conv("ffn2_w_gate", [D, DFF], 4); w2u = k.conv("ffn2_w_up", [D, DFF], 4); w2d = k.conv("ffn2_w_down", [DFF, D], 4)
            sc = {}
            for nm_, shp, dt_ in (("h1", [T, D], F32), ("h2", [T, D], F32), ("usT", [1024, T], F32), ("ysT", [1024, T], BF16),
                                  ("qT", [16, 64, T], BF16), ("kcT", [4, 64, T], BF16), ("vcT", [4, 64, T], BF16),
                                  ("ksT", [4, 64, T], BF16), ("kwT", [4, 64, T], BF16), ("Vs", [T, 256], BF16), ("Vw", [T, 256], BF16),
                                  ("gates", [T, 48], F32), ("gaT", [D, T], BF16), ("gbT", [D, T], BF16), ("onsa", [T, 1024], BF16)):
                if k.debug:
                    sc[nm_] = k.nc.dram_tensor("dbg_" + nm_, list(shp), dt_, kind="ExternalOutput").ap()
                else:
                    sc[nm_] = k.dscr("sc_" + nm_, shp, dt_)
            k.ffn("f1", x, sc["h1"], g["ffn1_pre_g"][0, :], g["ffn1_post_g"][0, :], w1g, w1u, w1d, psb, "x", "h1",
                  wkeys=("ffn1_w_gate_b", "ffn1_w_up_b", "ffn1_w_down_b"))
            k.proj(sc["h1"], g["mix_pre_g"][0, :], win, sc, psb)
            k.s5(sc["usT"], sc["ysT"], psb)
            with ExitStack() as em:
                KcT, Vc = k.compress(em, sc, w, psb)
                SKS, SKW, WS, WW = k.bias_tables(psb)
                k.nsa(sc, KcT, Vc, SKS, SKW, WS, WW, psb)
            k.merge(sc, w, sc["h1"], sc["h2"], g["mix_post_g"][0, :], psb)
            k.ffn("f2", sc["h2"], out, g["ffn2_pre_g"][0, :], g["ffn2_post_g"][0, :], w2g, w2u, w2d, psb, "h2", "out",
                  tiles=range(k.NT // 2, k.NT), dst_off=k.OWN)
        s.finish(k.final)
        es.close()
        return nc


def ssm_layouts(a_re, a_im, log_dt, b_re, b_im, c_re, c_im, d):
    a_re = np.asarray(a_re, np.float32); a_im = np.asarray(a_im, np.float32); log_dt = np.asarray(log_dt, np.float32)
    b_re = np.asarray(b_re, np.float32); b_im = np.asarray(b_im, np.float32)
    c_re = np.asarray(c_re, np.float32); c_im = np.asarray(c_im, np.float32); d = np.asarray(d, np.float32)
    m = {}
    m["ssm_aT_re2"] = np.concatenate([a_re.T, a_re.T], 0)
    m["ssm_aT_im2"] = np.concatenate([a_im.T, a_im.T], 0)
    m["ssm_ldt2"] = np.broadcast_to(log_dt[None, :], (128, 64))
    def lay_a(a):
        t = a.reshape(8, 8, 64)
        t = np.transpose(t, (1, 0, 2))
        return np.broadcast_to(t[:, None], (8, 16, 8, 64)).reshape(128, 512)
    m["ssm_a_re_b"] = lay_a(a_re); m["ssm_a_im_b"] = lay_a(a_im)
    m["ssm_ldt_b"] = lay_a(np.broadcast_to(log_dt[:, None], (64, 64)))
    def lay_b(b):
        t = b.reshape(8, 8, 64, 16)
        t = np.transpose(t, (1, 3, 0, 2))
        return t.reshape(128, 512)
    m["ssm_b_re_b"] = lay_b(b_re); m["ssm_b_im_b"] = lay_b(b_im)
    m["ssm_cT2"] = np.concatenate([np.transpose(c_re, (2, 0, 1)), np.transpose(c_im, (2, 0, 1))], 0)
    m["ssm_dT"] = d.reshape(8, 128).T
    m["c_sgn"] = np.concatenate([np.ones((64, 1)), -np.ones((64, 1))], 0)
    m["c_mask8"] = (np.arange(128)[:, None] // 16 == np.arange(8)[None, :]).astype(np.float32)
    m["c_I"] = np.eye(128)
    m["c_J"] = np.roll(np.eye(128), 64, axis=1)
    return {k_: np.ascontiguousarray(v, dtype=np.float32) for k_, v in m.items()}


def _bucket(d):
    d = np.maximum(d, 0)
    d_f = np.maximum(d, 1).astype(np.float32)
    large = 16 + (np.log(d_f / np.float32(16)) / np.float32(math.log(1024 / 16)) * np.float32(16)).astype(np.int32)
    large = np.minimum(large, 31)
    return np.where(d < 16, d, large)


def nsa_consts():
    m = {}
    WS, WW = 1152, 768
    x = np.arange(WS); d = 1023 - x
    oh = np.zeros((33, WS), np.float32)
    bk = _bucket(d)
    for b in range(32):
        oh[b] = ((d >= 0) & (bk == b))
    oh[32] = np.where(d < 0, -BIG, 0.0)
    m["c_ohs"] = oh
    x = np.arange(WW); d = 639 - x
    oh = np.zeros((33, WW), np.float32)
    bk = _bucket(d)
    ok = (d >= 0) & (d < 512)
    for b in range(32):
        oh[b] = (ok & (bk == b))
    oh[32] = np.where(ok, 0.0, -BIG)
    m["c_ohw"] = oh
    vm = np.zeros((128, 256), np.float32); ac = np.zeros((128, 256), np.float32)
    c = np.arange(256)
    for qi in range(128):
        hi = qi >= 64
        forced = (c == 127) | ((c == 128) if hi else (c == 126))
        invalid = (c > 128) | ((c == 128) & (not hi))
        vm[qi] = (~forced & ~invalid)
        ac[qi] = np.where(forced, 1e4, np.where(invalid, -1e4, 0.0))
    m["c_vmrel"] = vm; m["c_acrel"] = ac
    return m


def percore_consts(T, half):
    m = {}
    nbp = T // 128
    cm = np.ones((128, 128), np.float32); ca = np.zeros((128, 128), np.float32)
    if half == 0:
        cm[:, 0:nbp] = 0.0; ca[:, 0:nbp] = -2e4
        cm[:, nbp] = 0.0; ca[:, nbp] = 1e4
    else:
        cm[:, 0] = 0.0; ca[:, 0] = 1e4
    m["pc_cm"] = cm; m["pc_ca"] = ca
    cmpm = np.zeros((128, 512), np.float32)
    if half == 0:
        cmpm[:, 0:T // 32] = -BIG
    m["pc_cmpmask"] = cmpm
    m["pc_wmask"] = np.full((128, 1), -BIG if half == 0 else 0.0, np.float32)
    return m


def host_inputs(inputs, names):
    m = {}
    sq = lambda n: np.asarray(inputs[n], np.float32)[0]
    lay = ssm_layouts(sq("ssm_a_re"), sq("ssm_a_im"), sq("ssm_log_dt"), sq("ssm_b_re"), sq("ssm_b_im"),
                      sq("ssm_c_re"), sq("ssm_c_im"), sq("ssm_d"))
    lay.update(nsa_consts())
    lay["cmp_posT"] = np.ascontiguousarray(sq("cmp_pos").T)
    for n in names:
        if n == "x":
            continue
        if n in lay:
            m[n] = np.ascontiguousarray(lay[n], dtype=np.float32)
        elif n == "rel_bias":
            m[n] = np.ascontiguousarray(np.asarray(inputs[n], np.float32))
        elif n.endswith("_g"):
            m[n] = np.ascontiguousarray(np.asarray(inputs[n], np.float32).reshape(1, D))
        else:
            m[n] = np.ascontiguousarray(sq(n))
    return m

_CACHE = {}


def _get(T, stages, debug=False):
    key = (T, stages, debug)
    if key not in _CACHE:
        kb = K(T, stages, debug)
        kb.build()
        _CACHE[key] = kb
    return _CACHE[key]


def run(inputs, T, stages, ncores, debug=False):
    kb = _get(T, stages, debug)
    xs = np.asarray(inputs["x"], np.float32)
    in_maps = []
    if stages >= 3:
        shared = host_inputs(inputs, [n for n in kb.inp if n != "x" and not n.startswith("pc_")])
        pcs = [percore_consts(T, 0), percore_consts(T, 1)]
        for c in range(ncores):
            b, half = c // 2, c % 2
            m = dict(shared)
            m.update(pcs[half])
            xb = xs[b % xs.shape[0]].reshape(T, D)
            if half == 0:
                xl = np.concatenate([np.zeros((T // 2, D), np.float32), xb[:T // 2]], 0)
            else:
                xl = xb
            m["x"] = np.ascontiguousarray(xl)
            in_maps.append(m)
    else:
        for c in range(ncores):
            m = {}
            for name in kb.inp:
                if name == "x":
                    continue
                a = np.asarray(inputs[name], np.float32)
                m[name] = np.ascontiguousarray(a.reshape(kb.inp[name].shape))
            m["x"] = np.ascontiguousarray(xs[c % xs.shape[0]].reshape(T, D))
            in_maps.append(m)
    res = run_bass_kernel_spmd(kb.nc, in_maps, core_ids=list(range(ncores)))
    if debug:
        return res.results
    return [r["out"] for r in res.results]


def kernel(**inputs):
    outs = run(inputs, SEQ, 3, 8)
    full = np.empty((NB, SEQ, D), np.float32)
    for c in range(8):
        b, half = c // 2, c % 2
        full[b, half * (SEQ // 2):(half + 1) * (SEQ // 2)] = outs[c]
    return full
```

```python
import math
from contextlib import ExitStack
import numpy as np
import ml_dtypes
import concourse.bass as bass
import concourse.mybir as mybir
from concourse.bass_utils import run_bass_kernel_spmd

F32 = mybir.dt.float32
BF16 = mybir.dt.bfloat16
ALU = mybir.AluOpType
AF = mybir.ActivationFunctionType
AX = mybir.AxisListType

D = 2048
DFF = 5632
EPS = 1e-6
NB = 4
SEQ = 8192
INC = 7728
BIG = 30000.0


class Ins:
    __slots__ = ("eng", "fn", "deps", "dma", "idx", "sig", "val", "semi", "prewait")

    def __init__(s, eng, fn, dma):
        s.eng = eng; s.fn = fn; s.dma = dma; s.deps = []; s.idx = 0; s.sig = False; s.val = 0
        s.semi = 0; s.prewait = None


class Sch:
    ENGS = ["pe", "dve", "act", "pool", "sp"]
    NS = 10

    def __init__(s, nc, es):
        s.nc = nc
        s.es = es
        s.eobj = {"pe": nc.tensor, "dve": nc.vector, "act": nc.scalar, "pool": nc.gpsimd, "sp": nc.sync}
        s.prog = {e: [] for e in s.ENGS}
        s.csem = {e: es.enter_context(nc.semaphore("cs_" + e)) for e in s.ENGS}
        s.dsem = {e: [es.enter_context(nc.semaphore("ds_%s%d" % (e, i))) for i in range(s.NS)]
                  for e in ("sp", "pool", "act")}
        s.ndma = {e: 0 for e in ("sp", "pool", "act")}
        s.dmas = {e: [] for e in ("sp", "pool", "act")}
        s.lastw = {}
        s.readers = {}
        s.known = {e: {f: -1 for f in s.ENGS} for e in s.ENGS}
        s.seen = {e: set() for e in s.ENGS}
        s.bar_from = {e: 0 for e in ("sp", "pool", "act")}

    def _dep(s, ins, d):
        if d is None or d is ins:
            return
        if d.dma:
            if id(d) in s.seen[ins.eng]:
                return
            s.seen[ins.eng].add(id(d))
            ins.deps.append(d)
        else:
            if d.eng == "pe" and ins.eng == "pe" and not ins.dma:
                return
            if s.known[ins.eng][d.eng] >= d.idx:
                return
            s.known[ins.eng][d.eng] = d.idx
            ins.deps.append(d)

    def _emit(s, eng, fn, r, w, dma):
        ins = Ins(eng, fn, dma)
        ins.idx = len(s.prog[eng])
        for k in list(r) + list(w):
            s._dep(ins, s.lastw.get(k))
        for k in w:
            best = {}
            for d in s.readers.get(k, ()):
                if d.dma:
                    s._dep(ins, d)
                elif d.eng not in best or best[d.eng].idx < d.idx:
                    best[d.eng] = d
            for d in best.values():
                s._dep(ins, d)
        if dma:
            n = s.ndma[eng]; s.ndma[eng] += 1
            ins.semi = n % s.NS; ins.val = (n // s.NS + 1) * 16
            if n >= s.NS:
                ins.prewait = s.dmas[eng][n - s.NS]
            s.dmas[eng].append(ins)
            ins.sig = True
        s.prog[eng].append(ins)
        for k in w:
            s.lastw[k] = ins; s.readers[k] = []
        for k in r:
            s.readers.setdefault(k, []).append(ins)
        return ins

    def I(s, eng, fn, r=(), w=()):
        return s._emit(eng, fn, r, w, False)

    def barrier(s):
        lasts = []
        for e in s.ENGS:
            for ins in reversed(s.prog[e]):
                if not ins.dma:
                    lasts.append(ins); break
        dm = [d for e in s.dmas for d in s.dmas[e][s.bar_from[e]:]]
        for e in s.dmas:
            s.bar_from[e] = len(s.dmas[e])
        for e in s.ENGS:
            ins = Ins(e, lambda eng: eng.nop(), False)
            ins.idx = len(s.prog[e])
            for d in lasts + dm:
                s._dep(ins, d)
            s.prog[e].append(ins)

    def DMA(s, eng, out, in_, r=(), w=(), **kw):
        return s._emit(eng, lambda e: e.dma_start(out=out, in_=in_, **kw), r, w, True)

    def finish(s, final):
        for e in s.ENGS:
            for ins in s.prog[e]:
                for d in ins.deps:
                    d.sig = True
        EPOCH = 30000
        s.csems = {}
        for e in s.ENGS:
            c = 0; ep = 0
            s.csems[e] = [s.csem[e]]
            for ins in s.prog[e]:
                if not ins.dma and ins.sig:
                    if c == EPOCH:
                        c = 0; ep += 1
                        s.csems[e].append(s.es.enter_context(s.nc.semaphore("cs_%s_%d" % (e, ep))))
                    c += 1; ins.val = c; ins.semi = ep
        s.maxval = {e: max([i.val for i in s.prog[e] if not i.dma] + [0]) for e in s.ENGS}

        def semof(d):
            return s.dsem[d.eng][d.semi] if d.dma else s.csems[d.eng][d.semi]

        with s.nc.Block() as block:
            def run(e):
                def body(eng):
                    for ins in s.prog[e]:
                        if ins.prewait is not None:
                            eng.wait_ge(semof(ins.prewait), ins.prewait.val)
                        for d in ins.deps:
                            eng.wait_ge(semof(d), d.val)
                        o = ins.fn(eng)
                        if ins.sig:
                            o.then_inc(semof(ins), 16 if ins.dma else 1)
                    if e == "sp":
                        for d in final:
                            eng.wait_ge(semof(d), d.val)
                return body
            block.tensor(run("pe"))
            block.vector(run("dve"))
            block.scalar(run("act"))
            block.gpsimd(run("pool"))
            block.sync(run("sp"))


def dram_ap(t, off, pat):
    return bass.AP(t.tensor if hasattr(t, "tensor") else t, off, pat)


class K:
    def __init__(k, T, stages=9, debug=False):
        k.T = T
        k.debug = debug
        k.NT = T // 512
        k.OWN = T // 2
        k.stages = stages
        k.nc = nc = bass.Bass("TRN2", target_bir_lowering=False)
        k.es = ExitStack()
        k.s = Sch(nc, k.es)
        k.inp = {}
        k.final = []

    def din(k, name, shape, dt=F32):
        a = k.nc.dram_tensor(name, list(shape), dt, kind="ExternalInput").ap()
        k.inp[name] = a
        return a

    def dscr(k, name, shape, dt):
        return k.nc.dram_tensor(name, list(shape), dt, kind="Internal").ap()

    def sb(k, es, name, shape, dt):
        return es.enter_context(k.nc.sbuf_tensor(name, list(shape), dt))

    def conv(k, name, shape, nsplit=1):
        src = k.din(name, shape)
        dst = k.dscr(name + "_b", shape, BF16)
        rows = shape[0]
        step = rows // nsplit
        for i in range(nsplit):
            k.s.DMA("pool", dst[i * step:(i + 1) * step, :], src[i * step:(i + 1) * step, :], w=[name + "_b"])
        return dst

    def rstd(k, src_ap, junk, ss, rs, key_r, scale_out=1.0):
        s = k.s
        s.I("act", lambda e: e.activation(out=junk[:], in_=src_ap, func=AF.Square, accum_out=ss[:]),
            r=key_r, w=[junk.name, ss.name])
        s.I("act", lambda e: e.activation(out=rs[:], in_=ss[:], func=AF.Sqrt, bias=k.epsb[:],
                                          scale=1.0 / (D * scale_out * scale_out)),
            r=[ss.name], w=[rs.name])
        s.I("dve", lambda e: e.reciprocal(out=rs[:], in_=rs[:]), r=[rs.name], w=[rs.name])

    def ffn(k, tag, src, dst, pre_g, post_g, wg, wu, wd, psb, skey="x", dkey="out", tiles=None, dst_off=0, wkeys=None):
        s = k.s; nc = k.nc
        with ExitStack() as es:
            gpre = k.sb(es, tag + "gpre", [128, D], F32)
            gpost = k.sb(es, tag + "gpost", [128, D], F32)
            s.DMA("sp", gpre[:], pre_g.partition_broadcast(128), w=[gpre.name])
            s.DMA("sp", gpost[:], post_g.partition_broadcast(128), w=[gpost.name])
            kg_, ku_, kd_ = wkeys if wkeys is not None else (tag + "wg", tag + "wu", tag + "wd")
            xT = k.sb(es, tag + "xT", [128, 16, 512], BF16)
            hT = k.sb(es, tag + "hT", [128, 44, 512], BF16)
            wb = [k.sb(es, tag + "wb%d" % i, [128, 16, 256], BF16) for i in range(6)]
            f1 = k.sb(es, tag + "f1", [128, 4, D], F32)
            xs = [k.sb(es, tag + "xs%d" % i, [128, D], F32) for i in range(2)]
            xn = [k.sb(es, tag + "xn%d" % i, [128, D], BF16) for i in range(2)]
            junk = k.sb(es, tag + "junk", [128, D], F32)
            sg = [k.sb(es, tag + "sg%d" % i, [128, 512], F32) for i in range(2)]
            ss = [k.sb(es, tag + "ss%d" % i, [128, 1], F32) for i in range(2)]
            rs = [k.sb(es, tag + "rs%d" % i, [128, 1], F32) for i in range(2)]
            wbi = [0]

            def nextw():
                b = wb[wbi[0] % 6]; wbi[0] += 1
                return b

            xi = 0
            for tt in (tiles if tiles is not None else range(k.NT)):
                t0 = tt * 512
                for sub in range(4):
                    x_ = xs[xi % 2]; xn_ = xn[xi % 2]; ss_ = ss[xi % 2]; rs_ = rs[xi % 2]; xi += 1
                    rows = src[t0 + sub * 128: t0 + sub * 128 + 128, :]
                    s.DMA("sp", x_[:], rows, r=[skey], w=[x_.name])
                    k.rstd(x_[:], junk, ss_, rs_, [x_.name])
                    s.I("dve", lambda e, x_=x_, xn_=xn_, rs_=rs_: e.scalar_tensor_tensor(
                        out=xn_[:], in0=x_[:], scalar=rs_[:], in1=gpre[:], op0=ALU.mult, op1=ALU.mult),
                        r=[x_.name, rs_.name, gpre.name], w=[xn_.name])
                    for half in range(2):
                        pst = psb[6 + half]
                        pv = pst[:].bitcast(BF16)
                        for j in range(8):
                            kk = half * 8 + j
                            s.I("pe", lambda e, pv=pv, xn_=xn_, kk=kk, j=j: e.transpose(
                                out=pv[:, j * 128:(j + 1) * 128], in_=xn_[:, kk * 128:(kk + 1) * 128],
                                identity=k.ident[:]),
                                r=[xn_.name, "ident"], w=[pst.name])
                        eng = "act" if half == 0 else "dve"
                        if eng == "act":
                            s.I("act", lambda e, pv=pv, half=half, sub=sub: e.copy(
                                out=xT[:, half * 8:half * 8 + 8, sub * 128:(sub + 1) * 128],
                                in_=pv.rearrange("p (a b) -> p a b", a=8)),
                                r=[pst.name], w=[xT.name])
                        else:
                            s.I("dve", lambda e, pv=pv, half=half, sub=sub: e.tensor_copy(
                                out=xT[:, half * 8:half * 8 + 8, sub * 128:(sub + 1) * 128],
                                in_=pv.rearrange("p (a b) -> p a b", a=8)),
                                r=[pst.name], w=[xT.name])
                pi = 0
                for cc in range(22):
                    wgc = nextw()
                    s.DMA("sp", wgc[:], wg[:, cc * 256:(cc + 1) * 256].rearrange("(a p) c -> p a c", p=128),
                          r=[kg_], w=[wgc.name])
                    wuc = nextw()
                    s.DMA("sp", wuc[:], wu[:, cc * 256:(cc + 1) * 256].rearrange("(a p) c -> p a c", p=128),
                          r=[ku_], w=[wuc.name])
                    for fs in range(2):
                        f = cc * 2 + fs
                        pg = psb[(pi * 2) % 6]; pu = psb[(pi * 2 + 1) % 6]; sg_ = sg[pi % 2]; pi += 1
                        for kk in range(16):
                            s.I("pe", lambda e, pg=pg, wgc=wgc, kk=kk, fs=fs: e.matmul(
                                pg[:], lhsT=wgc[:, kk, fs * 128:(fs + 1) * 128], rhs=xT[:, kk, :],
                                start=(kk == 0), stop=(kk == 15)), r=[wgc.name, xT.name], w=[pg.name])
                        for kk in range(16):
                            s.I("pe", lambda e, pu=pu, wuc=wuc, kk=kk, fs=fs: e.matmul(
                                pu[:], lhsT=wuc[:, kk, fs * 128:(fs + 1) * 128], rhs=xT[:, kk, :],
                                start=(kk == 0), stop=(kk == 15)), r=[wuc.name, xT.name], w=[pu.name])
                        s.I("act", lambda e, pg=pg, sg_=sg_: e.activation(out=sg_[:], in_=pg[:], func=AF.Silu),
                            r=[pg.name], w=[sg_.name])
                        s.I("dve", lambda e, pu=pu, sg_=sg_, f=f: e.tensor_tensor(
                            out=hT[:, f, :], in0=sg_[:], in1=pu[:], op=ALU.mult),
                            r=[pu.name, sg_.name], w=[hT.name])
                for c4 in range(4):
                    base = 0 if c4 % 2 == 0 else 4
                    f0 = 0
                    for nf in (8, 8, 8, 8, 8, 4):
                        wdt_ = nextw()
                        wdc = wdt_[:].rearrange("p a c -> p (a c)").rearrange("p (a c) -> p a c", c=512)
                        s.DMA("sp", wdc[:, 0:nf, :],
                              wd[f0 * 128:(f0 + nf) * 128, c4 * 512:(c4 + 1) * 512].rearrange("(a p) c -> p a c", p=128),
                              r=[kd_], w=[wdt_.name])
                        for sub in range(4):
                            po = psb[base + sub]
                            for fi in range(nf):
                                f = f0 + fi
                                s.I("pe", lambda e, po=po, wdc=wdc, fi=fi, f=f, sub=sub: e.matmul(
                                    po[:], lhsT=hT[:, f, sub * 128:(sub + 1) * 128], rhs=wdc[:, fi, :],
                                    start=(f == 0), stop=(f == 43)), r=[wdt_.name, hT.name], w=[po.name])
                        f0 += nf
                    for sub in range(4):
                        po = psb[base + sub]
                        if sub % 2 == 0:
                            s.I("act", lambda e, po=po, sub=sub, c4=c4: e.copy(
                                out=f1[:, sub, c4 * 512:(c4 + 1) * 512], in_=po[:]),
                                r=[po.name], w=[f1.name + str(sub)])
                        else:
                            s.I("dve", lambda e, po=po, sub=sub, c4=c4: e.tensor_copy(
                                out=f1[:, sub, c4 * 512:(c4 + 1) * 512], in_=po[:]),
                                r=[po.name], w=[f1.name + str(sub)])
                for sub in range(4):
                    x_ = xs[xi % 2]; ss_ = ss[xi % 2]; rs_ = rs[xi % 2]; xi += 1
                    rows = src[t0 + sub * 128: t0 + sub * 128 + 128, :]
                    s.DMA("sp", x_[:], rows, r=[skey], w=[x_.name])
                    k.rstd(f1[:, sub, :], junk, ss_, rs_, [f1.name + str(sub)], scale_out=0.5)
                    s.I("dve", lambda e, sub=sub, rs_=rs_: e.scalar_tensor_tensor(
                        out=f1[:, sub, :], in0=f1[:, sub, :], scalar=rs_[:], in1=gpost[:],
                        op0=ALU.mult, op1=ALU.mult),
                        r=[f1.name + str(sub), rs_.name, gpost.name], w=[f1.name + str(sub)])
                    s.I("pool", lambda e, sub=sub, x_=x_: e.tensor_tensor(
                        out=x_[:], in0=f1[:, sub, :], in1=x_[:], op=ALU.add),
                        r=[f1.name + str(sub), x_.name], w=[x_.name])
                    d = s.DMA("sp", dst[t0 - dst_off + sub * 128: t0 - dst_off + sub * 128 + 128, :], x_[:], r=[x_.name], w=[dkey])
                    k.final.append(d)
            s.barrier()


    @staticmethod
    def _n(xs):
        return [x if isinstance(x, str) else x.name for x in xs]

    def TT(k, eng, out, in0, in1, op, r, w):
        k.s.I(eng, lambda e: e.tensor_tensor(out=out, in0=in0, in1=in1, op=op), k._n(r), k._n(w))

    def TS(k, eng, out, in0, s1, s2, op0, op1, r, w):
        if op1 is None:
            k.s.I(eng, lambda e: e.tensor_scalar(out=out, in0=in0, scalar1=s1, scalar2=None, op0=op0), k._n(r), k._n(w))
        else:
            k.s.I(eng, lambda e: e.tensor_scalar(out=out, in0=in0, scalar1=s1, scalar2=s2, op0=op0, op1=op1), k._n(r), k._n(w))

    def STT(k, out, in0, scalar, in1, op0, op1, r, w):
        k.s.I("dve", lambda e: e.scalar_tensor_tensor(out=out, in0=in0, scalar=scalar, in1=in1, op0=op0, op1=op1),
              k._n(r), k._n(w))

    def ACT(k, out, in_, func, r, w, **kw):
        k.s.I("act", lambda e: e.activation(out=out, in_=in_, func=func, **kw), k._n(r), k._n(w))

    def MM(k, out, lhsT, rhs, start, stop, r, w):
        k.s.I("pe", lambda e: e.matmul(out, lhsT=lhsT, rhs=rhs, start=start, stop=stop), k._n(r), k._n(w))

    def TR(k, out, in_, r, w):
        k.s.I("pe", lambda e: e.transpose(out=out, in_=in_, identity=k.ident[:]), k._n(r) + ["ident"], k._n(w))

    def CP(k, eng, out, in_, r, w):
        if eng == "act":
            k.s.I("act", lambda e: e.copy(out=out, in_=in_), k._n(r), k._n(w))
        else:
            k.s.I(eng, lambda e: e.tensor_copy(out=out, in_=in_), k._n(r), k._n(w))

    def LD(k, out, in_, r, w, eng="sp", **kw):
        return k.s.DMA(eng, out, in_, k._n(r), k._n(w), **kw)

    def gelu_tanh(k, y_, a_, b_, out_ap, wname):
        k.ACT(a_[:], y_[:], AF.Square, [y_], [a_])
        k.TS("dve", a_[:], a_[:], 0.044715, 1.0, ALU.mult, ALU.add, [a_], [a_])
        k.TT("dve", b_[:], a_[:], y_[:], ALU.mult, [a_, y_], [b_])
        k.ACT(b_[:], b_[:], AF.Sigmoid, [b_], [b_], scale=1.5957691216057308)
        k.TT("dve", out_ap, b_[:], y_[:], ALU.mult, [b_, y_], [wname])

    def prenormT(k, src, skey, t0, gpre, xT, bufs, psb):
        s = k.s
        xs, xn, junk, ss, rs, ctr = bufs
        for sub in range(4):
            i = ctr[0] % 2; ctr[0] += 1
            x_ = xs[i]; xn_ = xn[i]; ss_ = ss[i]; rs_ = rs[i]
            k.LD(x_[:], src[t0 + sub * 128: t0 + sub * 128 + 128, :], [skey], [x_])
            k.rstd(x_[:], junk, ss_, rs_, [x_.name])
            k.STT(xn_[:], x_[:], rs_[:], gpre[:], ALU.mult, ALU.mult, [x_, rs_, gpre], [xn_])
            for half in range(2):
                pst = psb[6 + half]
                pv = pst[:].bitcast(BF16)
                for j in range(8):
                    kk = half * 8 + j
                    k.TR(pv[:, j * 128:(j + 1) * 128], xn_[:, kk * 128:(kk + 1) * 128], [xn_], [pst])
                k.CP("act" if half == 0 else "dve", xT[:, half * 8:half * 8 + 8, sub * 128:(sub + 1) * 128],
                     pv.rearrange("p (a b) -> p a b", a=8), [pst], [xT])

    def down_post(k, tag, hT, KT, wd, wkey, src, skey, dst, dkey, gpost, scale_out, t0, f1, wb, wbi, bufs, psb):
        s = k.s
        xs, xn, junk, ss, rs, ctr = bufs
        groups = []
        f0 = 0
        while f0 < KT:
            nf = min(16, KT - f0); groups.append((f0, nf)); f0 += nf
        for c4 in range(4):
            base = 0 if c4 % 2 == 0 else 4
            for (f0, nf) in groups:
                wdc = wb[wbi[0] % 3]; wbi[0] += 1
                k.LD(wdc[:, 0:nf, :], wd[f0 * 128:(f0 + nf) * 128, c4 * 512:(c4 + 1) * 512].rearrange("(a p) c -> p a c", p=128),
                     [wkey], [wdc])
                for sub in range(4):
                    po = psb[base + sub]
                    for fi in range(nf):
                        f = f0 + fi
                        k.MM(po[:], hT[:, f, sub * 128:(sub + 1) * 128], wdc[:, fi, :], f == 0, f == KT - 1, [wdc, hT], [po])
            for sub in range(4):
                po = psb[base + sub]
                k.CP("act" if sub % 2 == 0 else "dve", f1[:, sub, c4 * 512:(c4 + 1) * 512], po[:], [po], [f1.name + str(sub)])
        for sub in range(4):
            i = ctr[0] % 2; ctr[0] += 1
            x_ = xs[i]; ss_ = ss[i]; rs_ = rs[i]
            k.LD(x_[:], src[t0 + sub * 128: t0 + sub * 128 + 128, :], [skey], [x_])
            k.rstd(f1[:, sub, :], junk, ss_, rs_, [f1.name + str(sub)], scale_out=scale_out)
            k.STT(f1[:, sub, :], f1[:, sub, :], rs_[:], gpost[:], ALU.mult, ALU.mult,
                  [f1.name + str(sub), rs_, gpost], [f1.name + str(sub)])
            k.TT("pool", x_[:], f1[:, sub, :], x_[:], ALU.add, [f1.name + str(sub), x_], [x_])
            d = k.LD(dst[t0 + sub * 128: t0 + sub * 128 + 128, :], x_[:], [x_], [dkey])
            k.final.append(d)

    def normbufs(k, es, tag):
        xs = [k.sb(es, tag + "xs%d" % i, [128, D], F32) for i in range(2)]
        xn = [k.sb(es, tag + "xn%d" % i, [128, D], BF16) for i in range(2)]
        junk = k.sb(es, tag + "junk", [128, D], F32)
        ss = [k.sb(es, tag + "ss%d" % i, [128, 1], F32) for i in range(2)]
        rs = [k.sb(es, tag + "rs%d" % i, [128, 1], F32) for i in range(2)]
        return (xs, xn, junk, ss, rs, [0])

    def proj(k, h1, gmix, win, sc, psb):
        s = k.s; T = k.T
        with ExitStack() as es:
            gpre = k.sb(es, "pj_g", [128, D], F32)
            k.LD(gpre[:], gmix.partition_broadcast(128), [], [gpre])
            uT = k.sb(es, "pj_uT", [128, 16, 512], BF16)
            wb = [k.sb(es, "pj_wb%d" % i, [128, 16, 512], BF16) for i in range(3)]
            bufs = k.normbufs(es, "pj_")
            s32 = [k.sb(es, "pj_s32%d" % i, [128, 512], F32) for i in range(2)]
            s16 = [k.sb(es, "pj_s16%d" % i, [128, 512], BF16) for i in range(4)]
            sg = [k.sb(es, "pj_sg%d" % i, [128, 48], F32) for i in range(2)]
            cnt = {"w": 0, "p": 0, "a": 0, "b": 0, "g": 0, "e": 0}

            def nps():
                p = psb[cnt["p"] % 6]; cnt["p"] += 1
                return p

            chunks = [(c * 512, 512) for c in range(7)] + [(3584, 48)] + [(3632 + 512 * i, 512) for i in range(8)]
            for tt in range(k.NT):
                t0 = tt * 512
                k.prenormT(h1, "h1", t0, gpre, uT, bufs, psb)
                for ci, (c0, wd_) in enumerate(chunks):
                    if t0 < k.OWN and ci not in (0, 1, 4, 5, 6):
                        continue
                    wc = wb[cnt["w"] % 3]; cnt["w"] += 1
                    k.LD(wc[:, :, 0:wd_], win[:, c0:c0 + wd_].rearrange("(a p) c -> p a c", p=128), ["win"], [wc])

                    def fm(off, M, kind, dst):
                        p = nps()
                        for kk in range(16):
                            k.MM(p[0:M, :], wc[:, kk, off:off + M], uT[:, kk, :], kk == 0, kk == 15, [wc, uT], [p])
                        if kind == "us":
                            st = s32[cnt["a"] % 2]; cnt["a"] += 1
                            cnt["e"] += 1
                            k.CP("act" if cnt["e"] % 2 else "dve", st[0:M, :], p[0:M, :], [p], [st])
                        else:
                            st = s16[cnt["b"] % 4]; cnt["b"] += 1
                            if kind == "q":
                                k.s.I("act", lambda e, st=st, p=p, M=M: e.mul(out=st[0:M, :], in_=p[0:M, :], mul=0.125), [p.name], [st.name])
                            elif kind == "sig":
                                k.ACT(st[0:M, :], p[0:M, :], AF.Sigmoid, [p], [st])
                            else:
                                k.CP("dve", st[0:M, :], p[0:M, :], [p], [st])
                        k.LD(dst, st[0:M, :], [st], ["pjout"])

                    def tm(off, N, kind, dst_t):
                        for sub in range(4):
                            p = nps()
                            for kk in range(16):
                                k.MM(p[:, 0:N], uT[:, kk, sub * 128:(sub + 1) * 128], wc[:, kk, off:off + N], kk == 0, kk == 15, [wc, uT], [p])
                            if kind == "sig":
                                st = sg[cnt["g"] % 2]; cnt["g"] += 1
                                k.ACT(st[:, 0:N], p[:, 0:N], AF.Sigmoid, [p], [st])
                            else:
                                st = s16[cnt["b"] % 4]; cnt["b"] += 1
                                k.CP("dve", st[:, 0:N], p[:, 0:N], [p], [st])
                            k.LD(dst_t[t0 + sub * 128: t0 + sub * 128 + 128, :], st[:, 0:N], [st], ["pjout"])

                    ts_ = slice(t0, t0 + 512)
                    if ci in (0, 1):
                        for j in range(4):
                            fm(j * 128, 128, "us", sc["usT"][(ci * 4 + j) * 128:(ci * 4 + j + 1) * 128, ts_])
                    elif ci in (2, 3):
                        for j in range(4):
                            h0 = (ci - 2) * 8 + 2 * j
                            fm(j * 128, 128, "q", sc["qT"][h0:h0 + 2, :, ts_].rearrange("a d t -> (a d) t"))
                    elif ci == 4:
                        for g2 in range(2):
                            fm(g2 * 128, 128, "k", sc["kcT"][2 * g2:2 * g2 + 2, :, ts_].rearrange("a d t -> (a d) t"))
                        for g2 in range(2):
                            fm(256 + g2 * 128, 128, "k", sc["vcT"][2 * g2:2 * g2 + 2, :, ts_].rearrange("a d t -> (a d) t"))
                    elif ci == 5:
                        for g2 in range(2):
                            fm(g2 * 128, 128, "k", sc["ksT"][2 * g2:2 * g2 + 2, :, ts_].rearrange("a d t -> (a d) t"))
                        tm(256, 256, "k", sc["Vs"])
                    elif ci == 6:
                        for g2 in range(2):
                            fm(g2 * 128, 128, "k", sc["kwT"][2 * g2:2 * g2 + 2, :, ts_].rearrange("a d t -> (a d) t"))
                        tm(256, 256, "k", sc["Vw"])
                    elif ci == 7:
                        tm(0, 48, "sig", sc["gates"])
                    elif ci < 12:
                        for j in range(4):
                            fm(j * 128, 128, "sig", sc["gaT"][((ci - 8) * 4 + j) * 128:((ci - 8) * 4 + j + 1) * 128, ts_])
                    else:
                        for j in range(4):
                            fm(j * 128, 128, "sig", sc["gbT"][((ci - 12) * 4 + j) * 128:((ci - 12) * 4 + j + 1) * 128, ts_])
            s.barrier()


    def compress(k, es_out, sc, w, psb):
        s = k.s; T = k.T
        NC = T // 16 - 1
        NIT = (NC + 127) // 128
        KcT = k.sb(es_out, "KcT", [128, 4, 512], BF16)
        Vc = k.sb(es_out, "Vc", [128, 4, 4, 64], BF16)
        s.I("pool", lambda e: e.memset(KcT[:], 0.0), w=[KcT.name])
        with ExitStack() as es:
            raw = k.sb(es, "cp_raw", [64, T], BF16)
            w1 = k.sb(es, "cp_w1", [64, 32, 256], BF16)
            w2 = k.sb(es, "cp_w2", [128, 2, 64], BF16)
            posf = k.sb(es, "cp_posf", [64, 32], F32)
            posb = k.sb(es, "cp_posb", [64, 32], BF16)
            bias = k.sb(es, "cp_bias", [128, 2], F32)
            y_ = k.sb(es, "cp_y", [128, 512], F32); a_ = k.sb(es, "cp_a", [128, 512], F32); b_ = k.sb(es, "cp_b", [128, 512], F32)
            hid = k.sb(es, "cp_hid", [128, 2, 512], BF16)
            k.LD(posf[:], k.din("cmp_posT", [64, 32]), [], [posf])
            k.CP("dve", posb[:], posf[:], [posf], [posb])
            for kind in range(2):
                rawsc = sc["kcT"] if kind == 0 else sc["vcT"]
                w1d, w2d = (w["ck1"], w["ck2"]) if kind == 0 else (w["cv1"], w["cv2"])
                k.LD(w1[:], w1d.rearrange("(l d) c -> d l c", d=64), ["cw"], [w1])
                k.LD(w2[:], w2d.rearrange("(a p) d -> p a d", p=128), ["cw"], [w2])
                for ht in range(2):
                    p = psb[ht]
                    for l in range(32):
                        k.MM(p[:, 0:1], w1[:, l, ht * 128:(ht + 1) * 128], posb[:, l:l + 1], l == 0, l == 31, [w1, posb], [p])
                    k.CP("dve", bias[:, ht:ht + 1], p[:, 0:1], [p], [bias])
                for g in range(4):
                    k.LD(raw[:], rawsc[g, :, :], ["pjout"], [raw])
                    for ht in range(2):
                        p = psb[2 + ht]
                        for l in range(32):
                            k.MM(p[:, 0:NC], w1[:, l, ht * 128:(ht + 1) * 128], raw[:, l: l + 16 * (NC - 1) + 1: 16],
                                 l == 0, l == 31, [w1, raw], [p])
                        k.TS("dve", y_[:, 0:NC], p[:, 0:NC], bias[:, ht:ht + 1], None, ALU.add, None, [p, bias], [y_])
                        k.gelu_tanh(y_, a_, b_, hid[:, ht, :], hid.name)
                    if kind == 0:
                        p = psb[4]
                        for ht in range(2):
                            k.MM(p[0:64, 0:NC], w2[:, ht, :], hid[:, ht, 0:NC], ht == 0, ht == 1, [w2, hid], [p])
                        k.CP("act", KcT[0:64, g, 0:NC], p[0:64, 0:NC], [p], [KcT])
                    else:
                        for it in range(NIT):
                            rows = min(128, NC - it * 128)
                            p = psb[4 + it % 2]
                            for ht in range(2):
                                k.MM(p[0:rows, 0:64], hid[:, ht, it * 128: it * 128 + rows], w2[:, ht, :], ht == 0, ht == 1, [w2, hid], [p])
                            k.CP("act", Vc[0:rows, it, g, :], p[0:rows, 0:64], [p], [Vc])
            s.barrier()
        return KcT, Vc

    def bias_tables(k, psb):
        s = k.s
        WS, WW = 1152, 768
        SKS = k.dscr("SKS", [16, 128 * (WS + 1)], F32)
        SKW = k.dscr("SKW", [16, 128 * (WW + 1)], F32)
        rb = k.din("rel_bias", [32, 16])
        with ExitStack() as es:
            ohs = k.sb(es, "bt_ohs", [33, WS], F32); ohw = k.sb(es, "bt_ohw", [33, WW], F32)
            k.LD(ohs[:], k.din("c_ohs", [33, WS]), [], [ohs]); k.LD(ohw[:], k.din("c_ohw", [33, WW]), [], [ohw])
            rbw = k.sb(es, "bt_rbw", [33, 16], F32); rbs = k.sb(es, "bt_rbs", [33, 16], F32); r31 = k.sb(es, "bt_r31", [33, 16], F32)
            s.I("dve", lambda e: e.memset(rbw[:], 1.0), w=[rbw.name])
            s.I("dve", lambda e: e.memset(rbs[:], 1.0), w=[rbs.name])
            s.I("dve", lambda e: e.memset(r31[:], 0.0), w=[r31.name])
            k.LD(rbw[0:32, :], rb, [rbw], [rbw])
            k.LD(r31[0:32, :], rb[31, :].partition_broadcast(32), [r31], [r31])
            k.TT("dve", rbs[0:32, :], rbw[0:32, :], r31[0:32, :], ALU.subtract, [rbw, r31], [rbs])
            ones = k.sb(es, "bt_ones", [33, 128], F32)
            s.I("dve", lambda e: e.memset(ones[:], 1.0), w=[ones.name])
            lh = [k.sb(es, "bt_lh%d" % i, [33, 128], F32) for i in range(2)]
            rep = [k.sb(es, "bt_rep%d" % i, [128, WS], F32) for i in range(2)]
            n = 0
            for (rbt, oh, W, SK) in ((rbs, ohs, WS, SKS), (rbw, ohw, WW, SKW)):
                for h in range(16):
                    l_ = lh[n % 2]; r_ = rep[n % 2]; n += 1
                    k.TS("dve", l_[:], ones[:], rbt[:, h:h + 1], None, ALU.mult, None, [ones, rbt], [l_])
                    for c0 in range(0, W, 512):
                        wd_ = min(512, W - c0)
                        p = psb[(c0 // 512) % 4 + 4 * (n % 2)]
                        k.MM(p[:, 0:wd_], l_[:], oh[:, c0:c0 + wd_], True, True, [l_, oh], [p])
                        k.CP("act" if (c0 // 512) % 2 else "dve", r_[:, c0:c0 + wd_], p[:, 0:wd_], [p], [r_])
                    dst = bass.AP(SK.tensor, h * 128 * (W + 1), [[W + 1, 128], [1, W]])
                    k.LD(dst, r_[:, 0:W], [r_], ["SK"])
            s.barrier()
        return SKS, SKW, WS, WW

    def nsa(k, sc, KcT, Vc, SKS, SKW, WS, WW, psb):
        s = k.s; T = k.T
        NC = T // 16 - 1
        NQ = T // 128
        NKT = T // 128
        LAGB, LAGC, NBUF = 4, 8, 12
        SB_ = (0, 1, 7)
        with ExitStack() as es:
            KsT = k.sb(es, "ns_KsT", [128, T], BF16); KwT = k.sb(es, "ns_KwT", [128, T], BF16)
            s.I("pool", lambda e: e.memset(KsT[64:128, :], 0.0), w=[KsT.name])
            s.I("pool", lambda e: e.memset(KwT[64:128, :], 0.0), w=[KwT.name])
            Vs = k.sb(es, "ns_Vs", [128, NKT, 64], BF16); Vw = k.sb(es, "ns_Vw", [128, NKT, 64], BF16)
            TS_ = k.sb(es, "ns_TS", [128, 4, 1024], F32); TW_ = k.sb(es, "ns_TW", [128, 4, 640], F32)
            TC_ = k.sb(es, "ns_TC", [128, 4, 58], F32)
            b31 = k.sb(es, "ns_b31", [128, 16], F32)
            k.LD(b31[:], k.inp["rel_bias"][31, :].partition_broadcast(128), [], [b31])
            vmrel = k.sb(es, "ns_vm", [128, 256], F32); acrel = k.sb(es, "ns_ac", [128, 256], F32)
            k.LD(vmrel[:], k.din("c_vmrel", [128, 256]), [], [vmrel]); k.LD(acrel[:], k.din("c_acrel", [128, 256]), [], [acrel])
            ccm = k.sb(es, "ns_ccm", [128, 128], F32); cca = k.sb(es, "ns_cca", [128, 128], F32)
            cmpm = k.sb(es, "ns_cmpm", [128, 512], F32); wmk = k.sb(es, "ns_wmk", [128, 1], F32)
            k.LD(ccm[:], k.din("pc_cm", [128, 128]), [], [ccm]); k.LD(cca[:], k.din("pc_ca", [128, 128]), [], [cca])
            k.LD(cmpm[:], k.din("pc_cmpmask", [128, 512]), [], [cmpm]); k.LD(wmk[:], k.din("pc_wmask", [128, 1]), [], [wmk])
            qT4 = [k.sb(es, "ns_q%d" % i, [128, 4, 128], BF16) for i in range(2)]
            for q__ in qT4:
                s.I("pool", lambda e, q__=q__: e.memset(q__[:], 0.0), w=[q__.name])
            gt = [k.sb(es, "ns_gt%d" % i, [128, 48], F32) for i in range(2)]
            pcn = k.sb(es, "ns_pcn", [128, 4, 512], F32)
            s.I("pool", lambda e: e.memset(pcn[:], 0.0), w=[pcn.name])
            tmp = [k.sb(es, "ns_tmp%d" % i, [128, 512], F32) for i in range(NBUF)]
            P_ = [k.sb(es, "ns_P%d" % i, [128, 512], BF16) for i in range(NBUF)]
            pT = [k.sb(es, "ns_pT%d" % i, [128, 4, 128], BF16) for i in range(NBUF)]
            ps4 = k.sb(es, "ns_ps4", [128, 512], F32)
            imp = k.sb(es, "ns_imp", [128, 128], F32); sc1 = k.sb(es, "ns_sc1", [128, 128], F32); sc2 = k.sb(es, "ns_sc2", [128, 128], F32)
            m8a = k.sb(es, "ns_m8a", [128, 8], F32); m8b = k.sb(es, "ns_m8b", [128, 8], F32)
            mk = [k.sb(es, "ns_mk%d" % i, [128, 128], F32) for i in range(2)]
            mkb = k.sb(es, "ns_mkb", [128, 128], BF16)
            mkT = [k.sb(es, "ns_mkT%d" % i, [128, 128], BF16) for i in range(2)]
            Ef = k.sb(es, "ns_Ef", [128, T], F32)
            Eb = k.sb(es, "ns_Eb", [128, T], BF16)
            s.I("pool", lambda e: e.memset(Ef[:], 1.0), w=[Ef.name])
            s.I("pool", lambda e: e.affine_select(out=Ef[:], in_=Ef[:], pattern=[[1, T]], compare_op=ALU.is_ge, fill=0.0,
                                                  base=0, channel_multiplier=-64), r=[Ef.name], w=[Ef.name])
            s.I("pool", lambda e: e.affine_select(out=Ef[:], in_=Ef[:], pattern=[[-1, T]], compare_op=ALU.is_ge, fill=0.0,
                                                  base=63, channel_multiplier=64), r=[Ef.name], w=[Ef.name])
            s.I("pool", lambda e: e.tensor_copy(out=Eb[:], in_=Ef[:]), r=[Ef.name], w=[Eb.name])
            rsc = [k.sb(es, "ns_rsc%d" % i, [128, 4], F32) for i in range(2)]
            rss = [k.sb(es, "ns_rss%d" % i, [128, 4, 32], F32) for i in range(2)]
            rsw = [k.sb(es, "ns_rsw%d" % i, [128, 4, 2], F32) for i in range(2)]
            rt = [k.sb(es, "ns_rt%d" % i, [128, 4, 4], F32) for i in range(2)]
            oacc = k.sb(es, "ns_oacc", [128, 256], F32)
            ob = [k.sb(es, "ns_ob%d" % i, [128, 256], BF16) for i in range(2)]
            cn = {"o": 0}

            def bc64(ap2):
                return bass.AP(ap2.tensor, ap2.offset, [list(ap2.ap[0]), list(ap2.ap[1]), [0, 64]])

            for g in range(4):
                k.LD(KsT[0:64, :], sc["ksT"][g, :, :], ["pjout"], [KsT]); k.LD(KwT[0:64, :], sc["kwT"][g, :, :], ["pjout"], [KwT])
                k.LD(Vs[:], sc["Vs"][:, g * 64:(g + 1) * 64].rearrange("(n p) d -> p n d", p=128), ["pjout"], [Vs])
                k.LD(Vw[:], sc["Vw"][:, g * 64:(g + 1) * 64].rearrange("(n p) d -> p n d", p=128), ["pjout"], [Vw])
                for hl in range(4):
                    h = 4 * g + hl
                    k.LD(TS_[:, hl, :], bass.AP(SKS.tensor, h * 128 * (WS + 1) + 127, [[WS, 128], [1, 1024]]), ["SK"], [TS_])
                    k.LD(TW_[:, hl, :], bass.AP(SKW.tensor, h * 128 * (WW + 1) + 127, [[WW, 128], [1, 640]]), ["SK"], [TW_])
                    k.LD(TC_[:, hl, :], bass.AP(SKS.tensor, h * 128 * (WS + 1) + 238, [[WS, 128], [16, 58]]), ["SK"], [TC_],
                         allow_slow_non_contiguous=True)
                items = []
                for n in range(NQ // 2, NQ):
                    q0 = 128 * n; par = n % 2
                    ncv = min(NC, 8 * n + 7)
                    blk = []
                    for hl in range(4):
                        blk.append(dict(kind="c", n=n, hl=hl, wdt=ncv))
                    w0 = max(0, q0 - 512)
                    pieces = []
                    a_ = w0
                    while a_ < q0 + 128:
                        b_ = min(a_ + 512, q0 + 128)
                        if a_ < k.OWN < b_:
                            b_ = k.OWN
                        pieces.append((a_, b_ - a_)); a_ = b_
                    assert len(pieces) <= 2
                    for hl in range(4):
                        for pi_, (s0, wdt) in enumerate(pieces):
                            blk.append(dict(kind="w", n=n, hl=hl, s0=s0, wdt=wdt, pi=pi_, first=pi_ == 0, last=pi_ == len(pieces) - 1,
                                            npc=len(pieces)))
                    nch = n // 4 + 1
                    for hl in range(4):
                        for c in range(nch):
                            s0 = 512 * c
                            blk.append(dict(kind="s", n=n, hl=hl, s0=s0, wdt=min(512, q0 + 128 - s0), c=c, first=c == 0, last=c == nch - 1, nch=nch))
                    blk[0]["load"] = True
                    blk[3]["topk"] = True
                    blk[-1]["combine"] = True
                    blk[-1]["npc_w"] = len(pieces)
                    items += blk
                for i, it in enumerate(items):
                    it["i"] = i

                def stageA(it):
                    n = it["n"]; hl = it["hl"]; h = 4 * g + hl; q0 = 128 * n; par = n % 2; i = it["i"]
                    q_ = qT4[par]; g_ = gt[par]
                    if it.get("load"):
                        k.LD(q_[0:64, :, :], sc["qT"][4 * g:4 * g + 4, :, q0:q0 + 128].rearrange("h d t -> d h t"), ["pjout"], [q_])
                        k.LD(g_[:], sc["gates"][q0:q0 + 128, :], ["pjout"], [g_])
                    p = psb[SB_[i % 3]]; t_ = tmp[i % NBUF]; Pt = P_[i % NBUF]
                    wdt = it["wdt"]
                    if it["kind"] == "c":
                        ncv = wdt
                        i_lo = max(0, 8 * n - 51); c_lo = i_lo - 8 * n + 51
                        k.MM(p[:, 0:ncv], q_[:, hl, :], KcT[:, g, 0:ncv], True, True, [q_, KcT], [p])
                        k.TT("dve", t_[:, 0:ncv], p[:, 0:ncv], cmpm[:, 0:ncv], ALU.add, [p, cmpm], [t_])
                        k.TT("pool", t_[:, i_lo:ncv], t_[:, i_lo:ncv], TC_[:, hl, c_lo:c_lo + ncv - i_lo], ALU.add, [t_, TC_], [t_])
                        rk = rt[par].name + str(hl)
                        k.ACT(t_[:, 0:ncv], t_[:, 0:ncv], AF.Exp, [t_, b31], [t_, rsc[par].name + str(hl)], bias=b31[:, h:h + 1],
                              accum_out=rsc[par][:, hl:hl + 1])
                        k.TS("dve", rt[par][:, hl, 0:1], rsc[par][:, hl:hl + 1], 1e-30, None, ALU.max, None, [rsc[par].name + str(hl)], [rk])
                        s.I("dve", lambda e, hl=hl, par=par: e.reciprocal(out=rt[par][:, hl, 0:1], in_=rt[par][:, hl, 0:1]), [rk], [rk])
                        k.TS("dve", pcn[:, hl, 0:ncv], t_[:, 0:ncv], rt[par][:, hl, 0:1], None, ALU.mult, None, [t_, rk], [pcn])
                        k.CP("pool", Pt[:, 0:ncv], pcn[:, hl, 0:ncv], [pcn], [Pt])
                    elif it["kind"] == "w":
                        s0 = it["s0"]
                        k.MM(p[:, 0:wdt], q_[:, hl, :], KwT[:, s0:s0 + wdt], True, True, [q_, KwT], [p])
                        v_lo = s0 - q0 + 512
                        if s0 < k.OWN:
                            k.STT(t_[:, 0:wdt], p[:, 0:wdt], wmk[:, 0:1], TW_[:, hl, v_lo:v_lo + wdt], ALU.add, ALU.add, [p, wmk, TW_], [t_])
                        else:
                            k.TT("dve", t_[:, 0:wdt], p[:, 0:wdt], TW_[:, hl, v_lo:v_lo + wdt], ALU.add, [p, TW_], [t_])
                        k.ACT(Pt[:, 0:wdt], t_[:, 0:wdt], AF.Exp, [t_], [Pt, rsw[par].name + str(hl)], accum_out=rsw[par][:, hl, it["pi"]:it["pi"] + 1])
                    else:
                        s0 = it["s0"]; c = it["c"]
                        k.MM(p[:, 0:wdt], q_[:, hl, :], KsT[:, s0:s0 + wdt], True, False, [q_, KsT], [p])
                        k.MM(p[:, 0:wdt], mkT[par][:], Eb[:, s0:s0 + wdt], False, True, [mkT[par], Eb], [p])
                        s_lo = max(s0, q0 - 896)
                        if s_lo < s0 + wdt:
                            v_lo = s_lo - q0 + 896
                            ln = s0 + wdt - s_lo
                            k.TT("dve", t_[:, s_lo - s0: wdt], p[:, s_lo - s0: wdt], TS_[:, hl, v_lo:v_lo + ln], ALU.add, [p, TS_], [t_])
                            if s_lo > s0:
                                k.CP("dve", t_[:, 0:s_lo - s0], p[:, 0:s_lo - s0], [p], [t_])
                            k.ACT(Pt[:, 0:wdt], t_[:, 0:wdt], AF.Exp, [t_, b31], [Pt, rss[par].name + str(hl)], bias=b31[:, h:h + 1],
                                  accum_out=rss[par][:, hl, c:c + 1])
                        else:
                            k.ACT(Pt[:, 0:wdt], p[:, 0:wdt], AF.Exp, [p, b31], [Pt, rss[par].name + str(hl)], bias=b31[:, h:h + 1],
                                  accum_out=rss[par][:, hl, c:c + 1])
                    if it.get("topk"):
                        k.TT("dve", ps4[:], pcn[:, 0, :], pcn[:, 1, :], ALU.add, [pcn], [ps4])
                        k.TT("dve", ps4[:], ps4[:], pcn[:, 2, :], ALU.add, [pcn, ps4], [ps4])
                        k.TT("dve", ps4[:], ps4[:], pcn[:, 3, :], ALU.add, [pcn, ps4], [ps4])
                        s.I("dve", lambda e: e.tensor_reduce(out=imp[:], in_=ps4[:].rearrange("p (j m) -> p j m", m=4), axis=AX.X, op=ALU.add),
                            [ps4.name], [imp.name])
                        k.TT("dve", imp[:, 1:128], imp[:, 1:128], ps4[:, 3:508:4], ALU.add, [imp, ps4], [imp])
                        so = 127 - 2 * n
                        k.TT("dve", sc1[:], imp[:], vmrel[:, so:so + 128], ALU.mult, [imp, vmrel], [sc1])
                        k.TT("dve", sc1[:], sc1[:], acrel[:, so:so + 128], ALU.add, [sc1, acrel], [sc1])
                        k.TT("dve", sc1[:], sc1[:], ccm[:], ALU.mult, [sc1, ccm], [sc1])
                        k.TT("dve", sc1[:], sc1[:], cca[:], ALU.add, [sc1, cca], [sc1])
                        s.I("dve", lambda e: e.max(out=m8a[:], in_=sc1[:]), [sc1.name], [m8a.name])
                        s.I("dve", lambda e: e.match_replace(out=sc2[:], in_to_replace=m8a[:], in_values=sc1[:], imm_value=-3e4),
                            [sc1.name, m8a.name], [sc2.name])
                        s.I("dve", lambda e: e.max(out=m8b[:], in_=sc2[:]), [sc2.name], [m8b.name])
                        k.TS("dve", mk[par][:], sc1[:], m8b[:, 7:8], BIG, ALU.is_ge, ALU.mult, [sc1, m8b], [mk[par]])
                        k.TS("dve", mkb[:], mk[par][:], -BIG, None, ALU.add, None, [mk[par]], [mkb])
                        pst = psb[2 + i % 2]; pvm = pst[:].bitcast(BF16)
                        k.TR(pvm[:, 0:128], mkb[:], [mkb], [pst])
                        k.CP("dve", mkT[par][:], pvm[:, 0:128], [pst], [mkT[par]])

                def stageB(it):
                    i = it["i"]; Pt = P_[i % NBUF]; ptile = pT[i % NBUF]; wdt = it["wdt"]
                    pst = psb[2 + i % 2]; pv = pst[:].bitcast(BF16)
                    nk = (wdt + 127) // 128
                    for kt in range(nk):
                        rows = min(128, wdt - kt * 128)
                        k.TR(pv[0:rows, kt * 128:(kt + 1) * 128], Pt[:, kt * 128: kt * 128 + rows], [Pt], [pst])
                    cn["o"] += 1
                    eng = "act" if cn["o"] % 2 else "dve"
                    if wdt % 128 == 0:
                        k.CP(eng, ptile[:, 0:nk, :], pv[:, 0:nk * 128].rearrange("p (a b) -> p a b", b=128), [pst], [ptile])
                    else:
                        for kt in range(nk):
                            rows = min(128, wdt - kt * 128)
                            k.CP(eng, ptile[0:rows, kt, :], pv[0:rows, kt * 128:(kt + 1) * 128], [pst], [ptile])

                def stageC(it):
                    n = it["n"]; hl = it["hl"]; par = n % 2; i = it["i"]; ptile = pT[i % NBUF]; wdt = it["wdt"]
                    nk = (wdt + 127) // 128
                    pcs = psb[4 + par]
                    if it["kind"] == "c":
                        po = pcs[:, hl * 64:(hl + 1) * 64]; pkey = "po_cs%d" % par
                        for kt in range(nk):
                            rows = min(128, wdt - kt * 128)
                            k.MM(po, ptile[0:rows, kt, :], Vc[0:rows, kt, g, :], kt == 0, kt == nk - 1, [ptile, Vc], [pkey])
                    elif it["kind"] == "w":
                        po = psb[6][:, par * 256 + hl * 64: par * 256 + (hl + 1) * 64]; pkey = "po_w%d" % par
                        kt0 = it["s0"] // 128
                        for kt in range(nk):
                            k.MM(po, ptile[:, kt, :], Vw[:, kt0 + kt, :], it["first"] and kt == 0, it["last"] and kt == nk - 1, [ptile, Vw], [pkey])
                    else:
                        po = pcs[:, 256 + hl * 64: 256 + (hl + 1) * 64]; pkey = "po_cs%d" % par
                        kt0 = it["s0"] // 128
                        for kt in range(nk):
                            k.MM(po, ptile[:, kt, :], Vs[:, kt0 + kt, :], it["first"] and kt == 0, it["last"] and kt == nk - 1, [ptile, Vs], [pkey])
                    if it.get("combine"):
                        q0 = 128 * n; g_ = gt[par]
                        o_ = ob[par]
                        for h2 in range(4):
                            col = g * 12 + h2 * 3
                            rk = rt[par].name + str(h2)
                            nch = n // 4 + 1
                            npc = it["npc_w"]
                            s.I("dve", lambda e, h2=h2, nch=nch, par=par: e.tensor_reduce(out=rt[par][:, h2, 1:2], in_=rss[par][:, h2, 0:nch], axis=AX.X, op=ALU.add),
                                [rss[par].name + str(h2)], [rk])
                            s.I("dve", lambda e, h2=h2, npc=npc, par=par: e.tensor_reduce(out=rt[par][:, h2, 2:3], in_=rsw[par][:, h2, 0:npc], axis=AX.X, op=ALU.add),
                                [rsw[par].name + str(h2)], [rk])
                            s.I("dve", lambda e, h2=h2, par=par: e.reciprocal(out=rt[par][:, h2, 1:3], in_=rt[par][:, h2, 1:3]), [rk], [rk])
                            k.TT("dve", rt[par][:, h2, 1:3], rt[par][:, h2, 1:3], g_[:, col + 1:col + 3], ALU.mult, [rk, g_], [rk])
                            hs = slice(h2 * 64, (h2 + 1) * 64)
                            k.TS("dve", oacc[:, hs], pcs[:, h2 * 64:(h2 + 1) * 64], g_[:, col:col + 1], None, ALU.mult, None, ["po_cs%d" % par, g_], [oacc])
                            k.STT(oacc[:, hs], pcs[:, 256 + h2 * 64: 256 + (h2 + 1) * 64], rt[par][:, h2, 1:2], oacc[:, hs], ALU.mult, ALU.add,
                                  ["po_cs%d" % par, rk, oacc], [oacc])
                            k.STT(o_[:, hs], psb[6][:, par * 256 + h2 * 64: par * 256 + (h2 + 1) * 64], rt[par][:, h2, 2:3], oacc[:, hs], ALU.mult, ALU.add,
                                  ["po_w%d" % par, rk, oacc], [o_])
                        k.LD(sc["onsa"][q0:q0 + 128, g * 256:(g + 1) * 256], o_[:], [o_], ["onsa"])

                N = len(items)
                for t in range(N + LAGC):
                    if t < N:
                        stageA(items[t])
                    if 0 <= t - LAGB < N:
                        stageB(items[t - LAGB])
                    if 0 <= t - LAGC < N:
                        stageC(items[t - LAGC])
            s.barrier()

    def merge(k, sc, w, h1, h2, gpost_ap, psb):
        s = k.s; T = k.T
        with ExitStack() as es:
            gpost = k.sb(es, "mg_g", [128, D], F32)
            k.LD(gpost[:], gpost_ap.partition_broadcast(128), [], [gpost])
            yT = k.sb(es, "mg_yT", [128, 8, 512], BF16)
            oT = k.sb(es, "mg_oT", [128, 8, 512], BF16)
            zT = k.sb(es, "mg_zT", [128, 16, 512], BF16)
            ot = [k.sb(es, "mg_ot%d" % i, [128, 1024], BF16) for i in range(2)]
            wb = [k.sb(es, "mg_wb%d" % i, [128, 16, 512], BF16) for i in range(3)]
            wbi = [0]
            f1 = k.sb(es, "mg_f1", [128, 4, D], F32)
            bufs = k.normbufs(es, "mg_")
            ga = [k.sb(es, "mg_ga%d" % i, [128, 512], BF16) for i in range(2)]
            gb = [k.sb(es, "mg_gb%d" % i, [128, 512], BF16) for i in range(2)]
            t1 = [k.sb(es, "mg_t1%d" % i, [128, 512], F32) for i in range(2)]
            t2 = [k.sb(es, "mg_t2%d" % i, [128, 512], F32) for i in range(2)]
            ci = 0
            for tt in range(k.NT // 2, k.NT):
                t0 = tt * 512
                k.LD(yT[:], sc["ysT"][:, t0:t0 + 512].rearrange("(a p) t -> p a t", p=128), ["ysT"], [yT])
                for sub in range(4):
                    o_ = ot[sub % 2]
                    k.LD(o_[:], sc["onsa"][t0 + sub * 128:t0 + sub * 128 + 128, :], ["onsa"], [o_])
                    pst = psb[6 + sub % 2]; pv = pst[:].bitcast(BF16)
                    for j in range(8):
                        k.TR(pv[:, j * 128:(j + 1) * 128], o_[:, j * 128:(j + 1) * 128], [o_], [pst])
                    k.CP("act" if sub % 2 else "dve", oT[:, :, sub * 128:(sub + 1) * 128], pv.rearrange("p (a b) -> p a b", a=8), [pst], [oT])
                for c4 in range(4):
                    wl = []
                    for nm_ in ("glu1", "glu2", "wo"):
                        wc = wb[wbi[0] % 3]; wbi[0] += 1
                        k.LD(wc[:, 0:8, :], w[nm_][:, c4 * 512:(c4 + 1) * 512].rearrange("(a p) c -> p a c", p=128), ["mw"], [wc])
                        wl.append(wc)
                    for j in range(4):
                        ct = c4 * 4 + j
                        pa, pb, pc_ = psb[(ci * 3) % 6], psb[(ci * 3 + 1) % 6], psb[(ci * 3 + 2) % 6]
                        ga_, gb_, t1_, t2_ = ga[ci % 2], gb[ci % 2], t1[ci % 2], t2[ci % 2]; ci += 1
                        k.LD(ga_[:], sc["gaT"][ct * 128:(ct + 1) * 128, t0:t0 + 512], ["pjout"], [ga_])
                        k.LD(gb_[:], sc["gbT"][ct * 128:(ct + 1) * 128, t0:t0 + 512], ["pjout"], [gb_])
                        for (pp, wc, src_) in ((pa, wl[0], yT), (pb, wl[1], yT), (pc_, wl[2], oT)):
                            for kk in range(8):
                                k.MM(pp[:], wc[:, kk, j * 128:(j + 1) * 128], src_[:, kk, :], kk == 0, kk == 7, [wc, src_], [pp])
                        k.ACT(t1_[:], pb[:], AF.Sigmoid, [pb], [t1_])
                        k.TT("dve", t1_[:], pa[:], t1_[:], ALU.mult, [pa, t1_], [t1_])
                        k.TT("pool", t1_[:], t1_[:], ga_[:], ALU.mult, [t1_, ga_], [t1_])
                        k.TT("dve", t2_[:], pc_[:], gb_[:], ALU.mult, [pc_, gb_], [t2_])
                        k.TT("pool", zT[:, ct, :], t1_[:], t2_[:], ALU.add, [t1_, t2_], [zT])
                k.down_post("mg", zT, 16, w["wout"], "mw", h1, "h1", h2, "h2", gpost, 1.0, t0, f1, wb, wbi, bufs, psb)
            s.barrier()

    def sincos(k, es, tag, ang, shape, want_cos):
        s = k.s
        I32 = mybir.dt.int32
        t = k.sb(es, tag + "t", shape, F32); ti = k.sb(es, tag + "ti", shape, I32)
        r = k.sb(es, tag + "r", shape, F32); m = k.sb(es, tag + "m", shape, F32)
        o = k.sb(es, tag + "o", shape, F32)
        a2 = ang
        if want_cos:
            a2 = k.sb(es, tag + "a2", shape, F32)
            s.I("dve", lambda e: e.tensor_scalar(out=a2[:], in0=ang[:], scalar1=math.pi / 2, scalar2=None, op0=ALU.add),
                r=[ang.name], w=[a2.name])
        s.I("dve", lambda e: e.tensor_scalar(out=t[:], in0=a2[:], scalar1=1.0 / (2 * math.pi), scalar2=None, op0=ALU.mult),
            r=[a2.name], w=[t.name])
        s.I("dve", lambda e: e.tensor_copy(out=ti[:], in_=t[:]), r=[t.name], w=[ti.name])
        s.I("dve", lambda e: e.tensor_copy(out=t[:], in_=ti[:]), r=[ti.name], w=[t.name])
        s.I("dve", lambda e: e.scalar_tensor_tensor(out=r[:], in0=t[:], scalar=-2 * math.pi, in1=a2[:],
                                                    op0=ALU.mult, op1=ALU.add), r=[t.name, a2.name], w=[r.name])
        for (thr, op, fix) in ((math.pi, ALU.is_gt, -2 * math.pi), (-math.pi, ALU.is_lt, 2 * math.pi)):
            s.I("dve", lambda e, thr=thr, op=op, fix=fix: e.tensor_scalar(
                out=m[:], in0=r[:], scalar1=thr, scalar2=fix, op0=op, op1=ALU.mult), r=[r.name], w=[m.name])
            s.I("dve", lambda e: e.tensor_tensor(out=r[:], in0=r[:], in1=m[:], op=ALU.add),
                r=[r.name, m.name], w=[r.name])
        s.I("dve", lambda e: e.tensor_scalar(out=r[:], in0=r[:], scalar1=-math.pi, scalar2=math.pi,
                                             op0=ALU.max, op1=ALU.min), r=[r.name], w=[r.name])
        s.I("act", lambda e: e.activation(out=o[:], in_=r[:], func=AF.Sin), r=[r.name], w=[o.name])
        return o

    def disc(k, es, tag, are, aim, ldt, shape):
        s = k.s
        dt = k.sb(es, tag + "dt", shape, F32); lam = k.sb(es, tag + "lam", shape, F32)
        mag = k.sb(es, tag + "mag", shape, F32); ang = k.sb(es, tag + "ang", shape, F32)
        abr = k.sb(es, tag + "abr", shape, F32); abi = k.sb(es, tag + "abi", shape, F32)
        s.I("act", lambda e: e.activation(out=dt[:], in_=ldt[:], func=AF.Exp), r=[ldt.name], w=[dt.name])
        s.I("dve", lambda e: e.tensor_scalar(out=lam[:], in0=are[:], scalar1=-1e-4, scalar2=None, op0=ALU.min),
            r=[are.name], w=[lam.name])
        s.I("dve", lambda e: e.tensor_tensor(out=mag[:], in0=lam[:], in1=dt[:], op=ALU.mult),
            r=[lam.name, dt.name], w=[mag.name])
        s.I("act", lambda e: e.activation(out=mag[:], in_=mag[:], func=AF.Exp), r=[mag.name], w=[mag.name])
        s.I("dve", lambda e: e.tensor_tensor(out=ang[:], in0=aim[:], in1=dt[:], op=ALU.mult),
            r=[aim.name, dt.name], w=[ang.name])
        sn = k.sincos(es, tag + "s", ang, shape, False)
        cs = k.sincos(es, tag + "c", ang, shape, True)
        s.I("dve", lambda e: e.tensor_tensor(out=abr[:], in0=mag[:], in1=cs[:], op=ALU.mult),
            r=[mag.name, cs.name], w=[abr.name])
        s.I("dve", lambda e: e.tensor_tensor(out=abi[:], in0=mag[:], in1=sn[:], op=ALU.mult),
            r=[mag.name, sn.name], w=[abi.name])
        return abr, abi, lam

    def s5(k, usT, ysT, psb):
        s = k.s; T = k.T
        LM = int(round(math.log2(T)))
        NLV = LM
        tt = lambda e, **kw: e.tensor_tensor(**kw)
        with ExitStack() as es:
            a2r = k.sb(es, "a2r", [128, 64], F32); a2i = k.sb(es, "a2i", [128, 64], F32); l2 = k.sb(es, "l2", [128, 64], F32)
            for t_, n_ in ((a2r, "ssm_aT_re2"), (a2i, "ssm_aT_im2"), (l2, "ssm_ldt2")):
                s.DMA("sp", t_[:], k.din(n_, [128, 64]), w=[t_.name])
            sgn = k.sb(es, "sgn", [128, 1], F32); s.DMA("sp", sgn[:], k.din("c_sgn", [128, 1]), w=[sgn.name])
            mk8 = k.sb(es, "mk8", [128, 8], F32); s.DMA("sp", mk8[:], k.din("c_mask8", [128, 8]), w=[mk8.name])
            Jm = k.sb(es, "Jm", [128, 128], F32); s.DMA("sp", Jm[:], k.din("c_J", [128, 128]), w=[Jm.name])
            Id = k.sb(es, "Idf", [128, 128], F32); s.DMA("sp", Id[:], k.din("c_I", [128, 128]), w=[Id.name])
            dsk = k.sb(es, "dsk", [128, 8], F32); s.DMA("sp", dsk[:], k.din("ssm_dT", [128, 8]), w=[dsk.name])
            PR = k.sb(es, "PR", [128, NLV, 64], F32); PI = k.sb(es, "PI", [128, NLV, 64], F32)
            with ExitStack() as e2:
                abr, abi, _ = k.disc(e2, "d1", a2r, a2i, l2, [128, 64])
                s.I("dve", lambda e, abr=abr: e.tensor_copy(out=PR[:, 0, :], in_=abr[:]), r=[abr.name], w=[PR.name])
                s.I("dve", lambda e, abi=abi: e.tensor_copy(out=PI[:, 0, :], in_=abi[:]), r=[abi.name], w=[PI.name])
                t1 = k.sb(e2, "pw1", [128, 64], F32); t2 = k.sb(e2, "pw2", [128, 64], F32)
                for i in range(NLV - 1):
                    s.I("dve", lambda e, i=i: tt(e, out=t1[:], in0=PR[:, i, :], in1=PR[:, i, :], op=ALU.mult), r=[PR.name], w=[t1.name])
                    s.I("dve", lambda e, i=i: tt(e, out=t2[:], in0=PI[:, i, :], in1=PI[:, i, :], op=ALU.mult), r=[PI.name], w=[t2.name])
                    s.I("dve", lambda e, i=i: tt(e, out=PR[:, i + 1, :], in0=t1[:], in1=t2[:], op=ALU.subtract), r=[t1.name, t2.name], w=[PR.name])
                    s.I("dve", lambda e, i=i: tt(e, out=t1[:], in0=PR[:, i, :], in1=PI[:, i, :], op=ALU.mult), r=[PR.name, PI.name], w=[t1.name])
                    s.I("dve", lambda e, i=i: e.tensor_scalar(out=PI[:, i + 1, :], in0=t1[:], scalar1=2.0, scalar2=None, op0=ALU.mult), r=[t1.name], w=[PI.name])
                s.I("dve", lambda e: e.tensor_scalar(out=PI[:], in0=PI[:], scalar1=sgn[:], scalar2=None, op0=ALU.mult), r=[PI.name, sgn.name], w=[PI.name])
                s.barrier()
            Bb = k.sb(es, "Bb", [128, 8, 128], F32)
            with ExitStack() as e2:
                sh = [128, 8 * 64]
                ar = k.sb(e2, "b_ar", sh, F32); ai = k.sb(e2, "b_ai", sh, F32); ld = k.sb(e2, "b_ld", sh, F32)
                br = k.sb(e2, "b_br", sh, F32); bi = k.sb(e2, "b_bi", sh, F32)
                for t_, n_ in ((ar, "ssm_a_re_b"), (ai, "ssm_a_im_b"), (ld, "ssm_ldt_b"), (br, "ssm_b_re_b"), (bi, "ssm_b_im_b")):
                    s.DMA("sp", t_[:], k.din(n_, sh), w=[t_.name])
                abr, abi, lam = k.disc(e2, "d2", ar, ai, ld, sh)
                den = k.sb(e2, "den", sh, F32); u1 = k.sb(e2, "u1", sh, F32); u2 = k.sb(e2, "u2", sh, F32)
                cor = k.sb(e2, "cor", sh, F32); coi = k.sb(e2, "coi", sh, F32)
                D_ = lambda fn, r, w: s.I("dve", fn, r=[x.name for x in r], w=[x.name for x in w])
                D_(lambda e: tt(e, out=den[:], in0=lam[:], in1=lam[:], op=ALU.mult), [lam], [den])
                D_(lambda e: tt(e, out=u1[:], in0=ai[:], in1=ai[:], op=ALU.mult), [ai], [u1])
                D_(lambda e: tt(e, out=den[:], in0=den[:], in1=u1[:], op=ALU.add), [den, u1], [den])
                D_(lambda e: e.reciprocal(out=den[:], in_=den[:]), [den], [den])
                D_(lambda e: e.tensor_scalar(out=abr[:], in0=abr[:], scalar1=-1.0, scalar2=None, op0=ALU.add), [abr], [abr])
                D_(lambda e: tt(e, out=u1[:], in0=abr[:], in1=lam[:], op=ALU.mult), [abr, lam], [u1])
                D_(lambda e: tt(e, out=u2[:], in0=abi[:], in1=ai[:], op=ALU.mult), [abi, ai], [u2])
                D_(lambda e: tt(e, out=u1[:], in0=u1[:], in1=u2[:], op=ALU.add), [u1, u2], [u1])
                D_(lambda e: tt(e, out=cor[:], in0=u1[:], in1=den[:], op=ALU.mult), [u1, den], [cor])
                D_(lambda e: tt(e, out=u1[:], in0=abi[:], in1=lam[:], op=ALU.mult), [abi, lam], [u1])
                D_(lambda e: tt(e, out=u2[:], in0=abr[:], in1=ai[:], op=ALU.mult), [abr, ai], [u2])
                D_(lambda e: tt(e, out=u1[:], in0=u1[:], in1=u2[:], op=ALU.subtract), [u1, u2], [u1])
                D_(lambda e: tt(e, out=coi[:], in0=u1[:], in1=den[:], op=ALU.mult), [u1, den], [coi])
                v3 = lambda t_: t_[:].rearrange("p (q x) -> p q x", q=8)
                D_(lambda e: tt(e, out=u1[:], in0=cor[:], in1=br[:], op=ALU.mult), [cor, br], [u1])
                D_(lambda e: tt(e, out=u2[:], in0=coi[:], in1=bi[:], op=ALU.mult), [coi, bi], [u2])
                D_(lambda e: tt(e, out=Bb[:, :, 0:64], in0=v3(u1), in1=v3(u2), op=ALU.subtract), [u1, u2], [Bb])
                D_(lambda e: tt(e, out=u1[:], in0=cor[:], in1=bi[:], op=ALU.mult), [cor, bi], [u1])
                D_(lambda e: tt(e, out=u2[:], in0=coi[:], in1=br[:], op=ALU.mult), [coi, br], [u2])
                D_(lambda e: tt(e, out=Bb[:, :, 64:128], in0=v3(u1), in1=v3(u2), op=ALU.add), [u1, u2], [Bb])
                s.barrier()
            Ct = k.sb(es, "Ct", [128, 64, 16], F32)
            s.DMA("sp", Ct[:], k.din("ssm_cT2", [128, 64, 16]), w=[Ct.name])
            s.I("dve", lambda e: e.tensor_scalar(out=Ct[:], in0=Ct[:], scalar1=sgn[:], scalar2=None, op0=ALU.mult),
                r=[Ct.name, sgn.name], w=[Ct.name])
            NG = 3
            OW = k.OWN
            us = k.sb(es, "us32", [128, T], F32)
            yacc = k.sb(es, "yacc", [128, T - OW], F32)
            Aa = [k.sb(es, "Ast%d" % i, [128, T], F32) for i in range(NG)]
            Mt = [k.sb(es, "Mt%d" % i, [128, NLV, 128], F32) for i in range(NG)]
            BmP = k.sb(es, "BmP", [128, 8, 128], F32)
            CmP = k.sb(es, "CmP", [128, 8, 128], F32)
            gl = [k.sb(es, "gl%d" % i, [128, 512], F32) for i in range(3)]
            yb = [k.sb(es, "yb%d" % i, [128, 512], BF16) for i in range(2)]
            pc = [0]

            def nps():
                p = psb[pc[0] % 8]; pc[0] += 1
                return p
            ev = [0]

            def evac_copy(dst, src, rk, wk):
                ev[0] += 1
                if ev[0] % 2:
                    s.I("act", lambda e: e.copy(out=dst, in_=src), r=rk, w=wk)
                else:
                    s.I("dve", lambda e: e.tensor_copy(out=dst, in_=src), r=rk, w=wk)

            def group_steps(q, j, slot):
                g = q * 8 + j
                A = Aa[slot]; M = Mt[slot]
                for i in range(NLV):
                    s.I("pool", lambda e, i=i: e.tensor_scalar(out=M[:, i, :], in0=Id[:], scalar1=PR[:, i, g:g + 1],
                                                               scalar2=None, op0=ALU.mult), r=[Id.name, PR.name], w=[M.name])
                    s.I("dve", lambda e, i=i: e.scalar_tensor_tensor(out=M[:, i, :], in0=Jm[:], scalar=PI[:, i, g:g + 1],
                                                                     in1=M[:, i, :], op0=ALU.mult, op1=ALU.add),
                        r=[Jm.name, PI.name, M.name], w=[M.name])
                yield
                for c0 in range(0, T, 512):
                    p = nps()
                    k.MM(p[:], BmP[:, j, :], us[:, c0:c0 + 512], True, True, [BmP, us], [p])
                    evac_copy(A[:, c0:c0 + 512], p[:], [p.name], [A.name])
                    yield
                for l in range(1, LM + 1):
                    st = 1 << l; h = st >> 1; n = T // st
                    for c0 in range(0, n, 512):
                        m_ = min(512, n - c0)
                        src = A[:, h - 1 + c0 * st: h - 1 + (c0 + m_ - 1) * st + 1: st]
                        dst = A[:, st - 1 + c0 * st: st - 1 + (c0 + m_ - 1) * st + 1: st]
                        p = nps()
                        k.MM(p[:, 0:m_], M[:, l - 1, :], src, True, True, [M, A], [p])
                        k.TT("dve", dst, p[:, 0:m_], dst, ALU.add, [p, A], [A])
                        yield
                for l in range(LM - 1, 0, -1):
                    st = 1 << l; h = st >> 1; n = T // st
                    lo_i = max(0, n // 2 - 1)
                    for c0 in range(lo_i, n - 1, 512):
                        m_ = min(512, n - 1 - c0)
                        src = A[:, st - 1 + c0 * st: st - 1 + (c0 + m_ - 1) * st + 1: st]
                        dst = A[:, st + h - 1 + c0 * st: st + h - 1 + (c0 + m_ - 1) * st + 1: st]
                        p = nps()
                        k.MM(p[:, 0:m_], M[:, l - 1, :], src, True, True, [M, A], [p])
                        k.TT("dve", dst, p[:, 0:m_], dst, ALU.add, [p, A], [A])
                        yield
                for c0 in range(OW, T, 512):
                    p = nps()
                    k.MM(p[:], CmP[:, j, :], A[:, c0:c0 + 512], True, True, [CmP, A], [p])
                    if j == 0:
                        evac_copy(yacc[:, c0 - OW:c0 - OW + 512], p[:], [p.name], [yacc.name])
                    else:
                        k.TT("dve", yacc[:, c0 - OW:c0 - OW + 512], p[:], yacc[:, c0 - OW:c0 - OW + 512], ALU.add, [p, yacc], [yacc])
                    yield

            for q in range(8):
                s.DMA("pool", us[:], usT[q * 128:(q + 1) * 128, :], r=["usT"], w=[us.name])
                for j in range(8):
                    s.I("dve", lambda e, j=j, q=q: e.tensor_scalar(out=BmP[:, j, :], in0=Bb[:, q, :], scalar1=mk8[:, j:j + 1],
                                                              scalar2=None, op0=ALU.mult), r=[Bb.name, mk8.name], w=[BmP.name])
                s.I("pool", lambda e: e.memset(CmP[:], 0.0), w=[CmP.name])
                for j in range(8):
                    s.I("pool", lambda e, j=j, q=q: e.tensor_copy(out=CmP[:, j, j * 16:(j + 1) * 16], in_=Ct[:, q * 8 + j, :]),
                        r=[Ct.name], w=[CmP.name])
                j0 = 0
                while j0 < 8:
                    js = list(range(j0, min(8, j0 + NG)))
                    gens = [group_steps(q, j, si) for si, j in enumerate(js)]
                    alive = True
                    while alive:
                        alive = False
                        for gen in gens:
                            try:
                                next(gen); alive = True
                            except StopIteration:
                                pass
                    j0 += len(js)
                for ci, c0 in enumerate(range(k.OWN, T, 512)):
                    y_ = gl[0]; a_ = gl[1]; b_ = gl[2]; o_ = yb[ci % 2]
                    s.I("dve", lambda e, c0=c0, q=q: e.scalar_tensor_tensor(out=y_[:], in0=us[:, c0:c0 + 512], scalar=dsk[:, q:q + 1],
                                                                        in1=yacc[:, c0 - k.OWN:c0 - k.OWN + 512], op0=ALU.mult, op1=ALU.add),
                        r=[us.name, dsk.name, yacc.name], w=[y_.name])
                    s.I("act", lambda e: e.activation(out=a_[:], in_=y_[:], func=AF.Square), r=[y_.name], w=[a_.name])
                    s.I("dve", lambda e: e.tensor_scalar(out=a_[:], in0=a_[:], scalar1=0.044715, scalar2=1.0, op0=ALU.mult, op1=ALU.add),
                        r=[a_.name], w=[a_.name])
                    s.I("dve", lambda e: tt(e, out=b_[:], in0=a_[:], in1=y_[:], op=ALU.mult), r=[a_.name, y_.name], w=[b_.name])
                    s.I("act", lambda e: e.activation(out=b_[:], in_=b_[:], func=AF.Sigmoid, scale=1.5957691216057308), r=[b_.name], w=[b_.name])
                    s.I("dve", lambda e, o_=o_: tt(e, out=o_[:], in0=b_[:], in1=y_[:], op=ALU.mult), r=[b_.name, y_.name], w=[o_.name])
                    d = s.DMA("sp", ysT[q * 128:(q + 1) * 128, c0:c0 + 512], o_[:], r=[o_.name], w=["ysT"])
                    k.final.append(d)
            s.barrier()

    def build(k):
        nc = k.nc; s = k.s; es = k.es; T = k.T
        x = k.din("x", [T, D])
        out = k.nc.dram_tensor("out", [T if k.stages < 3 else T // 2, D], F32, kind="ExternalOutput").ap()
        g = {n: k.din(n, [1, D]) for n in ("ffn1_pre_g", "ffn1_post_g", "mix_pre_g", "mix_post_g",
                                           "ffn2_pre_g", "ffn2_post_g")}
        w1g = k.conv("ffn1_w_gate", [D, DFF], 4)
        w1u = k.conv("ffn1_w_up", [D, DFF], 4)
        w1d = k.conv("ffn1_w_down", [DFF, D], 4)
        psb = [es.enter_context(nc.psum_tensor("psb%d" % i, [128, 512], F32)) for i in range(8)]
        k.epsb = k.sb(es, "epsb", [128, 1], F32)
        s.I("dve", lambda e: e.memset(k.epsb[:], EPS), w=[k.epsb.name])
        identf = k.sb(es, "identf", [128, 128], F32)
        k.ident = k.sb(es, "ident", [128, 128], BF16)
        s.I("pool", lambda e: e.memset(identf[:], 1.0), w=[identf.name])
        s.I("pool", lambda e: e.affine_select(out=identf[:], in_=identf[:], pattern=[[-1, 128]],
                                              compare_op=ALU.is_equal, fill=0.0, base=0, channel_multiplier=1),
            r=[identf.name], w=[identf.name])
        s.I("dve", lambda e: e.tensor_copy(out=k.ident[:], in_=identf[:]), r=[identf.name], w=["ident"])
        if k.stages == 1:
            k.ffn("f1", x, out, g["ffn1_pre_g"][0, :], g["ffn1_post_g"][0, :], w1g, w1u, w1d, psb)
        if k.stages == 2:
            usT = k.din("usT", [1024, T])
            ysT = k.nc.dram_tensor("ysT", [1024, T], BF16, kind="ExternalOutput").ap()
            k.s5(usT, ysT, psb)
        if k.stages >= 3:
            w = {}
            win = k.conv("w_in", [D, INC], 4)
            w["glu1"] = k.conv("ssm_glu_w1", [1024, D], 2); w["glu2"] = k.conv("ssm_glu_w2", [1024, D], 2)
            w["wo"] = k.conv("nsa_w_o", [1024, D], 2); w["wout"] = k.conv("w_out", [D, D], 2)
            w["ck1"] = k.conv("cmp_k_w1", [2048, 256]); w["ck2"] = k.conv("cmp_k_w2", [256, 64])
            w["cv1"] = k.conv("cmp_v_w1", [2048, 256]); w["cv2"] = k.conv("cmp_v_w2", [256, 64])
            w2g = k.conv("ffn2_w_gate", [D, DFF], 4); w2u = k.conv("ffn2_w_up", [D, DFF], 4); w2d = k.conv("ffn2_w_down", [DFF, D], 4)
            sc = {}
            for nm_, shp, dt_ in (("h1", [T, D], F32), ("h2", [T, D], F32), ("usT", [1024, T], F32), ("ysT", [1024, T], BF16),
                                  ("qT", [16, 64, T], BF16), ("kcT", [4, 64, T], BF16), ("vcT", [4, 64, T], BF16),
                                  ("ksT", [4, 64, T], BF16), ("kwT", [4, 64, T], BF16), ("Vs", [T, 256], BF16), ("Vw", [T, 256], BF16),
                                  ("gates", [T, 48], F32), ("gaT", [D, T], BF16), ("gbT", [D, T], BF16), ("onsa", [T, 1024], BF16)):
                if k.debug:
                    sc[nm_] = k.nc.dram_tensor("dbg_" + nm_, list(shp), dt_, kind="ExternalOutput").ap()
                else:
                    sc[nm_] = k.dscr("sc_" + nm_, shp, dt_)
            k.ffn("f1", x, sc["h1"], g["ffn1_pre_g"][0, :], g["ffn1_post_g"][0, :], w1g, w1u, w1d, psb, "x", "h1",
                  wkeys=("ffn1_w_gate_b", "ffn1_w_up_b", "ffn1_w_down_b"))
            k.proj(sc["h1"], g["mix_pre_g"][0, :], win, sc, psb)
            k.s5(sc["usT"], sc["ysT"], psb)
            with ExitStack() as em:
                KcT, Vc = k.compress(em, sc, w, psb)
                SKS, SKW, WS, WW = k.bias_tables(psb)
                k.nsa(sc, KcT, Vc, SKS, SKW, WS, WW, psb)
            k.merge(sc, w, sc["h1"], sc["h2"], g["mix_post_g"][0, :], psb)
            k.ffn("f2", sc["h2"], out, g["ffn2_pre_g"][0, :], g["ffn2_post_g"][0, :], w2g, w2u, w2d, psb, "h2", "out",
                  tiles=range(k.NT // 2, k.NT), dst_off=k.OWN)
        s.finish(k.final)
        es.close()
        return nc


def ssm_layouts(a_re, a_im, log_dt, b_re, b_im, c_re, c_im, d):
    a_re = np.asarray(a_re, np.float32); a_im = np.asarray(a_im, np.float32); log_dt = np.asarray(log_dt, np.float32)
    b_re = np.asarray(b_re, np.float32); b_im = np.asarray(b_im, np.float32)
    c_re = np.asarray(c_re, np.float32); c_im = np.asarray(c_im, np.float32); d = np.asarray(d, np.float32)
    m = {}
    m["ssm_aT_re2"] = np.concatenate([a_re.T, a_re.T], 0)
    m["ssm_aT_im2"] = np.concatenate([a_im.T, a_im.T], 0)
    m["ssm_ldt2"] = np.broadcast_to(log_dt[None, :], (128, 64))
    def lay_a(a):
        t = a.reshape(8, 8, 64)
        t = np.transpose(t, (1, 0, 2))
        return np.broadcast_to(t[:, None], (8, 16, 8, 64)).reshape(128, 512)
    m["ssm_a_re_b"] = lay_a(a_re); m["ssm_a_im_b"] = lay_a(a_im)
    m["ssm_ldt_b"] = lay_a(np.broadcast_to(log_dt[:, None], (64, 64)))
    def lay_b(b):
        t = b.reshape(8, 8, 64, 16)
        t = np.transpose(t, (1, 3, 0, 2))
        return t.reshape(128, 512)
    m["ssm_b_re_b"] = lay_b(b_re); m["ssm_b_im_b"] = lay_b(b_im)
    m["ssm_cT2"] = np.concatenate([np.transpose(c_re, (2, 0, 1)), np.transpose(c_im, (2, 0, 1))], 0)
    m["ssm_dT"] = d.reshape(8, 128).T
    m["c_sgn"] = np.concatenate([np.ones((64, 1)), -np.ones((64, 1))], 0)
    m["c_mask8"] = (np.arange(128)[:, None] // 16 == np.arange(8)[None, :]).astype(np.float32)
    m["c_I"] = np.eye(128)
    m["c_J"] = np.roll(np.eye(128), 64, axis=1)
    return {k_: np.ascontiguousarray(v, dtype=np.float32) for k_, v in m.items()}


def _bucket(d):
    d = np.maximum(d, 0)
    d_f = np.maximum(d, 1).astype(np.float32)
    large = 16 + (np.log(d_f / np.float32(16)) / np.float32(math.log(1024 / 16)) * np.float32(16)).astype(np.int32)
    large = np.minimum(large, 31)
    return np.where(d < 16, d, large)


def nsa_consts():
    m = {}
    WS, WW = 1152, 768
    x = np.arange(WS); d = 1023 - x
    oh = np.zeros((33, WS), np.float32)
    bk = _bucket(d)
    for b in range(32):
        oh[b] = ((d >= 0) & (bk == b))
    oh[32] = np.where(d < 0, -BIG, 0.0)
    m["c_ohs"] = oh
    x = np.arange(WW); d = 639 - x
    oh = np.zeros((33, WW), np.float32)
    bk = _bucket(d)
    ok = (d >= 0) & (d < 512)
    for b in range(32):
        oh[b] = (ok & (bk == b))
    oh[32] = np.where(ok, 0.0, -BIG)
    m["c_ohw"] = oh
    vm = np.zeros((128, 256), np.float32); ac = np.zeros((128, 256), np.float32)
    c = np.arange(256)
    for qi in range(128):
        hi = qi >= 64
        forced = (c == 127) | ((c == 128) if hi else (c == 126))
        invalid = (c > 128) | ((c == 128) & (not hi))
        vm[qi] = (~forced & ~invalid)
        ac[qi] = np.where(forced, 1e4, np.where(invalid, -1e4, 0.0))
    m["c_vmrel"] = vm; m["c_acrel"] = ac
    return m


def percore_consts(T, half):
    m = {}
    nbp = T // 128
    cm = np.ones((128, 128), np.float32); ca = np.zeros((128, 128), np.float32)
    if half == 0:
        cm[:, 0:nbp] = 0.0; ca[:, 0:nbp] = -2e4
        cm[:, nbp] = 0.0; ca[:, nbp] = 1e4
    else:
        cm[:, 0] = 0.0; ca[:, 0] = 1e4
    m["pc_cm"] = cm; m["pc_ca"] = ca
    cmpm = np.zeros((128, 512), np.float32)
    if half == 0:
        cmpm[:, 0:T // 32] = -BIG
    m["pc_cmpmask"] = cmpm
    m["pc_wmask"] = np.full((128, 1), -BIG if half == 0 else 0.0, np.float32)
    return m


def host_inputs(inputs, names):
    m = {}
    sq = lambda n: np.asarray(inputs[n], np.float32)[0]
    lay = ssm_layouts(sq("ssm_a_re"), sq("ssm_a_im"), sq("ssm_log_dt"), sq("ssm_b_re"), sq("ssm_b_im"),
                      sq("ssm_c_re"), sq("ssm_c_im"), sq("ssm_d"))
    lay.update(nsa_consts())
    lay["cmp_posT"] = np.ascontiguousarray(sq("cmp_pos").T)
    for n in names:
        if n == "x":
            continue
        if n in lay:
            m[n] = np.ascontiguousarray(lay[n], dtype=np.float32)
        elif n == "rel_bias":
            m[n] = np.ascontiguousarray(np.asarray(inputs[n], np.float32))
        elif n.endswith("_g"):
            m[n] = np.ascontiguousarray(np.asarray(inputs[n], np.float32).reshape(1, D))
        else:
            m[n] = np.ascontiguousarray(sq(n))
    return m

_CACHE = {}


def _get(T, stages, debug=False):
    key = (T, stages, debug)
    if key not in _CACHE:
        kb = K(T, stages, debug)
        kb.build()
        _CACHE[key] = kb
    return _CACHE[key]


def run(inputs, T, stages, ncores, debug=False):
    kb = _get(T, stages, debug)
    xs = np.asarray(inputs["x"], np.float32)
    in_maps = []
    if stages >= 3:
        shared = host_inputs(inputs, [n for n in kb.inp if n != "x" and not n.startswith("pc_")])
        pcs = [percore_consts(T, 0), percore_consts(T, 1)]
        for c in range(ncores):
            b, half = c // 2, c % 2
            m = dict(shared)
            m.update(pcs[half])
            xb = xs[b % xs.shape[0]].reshape(T, D)
            if half == 0:
                xl = np.concatenate([np.zeros((T // 2, D), np.float32), xb[:T // 2]], 0)
            else:
                xl = xb
            m["x"] = np.ascontiguousarray(xl)
            in_maps.append(m)
    else:
        for c in range(ncores):
            m = {}
            for name in kb.inp:
                if name == "x":
                    continue
                a = np.asarray(inputs[name], np.float32)
                m[name] = np.ascontiguousarray(a.reshape(kb.inp[name].shape))
            m["x"] = np.ascontiguousarray(xs[c % xs.shape[0]].reshape(T, D))
            in_maps.append(m)
    res = run_bass_kernel_spmd(kb.nc, in_maps, core_ids=list(range(ncores)))
    if debug:
        return res.results
    return [r["out"] for r in res.results]


def kernel(**inputs):
    outs = run(inputs, SEQ, 3, 8)
    full = np.empty((NB, SEQ, D), np.float32)
    for c in range(8):
        b, half = c // 2, c % 2
        full[b, half * (SEQ // 2):(half + 1) * (SEQ // 2)] = outs[c]
    return full
```

```python
import math
from contextlib import ExitStack
import numpy as np
import ml_dtypes
import concourse.bass as bass
import concourse.mybir as mybir
from concourse.bass_utils import run_bass_kernel_spmd

F32 = mybir.dt.float32
BF16 = mybir.dt.bfloat16
ALU = mybir.AluOpType
AF = mybir.ActivationFunctionType
AX = mybir.AxisListType

D = 2048
DFF = 5632
EPS = 1e-6
NB = 4
SEQ = 8192
INC = 7728
BIG = 30000.0


class Ins:
    __slots__ = ("eng", "fn", "deps", "dma", "idx", "sig", "val", "semi", "prewait")

    def __init__(s, eng, fn, dma):
        s.eng = eng; s.fn = fn; s.dma = dma; s.deps = []; s.idx = 0; s.sig = False; s.val = 0
        s.semi = 0; s.prewait = None


class Sch:
    ENGS = ["pe", "dve", "act", "pool", "sp"]
    NS = 10

    def __init__(s, nc, es):
        s.nc = nc
        s.es = es
        s.eobj = {"pe": nc.tensor, "dve": nc.vector, "act": nc.scalar, "pool": nc.gpsimd, "sp": nc.sync}
        s.prog = {e: [] for e in s.ENGS}
        s.csem = {e: es.enter_context(nc.semaphore("cs_" + e)) for e in s.ENGS}
        s.dsem = {e: [es.enter_context(nc.semaphore("ds_%s%d" % (e, i))) for i in range(s.NS)]
                  for e in ("sp", "pool", "act")}
        s.ndma = {e: 0 for e in ("sp", "pool", "act")}
        s.dmas = {e: [] for e in ("sp", "pool", "act")}
        s.lastw = {}
        s.readers = {}
        s.known = {e: {f: -1 for f in s.ENGS} for e in s.ENGS}
        s.seen = {e: set() for e in s.ENGS}
        s.bar_from = {e: 0 for e in ("sp", "pool", "act")}

    def _dep(s, ins, d):
        if d is None or d is ins:
            return
        if d.dma:
            if id(d) in s.seen[ins.eng]:
                return
            s.seen[ins.eng].add(id(d))
            ins.deps.append(d)
        else:
            if d.eng == "pe" and ins.eng == "pe" and not ins.dma:
                return
            if s.known[ins.eng][d.eng] >= d.idx:
                return
            s.known[ins.eng][d.eng] = d.idx
            ins.deps.append(d)

    def _emit(s, eng, fn, r, w, dma):
        ins = Ins(eng, fn, dma)
        ins.idx = len(s.prog[eng])
        for k in list(r) + list(w):
            s._dep(ins, s.lastw.get(k))
        for k in w:
            best = {}
            for d in s.readers.get(k, ()):
                if d.dma:
                    s._dep(ins, d)
                elif d.eng not in best or best[d.eng].idx < d.idx:
                    best[d.eng] = d
            for d in best.values():
                s._dep(ins, d)
        if dma:
            n = s.ndma[eng]; s.ndma[eng] += 1
            ins.semi = n % s.NS; ins.val = (n // s.NS + 1) * 16
            if n >= s.NS:
                ins.prewait = s.dmas[eng][n - s.NS]
            s.dmas[eng].append(ins)
            ins.sig = True
        s.prog[eng].append(ins)
        for k in w:
            s.lastw[k] = ins; s.readers[k] = []
        for k in r:
            s.readers.setdefault(k, []).append(ins)
        return ins

    def I(s, eng, fn, r=(), w=()):
        return s._emit(eng, fn, r, w, False)

    def barrier(s):
        lasts = []
        for e in s.ENGS:
            for ins in reversed(s.prog[e]):
                if not ins.dma:
                    lasts.append(ins); break
        dm = [d for e in s.dmas for d in s.dmas[e][s.bar_from[e]:]]
        for e in s.dmas:
            s.bar_from[e] = len(s.dmas[e])
        for e in s.ENGS:
            ins = Ins(e, lambda eng: eng.nop(), False)
            ins.idx = len(s.prog[e])
            for d in lasts + dm:
                s._dep(ins, d)
            s.prog[e].append(ins)

    def DMA(s, eng, out, in_, r=(), w=(), **kw):
        return s._emit(eng, lambda e: e.dma_start(out=out, in_=in_, **kw), r, w, True)

    def finish(s, final):
        for e in s.ENGS:
            for ins in s.prog[e]:
                for d in ins.deps:
                    d.sig = True
        EPOCH = 30000
        s.csems = {}
        for e in s.ENGS:
            c = 0; ep = 0
            s.csems[e] = [s.csem[e]]
            for ins in s.prog[e]:
                if not ins.dma and ins.sig:
                    if c == EPOCH:
                        c = 0; ep += 1
                        s.csems[e].append(s.es.enter_context(s.nc.semaphore("cs_%s_%d" % (e, ep))))
                    c += 1; ins.val = c; ins.semi = ep
        s.maxval = {e: max([i.val for i in s.prog[e] if not i.dma] + [0]) for e in s.ENGS}

        def semof(d):
            return s.dsem[d.eng][d.semi] if d.dma else s.csems[d.eng][d.semi]

        with s.nc.Block() as block:
            def run(e):
                def body(eng):
                    for ins in s.prog[e]:
                        if ins.prewait is not None:
                            eng.wait_ge(semof(ins.prewait), ins.prewait.val)
                        for d in ins.deps:
                            eng.wait_ge(semof(d), d.val)
                        o = ins.fn(eng)
                        if ins.sig:
                            o.then_inc(semof(ins), 16 if ins.dma else 1)
                    if e == "sp":
                        for d in final:
                            eng.wait_ge(semof(d), d.val)
                return body
            block.tensor(run("pe"))
            block.vector(run("dve"))
            block.scalar(run("act"))
            block.gpsimd(run("pool"))
            block.sync(run("sp"))


def dram_ap(t, off, pat):
    return bass.AP(t.tensor if hasattr(t, "tensor") else t, off, pat)


class K:
    def __init__(k, T, stages=9, debug=False):
        k.T = T
        k.debug = debug
        k.NT = T // 512
        k.OWN = T // 2
        k.stages = stages
        k.nc = nc = bass.Bass("TRN2", target_bir_lowering=False)
        k.es = ExitStack()
        k.s = Sch(nc, k.es)
        k.inp = {}
        k.final = []

    def din(k, name, shape, dt=F32):
        a = k.nc.dram_tensor(name, list(shape), dt, kind="ExternalInput").ap()
        k.inp[name] = a
        return a

    def dscr(k, name, shape, dt):
        return k.nc.dram_tensor(name, list(shape), dt, kind="Internal").ap()

    def sb(k, es, name, shape, dt):
        return es.enter_context(k.nc.sbuf_tensor(name, list(shape), dt))

    def conv(k, name, shape, nsplit=1):
        src = k.din(name, shape)
        dst = k.dscr(name + "_b", shape, BF16)
        rows = shape[0]
        step = rows // nsplit
        for i in range(nsplit):
            k.s.DMA("pool", dst[i * step:(i + 1) * step, :], src[i * step:(i + 1) * step, :], w=[name + "_b"])
        return dst

    def rstd(k, src_ap, junk, ss, rs, key_r, scale_out=1.0):
        s = k.s
        s.I("act", lambda e: e.activation(out=junk[:], in_=src_ap, func=AF.Square, accum_out=ss[:]),
            r=key_r, w=[junk.name, ss.name])
        s.I("act", lambda e: e.activation(out=rs[:], in_=ss[:], func=AF.Sqrt, bias=k.epsb[:],
                                          scale=1.0 / (D * scale_out * scale_out)),
            r=[ss.name], w=[rs.name])
        s.I("dve", lambda e: e.reciprocal(out=rs[:], in_=rs[:]), r=[rs.name], w=[rs.name])

    def ffn(k, tag, src, dst, pre_g, post_g, wg, wu, wd, psb, skey="x", dkey="out", tiles=None, dst_off=0, wkeys=None):
        s = k.s; nc = k.nc
        with ExitStack() as es:
            gpre = k.sb(es, tag + "gpre", [128, D], F32)
            gpost = k.sb(es, tag + "gpost", [128, D], F32)
            s.DMA("sp", gpre[:], pre_g.partition_broadcast(128), w=[gpre.name])
            s.DMA("sp", gpost[:], post_g.partition_broadcast(128), w=[gpost.name])
            kg_, ku_, kd_ = wkeys if wkeys is not None else (tag + "wg", tag + "wu", tag + "wd")
            xT = k.sb(es, tag + "xT", [128, 16, 512], BF16)
            hT = k.sb(es, tag + "hT", [128, 44, 512], BF16)
            wb = [k.sb(es, tag + "wb%d" % i, [128, 16, 256], BF16) for i in range(6)]
            f1 = k.sb(es, tag + "f1", [128, 4, D], F32)
            xs = [k.sb(es, tag + "xs%d" % i, [128, D], F32) for i in range(2)]
            xn = [k.sb(es, tag + "xn%d" % i, [128, D], BF16) for i in range(2)]
            junk = k.sb(es, tag + "junk", [128, D], F32)
            sg = [k.sb(es, tag + "sg%d" % i, [128, 512], F32) for i in range(2)]
            ss = [k.sb(es, tag + "ss%d" % i, [128, 1], F32) for i in range(2)]
            rs = [k.sb(es, tag + "rs%d" % i, [128, 1], F32) for i in range(2)]
            wbi = [0]

            def nextw():
                b = wb[wbi[0] % 6]; wbi[0] += 1
                return b

            xi = 0
            for tt in (tiles if tiles is not None else range(k.NT)):
                t0 = tt * 512
                for sub in range(4):
                    x_ = xs[xi % 2]; xn_ = xn[xi % 2]; ss_ = ss[xi % 2]; rs_ = rs[xi % 2]; xi += 1
                    rows = src[t0 + sub * 128: t0 + sub * 128 + 128, :]
                    s.DMA("sp", x_[:], rows, r=[skey], w=[x_.name])
                    k.rstd(x_[:], junk, ss_, rs_, [x_.name])
                    s.I("dve", lambda e, x_=x_, xn_=xn_, rs_=rs_: e.scalar_tensor_tensor(
                        out=xn_[:], in0=x_[:], scalar=rs_[:], in1=gpre[:], op0=ALU.mult, op1=ALU.mult),
                        r=[x_.name, rs_.name, gpre.name], w=[xn_.name])
                    for half in range(2):
                        pst = psb[6 + half]
                        pv = pst[:].bitcast(BF16)
                        for j in range(8):
                            kk = half * 8 + j
                            s.I("pe", lambda e, pv=pv, xn_=xn_, kk=kk, j=j: e.transpose(
                                out=pv[:, j * 128:(j + 1) * 128], in_=xn_[:, kk * 128:(kk + 1) * 128],
                                identity=k.ident[:]),
                                r=[xn_.name, "ident"], w=[pst.name])
                        eng = "act" if half == 0 else "dve"
                        if eng == "act":
                            s.I("act", lambda e, pv=pv, half=half, sub=sub: e.copy(
                                out=xT[:, half * 8:half * 8 + 8, sub * 128:(sub + 1) * 128],
                                in_=pv.rearrange("p (a b) -> p a b", a=8)),
                                r=[pst.name], w=[xT.name])
                        else:
                            s.I("dve", lambda e, pv=pv, half=half, sub=sub: e.tensor_copy(
                                out=xT[:, half * 8:half * 8 + 8, sub * 128:(sub + 1) * 128],
                                in_=pv.rearrange("p (a b) -> p a b", a=8)),
                                r=[pst.name], w=[xT.name])
                pi = 0
                for cc in range(22):
                    wgc = nextw()
                    s.DMA("sp", wgc[:], wg[:, cc * 256:(cc + 1) * 256].rearrange("(a p) c -> p a c", p=128),
                          r=[kg_], w=[wgc.name])
                    wuc = nextw()
                    s.DMA("sp", wuc[:], wu[:, cc * 256:(cc + 1) * 256].rearrange("(a p) c -> p a c", p=128),
                          r=[ku_], w=[wuc.name])
                    for fs in range(2):
                        f = cc * 2 + fs
                        pg = psb[(pi * 2) % 6]; pu = psb[(pi * 2 + 1) % 6]; sg_ = sg[pi % 2]; pi += 1
                        for kk in range(16):
                            s.I("pe", lambda e, pg=pg, wgc=wgc, kk=kk, fs=fs: e.matmul(
                                pg[:], lhsT=wgc[:, kk, fs * 128:(fs + 1) * 128], rhs=xT[:, kk, :],
                                start=(kk == 0), stop=(kk == 15)), r=[wgc.name, xT.name], w=[pg.name])
                        for kk in range(16):
                            s.I("pe", lambda e, pu=pu, wuc=wuc, kk=kk, fs=fs: e.matmul(
                                pu[:], lhsT=wuc[:, kk, fs * 128:(fs + 1) * 128], rhs=xT[:, kk, :],
                                start=(kk == 0), stop=(kk == 15)), r=[wuc.name, xT.name], w=[pu.name])
                        s.I("act", lambda e, pg=pg, sg_=sg_: e.activation(out=sg_[:], in_=pg[:], func=AF.Silu),
                            r=[pg.name], w=[sg_.name])
                        s.I("dve", lambda e, pu=pu, sg_=sg_, f=f: e.tensor_tensor(
                            out=hT[:, f, :], in0=sg_[:], in1=pu[:], op=ALU.mult),
                            r=[pu.name, sg_.name], w=[hT.name])
                for c4 in range(4):
                    base = 0 if c4 % 2 == 0 else 4
                    f0 = 0
                    for nf in (8, 8, 8, 8, 8, 4):
                        wdt_ = nextw()
                        wdc = wdt_[:].rearrange("p a c -> p (a c)").rearrange("p (a c) -> p a c", c=512)
                        s.DMA("sp", wdc[:, 0:nf, :],
                              wd[f0 * 128:(f0 + nf) * 128, c4 * 512:(c4 + 1) * 512].rearrange("(a p) c -> p a c", p=128),
                              r=[kd_], w=[wdt_.name])
                        for sub in range(4):
                            po = psb[base + sub]
                            for fi in range(nf):
                                f = f0 + fi
                                s.I("pe", lambda e, po=po, wdc=wdc, fi=fi, f=f, sub=sub: e.matmul(
                                    po[:], lhsT=hT[:, f, sub * 128:(sub + 1) * 128], rhs=wdc[:, fi, :],
                                    start=(f == 0), stop=(f == 43)), r=[wdt_.name, hT.name], w=[po.name])
                        f0 += nf
                    for sub in range(4):
                        po = psb[base + sub]
                        if sub % 2 == 0:
                            s.I("act", lambda e, po=po, sub=sub, c4=c4: e.copy(
                                out=f1[:, sub, c4 * 512:(c4 + 1) * 512], in_=po[:]),
                                r=[po.name], w=[f1.name + str(sub)])
                        else:
                            s.I("dve", lambda e, po=po, sub=sub, c4=c4: e.tensor_copy(
                                out=f1[:, sub, c4 * 512:(c4 + 1) * 512], in_=po[:]),
                                r=[po.name], w=[f1.name + str(sub)])
                for sub in range(4):
                    x_ = xs[xi % 2]; ss_ = ss[xi % 2]; rs_ = rs[xi % 2]; xi += 1
                    rows = src[t0 + sub * 128: t0 + sub * 128 + 128, :]
                    s.DMA("sp", x_[:], rows, r=[skey], w=[x_.name])
                    k.rstd(f1[:, sub, :], junk, ss_, rs_, [f1.name + str(sub)], scale_out=0.5)
                    s.I("dve", lambda e, sub=sub, rs_=rs_: e.scalar_tensor_tensor(
                        out=f1[:, sub, :], in0=f1[:, sub, :], scalar=rs_[:], in1=gpost[:],
                        op0=ALU.mult, op1=ALU.mult),
                        r=[f1.name + str(sub), rs_.name, gpost.name], w=[f1.name + str(sub)])
                    s.I("pool", lambda e, sub=sub, x_=x_: e.tensor_tensor(
                        out=x_[:], in0=f1[:, sub, :], in1=x_[:], op=ALU.add),
                        r=[f1.name + str(sub), x_.name], w=[x_.name])
                    d = s.DMA("sp", dst[t0 - dst_off + sub * 128: t0 - dst_off + sub * 128 + 128, :], x_[:], r=[x_.name], w=[dkey])
                    k.final.append(d)
            s.barrier()


    @staticmethod
    def _n(xs):
        return [x if isinstance(x, str) else x.name for x in xs]

    def TT(k, eng, out, in0, in1, op, r, w):
        k.s.I(eng, lambda e: e.tensor_tensor(out=out, in0=in0, in1=in1, op=op), k._n(r), k._n(w))

    def TS(k, eng, out, in0, s1, s2, op0, op1, r, w):
        if op1 is None:
            k.s.I(eng, lambda e: e.tensor_scalar(out=out, in0=in0, scalar1=s1, scalar2=None, op0=op0), k._n(r), k._n(w))
        else:
            k.s.I(eng, lambda e: e.tensor_scalar(out=out, in0=in0, scalar1=s1, scalar2=s2, op0=op0, op1=op1), k._n(r), k._n(w))

    def STT(k, out, in0, scalar, in1, op0, op1, r, w):
        k.s.I("dve", lambda e: e.scalar_tensor_tensor(out=out, in0=in0, scalar=scalar, in1=in1, op0=op0, op1=op1),
              k._n(r), k._n(w))

    def ACT(k, out, in_, func, r, w, **kw):
        k.s.I("act", lambda e: e.activation(out=out, in_=in_, func=func, **kw), k._n(r), k._n(w))

    def MM(k, out, lhsT, rhs, start, stop, r, w):
        k.s.I("pe", lambda e: e.matmul(out, lhsT=lhsT, rhs=rhs, start=start, stop=stop), k._n(r), k._n(w))

    def TR(k, out, in_, r, w):
        k.s.I("pe", lambda e: e.transpose(out=out, in_=in_, identity=k.ident[:]), k._n(r) + ["ident"], k._n(w))

    def CP(k, eng, out, in_, r, w):
        if eng == "act":
            k.s.I("act", lambda e: e.copy(out=out, in_=in_), k._n(r), k._n(w))
        else:
            k.s.I(eng, lambda e: e.tensor_copy(out=out, in_=in_), k._n(r), k._n(w))

    def LD(k, out, in_, r, w, eng="sp", **kw):
        return k.s.DMA(eng, out, in_, k._n(r), k._n(w), **kw)

    def gelu_tanh(k, y_, a_, b_, out_ap, wname):
        k.ACT(a_[:], y_[:], AF.Square, [y_], [a_])
        k.TS("dve", a_[:], a_[:], 0.044715, 1.0, ALU.mult, ALU.add, [a_], [a_])
        k.TT("dve", b_[:], a_[:], y_[:], ALU.mult, [a_, y_], [b_])
        k.ACT(b_[:], b_[:], AF.Sigmoid, [b_], [b_], scale=1.5957691216057308)
        k.TT("dve", out_ap, b_[:], y_[:], ALU.mult, [b_, y_], [wname])

    def prenormT(k, src, skey, t0, gpre, xT, bufs, psb):
        s = k.s
        xs, xn, junk, ss, rs, ctr = bufs
        for sub in range(4):
            i = ctr[0] % 2; ctr[0] += 1
            x_ = xs[i]; xn_ = xn[i]; ss_ = ss[i]; rs_ = rs[i]
            k.LD(x_[:], src[t0 + sub * 128: t0 + sub * 128 + 128, :], [skey], [x_])
            k.rstd(x_[:], junk, ss_, rs_, [x_.name])
            k.STT(xn_[:], x_[:], rs_[:], gpre[:], ALU.mult, ALU.mult, [x_, rs_, gpre], [xn_])
            for half in range(2):
                pst = psb[6 + half]
                pv = pst[:].bitcast(BF16)
                for j in range(8):
                    kk = half * 8 + j
                    k.TR(pv[:, j * 128:(j + 1) * 128], xn_[:, kk * 128:(kk + 1) * 128], [xn_], [pst])
                k.CP("act" if half == 0 else "dve", xT[:, half * 8:half * 8 + 8, sub * 128:(sub + 1) * 128],
                     pv.rearrange("p (a b) -> p a b", a=8), [pst], [xT])

    def down_post(k, tag, hT, KT, wd, wkey, src, skey, dst, dkey, gpost, scale_out, t0, f1, wb, wbi, bufs, psb):
        s = k.s
        xs, xn, junk, ss, rs, ctr = bufs
        groups = []
        f0 = 0
        while f0 < KT:
            nf = min(16, KT - f0); groups.append((f0, nf)); f0 += nf
        for c4 in range(4):
            base = 0 if c4 % 2 == 0 else 4
            for (f0, nf) in groups:
                wdc = wb[wbi[0] % 3]; wbi[0] += 1
                k.LD(wdc[:, 0:nf, :], wd[f0 * 128:(f0 + nf) * 128, c4 * 512:(c4 + 1) * 512].rearrange("(a p) c -> p a c", p=128),
                     [wkey], [wdc])
                for sub in range(4):
                    po = psb[base + sub]
                    for fi in range(nf):
                        f = f0 + fi
                        k.MM(po[:], hT[:, f, sub * 128:(sub + 1) * 128], wdc[:, fi, :], f == 0, f == KT - 1, [wdc, hT], [po])
            for sub in range(4):
                po = psb[base + sub]
                k.CP("act" if sub % 2 == 0 else "dve", f1[:, sub, c4 * 512:(c4 + 1) * 512], po[:], [po], [f1.name + str(sub)])
        for sub in range(4):
            i = ctr[0] % 2; ctr[0] += 1
            x_ = xs[i]; ss_ = ss[i]; rs_ = rs[i]
            k.LD(x_[:], src[t0 + sub * 128: t0 + sub * 128 + 128, :], [skey], [x_])
            k.rstd(f1[:, sub, :], junk, ss_, rs_, [f1.name + str(sub)], scale_out=scale_out)
            k.STT(f1[:, sub, :], f1[:, sub, :], rs_[:], gpost[:], ALU.mult, ALU.mult,
                  [f1.name + str(sub), rs_, gpost], [f1.name + str(sub)])
            k.TT("pool", x_[:], f1[:, sub, :], x_[:], ALU.add, [f1.name + str(sub), x_], [x_])
            d = k.LD(dst[t0 + sub * 128: t0 + sub * 128 + 128, :], x_[:], [x_], [dkey])
            k.final.append(d)

    def normbufs(k, es, tag):
        xs = [k.sb(es, tag + "xs%d" % i, [128, D], F32) for i in range(2)]
        xn = [k.sb(es, tag + "xn%d" % i, [128, D], BF16) for i in range(2)]
        junk = k.sb(es, tag + "junk", [128, D], F32)
        ss = [k.sb(es, tag + "ss%d" % i, [128, 1], F32) for i in range(2)]
        rs = [k.sb(es, tag + "rs%d" % i, [128, 1], F32) for i in range(2)]
        return (xs, xn, junk, ss, rs, [0])

    def proj(k, h1, gmix, win, sc, psb):
        s = k.s; T = k.T
        with ExitStack() as es:
            gpre = k.sb(es, "pj_g", [128, D], F32)
            k.LD(gpre[:], gmix.partition_broadcast(128), [], [gpre])
            uT = k.sb(es, "pj_uT", [128, 16, 512], BF16)
            wb = [k.sb(es, "pj_wb%d" % i, [128, 16, 512], BF16) for i in range(3)]
            bufs = k.normbufs(es, "pj_")
            s32 = [k.sb(es, "pj_s32%d" % i, [128, 512], F32) for i in range(2)]
            s16 = [k.sb(es, "pj_s16%d" % i, [128, 512], BF16) for i in range(4)]
            sg = [k.sb(es, "pj_sg%d" % i, [128, 48], F32) for i in range(2)]
            cnt = {"w": 0, "p": 0, "a": 0, "b": 0, "g": 0, "e": 0}

            def nps():
                p = psb[cnt["p"] % 6]; cnt["p"] += 1
                return p

            chunks = [(c * 512, 512) for c in range(7)] + [(3584, 48)] + [(3632 + 512 * i, 512) for i in range(8)]
            for tt in range(k.NT):
                t0 = tt * 512
                k.prenormT(h1, "h1", t0, gpre, uT, bufs, psb)
                for ci, (c0, wd_) in enumerate(chunks):
                    if t0 < k.OWN and ci not in (0, 1, 4, 5, 6):
                        continue
                    wc = wb[cnt["w"] % 3]; cnt["w"] += 1
                    k.LD(wc[:, :, 0:wd_], win[:, c0:c0 + wd_].rearrange("(a p) c -> p a c", p=128), ["win"], [wc])

                    def fm(off, M, kind, dst):
                        p = nps()
                        for kk in range(16):
                            k.MM(p[0:M, :], wc[:, kk, off:off + M], uT[:, kk, :], kk == 0, kk == 15, [wc, uT], [p])
                        if kind == "us":
                            st = s32[cnt["a"] % 2]; cnt["a"] += 1
                            cnt["e"] += 1
                            k.CP("act" if cnt["e"] % 2 else "dve", st[0:M, :], p[0:M, :], [p], [st])
                        else:
                            st = s16[cnt["b"] % 4]; cnt["b"] += 1
                            if kind == "q":
                                k.s.I("act", lambda e, st=st, p=p, M=M: e.mul(out=st[0:M, :], in_=p[0:M, :], mul=0.125), [p.name], [st.name])
                            elif kind == "sig":
                                k.ACT(st[0:M, :], p[0:M, :], AF.Sigmoid, [p], [st])
                            else:
                                k.CP("dve", st[0:M, :], p[0:M, :], [p], [st])
                        k.LD(dst, st[0:M, :], [st], ["pjout"])

                    def tm(off, N, kind, dst_t):
                        for sub in range(4):
                            p = nps()
                            for kk in range(16):
                                k.MM(p[:, 0:N], uT[:, kk, sub * 128:(sub + 1) * 128], wc[:, kk, off:off + N], kk == 0, kk == 15, [wc, uT], [p])
                            if kind == "sig":
                                st = sg[cnt["g"] % 2]; cnt["g"] += 1
                                k.ACT(st[:, 0:N], p[:, 0:N], AF.Sigmoid, [p], [st])
                            else:
                                st = s16[cnt["b"] % 4]; cnt["b"] += 1
                                k.CP("dve", st[:, 0:N], p[:, 0:N], [p], [st])
                            k.LD(dst_t[t0 + sub * 128: t0 + sub * 128 + 128, :], st[:, 0:N], [st], ["pjout"])

                    ts_ = slice(t0, t0 + 512)
                    if ci in (0, 1):
                        for j in range(4):
                            fm(j * 128, 128, "us", sc["usT"][(ci * 4 + j) * 128:(ci * 4 + j + 1) * 128, ts_])
                    elif ci in (2, 3):
                        for j in range(4):
                            h0 = (ci - 2) * 8 + 2 * j
                            fm(j * 128, 128, "q", sc["qT"][h0:h0 + 2, :, ts_].rearrange("a d t -> (a d) t"))
                    elif ci == 4:
                        for g2 in range(2):
                            fm(g2 * 128, 128, "k", sc["kcT"][2 * g2:2 * g2 + 2, :, ts_].rearrange("a d t -> (a d) t"))
                        for g2 in range(2):
                            fm(256 + g2 * 128, 128, "k", sc["vcT"][2 * g2:2 * g2 + 2, :, ts_].rearrange("a d t -> (a d) t"))
                    elif ci == 5:
                        for g2 in range(2):
                            fm(g2 * 128, 128, "k", sc["ksT"][2 * g2:2 * g2 + 2, :, ts_].rearrange("a d t -> (a d) t"))
                        tm(256, 256, "k", sc["Vs"])
                    elif ci == 6:
                        for g2 in range(2):
                            fm(g2 * 128, 128, "k", sc["kwT"][2 * g2:2 * g2 + 2, :, ts_].rearrange("a d t -> (a d) t"))
                        tm(256, 256, "k", sc["Vw"])
                    elif ci == 7:
                        tm(0, 48, "sig", sc["gates"])
                    elif ci < 12:
                        for j in range(4):
                            fm(j * 128, 128, "sig", sc["gaT"][((ci - 8) * 4 + j) * 128:((ci - 8) * 4 + j + 1) * 128, ts_])
                    else:
                        for j in range(4):
                            fm(j * 128, 128, "sig", sc["gbT"][((ci - 12) * 4 + j) * 128:((ci - 12) * 4 + j + 1) * 128, ts_])
            s.barrier()


    def compress(k, es_out, sc, w, psb):
        s = k.s; T = k.T
        NC = T // 16 - 1
        NIT = (NC + 127) // 128
        KcT = k.sb(es_out, "KcT", [128, 4, 512], BF16)
        Vc = k.sb(es_out, "Vc", [128, 4, 4, 64], BF16)
        s.I("pool", lambda e: e.memset(KcT[:], 0.0), w=[KcT.name])
        with ExitStack() as es:
            raw = k.sb(es, "cp_raw", [64, T], BF16)
            w1 = k.sb(es, "cp_w1", [64, 32, 256], BF16)
            w2 = k.sb(es, "cp_w2", [128, 2, 64], BF16)
            posf = k.sb(es, "cp_posf", [64, 32], F32)
            posb = k.sb(es, "cp_posb", [64, 32], BF16)
            bias = k.sb(es, "cp_bias", [128, 2], F32)
            y_ = k.sb(es, "cp_y", [128, 512], F32); a_ = k.sb(es, "cp_a", [128, 512], F32); b_ = k.sb(es, "cp_b", [128, 512], F32)
            hid = k.sb(es, "cp_hid", [128, 2, 512], BF16)
            k.LD(posf[:], k.din("cmp_posT", [64, 32]), [], [posf])
            k.CP("dve", posb[:], posf[:], [posf], [posb])
            for kind in range(2):
                rawsc = sc["kcT"] if kind == 0 else sc["vcT"]
                w1d, w2d = (w["ck1"], w["ck2"]) if kind == 0 else (w["cv1"], w["cv2"])
                k.LD(w1[:], w1d.rearrange("(l d) c -> d l c", d=64), ["cw"], [w1])
                k.LD(w2[:], w2d.rearrange("(a p) d -> p a d", p=128), ["cw"], [w2])
                for ht in range(2):
                    p = psb[ht]
                    for l in range(32):
                        k.MM(p[:, 0:1], w1[:, l, ht * 128:(ht + 1) * 128], posb[:, l:l + 1], l == 0, l == 31, [w1, posb], [p])
                    k.CP("dve", bias[:, ht:ht + 1], p[:, 0:1], [p], [bias])
                for g in range(4):
                    k.LD(raw[:], rawsc[g, :, :], ["pjout"], [raw])
                    for ht in range(2):
                        p = psb[2 + ht]
                        for l in range(32):
                            k.MM(p[:, 0:NC], w1[:, l, ht * 128:(ht + 1) * 128], raw[:, l: l + 16 * (NC - 1) + 1: 16],
                                 l == 0, l == 31, [w1, raw], [p])
                        k.TS("dve", y_[:, 0:NC], p[:, 0:NC], bias[:, ht:ht + 1], None, ALU.add, None, [p, bias], [y_])
                        k.gelu_tanh(y_, a_, b_, hid[:, ht, :], hid.name)
                    if kind == 0:
                        p = psb[4]
                        for ht in range(2):
                            k.MM(p[0:64, 0:NC], w2[:, ht, :], hid[:, ht, 0:NC], ht == 0, ht == 1, [w2, hid], [p])
                        k.CP("act", KcT[0:64, g, 0:NC], p[0:64, 0:NC], [p], [KcT])
                    else:
                        for it in range(NIT):
                            rows = min(128, NC - it * 128)
                            p = psb[4 + it % 2]
                            for ht in range(2):
                                k.MM(p[0:rows, 0:64], hid[:, ht, it * 128: it * 128 + rows], w2[:, ht, :], ht == 0, ht == 1, [w2, hid], [p])
                            k.CP("act", Vc[0:rows, it, g, :], p[0:rows, 0:64], [p], [Vc])
            s.barrier()
        return KcT, Vc

    def bias_tables(k, psb):
        s = k.s
        WS, WW = 1152, 768
        SKS = k.dscr("SKS", [16, 128 * (WS + 1)], F32)
        SKW = k.dscr("SKW", [16, 128 * (WW + 1)], F32)
        rb = k.din("rel_bias", [32, 16])
        with ExitStack() as es:
            ohs = k.sb(es, "bt_ohs", [33, WS], F32); ohw = k.sb(es, "bt_ohw", [33, WW], F32)
            k.LD(ohs[:], k.din("c_ohs", [33, WS]), [], [ohs]); k.LD(ohw[:], k.din("c_ohw", [33, WW]), [], [ohw])
            rbw = k.sb(es, "bt_rbw", [33, 16], F32); rbs = k.sb(es, "bt_rbs", [33, 16], F32); r31 = k.sb(es, "bt_r31", [33, 16], F32)
            s.I("dve", lambda e: e.memset(rbw[:], 1.0), w=[rbw.name])
            s.I("dve", lambda e: e.memset(rbs[:], 1.0), w=[rbs.name])
            s.I("dve", lambda e: e.memset(r31[:], 0.0), w=[r31.name])
            k.LD(rbw[0:32, :], rb, [rbw], [rbw])
            k.LD(r31[0:32, :], rb[31, :].partition_broadcast(32), [r31], [r31])
            k.TT("dve", rbs[0:32, :], rbw[0:32, :], r31[0:32, :], ALU.subtract, [rbw, r31], [rbs])
            ones = k.sb(es, "bt_ones", [33, 128], F32)
            s.I("dve", lambda e: e.memset(ones[:], 1.0), w=[ones.name])
            lh = [k.sb(es, "bt_lh%d" % i, [33, 128], F32) for i in range(2)]
            rep = [k.sb(es, "bt_rep%d" % i, [128, WS], F32) for i in range(2)]
            n = 0
            for (rbt, oh, W, SK) in ((rbs, ohs, WS, SKS), (rbw, ohw, WW, SKW)):
                for h in range(16):
                    l_ = lh[n % 2]; r_ = rep[n % 2]; n += 1
                    k.TS("dve", l_[:], ones[:], rbt[:, h:h + 1], None, ALU.mult, None, [ones, rbt], [l_])
                    for c0 in range(0, W, 512):
                        wd_ = min(512, W - c0)
                        p = psb[(c0 // 512) % 4 + 4 * (n % 2)]
                        k.MM(p[:, 0:wd_], l_[:], oh[:, c0:c0 + wd_], True, True, [l_, oh], [p])
                        k.CP("act" if (c0 // 512) % 2 else "dve", r_[:, c0:c0 + wd_], p[:, 0:wd_], [p], [r_])
                    dst = bass.AP(SK.tensor, h * 128 * (W + 1), [[W + 1, 128], [1, W]])
                    k.LD(dst, r_[:, 0:W], [r_], ["SK"])
            s.barrier()
        return SKS, SKW, WS, WW

    def nsa(k, sc, KcT, Vc, SKS, SKW, WS, WW, psb):
        s = k.s; T = k.T
        NC = T // 16 - 1
        NQ = T // 128
        NKT = T // 128
        LAGB, LAGC, NBUF = 4, 8, 12
        SB_ = (0, 1, 7)
        with ExitStack() as es:
            KsT = k.sb(es, "ns_KsT", [128, T], BF16); KwT = k.sb(es, "ns_KwT", [128, T], BF16)
            s.I("pool", lambda e: e.memset(KsT[64:128, :], 0.0), w=[KsT.name])
            s.I("pool", lambda e: e.memset(KwT[64:128, :], 0.0), w=[KwT.name])
            Vs = k.sb(es, "ns_Vs", [128, NKT, 64], BF16); Vw = k.sb(es, "ns_Vw", [128, NKT, 64], BF16)
            TS_ = k.sb(es, "ns_TS", [128, 4, 1024], F32); TW_ = k.sb(es, "ns_TW", [128, 4, 640], F32)
            TC_ = k.sb(es, "ns_TC", [128, 4, 58], F32)
            b31 = k.sb(es, "ns_b31", [128, 16], F32)
            k.LD(b31[:], k.inp["rel_bias"][31, :].partition_broadcast(128), [], [b31])
            vmrel = k.sb(es, "ns_vm", [128, 256], F32); acrel = k.sb(es, "ns_ac", [128, 256], F32)
            k.LD(vmrel[:], k.din("c_vmrel", [128, 256]), [], [vmrel]); k.LD(acrel[:], k.din("c_acrel", [128, 256]), [], [acrel])
            ccm = k.sb(es, "ns_ccm", [128, 128], F32); cca = k.sb(es, "ns_cca", [128, 128], F32)
            cmpm = k.sb(es, "ns_cmpm", [128, 512], F32); wmk = k.sb(es, "ns_wmk", [128, 1], F32)
            k.LD(ccm[:], k.din("pc_cm", [128, 128]), [], [ccm]); k.LD(cca[:], k.din("pc_ca", [128, 128]), [], [cca])
            k.LD(cmpm[:], k.din("pc_cmpmask", [128, 512]), [], [cmpm]); k.LD(wmk[:], k.din("pc_wmask", [128, 1]), [], [wmk])
            qT4 = [k.sb(es, "ns_q%d" % i, [128, 4, 128], BF16) for i in range(2)]
            for q__ in qT4:
                s.I("pool", lambda e, q__=q__: e.memset(q__[:], 0.0), w=[q__.name])
            gt = [k.sb(es, "ns_gt%d" % i, [128, 48], F32) for i in range(2)]
            pcn = k.sb(es, "ns_pcn", [128, 4, 512], F32)
            s.I("pool", lambda e: e.memset(pcn[:], 0.0), w=[pcn.name])
            tmp = [k.sb(es, "ns_tmp%d" % i, [128, 512], F32) for i in range(NBUF)]
            P_ = [k.sb(es, "ns_P%d" % i, [128, 512], BF16) for i in range(NBUF)]
            pT = [k.sb(es, "ns_pT%d" % i, [128, 4, 128], BF16) for i in range(NBUF)]
            ps4 = k.sb(es, "ns_ps4", [128, 512], F32)
            imp = k.sb(es, "ns_imp", [128, 128], F32); sc1 = k.sb(es, "ns_sc1", [128, 128], F32); sc2 = k.sb(es, "ns_sc2", [128, 128], F32)
            m8a = k.sb(es, "ns_m8a", [128, 8], F32); m8b = k.sb(es, "ns_m8b", [128, 8], F32)
            mk = [k.sb(es, "ns_mk%d" % i, [128, 128], F32) for i in range(2)]
            mkb = k.sb(es, "ns_mkb", [128, 128], BF16)
            mkT = [k.sb(es, "ns_mkT%d" % i, [128, 128], BF16) for i in range(2)]
            Ef = k.sb(es, "ns_Ef", [128, T], F32)
            Eb = k.sb(es, "ns_Eb", [128, T], BF16)
            s.I("pool", lambda e: e.memset(Ef[:], 1.0), w=[Ef.name])
            s.I("pool", lambda e: e.affine_select(out=Ef[:], in_=Ef[:], pattern=[[1, T]], compare_op=ALU.is_ge, fill=0.0,
                                                  base=0, channel_multiplier=-64), r=[Ef.name], w=[Ef.name])
            s.I("pool", lambda e: e.affine_select(out=Ef[:], in_=Ef[:], pattern=[[-1, T]], compare_op=ALU.is_ge, fill=0.0,
                                                  base=63, channel_multiplier=64), r=[Ef.name], w=[Ef.name])
            s.I("pool", lambda e: e.tensor_copy(out=Eb[:], in_=Ef[:]), r=[Ef.name], w=[Eb.name])
            rsc = [k.sb(es, "ns_rsc%d" % i, [128, 4], F32) for i in range(2)]
            rss = [k.sb(es, "ns_rss%d" % i, [128, 4, 32], F32) for i in range(2)]
            rsw = [k.sb(es, "ns_rsw%d" % i, [128, 4, 2], F32) for i in range(2)]
            rt = [k.sb(es, "ns_rt%d" % i, [128, 4, 4], F32) for i in range(2)]
            oacc = k.sb(es, "ns_oacc", [128, 256], F32)
            ob = [k.sb(es, "ns_ob%d" % i, [128, 256], BF16) for i in range(2)]
            cn = {"o": 0}

            def bc64(ap2):
                return bass.AP(ap2.tensor, ap2.offset, [list(ap2.ap[0]), list(ap2.ap[1]), [0, 64]])

            for g in range(4):
                k.LD(KsT[0:64, :], sc["ksT"][g, :, :], ["pjout"], [KsT]); k.LD(KwT[0:64, :], sc["kwT"][g, :, :], ["pjout"], [KwT])
                k.LD(Vs[:], sc["Vs"][:, g * 64:(g + 1) * 64].rearrange("(n p) d -> p n d", p=128), ["pjout"], [Vs])
                k.LD(Vw[:], sc["Vw"][:, g * 64:(g + 1) * 64].rearrange("(n p) d -> p n d", p=128), ["pjout"], [Vw])
                for hl in range(4):
                    h = 4 * g + hl
                    k.LD(TS_[:, hl, :], bass.AP(SKS.tensor, h * 128 * (WS + 1) + 127, [[WS, 128], [1, 1024]]), ["SK"], [TS_])
                    k.LD(TW_[:, hl, :], bass.AP(SKW.tensor, h * 128 * (WW + 1) + 127, [[WW, 128], [1, 640]]), ["SK"], [TW_])
                    k.LD(TC_[:, hl, :], bass.AP(SKS.tensor, h * 128 * (WS + 1) + 238, [[WS, 128], [16, 58]]), ["SK"], [TC_],
                         allow_slow_non_contiguous=True)
                items = []
                for n in range(NQ // 2, NQ):
                    q0 = 128 * n; par = n % 2
                    ncv = min(NC, 8 * n + 7)
                    blk = []
                    for hl in range(4):
                        blk.append(dict(kind="c", n=n, hl=hl, wdt=ncv))
                    w0 = max(0, q0 - 512)
                    pieces = []
                    a_ = w0
                    while a_ < q0 + 128:
                        b_ = min(a_ + 512, q0 + 128)
                        if a_ < k.OWN < b_:
                            b_ = k.OWN
                        pieces.append((a_, b_ - a_)); a_ = b_
                    assert len(pieces) <= 2
                    for hl in range(4):
                        for pi_, (s0, wdt) in enumerate(pieces):
                            blk.append(dict(kind="w", n=n, hl=hl, s0=s0, wdt=wdt, pi=pi_, first=pi_ == 0, last=pi_ == len(pieces) - 1,
                                            npc=len(pieces)))
                    nch = n // 4 + 1
                    for hl in range(4):
                        for c in range(nch):
                            s0 = 512 * c
                            blk.append(dict(kind="s", n=n, hl=hl, s0=s0, wdt=min(512, q0 + 128 - s0), c=c, first=c == 0, last=c == nch - 1, nch=nch))
                    blk[0]["load"] = True
                    blk[3]["topk"] = True
                    blk[-1]["combine"] = True
                    blk[-1]["npc_w"] = len(pieces)
                    items += blk
                for i, it in enumerate(items):
                    it["i"] = i

                def stageA(it):
                    n = it["n"]; hl = it["hl"]; h = 4 * g + hl; q0 = 128 * n; par = n % 2; i = it["i"]
                    q_ = qT4[par]; g_ = gt[par]
                    if it.get("load"):
                        k.LD(q_[0:64, :, :], sc["qT"][4 * g:4 * g + 4, :, q0:q0 + 128].rearrange("h d t -> d h t"), ["pjout"], [q_])
                        k.LD(g_[:], sc["gates"][q0:q0 + 128, :], ["pjout"], [g_])
                    p = psb[SB_[i % 3]]; t_ = tmp[i % NBUF]; Pt = P_[i % NBUF]
                    wdt = it["wdt"]
                    if it["kind"] == "c":
                        ncv = wdt
                        i_lo = max(0, 8 * n - 51); c_lo = i_lo - 8 * n + 51
                        k.MM(p[:, 0:ncv], q_[:, hl, :], KcT[:, g, 0:ncv], True, True, [q_, KcT], [p])
                        k.TT("dve", t_[:, 0:ncv], p[:, 0:ncv], cmpm[:, 0:ncv], ALU.add, [p, cmpm], [t_])
                        k.TT("pool", t_[:, i_lo:ncv], t_[:, i_lo:ncv], TC_[:, hl, c_lo:c_lo + ncv - i_lo], ALU.add, [t_, TC_], [t_])
                        rk = rt[par].name + str(hl)
                        k.ACT(t_[:, 0:ncv], t_[:, 0:ncv], AF.Exp, [t_, b31], [t_, rsc[par].name + str(hl)], bias=b31[:, h:h + 1],
                              accum_out=rsc[par][:, hl:hl + 1])
                        k.TS("dve", rt[par][:, hl, 0:1], rsc[par][:, hl:hl + 1], 1e-30, None, ALU.max, None, [rsc[par].name + str(hl)], [rk])
                        s.I("dve", lambda e, hl=hl, par=par: e.reciprocal(out=rt[par][:, hl, 0:1], in_=rt[par][:, hl, 0:1]), [rk], [rk])
                        k.TS("dve", pcn[:, hl, 0:ncv], t_[:, 0:ncv], rt[par][:, hl, 0:1], None, ALU.mult, None, [t_, rk], [pcn])
                        k.CP("pool", Pt[:, 0:ncv], pcn[:, hl, 0:ncv], [pcn], [Pt])
                    elif it["kind"] == "w":
                        s0 = it["s0"]
                        k.MM(p[:, 0:wdt], q_[:, hl, :], KwT[:, s0:s0 + wdt], True, True, [q_, KwT], [p])
                        v_lo = s0 - q0 + 512
                        if s0 < k.OWN:
                            k.STT(t_[:, 0:wdt], p[:, 0:wdt], wmk[:, 0:1], TW_[:, hl, v_lo:v_lo + wdt], ALU.add, ALU.add, [p, wmk, TW_], [t_])
                        else:
                            k.TT("dve", t_[:, 0:wdt], p[:, 0:wdt], TW_[:, hl, v_lo:v_lo + wdt], ALU.add, [p, TW_], [t_])
                        k.ACT(Pt[:, 0:wdt], t_[:, 0:wdt], AF.Exp, [t_], [Pt, rsw[par].name + "%d_%d" % (hl, it["pi"])], accum_out=rsw[par][:, hl, it["pi"]:it["pi"] + 1])
                    else:
                        s0 = it["s0"]; c = it["c"]
                        k.MM(p[:, 0:wdt], q_[:, hl, :], KsT[:, s0:s0 + wdt], True, False, [q_, KsT], [p])
                        k.MM(p[:, 0:wdt], mkT[par][:], Eb[:, s0:s0 + wdt], False, True, [mkT[par], Eb], [p])
                        s_lo = max(s0, q0 - 896)
                        if s_lo < s0 + wdt:
                            v_lo = s_lo - q0 + 896
                            ln = s0 + wdt - s_lo
                            k.TT("dve", t_[:, s_lo - s0: wdt], p[:, s_lo - s0: wdt], TS_[:, hl, v_lo:v_lo + ln], ALU.add, [p, TS_], [t_])
                            if s_lo > s0:
                                k.CP("dve", t_[:, 0:s_lo - s0], p[:, 0:s_lo - s0], [p], [t_])
                            k.ACT(Pt[:, 0:wdt], t_[:, 0:wdt], AF.Exp, [t_, b31], [Pt, rss[par].name + "%d_%d" % (hl, c)], bias=b31[:, h:h + 1],
                                  accum_out=rss[par][:, hl, c:c + 1])
                        else:
                            k.ACT(Pt[:, 0:wdt], p[:, 0:wdt], AF.Exp, [p, b31], [Pt, rss[par].name + "%d_%d" % (hl, c)], bias=b31[:, h:h + 1],
                                  accum_out=rss[par][:, hl, c:c + 1])
                    if it.get("topk"):
                        k.TT("dve", ps4[:], pcn[:, 0, :], pcn[:, 1, :], ALU.add, [pcn], [ps4])
                        k.TT("dve", ps4[:], ps4[:], pcn[:, 2, :], ALU.add, [pcn, ps4], [ps4])
                        k.TT("dve", ps4[:], ps4[:], pcn[:, 3, :], ALU.add, [pcn, ps4], [ps4])
                        s.I("dve", lambda e: e.tensor_reduce(out=imp[:], in_=ps4[:].rearrange("p (j m) -> p j m", m=4), axis=AX.X, op=ALU.add),
                            [ps4.name], [imp.name])
                        k.TT("dve", imp[:, 1:128], imp[:, 1:128], ps4[:, 3:508:4], ALU.add, [imp, ps4], [imp])
                        so = 127 - 2 * n
                        k.TT("dve", sc1[:], imp[:], vmrel[:, so:so + 128], ALU.mult, [imp, vmrel], [sc1])
                        k.TT("dve", sc1[:], sc1[:], acrel[:, so:so + 128], ALU.add, [sc1, acrel], [sc1])
                        k.TT("dve", sc1[:], sc1[:], ccm[:], ALU.mult, [sc1, ccm], [sc1])
                        k.TT("dve", sc1[:], sc1[:], cca[:], ALU.add, [sc1, cca], [sc1])
                        s.I("dve", lambda e: e.max(out=m8a[:], in_=sc1[:]), [sc1.name], [m8a.name])
                        s.I("dve", lambda e: e.match_replace(out=sc2[:], in_to_replace=m8a[:], in_values=sc1[:], imm_value=-3e4),
                            [sc1.name, m8a.name], [sc2.name])
                        s.I("dve", lambda e: e.max(out=m8b[:], in_=sc2[:]), [sc2.name], [m8b.name])
                        k.TS("dve", mk[par][:], sc1[:], m8b[:, 7:8], BIG, ALU.is_ge, ALU.mult, [sc1, m8b], [mk[par]])
                        k.TS("dve", mkb[:], mk[par][:], -BIG, None, ALU.add, None, [mk[par]], [mkb])
                        pst = psb[2 + i % 2]; pvm = pst[:].bitcast(BF16)
                        k.TR(pvm[:, 0:128], mkb[:], [mkb], [pst])
                        k.CP("dve", mkT[par][:], pvm[:, 0:128], [pst], [mkT[par]])

                def stageB(it):
                    i = it["i"]; Pt = P_[i % NBUF]; ptile = pT[i % NBUF]; wdt = it["wdt"]
                    pst = psb[2 + i % 2]; pv = pst[:].bitcast(BF16)
                    nk = (wdt + 127) // 128
                    for kt in range(nk):
                        rows = min(128, wdt - kt * 128)
                        k.TR(pv[0:rows, kt * 128:(kt + 1) * 128], Pt[:, kt * 128: kt * 128 + rows], [Pt], [pst])
                    cn["o"] += 1
                    eng = "act" if cn["o"] % 2 else "dve"
                    if wdt % 128 == 0:
                        k.CP(eng, ptile[:, 0:nk, :], pv[:, 0:nk * 128].rearrange("p (a b) -> p a b", b=128), [pst], [ptile])
                    else:
                        for kt in range(nk):
                            rows = min(128, wdt - kt * 128)
                            k.CP(eng, ptile[0:rows, kt, :], pv[0:rows, kt * 128:(kt + 1) * 128], [pst], [ptile])

                def stageC(it):
                    n = it["n"]; hl = it["hl"]; par = n % 2; i = it["i"]; ptile = pT[i % NBUF]; wdt = it["wdt"]
                    nk = (wdt + 127) // 128
                    pcs = psb[4 + par]
                    if it["kind"] == "c":
                        po = pcs[:, hl * 64:(hl + 1) * 64]; pkey = "po_cs%d" % par
                        for kt in range(nk):
                            rows = min(128, wdt - kt * 128)
                            k.MM(po, ptile[0:rows, kt, :], Vc[0:rows, kt, g, :], kt == 0, kt == nk - 1, [ptile, Vc], [pkey])
                    elif it["kind"] == "w":
                        po = psb[6][:, par * 256 + hl * 64: par * 256 + (hl + 1) * 64]; pkey = "po_w%d" % par
                        kt0 = it["s0"] // 128
                        for kt in range(nk):
                            k.MM(po, ptile[:, kt, :], Vw[:, kt0 + kt, :], it["first"] and kt == 0, it["last"] and kt == nk - 1, [ptile, Vw], [pkey])
                    else:
                        po = pcs[:, 256 + hl * 64: 256 + (hl + 1) * 64]; pkey = "po_cs%d" % par
                        kt0 = it["s0"] // 128
                        for kt in range(nk):
                            k.MM(po, ptile[:, kt, :], Vs[:, kt0 + kt, :], it["first"] and kt == 0, it["last"] and kt == nk - 1, [ptile, Vs], [pkey])
                    if it.get("combine"):
                        q0 = 128 * n; g_ = gt[par]
                        o_ = ob[par]
                        for h2 in range(4):
                            col = g * 12 + h2 * 3
                            rk = rt[par].name + str(h2)
                            nch = n // 4 + 1
                            npc = it["npc_w"]
                            s.I("dve", lambda e, h2=h2, nch=nch, par=par: e.tensor_reduce(out=rt[par][:, h2, 1:2], in_=rss[par][:, h2, 0:nch], axis=AX.X, op=ALU.add),
                                [rss[par].name + "%d_%d" % (h2, c_) for c_ in range(nch)], [rk])
                            s.I("dve", lambda e, h2=h2, npc=npc, par=par: e.tensor_reduce(out=rt[par][:, h2, 2:3], in_=rsw[par][:, h2, 0:npc], axis=AX.X, op=ALU.add),
                                [rsw[par].name + "%d_%d" % (h2, c_) for c_ in range(npc)], [rk])
                            s.I("dve", lambda e, h2=h2, par=par: e.reciprocal(out=rt[par][:, h2, 1:3], in_=rt[par][:, h2, 1:3]), [rk], [rk])
                            k.TT("dve", rt[par][:, h2, 1:3], rt[par][:, h2, 1:3], g_[:, col + 1:col + 3], ALU.mult, [rk, g_], [rk])
                            hs = slice(h2 * 64, (h2 + 1) * 64)
                            k.TS("dve", oacc[:, hs], pcs[:, h2 * 64:(h2 + 1) * 64], g_[:, col:col + 1], None, ALU.mult, None, ["po_cs%d" % par, g_], [oacc])
                            k.STT(oacc[:, hs], pcs[:, 256 + h2 * 64: 256 + (h2 + 1) * 64], rt[par][:, h2, 1:2], oacc[:, hs], ALU.mult, ALU.add,
                                  ["po_cs%d" % par, rk, oacc], [oacc])
                            k.STT(o_[:, hs], psb[6][:, par * 256 + h2 * 64: par * 256 + (h2 + 1) * 64], rt[par][:, h2, 2:3], oacc[:, hs], ALU.mult, ALU.add,
                                  ["po_w%d" % par, rk, oacc], [o_])
                        k.LD(sc["onsa"][q0:q0 + 128, g * 256:(g + 1) * 256], o_[:], [o_], ["onsa"])

                N = len(items)
                for t in range(N + LAGC):
                    if t < N:
                        stageA(items[t])
                    if 0 <= t - LAGB < N:
                        stageB(items[t - LAGB])
                    if 0 <= t - LAGC < N:
                        stageC(items[t - LAGC])
            s.barrier()

    def merge(k, sc, w, h1, h2, gpost_ap, psb):
        s = k.s; T = k.T
        with ExitStack() as es:
            gpost = k.sb(es, "mg_g", [128, D], F32)
            k.LD(gpost[:], gpost_ap.partition_broadcast(128), [], [gpost])
            yT = k.sb(es, "mg_yT", [128, 8, 512], BF16)
            oT = k.sb(es, "mg_oT", [128, 8, 512], BF16)
            zT = k.sb(es, "mg_zT", [128, 16, 512], BF16)
            ot = [k.sb(es, "mg_ot%d" % i, [128, 1024], BF16) for i in range(2)]
            wb = [k.sb(es, "mg_wb%d" % i, [128, 16, 512], BF16) for i in range(3)]
            wbi = [0]
            f1 = k.sb(es, "mg_f1", [128, 4, D], F32)
            bufs = k.normbufs(es, "mg_")
            ga = [k.sb(es, "mg_ga%d" % i, [128, 512], BF16) for i in range(2)]
            gb = [k.sb(es, "mg_gb%d" % i, [128, 512], BF16) for i in range(2)]
            t1 = [k.sb(es, "mg_t1%d" % i, [128, 512], F32) for i in range(2)]
            t2 = [k.sb(es, "mg_t2%d" % i, [128, 512], F32) for i in range(2)]
            ci = 0
            for tt in range(k.NT // 2, k.NT):
                t0 = tt * 512
                k.LD(yT[:], sc["ysT"][:, t0:t0 + 512].rearrange("(a p) t -> p a t", p=128), ["ysT"], [yT])
                for sub in range(4):
                    o_ = ot[sub % 2]
                    k.LD(o_[:], sc["onsa"][t0 + sub * 128:t0 + sub * 128 + 128, :], ["onsa"], [o_])
                    pst = psb[6 + sub % 2]; pv = pst[:].bitcast(BF16)
                    for j in range(8):
                        k.TR(pv[:, j * 128:(j + 1) * 128], o_[:, j * 128:(j + 1) * 128], [o_], [pst])
                    k.CP("act" if sub % 2 else "dve", oT[:, :, sub * 128:(sub + 1) * 128], pv.rearrange("p (a b) -> p a b", a=8), [pst], [oT])
                for c4 in range(4):
                    wl = []
                    for nm_ in ("glu1", "glu2", "wo"):
                        wc = wb[wbi[0] % 3]; wbi[0] += 1
                        k.LD(wc[:, 0:8, :], w[nm_][:, c4 * 512:(c4 + 1) * 512].rearrange("(a p) c -> p a c", p=128), ["mw"], [wc])
                        wl.append(wc)
                    for j in range(4):
                        ct = c4 * 4 + j
                        pa, pb, pc_ = psb[(ci * 3) % 6], psb[(ci * 3 + 1) % 6], psb[(ci * 3 + 2) % 6]
                        ga_, gb_, t1_, t2_ = ga[ci % 2], gb[ci % 2], t1[ci % 2], t2[ci % 2]; ci += 1
                        k.LD(ga_[:], sc["gaT"][ct * 128:(ct + 1) * 128, t0:t0 + 512], ["pjout"], [ga_])
                        k.LD(gb_[:], sc["gbT"][ct * 128:(ct + 1) * 128, t0:t0 + 512], ["pjout"], [gb_])
                        for (pp, wc, src_) in ((pa, wl[0], yT), (pb, wl[1], yT), (pc_, wl[2], oT)):
                            for kk in range(8):
                                k.MM(pp[:], wc[:, kk, j * 128:(j + 1) * 128], src_[:, kk, :], kk == 0, kk == 7, [wc, src_], [pp])
                        k.ACT(t1_[:], pb[:], AF.Sigmoid, [pb], [t1_])
                        k.TT("dve", t1_[:], pa[:], t1_[:], ALU.mult, [pa, t1_], [t1_])
                        k.TT("pool", t1_[:], t1_[:], ga_[:], ALU.mult, [t1_, ga_], [t1_])
                        k.TT("dve", t2_[:], pc_[:], gb_[:], ALU.mult, [pc_, gb_], [t2_])
                        k.TT("pool", zT[:, ct, :], t1_[:], t2_[:], ALU.add, [t1_, t2_], [zT])
                k.down_post("mg", zT, 16, w["wout"], "mw", h1, "h1", h2, "h2", gpost, 1.0, t0, f1, wb, wbi, bufs, psb)
            s.barrier()

    def sincos(k, es, tag, ang, shape, want_cos):
        s = k.s
        I32 = mybir.dt.int32
        t = k.sb(es, tag + "t", shape, F32); ti = k.sb(es, tag + "ti", shape, I32)
        r = k.sb(es, tag + "r", shape, F32); m = k.sb(es, tag + "m", shape, F32)
        o = k.sb(es, tag + "o", shape, F32)
        a2 = ang
        if want_cos:
            a2 = k.sb(es, tag + "a2", shape, F32)
            s.I("dve", lambda e: e.tensor_scalar(out=a2[:], in0=ang[:], scalar1=math.pi / 2, scalar2=None, op0=ALU.add),
                r=[ang.name], w=[a2.name])
        s.I("dve", lambda e: e.tensor_scalar(out=t[:], in0=a2[:], scalar1=1.0 / (2 * math.pi), scalar2=None, op0=ALU.mult),
            r=[a2.name], w=[t.name])
        s.I("dve", lambda e: e.tensor_copy(out=ti[:], in_=t[:]), r=[t.name], w=[ti.name])
        s.I("dve", lambda e: e.tensor_copy(out=t[:], in_=ti[:]), r=[ti.name], w=[t.name])
        s.I("dve", lambda e: e.scalar_tensor_tensor(out=r[:], in0=t[:], scalar=-2 * math.pi, in1=a2[:],
                                                    op0=ALU.mult, op1=ALU.add), r=[t.name, a2.name], w=[r.name])
        for (thr, op, fix) in ((math.pi, ALU.is_gt, -2 * math.pi), (-math.pi, ALU.is_lt, 2 * math.pi)):
            s.I("dve", lambda e, thr=thr, op=op, fix=fix: e.tensor_scalar(
                out=m[:], in0=r[:], scalar1=thr, scalar2=fix, op0=op, op1=ALU.mult), r=[r.name], w=[m.name])
            s.I("dve", lambda e: e.tensor_tensor(out=r[:], in0=r[:], in1=m[:], op=ALU.add),
                r=[r.name, m.name], w=[r.name])
        s.I("dve", lambda e: e.tensor_scalar(out=r[:], in0=r[:], scalar1=-math.pi, scalar2=math.pi,
                                             op0=ALU.max, op1=ALU.min), r=[r.name], w=[r.name])
        s.I("act", lambda e: e.activation(out=o[:], in_=r[:], func=AF.Sin), r=[r.name], w=[o.name])
        return o

    def disc(k, es, tag, are, aim, ldt, shape):
        s = k.s
        dt = k.sb(es, tag + "dt", shape, F32); lam = k.sb(es, tag + "lam", shape, F32)
        mag = k.sb(es, tag + "mag", shape, F32); ang = k.sb(es, tag + "ang", shape, F32)
        abr = k.sb(es, tag + "abr", shape, F32); abi = k.sb(es, tag + "abi", shape, F32)
        s.I("act", lambda e: e.activation(out=dt[:], in_=ldt[:], func=AF.Exp), r=[ldt.name], w=[dt.name])
        s.I("dve", lambda e: e.tensor_scalar(out=lam[:], in0=are[:], scalar1=-1e-4, scalar2=None, op0=ALU.min),
            r=[are.name], w=[lam.name])
        s.I("dve", lambda e: e.tensor_tensor(out=mag[:], in0=lam[:], in1=dt[:], op=ALU.mult),
            r=[lam.name, dt.name], w=[mag.name])
        s.I("act", lambda e: e.activation(out=mag[:], in_=mag[:], func=AF.Exp), r=[mag.name], w=[mag.name])
        s.I("dve", lambda e: e.tensor_tensor(out=ang[:], in0=aim[:], in1=dt[:], op=ALU.mult),
            r=[aim.name, dt.name], w=[ang.name])
        sn = k.sincos(es, tag + "s", ang, shape, False)
        cs = k.sincos(es, tag + "c", ang, shape, True)
        s.I("dve", lambda e: e.tensor_tensor(out=abr[:], in0=mag[:], in1=cs[:], op=ALU.mult),
            r=[mag.name, cs.name], w=[abr.name])
        s.I("dve", lambda e: e.tensor_tensor(out=abi[:], in0=mag[:], in1=sn[:], op=ALU.mult),
            r=[mag.name, sn.name], w=[abi.name])
        return abr, abi, lam

    def s5(k, usT, ysT, psb):
        s = k.s; T = k.T
        LM = int(round(math.log2(T)))
        NLV = LM
        tt = lambda e, **kw: e.tensor_tensor(**kw)
        with ExitStack() as es:
            a2r = k.sb(es, "a2r", [128, 64], F32); a2i = k.sb(es, "a2i", [128, 64], F32); l2 = k.sb(es, "l2", [128, 64], F32)
            for t_, n_ in ((a2r, "ssm_aT_re2"), (a2i, "ssm_aT_im2"), (l2, "ssm_ldt2")):
                s.DMA("sp", t_[:], k.din(n_, [128, 64]), w=[t_.name])
            sgn = k.sb(es, "sgn", [128, 1], F32); s.DMA("sp", sgn[:], k.din("c_sgn", [128, 1]), w=[sgn.name])
            mk8 = k.sb(es, "mk8", [128, 8], F32); s.DMA("sp", mk8[:], k.din("c_mask8", [128, 8]), w=[mk8.name])
            Jm = k.sb(es, "Jm", [128, 128], F32); s.DMA("sp", Jm[:], k.din("c_J", [128, 128]), w=[Jm.name])
            Id = k.sb(es, "Idf", [128, 128], F32); s.DMA("sp", Id[:], k.din("c_I", [128, 128]), w=[Id.name])
            dsk = k.sb(es, "dsk", [128, 8], F32); s.DMA("sp", dsk[:], k.din("ssm_dT", [128, 8]), w=[dsk.name])
            PR = k.sb(es, "PR", [128, NLV, 64], F32); PI = k.sb(es, "PI", [128, NLV, 64], F32)
            with ExitStack() as e2:
                abr, abi, _ = k.disc(e2, "d1", a2r, a2i, l2, [128, 64])
                s.I("dve", lambda e, abr=abr: e.tensor_copy(out=PR[:, 0, :], in_=abr[:]), r=[abr.name], w=[PR.name])
                s.I("dve", lambda e, abi=abi: e.tensor_copy(out=PI[:, 0, :], in_=abi[:]), r=[abi.name], w=[PI.name])
                t1 = k.sb(e2, "pw1", [128, 64], F32); t2 = k.sb(e2, "pw2", [128, 64], F32)
                for i in range(NLV - 1):
                    s.I("dve", lambda e, i=i: tt(e, out=t1[:], in0=PR[:, i, :], in1=PR[:, i, :], op=ALU.mult), r=[PR.name], w=[t1.name])
                    s.I("dve", lambda e, i=i: tt(e, out=t2[:], in0=PI[:, i, :], in1=PI[:, i, :], op=ALU.mult), r=[PI.name], w=[t2.name])
                    s.I("dve", lambda e, i=i: tt(e, out=PR[:, i + 1, :], in0=t1[:], in1=t2[:], op=ALU.subtract), r=[t1.name, t2.name], w=[PR.name])
                    s.I("dve", lambda e, i=i: tt(e, out=t1[:], in0=PR[:, i, :], in1=PI[:, i, :], op=ALU.mult), r=[PR.name, PI.name], w=[t1.name])
                    s.I("dve", lambda e, i=i: e.tensor_scalar(out=PI[:, i + 1, :], in0=t1[:], scalar1=2.0, scalar2=None, op0=ALU.mult), r=[t1.name], w=[PI.name])
                s.I("dve", lambda e: e.tensor_scalar(out=PI[:], in0=PI[:], scalar1=sgn[:], scalar2=None, op0=ALU.mult), r=[PI.name, sgn.name], w=[PI.name])
                s.barrier()
            Bb = k.sb(es, "Bb", [128, 8, 128], F32)
            with ExitStack() as e2:
                sh = [128, 8 * 64]
                ar = k.sb(e2, "b_ar", sh, F32); ai = k.sb(e2, "b_ai", sh, F32); ld = k.sb(e2, "b_ld", sh, F32)
                br = k.sb(e2, "b_br", sh, F32); bi = k.sb(e2, "b_bi", sh, F32)
                for t_, n_ in ((ar, "ssm_a_re_b"), (ai, "ssm_a_im_b"), (ld, "ssm_ldt_b"), (br, "ssm_b_re_b"), (bi, "ssm_b_im_b")):
                    s.DMA("sp", t_[:], k.din(n_, sh), w=[t_.name])
                abr, abi, lam = k.disc(e2, "d2", ar, ai, ld, sh)
                den = k.sb(e2, "den", sh, F32); u1 = k.sb(e2, "u1", sh, F32); u2 = k.sb(e2, "u2", sh, F32)
                cor = k.sb(e2, "cor", sh, F32); coi = k.sb(e2, "coi", sh, F32)
                D_ = lambda fn, r, w: s.I("dve", fn, r=[x.name for x in r], w=[x.name for x in w])
                D_(lambda e: tt(e, out=den[:], in0=lam[:], in1=lam[:], op=ALU.mult), [lam], [den])
                D_(lambda e: tt(e, out=u1[:], in0=ai[:], in1=ai[:], op=ALU.mult), [ai], [u1])
                D_(lambda e: tt(e, out=den[:], in0=den[:], in1=u1[:], op=ALU.add), [den, u1], [den])
                D_(lambda e: e.reciprocal(out=den[:], in_=den[:]), [den], [den])
                D_(lambda e: e.tensor_scalar(out=abr[:], in0=abr[:], scalar1=-1.0, scalar2=None, op0=ALU.add), [abr], [abr])
                D_(lambda e: tt(e, out=u1[:], in0=abr[:], in1=lam[:], op=ALU.mult), [abr, lam], [u1])
                D_(lambda e: tt(e, out=u2[:], in0=abi[:], in1=ai[:], op=ALU.mult), [abi, ai], [u2])
                D_(lambda e: tt(e, out=u1[:], in0=u1[:], in1=u2[:], op=ALU.add), [u1, u2], [u1])
                D_(lambda e: tt(e, out=cor[:], in0=u1[:], in1=den[:], op=ALU.mult), [u1, den], [cor])
                D_(lambda e: tt(e, out=u1[:], in0=abi[:], in1=lam[:], op=ALU.mult), [abi, lam], [u1])
                D_(lambda e: tt(e, out=u2[:], in0=abr[:], in1=ai[:], op=ALU.mult), [abr, ai], [u2])
                D_(lambda e: tt(e, out=u1[:], in0=u1[:], in1=u2[:], op=ALU.subtract), [u1, u2], [u1])
                D_(lambda e: tt(e, out=coi[:], in0=u1[:], in1=den[:], op=ALU.mult), [u1, den], [coi])
                v3 = lambda t_: t_[:].rearrange("p (q x) -> p q x", q=8)
                D_(lambda e: tt(e, out=u1[:], in0=cor[:], in1=br[:], op=ALU.mult), [cor, br], [u1])
                D_(lambda e: tt(e, out=u2[:], in0=coi[:], in1=bi[:], op=ALU.mult), [coi, bi], [u2])
                D_(lambda e: tt(e, out=Bb[:, :, 0:64], in0=v3(u1), in1=v3(u2), op=ALU.subtract), [u1, u2], [Bb])
                D_(lambda e: tt(e, out=u1[:], in0=cor[:], in1=bi[:], op=ALU.mult), [cor, bi], [u1])
                D_(lambda e: tt(e, out=u2[:], in0=coi[:], in1=br[:], op=ALU.mult), [coi, br], [u2])
                D_(lambda e: tt(e, out=Bb[:, :, 64:128], in0=v3(u1), in1=v3(u2), op=ALU.add), [u1, u2], [Bb])
                s.barrier()
            Ct = k.sb(es, "Ct", [128, 64, 16], F32)
            s.DMA("sp", Ct[:], k.din("ssm_cT2", [128, 64, 16]), w=[Ct.name])
            s.I("dve", lambda e: e.tensor_scalar(out=Ct[:], in0=Ct[:], scalar1=sgn[:], scalar2=None, op0=ALU.mult),
                r=[Ct.name, sgn.name], w=[Ct.name])
            NG = 3
            OW = k.OWN
            us = k.sb(es, "us32", [128, T], F32)
            yacc = k.sb(es, "yacc", [128, T - OW], F32)
            Aa = [k.sb(es, "Ast%d" % i, [128, T], F32) for i in range(NG)]
            Mt = [k.sb(es, "Mt%d" % i, [128, NLV, 128], F32) for i in range(NG)]
            BmP = k.sb(es, "BmP", [128, 8, 128], F32)
            CmP = k.sb(es, "CmP", [128, 8, 128], F32)
            gl = [k.sb(es, "gl%d" % i, [128, 512], F32) for i in range(3)]
            yb = [k.sb(es, "yb%d" % i, [128, 512], BF16) for i in range(2)]
            pc = [0]

            def nps():
                p = psb[pc[0] % 8]; pc[0] += 1
                return p
            ev = [0]

            def evac_copy(dst, src, rk, wk):
                ev[0] += 1
                if ev[0] % 2:
                    s.I("act", lambda e: e.copy(out=dst, in_=src), r=rk, w=wk)
                else:
                    s.I("dve", lambda e: e.tensor_copy(out=dst, in_=src), r=rk, w=wk)

            def group_steps(q, j, slot):
                g = q * 8 + j
                A = Aa[slot]; M = Mt[slot]
                for i in range(NLV):
                    s.I("pool", lambda e, i=i: e.tensor_scalar(out=M[:, i, :], in0=Id[:], scalar1=PR[:, i, g:g + 1],
                                                               scalar2=None, op0=ALU.mult), r=[Id.name, PR.name], w=[M.name])
                    s.I("dve", lambda e, i=i: e.scalar_tensor_tensor(out=M[:, i, :], in0=Jm[:], scalar=PI[:, i, g:g + 1],
                                                                     in1=M[:, i, :], op0=ALU.mult, op1=ALU.add),
                        r=[Jm.name, PI.name, M.name], w=[M.name])
                yield
                for c0 in range(0, T, 512):
                    p = nps()
                    k.MM(p[:], BmP[:, j, :], us[:, c0:c0 + 512], True, True, [BmP, us], [p])
                    evac_copy(A[:, c0:c0 + 512], p[:], [p.name], [A.name])
                    yield
                for l in range(1, LM + 1):
                    st = 1 << l; h = st >> 1; n = T // st
                    for c0 in range(0, n, 512):
                        m_ = min(512, n - c0)
                        src = A[:, h - 1 + c0 * st: h - 1 + (c0 + m_ - 1) * st + 1: st]
                        dst = A[:, st - 1 + c0 * st: st - 1 + (c0 + m_ - 1) * st + 1: st]
                        p = nps()
                        k.MM(p[:, 0:m_], M[:, l - 1, :], src, True, True, [M, A], [p])
                        k.TT("dve", dst, p[:, 0:m_], dst, ALU.add, [p, A], [A])
                        yield
                for l in range(LM - 1, 0, -1):
                    st = 1 << l; h = st >> 1; n = T // st
                    lo_i = max(0, n // 2 - 1)
                    for c0 in range(lo_i, n - 1, 512):
                        m_ = min(512, n - 1 - c0)
                        src = A[:, st - 1 + c0 * st: st - 1 + (c0 + m_ - 1) * st + 1: st]
                        dst = A[:, st + h - 1 + c0 * st: st + h - 1 + (c0 + m_ - 1) * st + 1: st]
                        p = nps()
                        k.MM(p[:, 0:m_], M[:, l - 1, :], src, True, True, [M, A], [p])
                        k.TT("dve", dst, p[:, 0:m_], dst, ALU.add, [p, A], [A])
                        yield
                for c0 in range(OW, T, 512):
                    p = nps()
                    k.MM(p[:], CmP[:, j, :], A[:, c0:c0 + 512], True, True, [CmP, A], [p])
                    if j == 0:
                        evac_copy(yacc[:, c0 - OW:c0 - OW + 512], p[:], [p.name], [yacc.name])
                    else:
                        k.TT("dve", yacc[:, c0 - OW:c0 - OW + 512], p[:], yacc[:, c0 - OW:c0 - OW + 512], ALU.add, [p, yacc], [yacc])
                    yield

            for q in range(8):
                s.DMA("pool", us[:], usT[q * 128:(q + 1) * 128, :], r=["usT"], w=[us.name])
                for j in range(8):
                    s.I("dve", lambda e, j=j, q=q: e.tensor_scalar(out=BmP[:, j, :], in0=Bb[:, q, :], scalar1=mk8[:, j:j + 1],
                                                              scalar2=None, op0=ALU.mult), r=[Bb.name, mk8.name], w=[BmP.name])
                s.I("pool", lambda e: e.memset(CmP[:], 0.0), w=[CmP.name])
                for j in range(8):
                    s.I("pool", lambda e, j=j, q=q: e.tensor_copy(out=CmP[:, j, j * 16:(j + 1) * 16], in_=Ct[:, q * 8 + j, :]),
                        r=[Ct.name], w=[CmP.name])
                j0 = 0
                while j0 < 8:
                    js = list(range(j0, min(8, j0 + NG)))
                    gens = [group_steps(q, j, si) for si, j in enumerate(js)]
                    alive = True
                    while alive:
                        alive = False
                        for gen in gens:
                            try:
                                next(gen); alive = True
                            except StopIteration:
                                pass
                    j0 += len(js)
                for ci, c0 in enumerate(range(k.OWN, T, 512)):
                    y_ = gl[0]; a_ = gl[1]; b_ = gl[2]; o_ = yb[ci % 2]
                    s.I("dve", lambda e, c0=c0, q=q: e.scalar_tensor_tensor(out=y_[:], in0=us[:, c0:c0 + 512], scalar=dsk[:, q:q + 1],
                                                                        in1=yacc[:, c0 - k.OWN:c0 - k.OWN + 512], op0=ALU.mult, op1=ALU.add),
                        r=[us.name, dsk.name, yacc.name], w=[y_.name])
                    s.I("act", lambda e: e.activation(out=a_[:], in_=y_[:], func=AF.Square), r=[y_.name], w=[a_.name])
                    s.I("dve", lambda e: e.tensor_scalar(out=a_[:], in0=a_[:], scalar1=0.044715, scalar2=1.0, op0=ALU.mult, op1=ALU.add),
                        r=[a_.name], w=[a_.name])
                    s.I("dve", lambda e: tt(e, out=b_[:], in0=a_[:], in1=y_[:], op=ALU.mult), r=[a_.name, y_.name], w=[b_.name])
                    s.I("act", lambda e: e.activation(out=b_[:], in_=b_[:], func=AF.Sigmoid, scale=1.5957691216057308), r=[b_.name], w=[b_.name])
                    s.I("dve", lambda e, o_=o_: tt(e, out=o_[:], in0=b_[:], in1=y_[:], op=ALU.mult), r=[b_.name, y_.name], w=[o_.name])
                    d = s.DMA("sp", ysT[q * 128:(q + 1) * 128, c0:c0 + 512], o_[:], r=[o_.name], w=["ysT"])
                    k.final.append(d)
            s.barrier()

    def build(k):
        nc = k.nc; s = k.s; es = k.es; T = k.T
        x = k.din("x", [T, D])
        out = k.nc.dram_tensor("out", [T if k.stages < 3 else T // 2, D], F32, kind="ExternalOutput").ap()
        g = {n: k.din(n, [1, D]) for n in ("ffn1_pre_g", "ffn1_post_g", "mix_pre_g", "mix_post_g",
                                           "ffn2_pre_g", "ffn2_post_g")}
        w1g = k.conv("ffn1_w_gate", [D, DFF], 4)
        w1u = k.conv("ffn1_w_up", [D, DFF], 4)
        w1d = k.conv("ffn1_w_down", [DFF, D], 4)
        psb = [es.enter_context(nc.psum_tensor("psb%d" % i, [128, 512], F32)) for i in range(8)]
        k.epsb = k.sb(es, "epsb", [128, 1], F32)
        s.I("dve", lambda e: e.memset(k.epsb[:], EPS), w=[k.epsb.name])
        identf = k.sb(es, "identf", [128, 128], F32)
        k.ident = k.sb(es, "ident", [128, 128], BF16)
        s.I("pool", lambda e: e.memset(identf[:], 1.0), w=[identf.name])
        s.I("pool", lambda e: e.affine_select(out=identf[:], in_=identf[:], pattern=[[-1, 128]],
                                              compare_op=ALU.is_equal, fill=0.0, base=0, channel_multiplier=1),
            r=[identf.name], w=[identf.name])
        s.I("dve", lambda e: e.tensor_copy(out=k.ident[:], in_=identf[:]), r=[identf.name], w=["ident"])
        if k.stages == 1:
            k.ffn("f1", x, out, g["ffn1_pre_g"][0, :], g["ffn1_post_g"][0, :], w1g, w1u, w1d, psb)
        if k.stages == 2:
            usT = k.din("usT", [1024, T])
            ysT = k.nc.dram_tensor("ysT", [1024, T], BF16, kind="ExternalOutput").ap()
            k.s5(usT, ysT, psb)
        if k.stages >= 3:
            w = {}
            win = k.conv("w_in", [D, INC], 4)
            w["glu1"] = k.conv("ssm_glu_w1", [1024, D], 2); w["glu2"] = k.conv("ssm_glu_w2", [1024, D], 2)
            w["wo"] = k.conv("nsa_w_o", [1024, D], 2); w["wout"] = k.conv("w_out", [D, D], 2)
            w["ck1"] = k.conv("cmp_k_w1", [2048, 256]); w["ck2"] = k.conv("cmp_k_w2", [256, 64])
            w["cv1"] = k.conv("cmp_v_w1", [2048, 256]); w["cv2"] = k.conv("cmp_v_w2", [256, 64])
            w2g = k.conv("ffn2_w_gate", [D, DFF], 4); w2u = k.conv("ffn2_w_up", [D, DFF], 4); w2d = k.conv("ffn2_w_down", [DFF, D], 4)
            sc = {}
            for nm_, shp, dt_ in (("h1", [T, D], F32), ("h2", [T, D], F32), ("usT", [1024, T], F32), ("ysT", [1024, T], BF16),
                                  ("qT", [16, 64, T], BF16), ("kcT", [4, 64, T], BF16), ("vcT", [4, 64, T], BF16),
                                  ("ksT", [4, 64, T], BF16), ("kwT", [4, 64, T], BF16), ("Vs", [T, 256], BF16), ("Vw", [T, 256], BF16),
                                  ("gates", [T, 48], F32), ("gaT", [D, T], BF16), ("gbT", [D, T], BF16), ("onsa", [T, 1024], BF16)):
                if k.debug:
                    sc[nm_] = k.nc.dram_tensor("dbg_" + nm_, list(shp), dt_, kind="ExternalOutput").ap()
                else:
                    sc[nm_] = k.dscr("sc_" + nm_, shp, dt_)
            k.ffn("f1", x, sc["h1"], g["ffn1_pre_g"][0, :], g["ffn1_post_g"][0, :], w1g, w1u, w1d, psb, "x", "h1",
                  wkeys=("ffn1_w_gate_b", "ffn1_w_up_b", "ffn1_w_down_b"))
            k.proj(sc["h1"], g["mix_pre_g"][0, :], win, sc, psb)
            k.s5(sc["usT"], sc["ysT"], psb)
            with ExitStack() as em:
                KcT, Vc = k.compress(em, sc, w, psb)
                SKS, SKW, WS, WW = k.bias_tables(psb)
                k.nsa(sc, KcT, Vc, SKS, SKW, WS, WW, psb)
            k.merge(sc, w, sc["h1"], sc["h2"], g["mix_post_g"][0, :], psb)
            k.ffn("f2", sc["h2"], out, g["ffn2_pre_g"][0, :], g["ffn2_post_g"][0, :], w2g, w2u, w2d, psb, "h2", "out",
                  tiles=range(k.NT // 2, k.NT), dst_off=k.OWN)
        s.finish(k.final)
        es.close()
        return nc


def ssm_layouts(a_re, a_im, log_dt, b_re, b_im, c_re, c_im, d):
    a_re = np.asarray(a_re, np.float32); a_im = np.asarray(a_im, np.float32); log_dt = np.asarray(log_dt, np.float32)
    b_re = np.asarray(b_re, np.float32); b_im = np.asarray(b_im, np.float32)
    c_re = np.asarray(c_re, np.float32); c_im = np.asarray(c_im, np.float32); d = np.asarray(d, np.float32)
    m = {}
    m["ssm_aT_re2"] = np.concatenate([a_re.T, a_re.T], 0)
    m["ssm_aT_im2"] = np.concatenate([a_im.T, a_im.T], 0)
    m["ssm_ldt2"] = np.broadcast_to(log_dt[None, :], (128, 64))
    def lay_a(a):
        t = a.reshape(8, 8, 64)
        t = np.transpose(t, (1, 0, 2))
        return np.broadcast_to(t[:, None], (8, 16, 8, 64)).reshape(128, 512)
    m["ssm_a_re_b"] = lay_a(a_re); m["ssm_a_im_b"] = lay_a(a_im)
    m["ssm_ldt_b"] = lay_a(np.broadcast_to(log_dt[:, None], (64, 64)))
    def lay_b(b):
        t = b.reshape(8, 8, 64, 16)
        t = np.transpose(t, (1, 3, 0, 2))
        return t.reshape(128, 512)
    m["ssm_b_re_b"] = lay_b(b_re); m["ssm_b_im_b"] = lay_b(b_im)
    m["ssm_cT2"] = np.concatenate([np.transpose(c_re, (2, 0, 1)), np.transpose(c_im, (2, 0, 1))], 0)
    m["ssm_dT"] = d.reshape(8, 128).T
    m["c_sgn"] = np.concatenate([np.ones((64, 1)), -np.ones((64, 1))], 0)
    m["c_mask8"] = (np.arange(128)[:, None] // 16 == np.arange(8)[None, :]).astype(np.float32)
    m["c_I"] = np.eye(128)
    m["c_J"] = np.roll(np.eye(128), 64, axis=1)
    return {k_: np.ascontiguousarray(v, dtype=np.float32) for k_, v in m.items()}


def _bucket(d):
    d = np.maximum(d, 0)
    d_f = np.maximum(d, 1).astype(np.float32)
    large = 16 + (np.log(d_f / np.float32(16)) / np.float32(math.log(1024 / 16)) * np.float32(16)).astype(np.int32)
    large = np.minimum(large, 31)
    return np.where(d < 16, d, large)


def nsa_consts():
    m = {}
    WS, WW = 1152, 768
    x = np.arange(WS); d = 1023 - x
    oh = np.zeros((33, WS), np.float32)
    bk = _bucket(d)
    for b in range(32):
        oh[b] = ((d >= 0) & (bk == b))
    oh[32] = np.where(d < 0, -BIG, 0.0)
    m["c_ohs"] = oh
    x = np.arange(WW); d = 639 - x
    oh = np.zeros((33, WW), np.float32)
    bk = _bucket(d)
    ok = (d >= 0) & (d < 512)
    for b in range(32):
        oh[b] = (ok & (bk == b))
    oh[32] = np.where(ok, 0.0, -BIG)
    m["c_ohw"] = oh
    vm = np.zeros((128, 256), np.float32); ac = np.zeros((128, 256), np.float32)
    c = np.arange(256)
    for qi in range(128):
        hi = qi >= 64
        forced = (c == 127) | ((c == 128) if hi else (c == 126))
        invalid = (c > 128) | ((c == 128) & (not hi))
        vm[qi] = (~forced & ~invalid)
        ac[qi] = np.where(forced, 1e4, np.where(invalid, -1e4, 0.0))
    m["c_vmrel"] = vm; m["c_acrel"] = ac
    return m


def percore_consts(T, half):
    m = {}
    nbp = T // 128
    cm = np.ones((128, 128), np.float32); ca = np.zeros((128, 128), np.float32)
    if half == 0:
        cm[:, 0:nbp] = 0.0; ca[:, 0:nbp] = -2e4
        cm[:, nbp] = 0.0; ca[:, nbp] = 1e4
    else:
        cm[:, 0] = 0.0; ca[:, 0] = 1e4
    m["pc_cm"] = cm; m["pc_ca"] = ca
    cmpm = np.zeros((128, 512), np.float32)
    if half == 0:
        cmpm[:, 0:T // 32] = -BIG
    m["pc_cmpmask"] = cmpm
    m["pc_wmask"] = np.full((128, 1), -BIG if half == 0 else 0.0, np.float32)
    return m


def host_inputs(inputs, names):
    m = {}
    sq = lambda n: np.asarray(inputs[n], np.float32)[0]
    lay = ssm_layouts(sq("ssm_a_re"), sq("ssm_a_im"), sq("ssm_log_dt"), sq("ssm_b_re"), sq("ssm_b_im"),
                      sq("ssm_c_re"), sq("ssm_c_im"), sq("ssm_d"))
    lay.update(nsa_consts())
    lay["cmp_posT"] = np.ascontiguousarray(sq("cmp_pos").T)
    for n in names:
        if n == "x":
            continue
        if n in lay:
            m[n] = np.ascontiguousarray(lay[n], dtype=np.float32)
        elif n == "rel_bias":
            m[n] = np.ascontiguousarray(np.asarray(inputs[n], np.float32))
        elif n.endswith("_g"):
            m[n] = np.ascontiguousarray(np.asarray(inputs[n], np.float32).reshape(1, D))
        else:
            m[n] = np.ascontiguousarray(sq(n))
    return m

_CACHE = {}


def _get(T, stages, debug=False):
    key = (T, stages, debug)
    if key not in _CACHE:
        kb = K(T, stages, debug)
        kb.build()
        _CACHE[key] = kb
    return _CACHE[key]


def run(inputs, T, stages, ncores, debug=False):
    kb = _get(T, stages, debug)
    xs = np.asarray(inputs["x"], np.float32)
    in_maps = []
    if stages >= 3:
        shared = host_inputs(inputs, [n for n in kb.inp if n != "x" and not n.startswith("pc_")])
        pcs = [percore_consts(T, 0), percore_consts(T, 1)]
        for c in range(ncores):
            b, half = c // 2, c % 2
            m = dict(shared)
            m.update(pcs[half])
            xb = xs[b % xs.shape[0]].reshape(T, D)
            if half == 0:
                xl = np.concatenate([np.zeros((T // 2, D), np.float32), xb[:T // 2]], 0)
            else:
                xl = xb
            m["x"] = np.ascontiguousarray(xl)
            in_maps.append(m)
    else:
        for c in range(ncores):
            m = {}
            for name in kb.inp:
                if name == "x":
                    continue
                a = np.asarray(inputs[name], np.float32)
                m[name] = np.ascontiguousarray(a.reshape(kb.inp[name].shape))
            m["x"] = np.ascontiguousarray(xs[c % xs.shape[0]].reshape(T, D))
            in_maps.append(m)
    res = run_bass_kernel_spmd(kb.nc, in_maps, core_ids=list(range(ncores)))
    if debug:
        return res.results
    return [r["out"] for r in res.results]


def kernel(**inputs):
    outs = run(inputs, SEQ, 3, 8)
    full = np.empty((NB, SEQ, D), np.float32)
    for c in range(8):
        b, half = c // 2, c % 2
        full[b, half * (SEQ // 2):(half + 1) * (SEQ // 2)] = outs[c]
    return full
```

```python
import math
from contextlib import ExitStack
import numpy as np
import ml_dtypes
import concourse.bass as bass
import concourse.mybir as mybir
from concourse.bass_utils import run_bass_kernel_spmd

F32 = mybir.dt.float32
BF16 = mybir.dt.bfloat16
ALU = mybir.AluOpType
AF = mybir.ActivationFunctionType
AX = mybir.AxisListType

D = 2048
DFF = 5632
EPS = 1e-6
NB = 4
SEQ = 8192
INC = 7728
BIG = 30000.0


class Ins:
    __slots__ = ("eng", "fn", "deps", "dma", "idx", "sig", "val", "semi", "prewait")

    def __init__(s, eng, fn, dma):
        s.eng = eng; s.fn = fn; s.dma = dma; s.deps = []; s.idx = 0; s.sig = False; s.val = 0
        s.semi = 0; s.prewait = None


class Sch:
    ENGS = ["pe", "dve", "act", "pool", "sp"]
    NS = 10

    def __init__(s, nc, es):
        s.nc = nc
        s.es = es
        s.eobj = {"pe": nc.tensor, "dve": nc.vector, "act": nc.scalar, "pool": nc.gpsimd, "sp": nc.sync}
        s.prog = {e: [] for e in s.ENGS}
        s.csem = {e: es.enter_context(nc.semaphore("cs_" + e)) for e in s.ENGS}
        s.dsem = {e: [es.enter_context(nc.semaphore("ds_%s%d" % (e, i))) for i in range(s.NS)]
                  for e in ("sp", "pool", "act")}
        s.ndma = {e: 0 for e in ("sp", "pool", "act")}
        s.dmas = {e: [] for e in ("sp", "pool", "act")}
        s.lastw = {}
        s.readers = {}
        s.known = {e: {f: -1 for f in s.ENGS} for e in s.ENGS}
        s.seen = {e: set() for e in s.ENGS}
        s.bar_from = {e: 0 for e in ("sp", "pool", "act")}

    def _dep(s, ins, d):
        if d is None or d is ins:
            return
        if d.dma:
            if id(d) in s.seen[ins.eng]:
                return
            s.seen[ins.eng].add(id(d))
            ins.deps.append(d)
        else:
            if d.eng == "pe" and ins.eng == "pe" and not ins.dma:
                return
            if s.known[ins.eng][d.eng] >= d.idx:
                return
            s.known[ins.eng][d.eng] = d.idx
            ins.deps.append(d)

    def _emit(s, eng, fn, r, w, dma):
        ins = Ins(eng, fn, dma)
        ins.idx = len(s.prog[eng])
        for k in list(r) + list(w):
            s._dep(ins, s.lastw.get(k))
        for k in w:
            best = {}
            for d in s.readers.get(k, ()):
                if d.dma:
                    s._dep(ins, d)
                elif d.eng not in best or best[d.eng].idx < d.idx:
                    best[d.eng] = d
            for d in best.values():
                s._dep(ins, d)
        if dma:
            n = s.ndma[eng]; s.ndma[eng] += 1
            ins.semi = n % s.NS; ins.val = (n // s.NS + 1) * 16
            if n >= s.NS:
                ins.prewait = s.dmas[eng][n - s.NS]
            s.dmas[eng].append(ins)
            ins.sig = True
        s.prog[eng].append(ins)
        for k in w:
            s.lastw[k] = ins; s.readers[k] = []
        for k in r:
            s.readers.setdefault(k, []).append(ins)
        return ins

    def I(s, eng, fn, r=(), w=()):
        return s._emit(eng, fn, r, w, False)

    def barrier(s):
        lasts = []
        for e in s.ENGS:
            for ins in reversed(s.prog[e]):
                if not ins.dma:
                    lasts.append(ins); break
        dm = [d for e in s.dmas for d in s.dmas[e][s.bar_from[e]:]]
        for e in s.dmas:
            s.bar_from[e] = len(s.dmas[e])
        for e in s.ENGS:
            ins = Ins(e, lambda eng: eng.nop(), False)
            ins.idx = len(s.prog[e])
            for d in lasts + dm:
                s._dep(ins, d)
            s.prog[e].append(ins)

    def DMA(s, eng, out, in_, r=(), w=(), **kw):
        return s._emit(eng, lambda e: e.dma_start(out=out, in_=in_, **kw), r, w, True)

    def finish(s, final):
        for e in s.ENGS:
            for ins in s.prog[e]:
                for d in ins.deps:
                    d.sig = True
        EPOCH = 30000
        s.csems = {}
        for e in s.ENGS:
            c = 0; ep = 0
            s.csems[e] = [s.csem[e]]
            for ins in s.prog[e]:
                if not ins.dma and ins.sig:
                    if c == EPOCH:
                        c = 0; ep += 1
                        s.csems[e].append(s.es.enter_context(s.nc.semaphore("cs_%s_%d" % (e, ep))))
                    c += 1; ins.val = c; ins.semi = ep
        s.maxval = {e: max([i.val for i in s.prog[e] if not i.dma] + [0]) for e in s.ENGS}

        def semof(d):
            return s.dsem[d.eng][d.semi] if d.dma else s.csems[d.eng][d.semi]

        with s.nc.Block() as block:
            def run(e):
                def body(eng):
                    for ins in s.prog[e]:
                        if ins.prewait is not None:
                            eng.wait_ge(semof(ins.prewait), ins.prewait.val)
                        for d in ins.deps:
                            eng.wait_ge(semof(d), d.val)
                        o = ins.fn(eng)
                        if ins.sig:
                            o.then_inc(semof(ins), 16 if ins.dma else 1)
                    if e == "sp":
                        for d in final:
                            eng.wait_ge(semof(d), d.val)
                return body
            block.tensor(run("pe"))
            block.vector(run("dve"))
            block.scalar(run("act"))
            block.gpsimd(run("pool"))
            block.sync(run("sp"))


def dram_ap(t, off, pat):
    return bass.AP(t.tensor if hasattr(t, "tensor") else t, off, pat)


class K:
    def __init__(k, T, stages=9, debug=False):
        k.T = T
        k.debug = debug
        k.NT = T // 512
        k.OWN = T // 2
        k.stages = stages
        k.nc = nc = bass.Bass("TRN2", target_bir_lowering=False)
        k.es = ExitStack()
        k.s = Sch(nc, k.es)
        k.inp = {}
        k.final = []

    def din(k, name, shape, dt=F32):
        a = k.nc.dram_tensor(name, list(shape), dt, kind="ExternalInput").ap()
        k.inp[name] = a
        return a

    def dscr(k, name, shape, dt):
        return k.nc.dram_tensor(name, list(shape), dt, kind="Internal").ap()

    def sb(k, es, name, shape, dt):
        return es.enter_context(k.nc.sbuf_tensor(name, list(shape), dt))

    def conv(k, name, shape, nsplit=1):
        src = k.din(name, shape)
        dst = k.dscr(name + "_b", shape, BF16)
        rows = shape[0]
        step = rows // nsplit
        for i in range(nsplit):
            k.s.DMA("pool", dst[i * step:(i + 1) * step, :], src[i * step:(i + 1) * step, :], w=[name + "_b"])
        return dst

    def rstd(k, src_ap, junk, ss, rs, key_r, scale_out=1.0):
        s = k.s
        s.I("act", lambda e: e.activation(out=junk[:], in_=src_ap, func=AF.Square, accum_out=ss[:]),
            r=key_r, w=[junk.name, ss.name])
        s.I("act", lambda e: e.activation(out=rs[:], in_=ss[:], func=AF.Sqrt, bias=k.epsb[:],
                                          scale=1.0 / (D * scale_out * scale_out)),
            r=[ss.name], w=[rs.name])
        s.I("dve", lambda e: e.reciprocal(out=rs[:], in_=rs[:]), r=[rs.name], w=[rs.name])

    def ffn(k, tag, src, dst, pre_g, post_g, wg, wu, wd, psb, skey="x", dkey="out", tiles=None, dst_off=0, wkeys=None):
        s = k.s; nc = k.nc
        with ExitStack() as es:
            gpre = k.sb(es, tag + "gpre", [128, D], F32)
            gpost = k.sb(es, tag + "gpost", [128, D], F32)
            s.DMA("sp", gpre[:], pre_g.partition_broadcast(128), w=[gpre.name])
            s.DMA("sp", gpost[:], post_g.partition_broadcast(128), w=[gpost.name])
            kg_, ku_, kd_ = wkeys if wkeys is not None else (tag + "wg", tag + "wu", tag + "wd")
            xT = k.sb(es, tag + "xT", [128, 16, 512], BF16)
            hT = k.sb(es, tag + "hT", [128, 44, 512], BF16)
            wb = [k.sb(es, tag + "wb%d" % i, [128, 16, 256], BF16) for i in range(6)]
            f1 = k.sb(es, tag + "f1", [128, 4, D], F32)
            xs = [k.sb(es, tag + "xs%d" % i, [128, D], F32) for i in range(2)]
            xn = [k.sb(es, tag + "xn%d" % i, [128, D], BF16) for i in range(2)]
            junk = k.sb(es, tag + "junk", [128, D], F32)
            sg = [k.sb(es, tag + "sg%d" % i, [128, 512], F32) for i in range(2)]
            ss = [k.sb(es, tag + "ss%d" % i, [128, 1], F32) for i in range(2)]
            rs = [k.sb(es, tag + "rs%d" % i, [128, 1], F32) for i in range(2)]
            wbi = [0]

            def nextw():
                b = wb[wbi[0] % 6]; wbi[0] += 1
                return b

            xi = 0
            for tt in (tiles if tiles is not None else range(k.NT)):
                t0 = tt * 512
                for sub in range(4):
                    x_ = xs[xi % 2]; xn_ = xn[xi % 2]; ss_ = ss[xi % 2]; rs_ = rs[xi % 2]; xi += 1
                    rows = src[t0 + sub * 128: t0 + sub * 128 + 128, :]
                    s.DMA("sp", x_[:], rows, r=[skey], w=[x_.name])
                    k.rstd(x_[:], junk, ss_, rs_, [x_.name])
                    s.I("dve", lambda e, x_=x_, xn_=xn_, rs_=rs_: e.scalar_tensor_tensor(
                        out=xn_[:], in0=x_[:], scalar=rs_[:], in1=gpre[:], op0=ALU.mult, op1=ALU.mult),
                        r=[x_.name, rs_.name, gpre.name], w=[xn_.name])
                    for half in range(2):
                        pst = psb[6 + half]
                        pv = pst[:].bitcast(BF16)
                        for j in range(8):
                            kk = half * 8 + j
                            s.I("pe", lambda e, pv=pv, xn_=xn_, kk=kk, j=j: e.transpose(
                                out=pv[:, j * 128:(j + 1) * 128], in_=xn_[:, kk * 128:(kk + 1) * 128],
                                identity=k.ident[:]),
                                r=[xn_.name, "ident"], w=[pst.name])
                        eng = "act" if half == 0 else "dve"
                        if eng == "act":
                            s.I("act", lambda e, pv=pv, half=half, sub=sub: e.copy(
                                out=xT[:, half * 8:half * 8 + 8, sub * 128:(sub + 1) * 128],
                                in_=pv.rearrange("p (a b) -> p a b", a=8)),
                                r=[pst.name], w=[xT.name])
                        else:
                            s.I("dve", lambda e, pv=pv, half=half, sub=sub: e.tensor_copy(
                                out=xT[:, half * 8:half * 8 + 8, sub * 128:(sub + 1) * 128],
                                in_=pv.rearrange("p (a b) -> p a b", a=8)),
                                r=[pst.name], w=[xT.name])
                pi = 0
                for cc in range(22):
                    wgc = nextw()
                    s.DMA("sp", wgc[:], wg[:, cc * 256:(cc + 1) * 256].rearrange("(a p) c -> p a c", p=128),
                          r=[kg_], w=[wgc.name])
                    wuc = nextw()
                    s.DMA("sp", wuc[:], wu[:, cc * 256:(cc + 1) * 256].rearrange("(a p) c -> p a c", p=128),
                          r=[ku_], w=[wuc.name])
                    for fs in range(2):
                        f = cc * 2 + fs
                        pg = psb[(pi * 2) % 6]; pu = psb[(pi * 2 + 1) % 6]; sg_ = sg[pi % 2]; pi += 1
                        for kk in range(16):
                            s.I("pe", lambda e, pg=pg, wgc=wgc, kk=kk, fs=fs: e.matmul(
                                pg[:], lhsT=wgc[:, kk, fs * 128:(fs + 1) * 128], rhs=xT[:, kk, :],
                                start=(kk == 0), stop=(kk == 15)), r=[wgc.name, xT.name], w=[pg.name])
                        for kk in range(16):
                            s.I("pe", lambda e, pu=pu, wuc=wuc, kk=kk, fs=fs: e.matmul(
                                pu[:], lhsT=wuc[:, kk, fs * 128:(fs + 1) * 128], rhs=xT[:, kk, :],
                                start=(kk == 0), stop=(kk == 15)), r=[wuc.name, xT.name], w=[pu.name])
                        s.I("act", lambda e, pg=pg, sg_=sg_: e.activation(out=sg_[:], in_=pg[:], func=AF.Silu),
                            r=[pg.name], w=[sg_.name])
                        s.I("dve", lambda e, pu=pu, sg_=sg_, f=f: e.tensor_tensor(
                            out=hT[:, f, :], in0=sg_[:], in1=pu[:], op=ALU.mult),
                            r=[pu.name, sg_.name], w=[hT.name])
                for c4 in range(4):
                    base = 0 if c4 % 2 == 0 else 4
                    f0 = 0
                    for nf in (8, 8, 8, 8, 8, 4):
                        wdt_ = nextw()
                        wdc = wdt_[:].rearrange("p a c -> p (a c)").rearrange("p (a c) -> p a c", c=512)
                        s.DMA("sp", wdc[:, 0:nf, :],
                              wd[f0 * 128:(f0 + nf) * 128, c4 * 512:(c4 + 1) * 512].rearrange("(a p) c -> p a c", p=128),
                              r=[kd_], w=[wdt_.name])
                        for sub in range(4):
                            po = psb[base + sub]
                            for fi in range(nf):
                                f = f0 + fi
                                s.I("pe", lambda e, po=po, wdc=wdc, fi=fi, f=f, sub=sub: e.matmul(
                                    po[:], lhsT=hT[:, f, sub * 128:(sub + 1) * 128], rhs=wdc[:, fi, :],
                                    start=(f == 0), stop=(f == 43)), r=[wdt_.name, hT.name], w=[po.name])
                        f0 += nf
                    for sub in range(4):
                        po = psb[base + sub]
                        if sub % 2 == 0:
                            s.I("act", lambda e, po=po, sub=sub, c4=c4: e.copy(
                                out=f1[:, sub, c4 * 512:(c4 + 1) * 512], in_=po[:]),
                                r=[po.name], w=[f1.name + str(sub)])
                        else:
                            s.I("dve", lambda e, po=po, sub=sub, c4=c4: e.tensor_copy(
                                out=f1[:, sub, c4 * 512:(c4 + 1) * 512], in_=po[:]),
                                r=[po.name], w=[f1.name + str(sub)])
                for sub in range(4):
                    x_ = xs[xi % 2]; ss_ = ss[xi % 2]; rs_ = rs[xi % 2]; xi += 1
                    rows = src[t0 + sub * 128: t0 + sub * 128 + 128, :]
                    s.DMA("sp", x_[:], rows, r=[skey], w=[x_.name])
                    k.rstd(f1[:, sub, :], junk, ss_, rs_, [f1.name + str(sub)], scale_out=0.5)
                    s.I("dve", lambda e, sub=sub, rs_=rs_: e.scalar_tensor_tensor(
                        out=f1[:, sub, :], in0=f1[:, sub, :], scalar=rs_[:], in1=gpost[:],
                        op0=ALU.mult, op1=ALU.mult),
                        r=[f1.name + str(sub), rs_.name, gpost.name], w=[f1.name + str(sub)])
                    s.I("pool", lambda e, sub=sub, x_=x_: e.tensor_tensor(
                        out=x_[:], in0=f1[:, sub, :], in1=x_[:], op=ALU.add),
                        r=[f1.name + str(sub), x_.name], w=[x_.name])
                    d = s.DMA("sp", dst[t0 - dst_off + sub * 128: t0 - dst_off + sub * 128 + 128, :], x_[:], r=[x_.name], w=[dkey])
                    k.final.append(d)
            s.barrier()


    @staticmethod
    def _n(xs):
        return [x if isinstance(x, str) else x.name for x in xs]

    def TT(k, eng, out, in0, in1, op, r, w):
        k.s.I(eng, lambda e: e.tensor_tensor(out=out, in0=in0, in1=in1, op=op), k._n(r), k._n(w))

    def TS(k, eng, out, in0, s1, s2, op0, op1, r, w):
        if op1 is None:
            k.s.I(eng, lambda e: e.tensor_scalar(out=out, in0=in0, scalar1=s1, scalar2=None, op0=op0), k._n(r), k._n(w))
        else:
            k.s.I(eng, lambda e: e.tensor_scalar(out=out, in0=in0, scalar1=s1, scalar2=s2, op0=op0, op1=op1), k._n(r), k._n(w))

    def STT(k, out, in0, scalar, in1, op0, op1, r, w):
        k.s.I("dve", lambda e: e.scalar_tensor_tensor(out=out, in0=in0, scalar=scalar, in1=in1, op0=op0, op1=op1),
              k._n(r), k._n(w))

    def ACT(k, out, in_, func, r, w, **kw):
        k.s.I("act", lambda e: e.activation(out=out, in_=in_, func=func, **kw), k._n(r), k._n(w))

    def MM(k, out, lhsT, rhs, start, stop, r, w):
        k.s.I("pe", lambda e: e.matmul(out, lhsT=lhsT, rhs=rhs, start=start, stop=stop), k._n(r), k._n(w))

    def TR(k, out, in_, r, w):
        k.s.I("pe", lambda e: e.transpose(out=out, in_=in_, identity=k.ident[:]), k._n(r) + ["ident"], k._n(w))

    def CP(k, eng, out, in_, r, w):
        if eng == "act":
            k.s.I("act", lambda e: e.copy(out=out, in_=in_), k._n(r), k._n(w))
        else:
            k.s.I(eng, lambda e: e.tensor_copy(out=out, in_=in_), k._n(r), k._n(w))

    def LD(k, out, in_, r, w, eng="sp", **kw):
        return k.s.DMA(eng, out, in_, k._n(r), k._n(w), **kw)

    def gelu_tanh(k, y_, a_, b_, out_ap, wname):
        k.ACT(a_[:], y_[:], AF.Square, [y_], [a_])
        k.TS("dve", a_[:], a_[:], 0.044715, 1.0, ALU.mult, ALU.add, [a_], [a_])
        k.TT("dve", b_[:], a_[:], y_[:], ALU.mult, [a_, y_], [b_])
        k.ACT(b_[:], b_[:], AF.Sigmoid, [b_], [b_], scale=1.5957691216057308)
        k.TT("dve", out_ap, b_[:], y_[:], ALU.mult, [b_, y_], [wname])

    def prenormT(k, src, skey, t0, gpre, xT, bufs, psb):
        s = k.s
        xs, xn, junk, ss, rs, ctr = bufs
        for sub in range(4):
            i = ctr[0] % 2; ctr[0] += 1
            x_ = xs[i]; xn_ = xn[i]; ss_ = ss[i]; rs_ = rs[i]
            k.LD(x_[:], src[t0 + sub * 128: t0 + sub * 128 + 128, :], [skey], [x_])
            k.rstd(x_[:], junk, ss_, rs_, [x_.name])
            k.STT(xn_[:], x_[:], rs_[:], gpre[:], ALU.mult, ALU.mult, [x_, rs_, gpre], [xn_])
            for half in range(2):
                pst = psb[6 + half]
                pv = pst[:].bitcast(BF16)
                for j in range(8):
                    kk = half * 8 + j
                    k.TR(pv[:, j * 128:(j + 1) * 128], xn_[:, kk * 128:(kk + 1) * 128], [xn_], [pst])
                k.CP("act" if half == 0 else "dve", xT[:, half * 8:half * 8 + 8, sub * 128:(sub + 1) * 128],
                     pv.rearrange("p (a b) -> p a b", a=8), [pst], [xT])

    def down_post(k, tag, hT, KT, wd, wkey, src, skey, dst, dkey, gpost, scale_out, t0, f1, wb, wbi, bufs, psb):
        s = k.s
        xs, xn, junk, ss, rs, ctr = bufs
        groups = []
        f0 = 0
        while f0 < KT:
            nf = min(16, KT - f0); groups.append((f0, nf)); f0 += nf
        for c4 in range(4):
            base = 0 if c4 % 2 == 0 else 4
            for (f0, nf) in groups:
                wdc = wb[wbi[0] % 3]; wbi[0] += 1
                k.LD(wdc[:, 0:nf, :], wd[f0 * 128:(f0 + nf) * 128, c4 * 512:(c4 + 1) * 512].rearrange("(a p) c -> p a c", p=128),
                     [wkey], [wdc])
                for sub in range(4):
                    po = psb[base + sub]
                    for fi in range(nf):
                        f = f0 + fi
                        k.MM(po[:], hT[:, f, sub * 128:(sub + 1) * 128], wdc[:, fi, :], f == 0, f == KT - 1, [wdc, hT], [po])
            for sub in range(4):
                po = psb[base + sub]
                k.CP("act" if sub % 2 == 0 else "dve", f1[:, sub, c4 * 512:(c4 + 1) * 512], po[:], [po], [f1.name + str(sub)])
        for sub in range(4):
            i = ctr[0] % 2; ctr[0] += 1
            x_ = xs[i]; ss_ = ss[i]; rs_ = rs[i]
            k.LD(x_[:], src[t0 + sub * 128: t0 + sub * 128 + 128, :], [skey], [x_])
            k.rstd(f1[:, sub, :], junk, ss_, rs_, [f1.name + str(sub)], scale_out=scale_out)
            k.STT(f1[:, sub, :], f1[:, sub, :], rs_[:], gpost[:], ALU.mult, ALU.mult,
                  [f1.name + str(sub), rs_, gpost], [f1.name + str(sub)])
            k.TT("pool", x_[:], f1[:, sub, :], x_[:], ALU.add, [f1.name + str(sub), x_], [x_])
            d = k.LD(dst[t0 + sub * 128: t0 + sub * 128 + 128, :], x_[:], [x_], [dkey])
            k.final.append(d)

    def normbufs(k, es, tag):
        xs = [k.sb(es, tag + "xs%d" % i, [128, D], F32) for i in range(2)]
        xn = [k.sb(es, tag + "xn%d" % i, [128, D], BF16) for i in range(2)]
        junk = k.sb(es, tag + "junk", [128, D], F32)
        ss = [k.sb(es, tag + "ss%d" % i, [128, 1], F32) for i in range(2)]
        rs = [k.sb(es, tag + "rs%d" % i, [128, 1], F32) for i in range(2)]
        return (xs, xn, junk, ss, rs, [0])

    def proj(k, h1, gmix, win, sc, psb):
        s = k.s; T = k.T
        with ExitStack() as es:
            gpre = k.sb(es, "pj_g", [128, D], F32)
            k.LD(gpre[:], gmix.partition_broadcast(128), [], [gpre])
            uT = k.sb(es, "pj_uT", [128, 16, 512], BF16)
            wb = [k.sb(es, "pj_wb%d" % i, [128, 16, 512], BF16) for i in range(3)]
            bufs = k.normbufs(es, "pj_")
            s32 = [k.sb(es, "pj_s32%d" % i, [128, 512], F32) for i in range(2)]
            s16 = [k.sb(es, "pj_s16%d" % i, [128, 512], BF16) for i in range(4)]
            sg = [k.sb(es, "pj_sg%d" % i, [128, 48], F32) for i in range(2)]
            cnt = {"w": 0, "p": 0, "a": 0, "b": 0, "g": 0, "e": 0}

            def nps():
                p = psb[cnt["p"] % 6]; cnt["p"] += 1
                return p

            chunks = [(c * 512, 512) for c in range(7)] + [(3584, 48)] + [(3632 + 512 * i, 512) for i in range(8)]
            for tt in range(k.NT):
                t0 = tt * 512
                k.prenormT(h1, "h1", t0, gpre, uT, bufs, psb)
                for ci, (c0, wd_) in enumerate(chunks):
                    if t0 < k.OWN and ci not in (0, 1, 4, 5, 6):
                        continue
                    wc = wb[cnt["w"] % 3]; cnt["w"] += 1
                    k.LD(wc[:, :, 0:wd_], win[:, c0:c0 + wd_].rearrange("(a p) c -> p a c", p=128), ["win"], [wc])

                    def fm(off, M, kind, dst):
                        p = nps()
                        for kk in range(16):
                            k.MM(p[0:M, :], wc[:, kk, off:off + M], uT[:, kk, :], kk == 0, kk == 15, [wc, uT], [p])
                        if kind == "us":
                            st = s32[cnt["a"] % 2]; cnt["a"] += 1
                            cnt["e"] += 1
                            k.CP("act" if cnt["e"] % 2 else "dve", st[0:M, :], p[0:M, :], [p], [st])
                        else:
                            st = s16[cnt["b"] % 4]; cnt["b"] += 1
                            if kind == "q":
                                k.s.I("act", lambda e, st=st, p=p, M=M: e.mul(out=st[0:M, :], in_=p[0:M, :], mul=0.125), [p.name], [st.name])
                            elif kind == "sig":
                                k.ACT(st[0:M, :], p[0:M, :], AF.Sigmoid, [p], [st])
                            else:
                                k.CP("dve", st[0:M, :], p[0:M, :], [p], [st])
                        k.LD(dst, st[0:M, :], [st], ["pjout"])

                    def tm(off, N, kind, dst_t):
                        for sub in range(4):
                            p = nps()
                            for kk in range(16):
                                k.MM(p[:, 0:N], uT[:, kk, sub * 128:(sub + 1) * 128], wc[:, kk, off:off + N], kk == 0, kk == 15, [wc, uT], [p])
                            if kind == "sig":
                                st = sg[cnt["g"] % 2]; cnt["g"] += 1
                                k.ACT(st[:, 0:N], p[:, 0:N], AF.Sigmoid, [p], [st])
                            else:
                                st = s16[cnt["b"] % 4]; cnt["b"] += 1
                                k.CP("dve", st[:, 0:N], p[:, 0:N], [p], [st])
                            k.LD(dst_t[t0 + sub * 128: t0 + sub * 128 + 128, :], st[:, 0:N], [st], ["pjout"])

                    ts_ = slice(t0, t0 + 512)
                    if ci in (0, 1):
                        for j in range(4):
                            fm(j * 128, 128, "us", sc["usT"][(ci * 4 + j) * 128:(ci * 4 + j + 1) * 128, ts_])
                    elif ci in (2, 3):
                        for j in range(4):
                            h0 = (ci - 2) * 8 + 2 * j
                            fm(j * 128, 128, "q", sc["qT"][h0:h0 + 2, :, ts_].rearrange("a d t -> (a d) t"))
                    elif ci == 4:
                        for g2 in range(2):
                            fm(g2 * 128, 128, "k", sc["kcT"][2 * g2:2 * g2 + 2, :, ts_].rearrange("a d t -> (a d) t"))
                        for g2 in range(2):
                            fm(256 + g2 * 128, 128, "k", sc["vcT"][2 * g2:2 * g2 + 2, :, ts_].rearrange("a d t -> (a d) t"))
                    elif ci == 5:
                        for g2 in range(2):
                            fm(g2 * 128, 128, "k", sc["ksT"][2 * g2:2 * g2 + 2, :, ts_].rearrange("a d t -> (a d) t"))
                        tm(256, 256, "k", sc["Vs"])
                    elif ci == 6:
                        for g2 in range(2):
                            fm(g2 * 128, 128, "k", sc["kwT"][2 * g2:2 * g2 + 2, :, ts_].rearrange("a d t -> (a d) t"))
                        tm(256, 256, "k", sc["Vw"])
                    elif ci == 7:
                        tm(0, 48, "sig", sc["gates"])
                    elif ci < 12:
                        for j in range(4):
                            fm(j * 128, 128, "sig", sc["gaT"][((ci - 8) * 4 + j) * 128:((ci - 8) * 4 + j + 1) * 128, ts_])
                    else:
                        for j in range(4):
                            fm(j * 128, 128, "sig", sc["gbT"][((ci - 12) * 4 + j) * 128:((ci - 12) * 4 + j + 1) * 128, ts_])
            s.barrier()


    def compress(k, es_out, sc, w, psb):
        s = k.s; T = k.T
        NC = T // 16 - 1
        NIT = (NC + 127) // 128
        KcT = k.sb(es_out, "KcT", [128, 4, 512], BF16)
        Vc = k.sb(es_out, "Vc", [128, 4, 4, 64], BF16)
        s.I("pool", lambda e: e.memset(KcT[:], 0.0), w=[KcT.name])
        with ExitStack() as es:
            raw = k.sb(es, "cp_raw", [64, T], BF16)
            w1 = k.sb(es, "cp_w1", [64, 32, 256], BF16)
            w2 = k.sb(es, "cp_w2", [128, 2, 64], BF16)
            posf = k.sb(es, "cp_posf", [64, 32], F32)
            posb = k.sb(es, "cp_posb", [64, 32], BF16)
            bias = k.sb(es, "cp_bias", [128, 2], F32)
            y_ = k.sb(es, "cp_y", [128, 512], F32); a_ = k.sb(es, "cp_a", [128, 512], F32); b_ = k.sb(es, "cp_b", [128, 512], F32)
            hid = k.sb(es, "cp_hid", [128, 2, 512], BF16)
            k.LD(posf[:], k.din("cmp_posT", [64, 32]), [], [posf])
            k.CP("dve", posb[:], posf[:], [posf], [posb])
            for kind in range(2):
                rawsc = sc["kcT"] if kind == 0 else sc["vcT"]
                w1d, w2d = (w["ck1"], w["ck2"]) if kind == 0 else (w["cv1"], w["cv2"])
                k.LD(w1[:], w1d.rearrange("(l d) c -> d l c", d=64), ["cw"], [w1])
                k.LD(w2[:], w2d.rearrange("(a p) d -> p a d", p=128), ["cw"], [w2])
                for ht in range(2):
                    p = psb[ht]
                    for l in range(32):
                        k.MM(p[:, 0:1], w1[:, l, ht * 128:(ht + 1) * 128], posb[:, l:l + 1], l == 0, l == 31, [w1, posb], [p])
                    k.CP("dve", bias[:, ht:ht + 1], p[:, 0:1], [p], [bias])
                for g in range(4):
                    k.LD(raw[:], rawsc[g, :, :], ["pjout"], [raw])
                    for ht in range(2):
                        p = psb[2 + ht]
                        for l in range(32):
                            k.MM(p[:, 0:NC], w1[:, l, ht * 128:(ht + 1) * 128], raw[:, l: l + 16 * (NC - 1) + 1: 16],
                                 l == 0, l == 31, [w1, raw], [p])
                        k.TS("dve", y_[:, 0:NC], p[:, 0:NC], bias[:, ht:ht + 1], None, ALU.add, None, [p, bias], [y_])
                        k.gelu_tanh(y_, a_, b_, hid[:, ht, :], hid.name)
                    if kind == 0:
                        p = psb[4]
                        for ht in range(2):
                            k.MM(p[0:64, 0:NC], w2[:, ht, :], hid[:, ht, 0:NC], ht == 0, ht == 1, [w2, hid], [p])
                        k.CP("act", KcT[0:64, g, 0:NC], p[0:64, 0:NC], [p], [KcT])
                    else:
                        for it in range(NIT):
                            rows = min(128, NC - it * 128)
                            p = psb[4 + it % 2]
                            for ht in range(2):
                                k.MM(p[0:rows, 0:64], hid[:, ht, it * 128: it * 128 + rows], w2[:, ht, :], ht == 0, ht == 1, [w2, hid], [p])
                            k.CP("act", Vc[0:rows, it, g, :], p[0:rows, 0:64], [p], [Vc])
            s.barrier()
        return KcT, Vc

    def bias_tables(k, psb):
        s = k.s
        WS, WW = 1152, 768
        SKS = k.dscr("SKS", [16, 128 * (WS + 1)], F32)
        SKW = k.dscr("SKW", [16, 128 * (WW + 1)], F32)
        rb = k.din("rel_bias", [32, 16])
        with ExitStack() as es:
            ohs = k.sb(es, "bt_ohs", [33, WS], F32); ohw = k.sb(es, "bt_ohw", [33, WW], F32)
            k.LD(ohs[:], k.din("c_ohs", [33, WS]), [], [ohs]); k.LD(ohw[:], k.din("c_ohw", [33, WW]), [], [ohw])
            rbw = k.sb(es, "bt_rbw", [33, 16], F32); rbs = k.sb(es, "bt_rbs", [33, 16], F32); r31 = k.sb(es, "bt_r31", [33, 16], F32)
            s.I("dve", lambda e: e.memset(rbw[:], 1.0), w=[rbw.name])
            s.I("dve", lambda e: e.memset(rbs[:], 1.0), w=[rbs.name])
            s.I("dve", lambda e: e.memset(r31[:], 0.0), w=[r31.name])
            k.LD(rbw[0:32, :], rb, [rbw], [rbw])
            k.LD(r31[0:32, :], rb[31, :].partition_broadcast(32), [r31], [r31])
            k.TT("dve", rbs[0:32, :], rbw[0:32, :], r31[0:32, :], ALU.subtract, [rbw, r31], [rbs])
            ones = k.sb(es, "bt_ones", [33, 128], F32)
            s.I("dve", lambda e: e.memset(ones[:], 1.0), w=[ones.name])
            lh = [k.sb(es, "bt_lh%d" % i, [33, 128], F32) for i in range(2)]
            rep = [k.sb(es, "bt_rep%d" % i, [128, WS], F32) for i in range(2)]
            n = 0
            for (rbt, oh, W, SK) in ((rbs, ohs, WS, SKS), (rbw, ohw, WW, SKW)):
                for h in range(16):
                    l_ = lh[n % 2]; r_ = rep[n % 2]; n += 1
                    k.TS("dve", l_[:], ones[:], rbt[:, h:h + 1], None, ALU.mult, None, [ones, rbt], [l_])
                    for c0 in range(0, W, 512):
                        wd_ = min(512, W - c0)
                        p = psb[(c0 // 512) % 4 + 4 * (n % 2)]
                        k.MM(p[:, 0:wd_], l_[:], oh[:, c0:c0 + wd_], True, True, [l_, oh], [p])
                        k.CP("act" if (c0 // 512) % 2 else "dve", r_[:, c0:c0 + wd_], p[:, 0:wd_], [p], [r_])
                    dst = bass.AP(SK.tensor, h * 128 * (W + 1), [[W + 1, 128], [1, W]])
                    k.LD(dst, r_[:, 0:W], [r_], ["SK"])
            s.barrier()
        return SKS, SKW, WS, WW

    def nsa(k, sc, KcT, Vc, SKS, SKW, WS, WW, psb):
        s = k.s; T = k.T
        NC = T // 16 - 1
        NQ = T // 128
        NKT = T // 128
        LAGB, LAGC, NBUF = 4, 8, 12
        SB_ = (0, 1, 7)
        with ExitStack() as es:
            KsT = k.sb(es, "ns_KsT", [128, T], BF16); KwT = k.sb(es, "ns_KwT", [128, T], BF16)
            s.I("pool", lambda e: e.memset(KsT[64:128, :], 0.0), w=[KsT.name])
            s.I("pool", lambda e: e.memset(KwT[64:128, :], 0.0), w=[KwT.name])
            Vs = k.sb(es, "ns_Vs", [128, NKT, 64], BF16); Vw = k.sb(es, "ns_Vw", [128, NKT, 64], BF16)
            TS_ = k.sb(es, "ns_TS", [128, 4, 1024], F32); TW_ = k.sb(es, "ns_TW", [128, 4, 640], F32)
            TC_ = k.sb(es, "ns_TC", [128, 4, 58], F32)
            b31 = k.sb(es, "ns_b31", [128, 16], F32)
            k.LD(b31[:], k.inp["rel_bias"][31, :].partition_broadcast(128), [], [b31])
            vmrel = k.sb(es, "ns_vm", [128, 256], F32); acrel = k.sb(es, "ns_ac", [128, 256], F32)
            k.LD(vmrel[:], k.din("c_vmrel", [128, 256]), [], [vmrel]); k.LD(acrel[:], k.din("c_acrel", [128, 256]), [], [acrel])
            ccm = k.sb(es, "ns_ccm", [128, 128], F32); cca = k.sb(es, "ns_cca", [128, 128], F32)
            cmpm = k.sb(es, "ns_cmpm", [128, 512], F32); wmk = k.sb(es, "ns_wmk", [128, 1], F32)
            k.LD(ccm[:], k.din("pc_cm", [128, 128]), [], [ccm]); k.LD(cca[:], k.din("pc_ca", [128, 128]), [], [cca])
            k.LD(cmpm[:], k.din("pc_cmpmask", [128, 512]), [], [cmpm]); k.LD(wmk[:], k.din("pc_wmask", [128, 1]), [], [wmk])
            qT4 = [k.sb(es, "ns_q%d" % i, [128, 4, 128], BF16) for i in range(2)]
            for q__ in qT4:
                s.I("pool", lambda e, q__=q__: e.memset(q__[:], 0.0), w=[q__.name])
            gt = [k.sb(es, "ns_gt%d" % i, [128, 48], F32) for i in range(2)]
            pcn = k.sb(es, "ns_pcn", [128, 4, 512], F32)
            s.I("pool", lambda e: e.memset(pcn[:], 0.0), w=[pcn.name])
            tmp = [k.sb(es, "ns_tmp%d" % i, [128, 512], F32) for i in range(NBUF)]
            P_ = [k.sb(es, "ns_P%d" % i, [128, 512], BF16) for i in range(NBUF)]
            pT = [k.sb(es, "ns_pT%d" % i, [128, 4, 128], BF16) for i in range(NBUF)]
            ps4 = k.sb(es, "ns_ps4", [128, 512], F32)
            imp = k.sb(es, "ns_imp", [128, 128], F32); sc1 = k.sb(es, "ns_sc1", [128, 128], F32); sc2 = k.sb(es, "ns_sc2", [128, 128], F32)
            m8a = k.sb(es, "ns_m8a", [128, 8], F32); m8b = k.sb(es, "ns_m8b", [128, 8], F32)
            mk = [k.sb(es, "ns_mk%d" % i, [128, 128], F32) for i in range(2)]
            mkb = k.sb(es, "ns_mkb", [128, 128], BF16)
            mkT = [k.sb(es, "ns_mkT%d" % i, [128, 128], BF16) for i in range(2)]
            Ef = k.sb(es, "ns_Ef", [128, T], F32)
            Eb = k.sb(es, "ns_Eb", [128, T], BF16)
            s.I("pool", lambda e: e.memset(Ef[:], 1.0), w=[Ef.name])
            s.I("pool", lambda e: e.affine_select(out=Ef[:], in_=Ef[:], pattern=[[1, T]], compare_op=ALU.is_ge, fill=0.0,
                                                  base=0, channel_multiplier=-64), r=[Ef.name], w=[Ef.name])
            s.I("pool", lambda e: e.affine_select(out=Ef[:], in_=Ef[:], pattern=[[-1, T]], compare_op=ALU.is_ge, fill=0.0,
                                                  base=63, channel_multiplier=64), r=[Ef.name], w=[Ef.name])
            s.I("pool", lambda e: e.tensor_copy(out=Eb[:], in_=Ef[:]), r=[Ef.name], w=[Eb.name])
            rsc = [k.sb(es, "ns_rsc%d" % i, [128, 4], F32) for i in range(2)]
            rss = [k.sb(es, "ns_rss%d" % i, [128, 4, 32], F32) for i in range(2)]
            rsw = [k.sb(es, "ns_rsw%d" % i, [128, 4, 2], F32) for i in range(2)]
            rt = [k.sb(es, "ns_rt%d" % i, [128, 4, 4], F32) for i in range(2)]
            oacc = k.sb(es, "ns_oacc", [128, 256], F32)
            ob = [k.sb(es, "ns_ob%d" % i, [128, 256], BF16) for i in range(2)]
            cn = {"o": 0}

            def bc64(ap2):
                return bass.AP(ap2.tensor, ap2.offset, [list(ap2.ap[0]), list(ap2.ap[1]), [0, 64]])

            for g in range(4):
                k.LD(KsT[0:64, :], sc["ksT"][g, :, :], ["pjout"], [KsT]); k.LD(KwT[0:64, :], sc["kwT"][g, :, :], ["pjout"], [KwT])
                k.LD(Vs[:], sc["Vs"][:, g * 64:(g + 1) * 64].rearrange("(n p) d -> p n d", p=128), ["pjout"], [Vs])
                k.LD(Vw[:], sc["Vw"][:, g * 64:(g + 1) * 64].rearrange("(n p) d -> p n d", p=128), ["pjout"], [Vw])
                for hl in range(4):
                    h = 4 * g + hl
                    k.LD(TS_[:, hl, :], bass.AP(SKS.tensor, h * 128 * (WS + 1) + 127, [[WS, 128], [1, 1024]]), ["SK"], [TS_])
                    k.LD(TW_[:, hl, :], bass.AP(SKW.tensor, h * 128 * (WW + 1) + 127, [[WW, 128], [1, 640]]), ["SK"], [TW_])
                    k.LD(TC_[:, hl, :], bass.AP(SKS.tensor, h * 128 * (WS + 1) + 238, [[WS, 128], [16, 58]]), ["SK"], [TC_],
                         allow_slow_non_contiguous=True)
                items = []
                for n in range(NQ // 2, NQ):
                    q0 = 128 * n; par = n % 2
                    ncv = min(NC, 8 * n + 7)
                    blk = []
                    for hl in range(4):
                        blk.append(dict(kind="c", n=n, hl=hl, wdt=ncv))
                    w0 = max(0, q0 - 512)
                    pieces = []
                    a_ = w0
                    while a_ < q0 + 128:
                        b_ = min(a_ + 512, q0 + 128)
                        if a_ < k.OWN < b_:
                            b_ = k.OWN
                        pieces.append((a_, b_ - a_)); a_ = b_
                    assert len(pieces) <= 2
                    for hl in range(4):
                        for pi_, (s0, wdt) in enumerate(pieces):
                            blk.append(dict(kind="w", n=n, hl=hl, s0=s0, wdt=wdt, pi=pi_, first=pi_ == 0, last=pi_ == len(pieces) - 1,
                                            npc=len(pieces)))
                    nch = n // 4 + 1
                    for hl in range(4):
                        for c in range(nch):
                            s0 = 512 * c
                            blk.append(dict(kind="s", n=n, hl=hl, s0=s0, wdt=min(512, q0 + 128 - s0), c=c, first=c == 0, last=c == nch - 1, nch=nch))
                    blk[0]["load"] = True
                    blk[3]["topk"] = True
                    blk[-1]["combine"] = True
                    blk[-1]["npc_w"] = len(pieces)
                    items += blk
                for i, it in enumerate(items):
                    it["i"] = i

                def stageA(it):
                    n = it["n"]; hl = it["hl"]; h = 4 * g + hl; q0 = 128 * n; par = n % 2; i = it["i"]
                    q_ = qT4[par]; g_ = gt[par]
                    if it.get("load"):
                        k.LD(q_[0:64, :, :], sc["qT"][4 * g:4 * g + 4, :, q0:q0 + 128].rearrange("h d t -> d h t"), ["pjout"], [q_])
                        k.LD(g_[:], sc["gates"][q0:q0 + 128, :], ["pjout"], [g_])
                    p = psb[SB_[i % 3]]; t_ = tmp[i % NBUF]; Pt = P_[i % NBUF]
                    wdt = it["wdt"]
                    if it["kind"] == "c":
                        ncv = wdt
                        i_lo = max(0, 8 * n - 51); c_lo = i_lo - 8 * n + 51
                        k.MM(p[:, 0:ncv], q_[:, hl, :], KcT[:, g, 0:ncv], True, True, [q_, KcT], [p])
                        k.TT("dve", t_[:, 0:ncv], p[:, 0:ncv], cmpm[:, 0:ncv], ALU.add, [p, cmpm], [t_])
                        k.TT("pool", t_[:, i_lo:ncv], t_[:, i_lo:ncv], TC_[:, hl, c_lo:c_lo + ncv - i_lo], ALU.add, [t_, TC_], [t_])
                        rk = rt[par].name + str(hl)
                        k.ACT(t_[:, 0:ncv], t_[:, 0:ncv], AF.Exp, [t_, b31], [t_, rsc[par].name + str(hl)], bias=b31[:, h:h + 1],
                              accum_out=rsc[par][:, hl:hl + 1])
                        k.TS("dve", rt[par][:, hl, 0:1], rsc[par][:, hl:hl + 1], 1e-30, None, ALU.max, None, [rsc[par].name + str(hl)], [rk])
                        s.I("dve", lambda e, hl=hl, par=par: e.reciprocal(out=rt[par][:, hl, 0:1], in_=rt[par][:, hl, 0:1]), [rk], [rk])
                        k.TS("dve", pcn[:, hl, 0:ncv], t_[:, 0:ncv], rt[par][:, hl, 0:1], None, ALU.mult, None, [t_, rk], [pcn])
                        k.CP("pool", Pt[:, 0:ncv], pcn[:, hl, 0:ncv], [pcn], [Pt])
                    elif it["kind"] == "w":
                        s0 = it["s0"]
                        k.MM(p[:, 0:wdt], q_[:, hl, :], KwT[:, s0:s0 + wdt], True, True, [q_, KwT], [p])
                        v_lo = s0 - q0 + 512
                        if s0 < k.OWN:
                            k.STT(t_[:, 0:wdt], p[:, 0:wdt], wmk[:, 0:1], TW_[:, hl, v_lo:v_lo + wdt], ALU.add, ALU.add, [p, wmk, TW_], [t_])
                        else:
                            k.TT("dve", t_[:, 0:wdt], p[:, 0:wdt], TW_[:, hl, v_lo:v_lo + wdt], ALU.add, [p, TW_], [t_])
                        k.ACT(Pt[:, 0:wdt], t_[:, 0:wdt], AF.Exp, [t_], [Pt, rsw[par].name + "%d_%d" % (hl, it["pi"])], accum_out=rsw[par][:, hl, it["pi"]:it["pi"] + 1])
                    else:
                        s0 = it["s0"]; c = it["c"]
                        k.MM(p[:, 0:wdt], q_[:, hl, :], KsT[:, s0:s0 + wdt], True, False, [q_, KsT], [p])
                        k.MM(p[:, 0:wdt], mkT[par][:], Eb[:, s0:s0 + wdt], False, True, [mkT[par], Eb], [p])
                        s_lo = max(s0, q0 - 896)
                        if s_lo < s0 + wdt:
                            v_lo = s_lo - q0 + 896
                            ln = s0 + wdt - s_lo
                            k.TT("dve", t_[:, s_lo - s0: wdt], p[:, s_lo - s0: wdt], TS_[:, hl, v_lo:v_lo + ln], ALU.add, [p, TS_], [t_])
                            if s_lo > s0:
                                k.CP("dve", t_[:, 0:s_lo - s0], p[:, 0:s_lo - s0], [p], [t_])
                            k.ACT(Pt[:, 0:wdt], t_[:, 0:wdt], AF.Exp, [t_, b31], [Pt, rss[par].name + "%d_%d" % (hl, c)], bias=b31[:, h:h + 1],
                                  accum_out=rss[par][:, hl, c:c + 1])
                        else:
                            k.ACT(Pt[:, 0:wdt], p[:, 0:wdt], AF.Exp, [p, b31], [Pt, rss[par].name + "%d_%d" % (hl, c)], bias=b31[:, h:h + 1],
                                  accum_out=rss[par][:, hl, c:c + 1])
                    if it.get("topk"):
                        k.TT("dve", ps4[:], pcn[:, 0, :], pcn[:, 1, :], ALU.add, [pcn], [ps4])
                        k.TT("dve", ps4[:], ps4[:], pcn[:, 2, :], ALU.add, [pcn, ps4], [ps4])
                        k.TT("dve", ps4[:], ps4[:], pcn[:, 3, :], ALU.add, [pcn, ps4], [ps4])
                        s.I("dve", lambda e: e.tensor_reduce(out=imp[:], in_=ps4[:].rearrange("p (j m) -> p j m", m=4), axis=AX.X, op=ALU.add),
                            [ps4.name], [imp.name])
                        k.TT("dve", imp[:, 1:128], imp[:, 1:128], ps4[:, 3:508:4], ALU.add, [imp, ps4], [imp])
                        so = 127 - 2 * n
                        k.TT("dve", sc1[:], imp[:], vmrel[:, so:so + 128], ALU.mult, [imp, vmrel], [sc1])
                        k.TT("dve", sc1[:], sc1[:], acrel[:, so:so + 128], ALU.add, [sc1, acrel], [sc1])
                        k.TT("dve", sc1[:], sc1[:], ccm[:], ALU.mult, [sc1, ccm], [sc1])
                        k.TT("dve", sc1[:], sc1[:], cca[:], ALU.add, [sc1, cca], [sc1])
                        s.I("dve", lambda e: e.max(out=m8a[:], in_=sc1[:]), [sc1.name], [m8a.name])
                        s.I("dve", lambda e: e.match_replace(out=sc2[:], in_to_replace=m8a[:], in_values=sc1[:], imm_value=-3e4),
                            [sc1.name, m8a.name], [sc2.name])
                        s.I("dve", lambda e: e.max(out=m8b[:], in_=sc2[:]), [sc2.name], [m8b.name])
                        k.TS("dve", mk[par][:], sc1[:], m8b[:, 7:8], BIG, ALU.is_ge, ALU.mult, [sc1, m8b], [mk[par]])
                        k.TS("dve", mkb[:], mk[par][:], -BIG, None, ALU.add, None, [mk[par]], [mkb])
                        pst = psb[2 + i % 2]; pvm = pst[:].bitcast(BF16)
                        k.TR(pvm[:, 0:128], mkb[:], [mkb], [pst])
                        k.CP("dve", mkT[par][:], pvm[:, 0:128], [pst], [mkT[par]])

                def stageB(it):
                    i = it["i"]; Pt = P_[i % NBUF]; ptile = pT[i % NBUF]; wdt = it["wdt"]
                    pst = psb[2 + i % 2]; pv = pst[:].bitcast(BF16)
                    nk = (wdt + 127) // 128
                    for kt in range(nk):
                        rows = min(128, wdt - kt * 128)
                        k.TR(pv[0:rows, kt * 128:(kt + 1) * 128], Pt[:, kt * 128: kt * 128 + rows], [Pt], [pst])
                    cn["o"] += 1
                    eng = "act" if cn["o"] % 4 == 0 else "dve"
                    if wdt % 128 == 0:
                        k.CP(eng, ptile[:, 0:nk, :], pv[:, 0:nk * 128].rearrange("p (a b) -> p a b", b=128), [pst], [ptile])
                    else:
                        for kt in range(nk):
                            rows = min(128, wdt - kt * 128)
                            k.CP(eng, ptile[0:rows, kt, :], pv[0:rows, kt * 128:(kt + 1) * 128], [pst], [ptile])

                def stageC(it):
                    n = it["n"]; hl = it["hl"]; par = n % 2; i = it["i"]; ptile = pT[i % NBUF]; wdt = it["wdt"]
                    nk = (wdt + 127) // 128
                    pcs = psb[4 + par]
                    if it["kind"] == "c":
                        po = pcs[:, hl * 64:(hl + 1) * 64]; pkey = "po_cs%d" % par
                        for kt in range(nk):
                            rows = min(128, wdt - kt * 128)
                            k.MM(po, ptile[0:rows, kt, :], Vc[0:rows, kt, g, :], kt == 0, kt == nk - 1, [ptile, Vc], [pkey])
                    elif it["kind"] == "w":
                        po = psb[6][:, par * 256 + hl * 64: par * 256 + (hl + 1) * 64]; pkey = "po_w%d" % par
                        kt0 = it["s0"] // 128
                        for kt in range(nk):
                            k.MM(po, ptile[:, kt, :], Vw[:, kt0 + kt, :], it["first"] and kt == 0, it["last"] and kt == nk - 1, [ptile, Vw], [pkey])
                    else:
                        po = pcs[:, 256 + hl * 64: 256 + (hl + 1) * 64]; pkey = "po_cs%d" % par
                        kt0 = it["s0"] // 128
                        for kt in range(nk):
                            k.MM(po, ptile[:, kt, :], Vs[:, kt0 + kt, :], it["first"] and kt == 0, it["last"] and kt == nk - 1, [ptile, Vs], [pkey])
                    if it.get("combine"):
                        q0 = 128 * n; g_ = gt[par]
                        o_ = ob[par]
                        for h2 in range(4):
                            col = g * 12 + h2 * 3
                            rk = rt[par].name + str(h2)
                            nch = n // 4 + 1
                            npc = it["npc_w"]
                            s.I("dve", lambda e, h2=h2, nch=nch, par=par: e.tensor_reduce(out=rt[par][:, h2, 1:2], in_=rss[par][:, h2, 0:nch], axis=AX.X, op=ALU.add),
                                [rss[par].name + "%d_%d" % (h2, c_) for c_ in range(nch)], [rk])
                            s.I("dve", lambda e, h2=h2, npc=npc, par=par: e.tensor_reduce(out=rt[par][:, h2, 2:3], in_=rsw[par][:, h2, 0:npc], axis=AX.X, op=ALU.add),
                                [rsw[par].name + "%d_%d" % (h2, c_) for c_ in range(npc)], [rk])
                            s.I("dve", lambda e, h2=h2, par=par: e.reciprocal(out=rt[par][:, h2, 1:3], in_=rt[par][:, h2, 1:3]), [rk], [rk])
                            k.TT("dve", rt[par][:, h2, 1:3], rt[par][:, h2, 1:3], g_[:, col + 1:col + 3], ALU.mult, [rk, g_], [rk])
                            hs = slice(h2 * 64, (h2 + 1) * 64)
                            k.TS("dve", oacc[:, hs], pcs[:, h2 * 64:(h2 + 1) * 64], g_[:, col:col + 1], None, ALU.mult, None, ["po_cs%d" % par, g_], [oacc])
                            k.STT(oacc[:, hs], pcs[:, 256 + h2 * 64: 256 + (h2 + 1) * 64], rt[par][:, h2, 1:2], oacc[:, hs], ALU.mult, ALU.add,
                                  ["po_cs%d" % par, rk, oacc], [oacc])
                            k.STT(o_[:, hs], psb[6][:, par * 256 + h2 * 64: par * 256 + (h2 + 1) * 64], rt[par][:, h2, 2:3], oacc[:, hs], ALU.mult, ALU.add,
                                  ["po_w%d" % par, rk, oacc], [o_])
                        k.LD(sc["onsa"][q0:q0 + 128, g * 256:(g + 1) * 256], o_[:], [o_], ["onsa"])

                N = len(items)
                for t in range(N + LAGC):
                    if t < N:
                        stageA(items[t])
                    if 0 <= t - LAGB < N:
                        stageB(items[t - LAGB])
                    if 0 <= t - LAGC < N:
                        stageC(items[t - LAGC])
            s.barrier()

    def merge(k, sc, w, h1, h2, gpost_ap, psb):
        s = k.s; T = k.T
        with ExitStack() as es:
            gpost = k.sb(es, "mg_g", [128, D], F32)
            k.LD(gpost[:], gpost_ap.partition_broadcast(128), [], [gpost])
            yT = k.sb(es, "mg_yT", [128, 8, 512], BF16)
            oT = k.sb(es, "mg_oT", [128, 8, 512], BF16)
            zT = k.sb(es, "mg_zT", [128, 16, 512], BF16)
            ot = [k.sb(es, "mg_ot%d" % i, [128, 1024], BF16) for i in range(2)]
            wb = [k.sb(es, "mg_wb%d" % i, [128, 16, 512], BF16) for i in range(3)]
            wbi = [0]
            f1 = k.sb(es, "mg_f1", [128, 4, D], F32)
            bufs = k.normbufs(es, "mg_")
            ga = [k.sb(es, "mg_ga%d" % i, [128, 512], BF16) for i in range(2)]
            gb = [k.sb(es, "mg_gb%d" % i, [128, 512], BF16) for i in range(2)]
            t1 = [k.sb(es, "mg_t1%d" % i, [128, 512], F32) for i in range(2)]
            t2 = [k.sb(es, "mg_t2%d" % i, [128, 512], F32) for i in range(2)]
            ci = 0
            for tt in range(k.NT // 2, k.NT):
                t0 = tt * 512
                k.LD(yT[:], sc["ysT"][:, t0:t0 + 512].rearrange("(a p) t -> p a t", p=128), ["ysT"], [yT])
                for sub in range(4):
                    o_ = ot[sub % 2]
                    k.LD(o_[:], sc["onsa"][t0 + sub * 128:t0 + sub * 128 + 128, :], ["onsa"], [o_])
                    pst = psb[6 + sub % 2]; pv = pst[:].bitcast(BF16)
                    for j in range(8):
                        k.TR(pv[:, j * 128:(j + 1) * 128], o_[:, j * 128:(j + 1) * 128], [o_], [pst])
                    k.CP("act" if sub % 2 else "dve", oT[:, :, sub * 128:(sub + 1) * 128], pv.rearrange("p (a b) -> p a b", a=8), [pst], [oT])
                for c4 in range(4):
                    wl = []
                    for nm_ in ("glu1", "glu2", "wo"):
                        wc = wb[wbi[0] % 3]; wbi[0] += 1
                        k.LD(wc[:, 0:8, :], w[nm_][:, c4 * 512:(c4 + 1) * 512].rearrange("(a p) c -> p a c", p=128), ["mw"], [wc])
                        wl.append(wc)
                    for j in range(4):
                        ct = c4 * 4 + j
                        pa, pb, pc_ = psb[(ci * 3) % 6], psb[(ci * 3 + 1) % 6], psb[(ci * 3 + 2) % 6]
                        ga_, gb_, t1_, t2_ = ga[ci % 2], gb[ci % 2], t1[ci % 2], t2[ci % 2]; ci += 1
                        k.LD(ga_[:], sc["gaT"][ct * 128:(ct + 1) * 128, t0:t0 + 512], ["pjout"], [ga_])
                        k.LD(gb_[:], sc["gbT"][ct * 128:(ct + 1) * 128, t0:t0 + 512], ["pjout"], [gb_])
                        for (pp, wc, src_) in ((pa, wl[0], yT), (pb, wl[1], yT), (pc_, wl[2], oT)):
                            for kk in range(8):
                                k.MM(pp[:], wc[:, kk, j * 128:(j + 1) * 128], src_[:, kk, :], kk == 0, kk == 7, [wc, src_], [pp])
                        k.ACT(t1_[:], pb[:], AF.Sigmoid, [pb], [t1_])
                        k.TT("dve", t1_[:], pa[:], t1_[:], ALU.mult, [pa, t1_], [t1_])
                        k.TT("pool", t1_[:], t1_[:], ga_[:], ALU.mult, [t1_, ga_], [t1_])
                        k.TT("dve", t2_[:], pc_[:], gb_[:], ALU.mult, [pc_, gb_], [t2_])
                        k.TT("pool", zT[:, ct, :], t1_[:], t2_[:], ALU.add, [t1_, t2_], [zT])
                k.down_post("mg", zT, 16, w["wout"], "mw", h1, "h1", h2, "h2", gpost, 1.0, t0, f1, wb, wbi, bufs, psb)
            s.barrier()

    def sincos(k, es, tag, ang, shape, want_cos):
        s = k.s
        I32 = mybir.dt.int32
        t = k.sb(es, tag + "t", shape, F32); ti = k.sb(es, tag + "ti", shape, I32)
        r = k.sb(es, tag + "r", shape, F32); m = k.sb(es, tag + "m", shape, F32)
        o = k.sb(es, tag + "o", shape, F32)
        a2 = ang
        if want_cos:
            a2 = k.sb(es, tag + "a2", shape, F32)
            s.I("dve", lambda e: e.tensor_scalar(out=a2[:], in0=ang[:], scalar1=math.pi / 2, scalar2=None, op0=ALU.add),
                r=[ang.name], w=[a2.name])
        s.I("dve", lambda e: e.tensor_scalar(out=t[:], in0=a2[:], scalar1=1.0 / (2 * math.pi), scalar2=None, op0=ALU.mult),
            r=[a2.name], w=[t.name])
        s.I("dve", lambda e: e.tensor_copy(out=ti[:], in_=t[:]), r=[t.name], w=[ti.name])
        s.I("dve", lambda e: e.tensor_copy(out=t[:], in_=ti[:]), r=[ti.name], w=[t.name])
        s.I("dve", lambda e: e.scalar_tensor_tensor(out=r[:], in0=t[:], scalar=-2 * math.pi, in1=a2[:],
                                                    op0=ALU.mult, op1=ALU.add), r=[t.name, a2.name], w=[r.name])
        for (thr, op, fix) in ((math.pi, ALU.is_gt, -2 * math.pi), (-math.pi, ALU.is_lt, 2 * math.pi)):
            s.I("dve", lambda e, thr=thr, op=op, fix=fix: e.tensor_scalar(
                out=m[:], in0=r[:], scalar1=thr, scalar2=fix, op0=op, op1=ALU.mult), r=[r.name], w=[m.name])
            s.I("dve", lambda e: e.tensor_tensor(out=r[:], in0=r[:], in1=m[:], op=ALU.add),
                r=[r.name, m.name], w=[r.name])
        s.I("dve", lambda e: e.tensor_scalar(out=r[:], in0=r[:], scalar1=-math.pi, scalar2=math.pi,
                                             op0=ALU.max, op1=ALU.min), r=[r.name], w=[r.name])
        s.I("act", lambda e: e.activation(out=o[:], in_=r[:], func=AF.Sin), r=[r.name], w=[o.name])
        return o

    def disc(k, es, tag, are, aim, ldt, shape):
        s = k.s
        dt = k.sb(es, tag + "dt", shape, F32); lam = k.sb(es, tag + "lam", shape, F32)
        mag = k.sb(es, tag + "mag", shape, F32); ang = k.sb(es, tag + "ang", shape, F32)
        abr = k.sb(es, tag + "abr", shape, F32); abi = k.sb(es, tag + "abi", shape, F32)
        s.I("act", lambda e: e.activation(out=dt[:], in_=ldt[:], func=AF.Exp), r=[ldt.name], w=[dt.name])
        s.I("dve", lambda e: e.tensor_scalar(out=lam[:], in0=are[:], scalar1=-1e-4, scalar2=None, op0=ALU.min),
            r=[are.name], w=[lam.name])
        s.I("dve", lambda e: e.tensor_tensor(out=mag[:], in0=lam[:], in1=dt[:], op=ALU.mult),
            r=[lam.name, dt.name], w=[mag.name])
        s.I("act", lambda e: e.activation(out=mag[:], in_=mag[:], func=AF.Exp), r=[mag.name], w=[mag.name])
        s.I("dve", lambda e: e.tensor_tensor(out=ang[:], in0=aim[:], in1=dt[:], op=ALU.mult),
            r=[aim.name, dt.name], w=[ang.name])
        sn = k.sincos(es, tag + "s", ang, shape, False)
        cs = k.sincos(es, tag + "c", ang, shape, True)
        s.I("dve", lambda e: e.tensor_tensor(out=abr[:], in0=mag[:], in1=cs[:], op=ALU.mult),
            r=[mag.name, cs.name], w=[abr.name])
        s.I("dve", lambda e: e.tensor_tensor(out=abi[:], in0=mag[:], in1=sn[:], op=ALU.mult),
            r=[mag.name, sn.name], w=[abi.name])
        return abr, abi, lam

    def s5(k, usT, ysT, psb):
        s = k.s; T = k.T
        LM = int(round(math.log2(T)))
        NLV = LM
        tt = lambda e, **kw: e.tensor_tensor(**kw)
        with ExitStack() as es:
            a2r = k.sb(es, "a2r", [128, 64], F32); a2i = k.sb(es, "a2i", [128, 64], F32); l2 = k.sb(es, "l2", [128, 64], F32)
            for t_, n_ in ((a2r, "ssm_aT_re2"), (a2i, "ssm_aT_im2"), (l2, "ssm_ldt2")):
                s.DMA("sp", t_[:], k.din(n_, [128, 64]), w=[t_.name])
            sgn = k.sb(es, "sgn", [128, 1], F32); s.DMA("sp", sgn[:], k.din("c_sgn", [128, 1]), w=[sgn.name])
            mk8 = k.sb(es, "mk8", [128, 8], F32); s.DMA("sp", mk8[:], k.din("c_mask8", [128, 8]), w=[mk8.name])
            Jm = k.sb(es, "Jm", [128, 128], F32); s.DMA("sp", Jm[:], k.din("c_J", [128, 128]), w=[Jm.name])
            Id = k.sb(es, "Idf", [128, 128], F32); s.DMA("sp", Id[:], k.din("c_I", [128, 128]), w=[Id.name])
            dsk = k.sb(es, "dsk", [128, 8], F32); s.DMA("sp", dsk[:], k.din("ssm_dT", [128, 8]), w=[dsk.name])
            PR = k.sb(es, "PR", [128, NLV, 64], F32); PI = k.sb(es, "PI", [128, NLV, 64], F32)
            with ExitStack() as e2:
                abr, abi, _ = k.disc(e2, "d1", a2r, a2i, l2, [128, 64])
                s.I("dve", lambda e, abr=abr: e.tensor_copy(out=PR[:, 0, :], in_=abr[:]), r=[abr.name], w=[PR.name])
                s.I("dve", lambda e, abi=abi: e.tensor_copy(out=PI[:, 0, :], in_=abi[:]), r=[abi.name], w=[PI.name])
                t1 = k.sb(e2, "pw1", [128, 64], F32); t2 = k.sb(e2, "pw2", [128, 64], F32)
                for i in range(NLV - 1):
                    s.I("dve", lambda e, i=i: tt(e, out=t1[:], in0=PR[:, i, :], in1=PR[:, i, :], op=ALU.mult), r=[PR.name], w=[t1.name])
                    s.I("dve", lambda e, i=i: tt(e, out=t2[:], in0=PI[:, i, :], in1=PI[:, i, :], op=ALU.mult), r=[PI.name], w=[t2.name])
                    s.I("dve", lambda e, i=i: tt(e, out=PR[:, i + 1, :], in0=t1[:], in1=t2[:], op=ALU.subtract), r=[t1.name, t2.name], w=[PR.name])
                    s.I("dve", lambda e, i=i: tt(e, out=t1[:], in0=PR[:, i, :], in1=PI[:, i, :], op=ALU.mult), r=[PR.name, PI.name], w=[t1.name])
                    s.I("dve", lambda e, i=i: e.tensor_scalar(out=PI[:, i + 1, :], in0=t1[:], scalar1=2.0, scalar2=None, op0=ALU.mult), r=[t1.name], w=[PI.name])
                s.I("dve", lambda e: e.tensor_scalar(out=PI[:], in0=PI[:], scalar1=sgn[:], scalar2=None, op0=ALU.mult), r=[PI.name, sgn.name], w=[PI.name])
                s.barrier()
            Bb = k.sb(es, "Bb", [128, 8, 128], F32)
            with ExitStack() as e2:
                sh = [128, 8 * 64]
                ar = k.sb(e2, "b_ar", sh, F32); ai = k.sb(e2, "b_ai", sh, F32); ld = k.sb(e2, "b_ld", sh, F32)
                br = k.sb(e2, "b_br", sh, F32); bi = k.sb(e2, "b_bi", sh, F32)
                for t_, n_ in ((ar, "ssm_a_re_b"), (ai, "ssm_a_im_b"), (ld, "ssm_ldt_b"), (br, "ssm_b_re_b"), (bi, "ssm_b_im_b")):
                    s.DMA("sp", t_[:], k.din(n_, sh), w=[t_.name])
                abr, abi, lam = k.disc(e2, "d2", ar, ai, ld, sh)
                den = k.sb(e2, "den", sh, F32); u1 = k.sb(e2, "u1", sh, F32); u2 = k.sb(e2, "u2", sh, F32)
                cor = k.sb(e2, "cor", sh, F32); coi = k.sb(e2, "coi", sh, F32)
                D_ = lambda fn, r, w: s.I("dve", fn, r=[x.name for x in r], w=[x.name for x in w])
                D_(lambda e: tt(e, out=den[:], in0=lam[:], in1=lam[:], op=ALU.mult), [lam], [den])
                D_(lambda e: tt(e, out=u1[:], in0=ai[:], in1=ai[:], op=ALU.mult), [ai], [u1])
                D_(lambda e: tt(e, out=den[:], in0=den[:], in1=u1[:], op=ALU.add), [den, u1], [den])
                D_(lambda e: e.reciprocal(out=den[:], in_=den[:]), [den], [den])
                D_(lambda e: e.tensor_scalar(out=abr[:], in0=abr[:], scalar1=-1.0, scalar2=None, op0=ALU.add), [abr], [abr])
                D_(lambda e: tt(e, out=u1[:], in0=abr[:], in1=lam[:], op=ALU.mult), [abr, lam], [u1])
                D_(lambda e: tt(e, out=u2[:], in0=abi[:], in1=ai[:], op=ALU.mult), [abi, ai], [u2])
                D_(lambda e: tt(e, out=u1[:], in0=u1[:], in1=u2[:], op=ALU.add), [u1, u2], [u1])
                D_(lambda e: tt(e, out=cor[:], in0=u1[:], in1=den[:], op=ALU.mult), [u1, den], [cor])
                D_(lambda e: tt(e, out=u1[:], in0=abi[:], in1=lam[:], op=ALU.mult), [abi, lam], [u1])
                D_(lambda e: tt(e, out=u2[:], in0=abr[:], in1=ai[:], op=ALU.mult), [abr, ai], [u2])
                D_(lambda e: tt(e, out=u1[:], in0=u1[:], in1=u2[:], op=ALU.subtract), [u1, u2], [u1])
                D_(lambda e: tt(e, out=coi[:], in0=u1[:], in1=den[:], op=ALU.mult), [u1, den], [coi])
                v3 = lambda t_: t_[:].rearrange("p (q x) -> p q x", q=8)
                D_(lambda e: tt(e, out=u1[:], in0=cor[:], in1=br[:], op=ALU.mult), [cor, br], [u1])
                D_(lambda e: tt(e, out=u2[:], in0=coi[:], in1=bi[:], op=ALU.mult), [coi, bi], [u2])
                D_(lambda e: tt(e, out=Bb[:, :, 0:64], in0=v3(u1), in1=v3(u2), op=ALU.subtract), [u1, u2], [Bb])
                D_(lambda e: tt(e, out=u1[:], in0=cor[:], in1=bi[:], op=ALU.mult), [cor, bi], [u1])
                D_(lambda e: tt(e, out=u2[:], in0=coi[:], in1=br[:], op=ALU.mult), [coi, br], [u2])
                D_(lambda e: tt(e, out=Bb[:, :, 64:128], in0=v3(u1), in1=v3(u2), op=ALU.add), [u1, u2], [Bb])
                s.barrier()
            Ct = k.sb(es, "Ct", [128, 64, 16], F32)
            s.DMA("sp", Ct[:], k.din("ssm_cT2", [128, 64, 16]), w=[Ct.name])
            s.I("dve", lambda e: e.tensor_scalar(out=Ct[:], in0=Ct[:], scalar1=sgn[:], scalar2=None, op0=ALU.mult),
                r=[Ct.name, sgn.name], w=[Ct.name])
            NG = 3
            OW = k.OWN
            us = k.sb(es, "us32", [128, T], F32)
            yacc = k.sb(es, "yacc", [128, T - OW], F32)
            Aa = [k.sb(es, "Ast%d" % i, [128, T], F32) for i in range(NG)]
            Mt = [k.sb(es, "Mt%d" % i, [128, NLV, 128], F32) for i in range(NG)]
            BmP = k.sb(es, "BmP", [128, 8, 128], F32)
            CmP = k.sb(es, "CmP", [128, 8, 128], F32)
            gl = [k.sb(es, "gl%d" % i, [128, 512], F32) for i in range(3)]
            yb = [k.sb(es, "yb%d" % i, [128, 512], BF16) for i in range(2)]
            pc = [0]

            def nps():
                p = psb[pc[0] % 8]; pc[0] += 1
                return p
            ev = [0]

            def evac_copy(dst, src, rk, wk):
                ev[0] += 1
                if ev[0] % 2:
                    s.I("act", lambda e: e.copy(out=dst, in_=src), r=rk, w=wk)
                else:
                    s.I("dve", lambda e: e.tensor_copy(out=dst, in_=src), r=rk, w=wk)

            def group_steps(q, j, slot):
                g = q * 8 + j
                A = Aa[slot]; M = Mt[slot]
                for i in range(NLV):
                    s.I("pool", lambda e, i=i: e.tensor_scalar(out=M[:, i, :], in0=Id[:], scalar1=PR[:, i, g:g + 1],
                                                               scalar2=None, op0=ALU.mult), r=[Id.name, PR.name], w=[M.name])
                    s.I("dve", lambda e, i=i: e.scalar_tensor_tensor(out=M[:, i, :], in0=Jm[:], scalar=PI[:, i, g:g + 1],
                                                                     in1=M[:, i, :], op0=ALU.mult, op1=ALU.add),
                        r=[Jm.name, PI.name, M.name], w=[M.name])
                yield
                for c0 in range(0, T, 512):
                    p = nps()
                    k.MM(p[:], BmP[:, j, :], us[:, c0:c0 + 512], True, True, [BmP, us], [p])
                    evac_copy(A[:, c0:c0 + 512], p[:], [p.name], [A.name])
                    yield
                for l in range(1, LM + 1):
                    st = 1 << l; h = st >> 1; n = T // st
                    for c0 in range(0, n, 512):
                        m_ = min(512, n - c0)
                        src = A[:, h - 1 + c0 * st: h - 1 + (c0 + m_ - 1) * st + 1: st]
                        dst = A[:, st - 1 + c0 * st: st - 1 + (c0 + m_ - 1) * st + 1: st]
                        p = nps()
                        k.MM(p[:, 0:m_], M[:, l - 1, :], src, True, True, [M, A], [p])
                        k.TT("dve", dst, p[:, 0:m_], dst, ALU.add, [p, A], [A])
                        yield
                for l in range(LM - 1, 0, -1):
                    st = 1 << l; h = st >> 1; n = T // st
                    lo_i = max(0, n // 2 - 1)
                    for c0 in range(lo_i, n - 1, 512):
                        m_ = min(512, n - 1 - c0)
                        src = A[:, st - 1 + c0 * st: st - 1 + (c0 + m_ - 1) * st + 1: st]
                        dst = A[:, st + h - 1 + c0 * st: st + h - 1 + (c0 + m_ - 1) * st + 1: st]
                        p = nps()
                        k.MM(p[:, 0:m_], M[:, l - 1, :], src, True, True, [M, A], [p])
                        k.TT("dve", dst, p[:, 0:m_], dst, ALU.add, [p, A], [A])
                        yield
                for c0 in range(OW, T, 512):
                    p = nps()
                    k.MM(p[:], CmP[:, j, :], A[:, c0:c0 + 512], True, True, [CmP, A], [p])
                    if j == 0:
                        evac_copy(yacc[:, c0 - OW:c0 - OW + 512], p[:], [p.name], [yacc.name])
                    else:
                        k.TT("dve", yacc[:, c0 - OW:c0 - OW + 512], p[:], yacc[:, c0 - OW:c0 - OW + 512], ALU.add, [p, yacc], [yacc])
                    yield

            for q in range(8):
                s.DMA("pool", us[:], usT[q * 128:(q + 1) * 128, :], r=["usT"], w=[us.name])
                for j in range(8):
                    s.I("dve", lambda e, j=j, q=q: e.tensor_scalar(out=BmP[:, j, :], in0=Bb[:, q, :], scalar1=mk8[:, j:j + 1],
                                                              scalar2=None, op0=ALU.mult), r=[Bb.name, mk8.name], w=[BmP.name])
                s.I("pool", lambda e: e.memset(CmP[:], 0.0), w=[CmP.name])
                for j in range(8):
                    s.I("pool", lambda e, j=j, q=q: e.tensor_copy(out=CmP[:, j, j * 16:(j + 1) * 16], in_=Ct[:, q * 8 + j, :]),
                        r=[Ct.name], w=[CmP.name])
                j0 = 0
                while j0 < 8:
                    js = list(range(j0, min(8, j0 + NG)))
                    gens = [group_steps(q, j, si) for si, j in enumerate(js)]
                    alive = True
                    while alive:
                        alive = False
                        for gen in gens:
                            try:
                                next(gen); alive = True
                            except StopIteration:
                                pass
                    j0 += len(js)
                for ci, c0 in enumerate(range(k.OWN, T, 512)):
                    y_ = gl[0]; a_ = gl[1]; b_ = gl[2]; o_ = yb[ci % 2]
                    s.I("dve", lambda e, c0=c0, q=q: e.scalar_tensor_tensor(out=y_[:], in0=us[:, c0:c0 + 512], scalar=dsk[:, q:q + 1],
                                                                        in1=yacc[:, c0 - k.OWN:c0 - k.OWN + 512], op0=ALU.mult, op1=ALU.add),
                        r=[us.name, dsk.name, yacc.name], w=[y_.name])
                    s.I("act", lambda e: e.activation(out=a_[:], in_=y_[:], func=AF.Square), r=[y_.name], w=[a_.name])
                    s.I("dve", lambda e: e.tensor_scalar(out=a_[:], in0=a_[:], scalar1=0.044715, scalar2=1.0, op0=ALU.mult, op1=ALU.add),
                        r=[a_.name], w=[a_.name])
                    s.I("dve", lambda e: tt(e, out=b_[:], in0=a_[:], in1=y_[:], op=ALU.mult), r=[a_.name, y_.name], w=[b_.name])
                    s.I("act", lambda e: e.activation(out=b_[:], in_=b_[:], func=AF.Sigmoid, scale=1.5957691216057308), r=[b_.name], w=[b_.name])
                    s.I("dve", lambda e, o_=o_: tt(e, out=o_[:], in0=b_[:], in1=y_[:], op=ALU.mult), r=[b_.name, y_.name], w=[o_.name])
                    d = s.DMA("sp", ysT[q * 128:(q + 1) * 128, c0:c0 + 512], o_[:], r=[o_.name], w=["ysT"])
                    k.final.append(d)
            s.barrier()

    def build(k):
        nc = k.nc; s = k.s; es = k.es; T = k.T
        x = k.din("x", [T, D])
        out = k.nc.dram_tensor("out", [T if k.stages < 3 else T // 2, D], F32, kind="ExternalOutput").ap()
        g = {n: k.din(n, [1, D]) for n in ("ffn1_pre_g", "ffn1_post_g", "mix_pre_g", "mix_post_g",
                                           "ffn2_pre_g", "ffn2_post_g")}
        w1g = k.conv("ffn1_w_gate", [D, DFF], 4)
        w1u = k.conv("ffn1_w_up", [D, DFF], 4)
        w1d = k.conv("ffn1_w_down", [DFF, D], 4)
        psb = [es.enter_context(nc.psum_tensor("psb%d" % i, [128, 512], F32)) for i in range(8)]
        k.epsb = k.sb(es, "epsb", [128, 1], F32)
        s.I("dve", lambda e: e.memset(k.epsb[:], EPS), w=[k.epsb.name])
        identf = k.sb(es, "identf", [128, 128], F32)
        k.ident = k.sb(es, "ident", [128, 128], BF16)
        s.I("pool", lambda e: e.memset(identf[:], 1.0), w=[identf.name])
        s.I("pool", lambda e: e.affine_select(out=identf[:], in_=identf[:], pattern=[[-1, 128]],
                                              compare_op=ALU.is_equal, fill=0.0, base=0, channel_multiplier=1),
            r=[identf.name], w=[identf.name])
        s.I("dve", lambda e: e.tensor_copy(out=k.ident[:], in_=identf[:]), r=[identf.name], w=["ident"])
        if k.stages == 1:
            k.ffn("f1", x, out, g["ffn1_pre_g"][0, :], g["ffn1_post_g"][0, :], w1g, w1u, w1d, psb)
        if k.stages == 2:
            usT = k.din("usT", [1024, T])
            ysT = k.nc.dram_tensor("ysT", [1024, T], BF16, kind="ExternalOutput").ap()
            k.s5(usT, ysT, psb)
        if k.stages >= 3:
            w = {}
            win = k.conv("w_in", [D, INC], 4)
            w["glu1"] = k.conv("ssm_glu_w1", [1024, D], 2); w["glu2"] = k.conv("ssm_glu_w2", [1024, D], 2)
            w["wo"] = k.conv("nsa_w_o", [1024, D], 2); w["wout"] = k.conv("w_out", [D, D], 2)
            w["ck1"] = k.conv("cmp_k_w1", [2048, 256]); w["ck2"] = k.conv("cmp_k_w2", [256, 64])
            w["cv1"] = k.conv("cmp_v_w1", [2048, 256]); w["cv2"] = k.conv("cmp_v_w2", [256, 64])
            w2g = k.conv("ffn2_w_gate", [D, DFF], 4); w2u = k.conv("ffn2_w_up", [D, DFF], 4); w2d = k.conv("ffn2_w_down", [DFF, D], 4)
            sc = {}
            for nm_, shp, dt_ in (("h1", [T, D], F32), ("h2", [T, D], F32), ("usT", [1024, T], F32), ("ysT", [1024, T], BF16),
                                  ("qT", [16, 64, T], BF16), ("kcT", [4, 64, T], BF16), ("vcT", [4, 64, T], BF16),
                                  ("ksT", [4, 64, T], BF16), ("kwT", [4, 64, T], BF16), ("Vs", [T, 256], BF16), ("Vw", [T, 256], BF16),
                                  ("gates", [T, 48], F32), ("gaT", [D, T], BF16), ("gbT", [D, T], BF16), ("onsa", [T, 1024], BF16)):
                if k.debug:
                    sc[nm_] = k.nc.dram_tensor("dbg_" + nm_, list(shp), dt_, kind="ExternalOutput").ap()
                else:
                    sc[nm_] = k.dscr("sc_" + nm_, shp, dt_)
            k.ffn("f1", x, sc["h1"], g["ffn1_pre_g"][0, :], g["ffn1_post_g"][0, :], w1g, w1u, w1d, psb, "x", "h1",
                  wkeys=("ffn1_w_gate_b", "ffn1_w_up_b", "ffn1_w_down_b"))
            k.proj(sc["h1"], g["mix_pre_g"][0, :], win, sc, psb)
            k.s5(sc["usT"], sc["ysT"], psb)
            with ExitStack() as em:
                KcT, Vc = k.compress(em, sc, w, psb)
                SKS, SKW, WS, WW = k.bias_tables(psb)
                k.nsa(sc, KcT, Vc, SKS, SKW, WS, WW, psb)
            k.merge(sc, w, sc["h1"], sc["h2"], g["mix_post_g"][0, :], psb)
            k.ffn("f2", sc["h2"], out, g["ffn2_pre_g"][0, :], g["ffn2_post_g"][0, :], w2g, w2u, w2d, psb, "h2", "out",
                  tiles=range(k.NT // 2, k.NT), dst_off=k.OWN)
        s.finish(k.final)
        es.close()
        return nc


def ssm_layouts(a_re, a_im, log_dt, b_re, b_im, c_re, c_im, d):
    a_re = np.asarray(a_re, np.float32); a_im = np.asarray(a_im, np.float32); log_dt = np.asarray(log_dt, np.float32)
    b_re = np.asarray(b_re, np.float32); b_im = np.asarray(b_im, np.float32)
    c_re = np.asarray(c_re, np.float32); c_im = np.asarray(c_im, np.float32); d = np.asarray(d, np.float32)
    m = {}
    m["ssm_aT_re2"] = np.concatenate([a_re.T, a_re.T], 0)
    m["ssm_aT_im2"] = np.concatenate([a_im.T, a_im.T], 0)
    m["ssm_ldt2"] = np.broadcast_to(log_dt[None, :], (128, 64))
    def lay_a(a):
        t = a.reshape(8, 8, 64)
        t = np.transpose(t, (1, 0, 2))
        return np.broadcast_to(t[:, None], (8, 16, 8, 64)).reshape(128, 512)
    m["ssm_a_re_b"] = lay_a(a_re); m["ssm_a_im_b"] = lay_a(a_im)
    m["ssm_ldt_b"] = lay_a(np.broadcast_to(log_dt[:, None], (64, 64)))
    def lay_b(b):
        t = b.reshape(8, 8, 64, 16)
        t = np.transpose(t, (1, 3, 0, 2))
        return t.reshape(128, 512)
    m["ssm_b_re_b"] = lay_b(b_re); m["ssm_b_im_b"] = lay_b(b_im)
    m["ssm_cT2"] = np.concatenate([np.transpose(c_re, (2, 0, 1)), np.transpose(c_im, (2, 0, 1))], 0)
    m["ssm_dT"] = d.reshape(8, 128).T
    m["c_sgn"] = np.concatenate([np.ones((64, 1)), -np.ones((64, 1))], 0)
    m["c_mask8"] = (np.arange(128)[:, None] // 16 == np.arange(8)[None, :]).astype(np.float32)
    m["c_I"] = np.eye(128)
    m["c_J"] = np.roll(np.eye(128), 64, axis=1)
    return {k_: np.ascontiguousarray(v, dtype=np.float32) for k_, v in m.items()}


def _bucket(d):
    d = np.maximum(d, 0)
    d_f = np.maximum(d, 1).astype(np.float32)
    large = 16 + (np.log(d_f / np.float32(16)) / np.float32(math.log(1024 / 16)) * np.float32(16)).astype(np.int32)
    large = np.minimum(large, 31)
    return np.where(d < 16, d, large)


def nsa_consts():
    m = {}
    WS, WW = 1152, 768
    x = np.arange(WS); d = 1023 - x
    oh = np.zeros((33, WS), np.float32)
    bk = _bucket(d)
    for b in range(32):
        oh[b] = ((d >= 0) & (bk == b))
    oh[32] = np.where(d < 0, -BIG, 0.0)
    m["c_ohs"] = oh
    x = np.arange(WW); d = 639 - x
    oh = np.zeros((33, WW), np.float32)
    bk = _bucket(d)
    ok = (d >= 0) & (d < 512)
    for b in range(32):
        oh[b] = (ok & (bk == b))
    oh[32] = np.where(ok, 0.0, -BIG)
    m["c_ohw"] = oh
    vm = np.zeros((128, 256), np.float32); ac = np.zeros((128, 256), np.float32)
    c = np.arange(256)
    for qi in range(128):
        hi = qi >= 64
        forced = (c == 127) | ((c == 128) if hi else (c == 126))
        invalid = (c > 128) | ((c == 128) & (not hi))
        vm[qi] = (~forced & ~invalid)
        ac[qi] = np.where(forced, 1e4, np.where(invalid, -1e4, 0.0))
    m["c_vmrel"] = vm; m["c_acrel"] = ac
    return m


def percore_consts(T, half):
    m = {}
    nbp = T // 128
    cm = np.ones((128, 128), np.float32); ca = np.zeros((128, 128), np.float32)
    if half == 0:
        cm[:, 0:nbp] = 0.0; ca[:, 0:nbp] = -2e4
        cm[:, nbp] = 0.0; ca[:, nbp] = 1e4
    else:
        cm[:, 0] = 0.0; ca[:, 0] = 1e4
    m["pc_cm"] = cm; m["pc_ca"] = ca
    cmpm = np.zeros((128, 512), np.float32)
    if half == 0:
        cmpm[:, 0:T // 32] = -BIG
    m["pc_cmpmask"] = cmpm
    m["pc_wmask"] = np.full((128, 1), -BIG if half == 0 else 0.0, np.float32)
    return m


def host_inputs(inputs, names):
    m = {}
    sq = lambda n: np.asarray(inputs[n], np.float32)[0]
    lay = ssm_layouts(sq("ssm_a_re"), sq("ssm_a_im"), sq("ssm_log_dt"), sq("ssm_b_re"), sq("ssm_b_im"),
                      sq("ssm_c_re"), sq("ssm_c_im"), sq("ssm_d"))
    lay.update(nsa_consts())
    lay["cmp_posT"] = np.ascontiguousarray(sq("cmp_pos").T)
    for n in names:
        if n == "x":
            continue
        if n in lay:
            m[n] = np.ascontiguousarray(lay[n], dtype=np.float32)
        elif n == "rel_bias":
            m[n] = np.ascontiguousarray(np.asarray(inputs[n], np.float32))
        elif n.endswith("_g"):
            m[n] = np.ascontiguousarray(np.asarray(inputs[n], np.float32).reshape(1, D))
        else:
            m[n] = np.ascontiguousarray(sq(n))
    return m

_CACHE = {}


def _get(T, stages, debug=False):
    key = (T, stages, debug)
    if key not in _CACHE:
        kb = K(T, stages, debug)
        kb.build()
        _CACHE[key] = kb
    return _CACHE[key]


def run(inputs, T, stages, ncores, debug=False):
    kb = _get(T, stages, debug)
    xs = np.asarray(inputs["x"], np.float32)
    in_maps = []
    if stages >= 3:
        shared = host_inputs(inputs, [n for n in kb.inp if n != "x" and not n.startswith("pc_")])
        pcs = [percore_consts(T, 0), percore_consts(T, 1)]
        for c in range(ncores):
            b, half = c // 2, c % 2
            m = dict(shared)
            m.update(pcs[half])
            xb = xs[b % xs.shape[0]].reshape(T, D)
            if half == 0:
                xl = np.concatenate([np.zeros((T // 2, D), np.float32), xb[:T // 2]], 0)
            else:
                xl = xb
            m["x"] = np.ascontiguousarray(xl)
            in_maps.append(m)
    else:
        for c in range(ncores):
            m = {}
            for name in kb.inp:
                if name == "x":
                    continue
                a = np.asarray(inputs[name], np.float32)
                m[name] = np.ascontiguousarray(a.reshape(kb.inp[name].shape))
            m["x"] = np.ascontiguousarray(xs[c % xs.shape[0]].reshape(T, D))
            in_maps.append(m)
    res = run_bass_kernel_spmd(kb.nc, in_maps, core_ids=list(range(ncores)))
    if debug:
        return res.results
    return [r["out"] for r in res.results]


def kernel(**inputs):
    outs = run(inputs, SEQ, 3, 8)
    full = np.empty((NB, SEQ, D), np.float32)
    for c in range(8):
        b, half = c // 2, c % 2
        full[b, half * (SEQ // 2):(half + 1) * (SEQ // 2)] = outs[c]
    return full
```
